# Optimizing a Trainium2 kernel written in Bass

```python
import math
import jax, jax.numpy as jnp
from jax import lax
import numpy as np

D_MODEL = 1024
BATCH = 16
SEQ = 256
DEPTH = 4
DEC_BATCH = 4
DEC_SEQ = 1024
PAST_LEN = 512

GRID_W = 64
HEAD_DIM = 64
A_HEADS = 8
A_KV_HEADS = 2
A_REP = A_HEADS // A_KV_HEADS
A_WIDTH = A_HEADS * HEAD_DIM
A_KV_WIDTH = A_KV_HEADS * HEAD_DIM
Q_BLOCK = 128
ROPE_THETA = 10000.0
B_HEADS = 4
B_DK = 32
B_DV = 64
B_QK_WIDTH = B_HEADS * B_DK
B_WIDTH = B_HEADS * B_DV
GATE_RANK = 16
GATE_TAU = 16.0
C_HEADS = 4
C_DK = 64
C_DV = 64
C_QK_WIDTH = C_HEADS * C_DK
C_WIDTH = C_HEADS * C_DV
CONV_WIDTH = 5
CONV_CH = 2 * C_QK_WIDTH + C_WIDTH
CHUNK = 64
N_DIR = 2
MIX_WIDTH = A_WIDTH + B_WIDTH + C_WIDTH
D_FF = -(-(8 * D_MODEL) // (3 * 256)) * 256
PROJ_SIZES = (A_WIDTH, A_KV_WIDTH, A_KV_WIDTH,
              B_QK_WIDTH, B_QK_WIDTH, B_WIDTH, B_WIDTH, N_DIR * GATE_RANK,
              C_QK_WIDTH, C_QK_WIDTH, C_WIDTH, C_WIDTH, N_DIR * C_HEADS, N_DIR * C_HEADS)
PROJ_WIDTH = sum(PROJ_SIZES)
EPS = 1e-6

kernel_name = 'hybrid_dit_gqa_gla_deltanet_step'


def rms_norm(x, g):
    xf = x.astype(jnp.float32)
    y = xf * lax.rsqrt(jnp.mean(xf * xf, axis=-1, keepdims=True) + EPS)
    return (y * g.astype(jnp.float32)).astype(x.dtype)


def l2_normalize(x):
    return x * lax.rsqrt(jnp.sum(x * x, axis=-1, keepdims=True) + EPS)


def split_cols(z, sizes):
    return jnp.split(z, np.cumsum(sizes)[:-1].tolist(), axis=-1)


def to_heads(z, n):
    b, t, _ = z.shape
    return z.reshape(b, t, n, -1).transpose(0, 2, 1, 3).astype(jnp.float32)


def axial_rope(n_tokens):
    rows = n_tokens // GRID_W
    row = jnp.repeat(jnp.arange(rows, dtype=jnp.float32), GRID_W)
    col = jnp.tile(jnp.arange(GRID_W, dtype=jnp.float32), rows)
    n_freq = HEAD_DIM // 4
    inv_freq = ROPE_THETA ** (-jnp.arange(n_freq, dtype=jnp.float32) / n_freq)
    ang_r = row[:, None] * inv_freq
    ang_c = col[:, None] * inv_freq
    ang = jnp.concatenate([ang_r, ang_r, ang_c, ang_c], axis=-1)
    return jnp.cos(ang), jnp.sin(ang)


def apply_rope(x, cos, sin):
    a1, a2, b1, b2 = jnp.split(x, 4, axis=-1)
    rot = jnp.concatenate([-a2, a1, -b2, b1], axis=-1)
    y = x.astype(jnp.float32) * cos[:, None, :] + rot.astype(jnp.float32) * sin[:, None, :]
    return y.astype(x.dtype)


def attend_blocks(q, k, v):
    bsz, t = q.shape[0], q.shape[1]
    nb = t // Q_BLOCK
    qb = q.reshape(bsz, nb, Q_BLOCK, A_KV_HEADS, A_REP, HEAD_DIM).transpose(1, 0, 2, 3, 4, 5)
    kf = k.astype(jnp.float32)
    vf = v.astype(jnp.float32)
    scale = HEAD_DIM ** -0.5

    def one_block(q_blk):
        s = jnp.einsum('bqgrd,bsgd->bgrqs', q_blk.astype(jnp.float32), kf) * scale
        p = jax.nn.softmax(s, axis=-1)
        return jnp.einsum('bgrqs,bsgd->bqgrd', p, vf).astype(q.dtype)

    o = lax.map(one_block, qb)
    return o.transpose(1, 0, 2, 3, 4, 5).reshape(bsz, t, A_WIDTH)


def to_chunks(a):
    n = a.shape[2] // CHUNK
    return jnp.moveaxis(a.reshape(a.shape[:2] + (n, CHUNK) + a.shape[3:]), 2, 0)


def from_chunks(o):
    o = jnp.moveaxis(o, 0, 2)
    return o.reshape(o.shape[:2] + (-1,) + o.shape[4:])


def gla_scan(q, k, v, g, s0):
    causal = jnp.tril(jnp.ones((CHUNK, CHUNK), dtype=bool))[:, :, None]

    def body(s, inp):
        qc, kc, vc, gc = inp
        b = jnp.cumsum(gc, axis=2)
        diff = b[:, :, :, None, :] - b[:, :, None, :, :]
        decay = jnp.exp(jnp.where(causal, diff, -jnp.inf))
        att = jnp.einsum('bhtk,bhsk,bhtsk->bhts', qc, kc, decay)
        o = (jnp.einsum('bhtk,bhkv->bhtv', qc * jnp.exp(b), s)
             + jnp.einsum('bhts,bhsv->bhtv', att, vc))
        bl = b[:, :, -1:, :]
        s = (jnp.exp(bl)[:, :, 0, :, None] * s
             + jnp.einsum('bhsk,bhsv->bhkv', kc * jnp.exp(bl - b), vc))
        return s, o

    s, o = lax.scan(body, s0, (to_chunks(q), to_chunks(k), to_chunks(v), to_chunks(g)))
    return from_chunks(o), s


def delta_scan(q, k, v, beta, g, s0):
    incl = jnp.tril(jnp.ones((CHUNK, CHUNK), dtype=bool))
    strict = jnp.tril(jnp.ones((CHUNK, CHUNK), dtype=bool), -1)
    eye = jnp.eye(CHUNK, dtype=jnp.float32)
    dv = v.shape[-1]

    def body(s, inp):
        qc, kc, vc, bc, gc = inp
        gam = jnp.cumsum(gc, axis=-1)
        diff = gam[..., :, None] - gam[..., None, :]
        dec = jnp.exp(jnp.where(incl, diff, -jnp.inf))
        kb = kc * bc[..., None]
        m = jnp.where(strict, jnp.einsum('bhtk,bhsk->bhts', kb, kc) * dec, 0.0)
        rhs = jnp.concatenate([vc * bc[..., None], kb * jnp.exp(gam)[..., None]], axis=-1)
        sol = lax.linalg.triangular_solve(eye + m, rhs, left_side=True, lower=True,
                                          unit_diagonal=True)
        u, w = sol[..., :dv], sol[..., dv:]
        v_new = u - jnp.einsum('bhtk,bhkv->bhtv', w, s)
        att = jnp.einsum('bhtk,bhsk->bhts', qc, kc) * dec
        o = (jnp.einsum('bhtk,bhkv->bhtv', qc * jnp.exp(gam)[..., None], s)
             + jnp.einsum('bhts,bhsv->bhtv', att, v_new))
        gl = gam[..., -1:]
        s = (jnp.exp(gl)[..., None] * s
             + jnp.einsum('bhsk,bhsv->bhkv', kc * jnp.exp(gl - gam)[..., None], v_new))
        return s, o

    s, o = lax.scan(body, s0, (to_chunks(q), to_chunks(k), to_chunks(v),
                              to_chunks(beta), to_chunks(g)))
    return from_chunks(o), s


def run_direction(scan_fn, arrays, s0, reverse):
    if reverse:
        arrays = tuple(jnp.flip(a, axis=2) for a in arrays)
    o, s = scan_fn(*arrays, s0)
    if reverse:
        o = jnp.flip(o, axis=2)
    return o, s


def short_conv(z, w):
    y = lax.conv_general_dilated(
        z, w[:, None, :].astype(z.dtype), window_strides=(1,),
        padding=((CONV_WIDTH // 2, CONV_WIDTH // 2),),
        dimension_numbers=('NWC', 'WIO', 'NWC'), feature_group_count=z.shape[-1])
    return jax.nn.silu(y)


def mixer(h, w_in, qk_g, w_gg, b_gg, gla_g, conv_w, a_log, dt_bias, delta_g, w_out,
          ctx_kv, s0_gla, s0_delta, rope):
    f32 = jnp.float32
    bsz, t, _ = h.shape
    (qa, ka, va, qb, kb, vb, rb, gcode, qc, kc, vc, gc, bc, ac) = split_cols(h @ w_in, PROJ_SIZES)

    qa = rms_norm(qa.reshape(bsz, t, A_HEADS, HEAD_DIM), qk_g[0])
    ka = rms_norm(ka.reshape(bsz, t, A_KV_HEADS, HEAD_DIM), qk_g[1])
    va = va.reshape(bsz, t, A_KV_HEADS, HEAD_DIM)
    if ctx_kv is None:
        oa = attend_blocks(qa, ka, va)
    else:
        cos, sin = rope
        k_all = jnp.concatenate([ctx_kv[0].astype(ka.dtype), apply_rope(ka, cos, sin)], axis=1)
        v_all = jnp.concatenate([ctx_kv[1].astype(va.dtype), va], axis=1)
        oa = attend_blocks(apply_rope(qa, cos, sin), k_all, v_all)

    qb = to_heads(qb, B_HEADS) * B_DK ** -0.5
    kb = to_heads(kb, B_HEADS)
    vb = to_heads(vb, B_HEADS)
    glog = jax.nn.log_sigmoid(
        jnp.einsum('btdr,drk->btdk', gcode.reshape(bsz, t, N_DIR, GATE_RANK).astype(f32),
                   w_gg.astype(f32)) + b_gg.astype(f32)) / GATE_TAU
    ob = 0.0
    s_gla = []
    for d in range(N_DIR):
        o_d, s_d = run_direction(gla_scan, (qb, kb, vb, to_heads(glog[:, :, d], B_HEADS)),
                                 s0_gla[:, d], d == 1)
        ob = ob + o_d
        s_gla.append(s_d)
    ob = rms_norm(ob.transpose(0, 2, 1, 3), gla_g).reshape(bsz, t, B_WIDTH) * jax.nn.silu(rb.astype(f32))

    qkv = short_conv(jnp.concatenate([qc, kc, vc], axis=-1), conv_w)
    qc, kc, vc = split_cols(qkv, (C_QK_WIDTH, C_QK_WIDTH, C_WIDTH))
    qc = l2_normalize(to_heads(qc, C_HEADS)) * C_DK ** -0.5
    kc = l2_normalize(to_heads(kc, C_HEADS))
    vc = to_heads(vc, C_HEADS)
    beta = jax.nn.sigmoid(bc.reshape(bsz, t, N_DIR, C_HEADS).astype(f32))
    glog_c = -jnp.exp(a_log.astype(f32)) * jax.nn.softplus(
        ac.reshape(bsz, t, N_DIR, C_HEADS).astype(f32) + dt_bias.astype(f32))
    oc = 0.0
    s_delta = []
    for d in range(N_DIR):
        o_d, s_d = run_direction(delta_scan,
                                 (qc, kc, vc, beta[:, :, d].transpose(0, 2, 1),
                                  glog_c[:, :, d].transpose(0, 2, 1)),
                                 s0_delta[:, d], d == 1)
        oc = oc + o_d
        s_delta.append(s_d)
    oc = rms_norm(oc.transpose(0, 2, 1, 3), delta_g).reshape(bsz, t, C_WIDTH) * jax.nn.silu(gc.astype(f32))

    out = jnp.concatenate([oa, ob.astype(h.dtype), oc.astype(h.dtype)], axis=-1) @ w_out
    return out, ka, va, jnp.stack(s_gla, axis=1), jnp.stack(s_delta, axis=1)


def block(x, mod, lw, ctx_kv, s0_gla, s0_delta, rope):
    (norm_g, w_in, qk_g, w_gg, b_gg, gla_g, conv_w, a_log, dt_bias, delta_g, w_out,
     w_gate, w_up, w_down) = lw
    shift_m, scale_m, gate_m, shift_f, scale_f, gate_f = jnp.split(mod, 6, axis=-1)
    h = rms_norm(x, norm_g[0]) * (1 + scale_m) + shift_m
    mix, k_l, v_l, s_gla, s_delta = mixer(h, w_in, qk_g, w_gg, b_gg, gla_g, conv_w, a_log,
                                          dt_bias, delta_g, w_out, ctx_kv, s0_gla, s0_delta, rope)
    x = x + gate_m * rms_norm(mix, norm_g[1])
    h = rms_norm(x, norm_g[2]) * (1 + scale_f) + shift_f
    f = (jax.nn.silu(h @ w_gate) * (h @ w_up)) @ w_down
    x = x + gate_f * rms_norm(f, norm_g[3])
    return x, k_l, v_l, s_gla, s_delta


def setup_inputs(seed: int = 0) -> dict:
    key = jax.random.key(seed)
    ks = jax.random.split(key, 26)
    f32 = jnp.float32

    def nrm(k, shape, s):
        return jax.random.normal(k, shape, f32) * s

    dt = jnp.exp(jax.random.uniform(ks[18], (DEPTH, N_DIR, C_HEADS), f32,
                                    math.log(1e-3), math.log(1e-1)))
    return {
        'x_prompt': nrm(ks[0], (BATCH, SEQ, D_MODEL), 1.0),
        'x_sample': nrm(ks[1], (DEC_BATCH, DEC_SEQ, D_MODEL), 1.0),
        'cache_k': nrm(ks[3], (DEC_BATCH, DEPTH, PAST_LEN, A_KV_HEADS, HEAD_DIM), 1.0),
        'cache_v': nrm(ks[4], (DEC_BATCH, DEPTH, PAST_LEN, A_KV_HEADS, HEAD_DIM), 1.0),
        'state_gla': nrm(ks[5], (DEC_BATCH, DEPTH, N_DIR, B_HEADS, B_DK, B_DV), 0.3),
        'state_delta': nrm(ks[6], (DEC_BATCH, DEPTH, N_DIR, C_HEADS, C_DK, C_DV), 0.3),
        'c': nrm(ks[2], (DEC_BATCH, D_MODEL), 1.0),
        'c_ctx': nrm(ks[7], (D_MODEL,), 1.0),
        'w_mod': nrm(ks[8], (DEPTH, D_MODEL, 6 * D_MODEL), D_MODEL ** -0.5),
        'b_mod': nrm(ks[9], (DEPTH, 6 * D_MODEL), 0.02),
        'norm_gains': 1.0 + nrm(ks[10], (DEPTH, 4, D_MODEL), 0.05),
        'w_in': nrm(ks[11], (DEPTH, D_MODEL, PROJ_WIDTH), D_MODEL ** -0.5),
        'qk_gain': 1.0 + nrm(ks[12], (DEPTH, 2, HEAD_DIM), 0.05),
        'w_gla_gate': nrm(ks[13], (DEPTH, N_DIR, GATE_RANK, B_QK_WIDTH), GATE_RANK ** -0.5),
        'b_gla_gate': nrm(ks[14], (DEPTH, N_DIR, B_QK_WIDTH), 0.1),
        'gla_norm': 1.0 + nrm(ks[15], (DEPTH, B_DV), 0.05),
        'conv_w': nrm(ks[16], (DEPTH, CONV_WIDTH, CONV_CH), CONV_WIDTH ** -0.5),
        'a_log': jnp.log(jax.random.uniform(ks[17], (DEPTH, N_DIR, C_HEADS), f32, 1.0, 16.0)),
        'dt_bias': dt + jnp.log(-jnp.expm1(-dt)),
        'delta_norm': 1.0 + nrm(ks[19], (DEPTH, C_DV), 0.05),
        'w_out': nrm(ks[20], (DEPTH, MIX_WIDTH, D_MODEL), MIX_WIDTH ** -0.5),
        'w_gate': nrm(ks[21], (DEPTH, D_MODEL, D_FF), D_MODEL ** -0.5),
        'w_up': nrm(ks[22], (DEPTH, D_MODEL, D_FF), D_MODEL ** -0.5),
        'w_down': nrm(ks[23], (DEPTH, D_FF, D_MODEL), D_FF ** -0.5),
    }


def reference(x_prompt, x_sample, cache_k, cache_v, state_gla, state_delta, c, c_ctx,
              w_mod, b_mod, norm_gains, w_in, qk_gain, w_gla_gate, b_gla_gate, gla_norm,
              conv_w, a_log, dt_bias, delta_norm, w_out, w_gate, w_up, w_down):
    f32 = jnp.float32
    rope = axial_rope(x_sample.shape[1])
    n_ctx_req = x_prompt.shape[0]
    zero_gla = jnp.zeros((n_ctx_req, N_DIR, B_HEADS, B_DK, B_DV), f32)
    zero_delta = jnp.zeros((n_ctx_req, N_DIR, C_HEADS, C_DK, C_DV), f32)
    xp, xs = x_prompt, x_sample
    new_k, new_v, new_gla, new_delta = [], [], [], []
    for l in range(DEPTH):
        lw = (norm_gains[l], w_in[l], qk_gain[l], w_gla_gate[l], b_gla_gate[l], gla_norm[l],
              conv_w[l], a_log[l], dt_bias[l], delta_norm[l], w_out[l],
              w_gate[l], w_up[l], w_down[l])
        mod_ctx = (jax.nn.silu(c_ctx) @ w_mod[l] + b_mod[l])[None, None, :]
        xp, k_l, v_l, sg_l, sd_l = block(xp, mod_ctx, lw, None, zero_gla, zero_delta, None)
        new_k.append(k_l)
        new_v.append(v_l)
        new_gla.append(sg_l)
        new_delta.append(sd_l)
        mod_lat = (jax.nn.silu(c) @ w_mod[l] + b_mod[l])[:, None, :]
        xs = block(xs, mod_lat, lw, (cache_k[:, l], cache_v[:, l]),
                   state_gla[:, l].astype(f32), state_delta[:, l].astype(f32), rope)[0]
    out_dtype = x_prompt.dtype
    new_cache_k = jnp.stack(new_k, axis=1)
    new_cache_v = jnp.stack(new_v, axis=1)
    new_state_gla = jnp.stack(new_gla, axis=1).astype(out_dtype)
    new_state_delta = jnp.stack(new_delta, axis=1).astype(out_dtype)
    return (xp, xs, new_cache_k, new_cache_v, new_state_gla, new_state_delta)
```

```python
import numpy as np
import concourse.bass as bass
import concourse.mybir as mybir
from concourse.bass_utils import run_bass_kernel_spmd

F32 = mybir.dt.float32
BF16 = mybir.dt.bfloat16
AF = mybir.ActivationFunctionType
ALU = mybir.AluOpType
AX = mybir.AxisListType

DEPTH = 4
D = 1024
T = 1024
NT = 8
DFF = 2816
NFC = 22
PROJ = 2608
EPS = 1e-6
NEG = -30000.0


STRICT_SAME_ENGINE = True


class Op:
    __slots__ = ("eng", "fn", "deps", "sig", "sigval", "dma", "dsem", "dval", "name")

    def __init__(self, eng, fn, dma, name):
        self.eng = eng
        self.fn = fn
        self.dma = dma
        self.deps = []
        self.sig = False
        self.sigval = 0
        self.dsem = None
        self.dval = 0
        self.name = name


class Sched:
    ENGS = ("tensor", "vector", "scalar", "gpsimd", "sync")

    def __init__(self, nc):
        self.nc = nc
        self.ops = []
        self.writers = {}
        self.readers = {}
        self.prev_readers = {}
        self.dsems = {}
        self.store_ops = []

    def _add(self, op, reads, writes, partial):
        reads = list(reads)
        if op.name != "barrier":
            for k in list(reads) + list(writes):
                nm = k[0] if isinstance(k, tuple) else k
                if nm.startswith("M_") or nm.startswith("F_"):
                    reads.append("ARENA")
                    break
        for r in reads:
            for w in self.writers.get(r, ()):
                op.deps.append((w, "raw"))
            self.readers.setdefault(r, []).append(op)
        for r in writes:
            rd = self.readers.get(r)
            if rd:
                for x in rd:
                    if x is not op:
                        op.deps.append((x, "war"))
                for x in self.writers.get(r, ()):
                    if x is not op:
                        op.deps.append((x, "war"))
                self.prev_readers[r] = [x for x in rd if x is not op] + list(self.writers.get(r, ()))
                self.readers[r] = []
                self.writers[r] = [op]
            else:
                if partial:
                    for x in self.prev_readers.get(r, ()):
                        op.deps.append((x, "war"))
                    self.writers.setdefault(r, []).append(op)
                else:
                    for x in self.writers.get(r, ()):
                        op.deps.append((x, "war"))
                    self.prev_readers[r] = list(self.writers.get(r, ()))
                    self.writers[r] = [op]
        self.ops.append(op)
        return op

    def op(self, eng, fn, reads=(), writes=(), partial=False, name=""):
        return self._add(Op(eng, fn, False, name), reads, writes, partial)

    def dma(self, eng, fn, semkey, reads=(), writes=(), partial=False, store=False, name=""):
        o = Op(eng, fn, True, name)
        ent = self.dsems.setdefault(semkey, [None, 0])
        ent[1] += 16
        o.dsem = semkey
        o.dval = ent[1]
        if store:
            self.store_ops.append(o)
        return self._add(o, reads, writes, partial)

    def barrier(self, eng, fn):
        return self._add(Op(eng, fn, False, "barrier"), (), ["ARENA"], False)

    def emit(self):
        nc = self.nc
        per_eng = {e: [] for e in self.ENGS}
        for o in self.ops:
            per_eng[o.eng].append(o)
        for o in self.ops:
            for (p, kind) in o.deps:
                if p.dma:
                    continue
                if p.eng == o.eng and not o.dma and (p.eng == "tensor" or (kind != "raw" and not STRICT_SAME_ENGINE)):
                    continue
                p.sig = True
        for e in self.ENGS:
            c = 0
            for o in per_eng[e]:
                if o.sig:
                    c += 1
                    o.sigval = c
        esem = {e: nc.alloc_semaphore(name="es_" + e) for e in self.ENGS}
        for i, (k, ent) in enumerate(self.dsems.items()):
            ent[0] = nc.alloc_semaphore(name="ds_%d" % i)
        stats = {e: [0, 0] for e in self.ENGS}

        def emit_engine(ename, eng):
            seen = {}
            for o in per_eng[ename]:
                need = {}
                for (p, kind) in o.deps:
                    if p.dma:
                        key = ("d", p.dsem)
                        sem = self.dsems[p.dsem][0]
                        val = p.dval
                    else:
                        if p.eng == ename and not o.dma and (ename == "tensor" or (kind != "raw" and not STRICT_SAME_ENGINE)):
                            continue
                        key = ("e", p.eng)
                        sem = esem[p.eng]
                        val = p.sigval
                    if seen.get(key, 0) >= val:
                        continue
                    if key not in need or need[key][1] < val:
                        need[key] = (sem, val)
                for key, (sem, val) in need.items():
                    eng.wait_ge(sem, val)
                    seen[key] = val
                    stats[ename][1] += 1
                ins = o.fn(eng)
                stats[ename][0] += 1
                if o.dma:
                    ins.then_inc(self.dsems[o.dsem][0], 16)
                elif o.sig:
                    ins.then_inc(esem[ename], 1)
            if ename == "sync":
                fin = {}
                for o in self.store_ops:
                    fin[o.dsem] = max(fin.get(o.dsem, 0), o.dval)
                for k, v in fin.items():
                    if seen.get(("d", k), 0) < v:
                        eng.wait_ge(self.dsems[k][0], v)

        with nc.Block() as block:
            @block.tensor
            def _(eng):
                emit_engine("tensor", eng)

            @block.vector
            def _(eng):
                emit_engine("vector", eng)

            @block.scalar
            def _(eng):
                emit_engine("scalar", eng)

            @block.gpsimd
            def _(eng):
                emit_engine("gpsimd", eng)

            @block.sync
            def _(eng):
                emit_engine("sync", eng)
        self.stats = stats
        return stats


W_IN_PIECES = [(0, 512), (512, 768), (768, 1280), (1280, 1568), (1568, 2080), (2080, 2336), (2336, 2608)]


class Builder:
    def __init__(self, depth=DEPTH, debug=()):
        self.depth = depth
        self.debug = set(debug)
        nc = bass.Bass("TRN2", target_bir_lowering=False)
        self.nc = nc
        self.S = Sched(nc)
        self.ring_i = 0
        self.ps_i = 0
        self.dbg_out = {}
        self.declare_io()
        self.alloc()

    def din(self, name, shape, dt=F32):
        return self.nc.dram_tensor(name, list(shape), dt, kind="ExternalInput").ap()

    def dout(self, name, shape, dt=F32):
        return self.nc.dram_tensor(name, list(shape), dt, kind="ExternalOutput").ap()

    def declare_io(self):
        L = self.depth
        self.x_in = self.din("x", [T, D])
        self.condT = self.din("condT", [128, 8])
        self.ident = self.din("ident", [128, 128])
        self.ctx_k = self.din("ctx_k", [L, 512, 128])
        self.ctx_v = self.din("ctx_v", [L, 512, 128])
        self.s0_gla = self.din("s0_gla", [L, 2, 128, 64])
        self.s0_delta = self.din("s0_delta", [L, 2, 256, 64])
        self.w_mod = self.din("w_mod", [L, D, 6 * D])
        self.bmodT = self.din("bmodT", [L, 128, 48])
        self.bmod = self.din("bmod", [L, 6 * D])
        self.ngT = self.din("ngT", [L, 128, 4, 8])
        self.ng = self.din("ng", [L, 4, D])
        self.w_in = self.din("w_in", [L, D, PROJ])
        self.qkg = self.din("qkg", [L, 640])
        self.wgg = self.din("wgg", [L, 33, 256])
        self.gla_norm = self.din("gla_norm", [L, 64])
        self.cw = self.din("cw", [L, 128, 6, 5])
        self.alog = self.din("alog", [L, 8])
        self.dtb = self.din("dtb", [L, 8])
        self.delta_norm = self.din("delta_norm", [L, 64])
        self.w_out = self.din("w_out", [L, D, D])
        self.w_gate = self.din("w_gate", [L, D, DFF])
        self.w_up = self.din("w_up", [L, D, DFF])
        self.w_down = self.din("w_down", [L, DFF, D])
        self.rope = self.din("rope", [T, 2, 640])
        self.abias = self.din("abias", [128, 12 * NT])
        self.keep = self.din("keep", [128, 2 * NT])
        self.cflag = self.din("cflag", [128, 1])
        self.hmask = self.din("hmask", [128, 4])
        self.sel8 = self.din("sel8", [128, 16])
        self.masks = self.din("masks", [128, 4, 128])
        self.cmasks = self.din("cmasks", [128, 7, 128])
        self.y = self.dout("y", [T, D])
        self.kout = self.dout("kout", [L, T, 128])
        self.vout = self.dout("vout", [L, T, 128])
        self.sg_out = self.dout("sg_out", [L, 2, 4, 128, 64])
        self.sd_out = self.dout("sd_out", [L, 2, 4, 256, 64])

    def dbg(self, name, shape, dt=F32):
        t = self.dout("dbg_" + name, shape, dt)
        self.dbg_out[name] = t
        return t

    def sb(self, name, shape, dt=F32):
        return self.nc.alloc_sbuf_tensor(name, list(shape), dt)

    def alloc(self):
        nc = self.nc
        self.xs = self.sb("xs", [128, NT, D])
        self.actT = self.sb("actT", [128, 8, T], BF16)
        self.idf = self.sb("idf", [128, 128])
        self.idb = self.sb("idb", [128, 128], BF16)
        self.ones_f = self.sb("ones_f", [128, 128])
        self.condT_s = self.sb("condT_s", [128, 8])
        self.scond = self.sb("scond", [128, 8], BF16)
        self.screp = self.sb("screp", [128, 8, 128], BF16)
        self.RING = 4
        self.ring = [self.sb("ring%d" % i, [128, 8 * 512], BF16) for i in range(self.RING)]
        self.GGm = self.sb("GGm", [128, D])
        self.GGf = self.sb("GGf", [128, D])
        self.ngrep = self.sb("ngrep", [128, D])
        self.brep = self.sb("brep", [128, D])
        self.bmodT_s = self.sb("bmodT_s", [128, 48])
        self.ngT_s = self.sb("ngT_s", [128, 4, 8])
        self.modT = self.sb("modT", [128, 48])
        self.AB = self.sb("AB", [128, 4, 8])
        self.ssq = self.sb("ssq", [128, NT])
        self.ssq2 = self.sb("ssq2", [128, 4])
        self.rstd = self.sb("rstd", [128, NT])
        self.xn = [self.sb("xn%d" % i, [128, D], BF16) for i in range(2)]
        self.tmpf = [self.sb("tmpf%d" % i, [128, D]) for i in range(2)]
        self.junk = self.tmpf[1]
        self.abias_s = self.sb("abias_s", [128, 12 * NT])
        self.keep_s = self.sb("keep_s", [128, 2 * NT])
        self.cflag_s = self.sb("cflag_s", [128, 1])
        self.hmask_s = self.sb("hmask_s", [128, 4])
        self.sel8_s = self.sb("sel8_s", [128, 16])
        self.masks_s = self.sb("masks_s", [128, 4, 128])
        self.bar_s = self.sb("bar_s", [128, 1])
        self.trif = self.sb("trif", [128, 5, 128])
        self.mask4 = self.sb("mask4", [128, 2, 512])
        self.ps = [nc.alloc_psum_tensor("ps%d" % i, [128, 512], F32) for i in range(7)]
        self.psT = nc.alloc_psum_tensor("psT", [128, 8, 128], BF16)
        self.cat = self.sb("cat", [128, NT, D], BF16)
        self.ARENA_W = 16896
        self.arena = self.sb("arena", [128, self.ARENA_W])
        self.ar_off = 0

    def ar_reset(self):
        self.S.barrier("gpsimd", lambda e: e.memset(self.bar_s[:], 0.0))
        self.ar_off = 0

    def ar(self, shape, dt=F32):
        n = int(np.prod(shape))
        words = n if dt == F32 else (n + 1) // 2
        words = (words + 31) // 32 * 32
        assert self.ar_off + words <= self.ARENA_W, ("arena overflow", self.ar_off, words)
        v = self.arena[:, self.ar_off:self.ar_off + words]
        self.ar_off += words
        if dt != F32:
            v = v.bitcast(dt)[:, 0:n]
        else:
            v = v[:, 0:n]
        if len(shape) == 2:
            v = v.rearrange("p (a b) -> p a b", a=shape[0])
        elif len(shape) == 3:
            v = v.rearrange("p (a b c) -> p a b c", a=shape[0], b=shape[1])
        return v

    def next_ring(self):
        i = self.ring_i % self.RING
        self.ring_i += 1
        return i

    def load_w_piece(self, w_l, c0, c1, r0=0, nk=8):
        i = self.next_ring()
        n = c1 - c0
        dst = self.ring[i][:, 0:nk * n].rearrange("p (k n) -> p k n", k=nk)
        src = w_l[r0:r0 + nk * 128, c0:c1].rearrange("(k p) n -> p k n", p=128)
        self.S.dma("gpsimd", lambda e: e.dma_start(out=dst, in_=src), "ring%d" % i, writes=["ring%d" % i])
        return i, dst

    def load(self, dst_ap, src_ap, key, eng="sync", partial=False):
        self.S.dma(eng, lambda e: e.dma_start(out=dst_ap, in_=src_ap), ("ld", key), writes=[key], partial=partial)

    def store(self, dst_ap, src_ap, key):
        self.S.dma("sync", lambda e: e.dma_start(out=dst_ap, in_=src_ap), ("st", key), reads=[key], store=True)

    def mm(self, out, lhsT, rhs, start, stop, reads, wkey, partial=None):
        if partial is None:
            partial = not start
        self.S.op("tensor", lambda e: e.matmul(out, lhsT=lhsT, rhs=rhs, start=start, stop=stop),
                  reads=reads, writes=[wkey], partial=partial)

    def tr(self, out, in_, ident, reads, wkey, partial):
        self.S.op("tensor", lambda e: e.transpose(out=out, in_=in_, identity=ident),
                  reads=reads + ["ident"], writes=[wkey], partial=partial)

    def V(self, fn, reads, writes, partial=False):
        self.S.op("vector", fn, reads=reads, writes=writes, partial=partial)

    def A(self, fn, reads, writes, partial=False):
        self.S.op("scalar", fn, reads=reads, writes=writes, partial=partial)

    def G(self, fn, reads, writes, partial=False):
        self.S.op("gpsimd", fn, reads=reads, writes=writes, partial=partial)

    def act(self, out, in_, func, reads, writes, bias=0.0, scale=1.0, accum_out=None, partial=False):
        assert not (func == AF.Copy and not (isinstance(scale, float) and scale == 1.0)), "scaled ACT copy faults on HW"
        if accum_out is None:
            self.A(lambda e: e.activation(out=out, in_=in_, func=func, bias=bias, scale=scale), reads, writes, partial)
        else:
            self.A(lambda e: e.activation(out=out, in_=in_, func=func, bias=bias, scale=scale, accum_out=accum_out),
                   reads, writes, partial)

    def setup(self):
        S = self.S
        for tt in range(NT):
            self.load(self.xs[:, tt, :], self.x_in[tt * 128:(tt + 1) * 128, :], ("xs", tt))
        self.load(self.idf[:], self.ident, "idf")
        self.load(self.condT_s[:], self.condT, "condT_s")
        self.load(self.abias_s[:], self.abias, "abias_s")
        self.load(self.keep_s[:], self.keep, "keep_s")
        self.load(self.cflag_s[:], self.cflag, "cflag_s")
        self.load(self.hmask_s[:], self.hmask, "hmask_s")
        self.load(self.sel8_s[:], self.sel8, "sel8_s")
        self.load(self.masks_s[:], self.masks, "masks_s")
        self.V(lambda e: e.tensor_copy(out=self.idb[:], in_=self.idf[:]), ["idf"], ["ident"])
        self.V(lambda e: e.memset(self.ones_f[:], 1.0), [], ["ones_f"])
        for i, mi in enumerate((0, 1, 3, 2)):
            self.V(lambda e, i=i, mi=mi: e.tensor_scalar(out=self.trif[:, i, :], in0=self.masks_s[:, mi, :], scalar1=-1.0 / 16,
                                                        scalar2=None, op0=ALU.mult), ["masks_s"], ["trif"], partial=(i > 0))
        self.V(lambda e: e.memset(self.trif[:, 4, :], -1.0 / 16), [], ["trif"], partial=True)
        for d in range(2):
            for h in range(4):
                self.V(lambda e, d=d, h=h: e.tensor_copy(out=self.mask4[:, d, h * 128:(h + 1) * 128], in_=self.masks_s[:, d, :]),
                       ["masks_s"], ["mask4"], partial=not (d == 0 and h == 0))
        for tt in range(NT):
            self.G(lambda e, tt=tt: e.memset(self.cat[:, tt, 512:768], 0.0), [], [("cat", tt, "b")])
            self.G(lambda e, tt=tt: e.memset(self.cat[:, tt, 768:1024], 0.0), [], [("cat", tt, "c")])
        self.act(self.scond[:], self.condT_s[:], AF.Silu, ["condT_s"], ["scond"])
        for kc in range(8):
            self.V(lambda e, kc=kc: e.tensor_copy(out=self.screp[:, kc, :],
                                                  in_=self.scond[:, kc:kc + 1].to_broadcast([128, 128])),
                   ["scond"], ["screp"], partial=(kc > 0))

    def mod_stage(self, l):
        S = self.S
        self.load(self.bmodT_s[:], self.bmodT[l], "bmodT_s")
        self.load(self.ngT_s[:], self.ngT[l], "ngT_s")
        pm = self.ps[6]
        for p in range(12):
            j = p // 2
            ri, wt = self.load_w_piece(self.w_mod[l], p * 512, (p + 1) * 512)
            rk = "ring%d" % ri
            if j in (2, 5):
                pb = self.ps[p % 2]
                pk = "ps%d" % (p % 2)
                GG = self.GGm if j == 2 else self.GGf
                gi = 0 if j == 2 else 1
                half = p % 2
                if half == 0:
                    self.load(self.ngrep[:], self.ng[l, 1 + 2 * gi:2 + 2 * gi, :].partition_broadcast(128), "ngrep")
                    self.load(self.brep[:], self.bmod[l:l + 1, j * D:(j + 1) * D].partition_broadcast(128), "brep")
                for kc in range(8):
                    self.mm(pb[:], self.screp[:, kc, :], wt[:, kc, :], kc == 0, kc == 7, [rk, "screp"], pk)
                hs = slice(half * 512, (half + 1) * 512)
                self.V(lambda e, GG=GG, hs=hs, pb=pb: e.tensor_tensor(
                    out=GG[:, hs], in0=pb[:], in1=self.brep[:, hs], op=ALU.add), [pk, "brep"], [("GG", gi, half)])
                self.G(lambda e, GG=GG, hs=hs: e.tensor_tensor(
                    out=GG[:, hs], in0=GG[:, hs], in1=self.ngrep[:, hs], op=ALU.mult),
                    [("GG", gi, half), "ngrep"], [("GG", gi, half)])
            else:
                for sub in range(4):
                    c = p * 4 + sub
                    for kc in range(8):
                        self.mm(pm[:, c:c + 1], wt[:, kc, sub * 128:(sub + 1) * 128], self.scond[:, kc:kc + 1],
                                kc == 0, kc == 7, [rk, "scond"], "ps6", partial=not (p == 0 and sub == 0 and kc == 0))
        for (a, b) in ((0, 16), (24, 40)):
            self.V(lambda e, a=a, b=b: e.tensor_tensor(out=self.modT[:, a:b], in0=pm[:, a:b], in1=self.bmodT_s[:, a:b],
                                                       op=ALU.add), ["ps6", "bmodT_s"], ["modT"], partial=(a > 0))
        for which, (jsh, jsc, gi) in enumerate(((0, 1, 0), (3, 4, 2))):
            self.V(lambda e, which=which, jsc=jsc, gi=gi: e.scalar_tensor_tensor(
                out=self.AB[:, 2 * which, :], in0=self.modT[:, jsc * 8:(jsc + 1) * 8], scalar=1.0,
                in1=self.ngT_s[:, gi, :], op0=ALU.add, op1=ALU.mult), ["modT", "ngT_s"], [("AB", 2 * which)])
            self.V(lambda e, which=which, jsh=jsh: e.tensor_copy(
                out=self.AB[:, 2 * which + 1, :], in_=self.modT[:, jsh * 8:(jsh + 1) * 8]), ["modT"],
                [("AB", 2 * which + 1)])

    def norm_to_actT(self, which):
        for tt in range(NT):
            self.act(self.junk[:], self.xs[:, tt, :], AF.Square, [("xs", tt)], ["junk", ("ssq", tt)],
                     accum_out=self.ssq[:, tt:tt + 1])
        self.act(self.rstd[:], self.ssq[:], AF.Ln, [("ssq", t) for t in range(NT)], ["rstd"], bias=EPS, scale=1.0 / D)
        self.act(self.rstd[:], self.rstd[:], AF.Exp, ["rstd"], ["rstd"], scale=-0.5)
        for tt in range(NT):
            xn = self.xn[tt % 2]
            xk = "xn%d" % (tt % 2)
            self.V(lambda e, tt=tt, xn=xn: e.tensor_scalar(out=xn[:], in0=self.xs[:, tt, :],
                                                           scalar1=self.rstd[:, tt:tt + 1], scalar2=None, op0=ALU.mult),
                   [("xs", tt), "rstd"], [xk])
            self.transpose_tile_to_actT(xn, xk, tt, A=self.AB[:, 2 * which, :], B=self.AB[:, 2 * which + 1, :],
                                        abkeys=[("AB", 2 * which), ("AB", 2 * which + 1)])

    def transpose_tile_to_actT(self, src, srckey, tt, A=None, B=None, abkeys=()):
        for kc in range(8):
            self.tr(self.psT[:, kc, :], src[:, kc * 128:(kc + 1) * 128], self.idb[:],
                    list(srckey) if isinstance(srckey, list) else [srckey], "psT", partial=(kc > 0))
        dst = self.actT[:, :, tt * 128:(tt + 1) * 128]
        if A is None:
            self.V(lambda e: e.tensor_copy(out=dst, in_=self.psT[:]), ["psT"], [("actT", tt)])
        else:
            tmp = self.tmpf[tt % 2]
            tk = "tmpf%d" % (tt % 2)
            tv = tmp[:].rearrange("p (k n) -> p k n", k=8)
            self.V(lambda e: e.tensor_tensor(out=tv, in0=self.psT[:], in1=A.unsqueeze(2).to_broadcast([128, 8, 128]),
                                             op=ALU.mult), ["psT"] + list(abkeys), [tk])
            self.G(lambda e: e.tensor_tensor(out=dst, in0=tv, in1=B.unsqueeze(2).to_broadcast([128, 8, 128]),
                                             op=ALU.add), [tk] + list(abkeys), [("actT", tt)])


    def attention(self, l):
        S = self.S
        self.ar_reset()
        stage = [self.ar([768]) for _ in range(2)]
        qkn = [self.ar([640]) for _ in range(2)]
        t1 = self.ar([640])
        t2 = self.ar([640])
        qkr = [self.ar([640], BF16) for _ in range(2)]
        qkgrep = self.ar([640])
        ropet = [self.ar([2, 640]) for _ in range(2)]
        qT = self.ar([NT, 512], BF16)
        kT = self.ar([512 + T], BF16)
        vA = self.ar([12, 2, 80], BF16)
        ctxk = self.ar([4, 128])
        ctxv = self.ar([4, 128])
        ctxkb = self.ar([4, 128], BF16)
        pTs = [self.ar([512], BF16) for _ in range(3)]
        st10 = self.ar([16])
        rs10 = self.ar([16])
        rec = [self.ar([4]) for _ in range(2)]

        self.load(qkgrep, self.qkg[l:l + 1, :].partition_broadcast(128), "M_qkgrep")
        self.load(ctxk, self.ctx_k[l].rearrange("(c p) n -> p c n", p=128), "M_ctxk")
        self.load(ctxv, self.ctx_v[l].rearrange("(c p) n -> p c n", p=128), "M_ctxv")
        self.V(lambda e: e.memset(vA[:, :, :, 64:80], 1.0), [], ["M_vA1"])
        self.V(lambda e: e.tensor_copy(out=ctxkb, in_=ctxk), ["M_ctxk"], ["M_ctxkb"])
        self.V(lambda e: e.tensor_copy(out=vA[:, 0:4, :, 0:64], in_=ctxv.rearrange("p c (g d) -> p c g d", g=2)),
               ["M_ctxv"], ["M_vA_ctx"])
        for c in range(4):
            self.tr(self.psT[:, c, :], ctxkb[:, c, :], self.idb[:], ["M_ctxkb"], "psT", partial=(c > 0))
        self.V(lambda e: e.tensor_copy(out=kT[:, 0:512].rearrange("p (c n) -> p c n", c=4), in_=self.psT[:, 0:4, :]),
               ["psT"], ["M_kT_ctx"])

        if "stopA1" in self.debug:
            return
        r0, w0 = self.load_w_piece(self.w_in[l], 0, 512)
        r1, w1 = self.load_w_piece(self.w_in[l], 512, 768)
        for tt in range(NT):
            b = tt % 2
            pa, pb = self.ps[0 + b], self.ps[2 + b]
            pak, pbk = "ps%d" % b, "ps%d" % (2 + b)
            ts = slice(tt * 128, (tt + 1) * 128)
            for kc in range(8):
                self.mm(pa[:], self.actT[:, kc, ts], w0[:, kc, :], kc == 0, kc == 7, [("actT", tt), "ring%d" % r0], pak)
            for kc in range(8):
                self.mm(pb[:, 0:256], self.actT[:, kc, ts], w1[:, kc, :], kc == 0, kc == 7,
                        [("actT", tt), "ring%d" % r1], pbk)
            stg, sk = stage[b], "M_stage%d" % b
            self.act(stg[:, 0:512], pa[:], AF.Copy, [pak], [sk])
            self.V(lambda e, stg=stg, pb=pb: e.tensor_copy(out=stg[:, 512:768], in_=pb[:, 0:256]), [pbk], [sk], partial=True)
            if "stopA2" in self.debug:
                continue
            qn, qnk = qkn[b], "M_qkn%d" % b
            sv = stg[:, 0:640].rearrange("p (h d) -> p h d", h=10)
            qv = qn.rearrange("p (h d) -> p h d", h=10)
            self.V(lambda e, qn=qn, stg=stg: e.tensor_tensor(out=qn, in0=stg[:, 0:640], in1=stg[:, 0:640], op=ALU.mult),
                   [sk], [qnk])
            self.V(lambda e, qv=qv: e.tensor_reduce(out=st10[:, 0:10], in_=qv, axis=AX.X, op=ALU.add), [qnk], ["M_st10"])
            self.act(rs10[:, 0:10], st10[:, 0:10], AF.Ln, ["M_st10"], ["M_rs10"], bias=EPS, scale=1.0 / 64)
            self.act(rs10[:, 0:10], rs10[:, 0:10], AF.Exp, ["M_rs10"], ["M_rs10"], scale=-0.5)
            self.V(lambda e, qv=qv, sv=sv: e.tensor_tensor(out=qv, in0=sv, in1=rs10[:, 0:10].unsqueeze(2).to_broadcast([128, 10, 64]),
                                                          op=ALU.mult), [sk, "M_rs10"], [qnk])
            self.V(lambda e, qn=qn: e.tensor_tensor(out=qn, in0=qn, in1=qkgrep, op=ALU.mult), [qnk, "M_qkgrep"], [qnk])
            if "stopA3" in self.debug:
                continue
            self.store(self.kout[l, ts, :], qn[:, 512:640], qnk)
            self.store(self.vout[l, ts, :], stg[:, 640:768], sk)
            self.V(lambda e, stg=stg, tt=tt: e.tensor_copy(out=vA[:, 4 + tt, :, 0:64],
                                                          in_=stg[:, 640:768].rearrange("p (g d) -> p g d", g=2)),
                   [sk], [("M_vA", tt)])
            if "stopA4" in self.debug:
                continue
            rp, rpk = ropet[b], "M_rope%d" % b
            self.load(rp, self.rope[ts, :, :], rpk)
            self.V(lambda e, qn=qn, rp=rp: e.tensor_tensor(out=t1, in0=qn, in1=rp[:, 0, :], op=ALU.mult), [qnk, rpk], ["M_t1"])
            q3 = qn.rearrange("p (h two s) -> p h two s", h=20, two=2)
            t23 = t2.rearrange("p (h two s) -> p h two s", h=20, two=2)
            sn3 = rp[:, 1, :].rearrange("p (h two s) -> p h two s", h=20, two=2)
            for two in range(2):
                self.V(lambda e, two=two, q3=q3, t23=t23, sn3=sn3: e.tensor_tensor(
                    out=t23[:, :, two, :], in0=q3[:, :, 1 - two, :], in1=sn3[:, :, two, :], op=ALU.mult),
                    [qnk, rpk], ["M_t2"], partial=(two > 0))
            qr, qrk = qkr[b], "M_qkr%d" % b
            for g in range(2):
                self.V(lambda e, qr=qr, g=g: e.tensor_tensor(
                    out=qr[:, 0:512].rearrange("p (j c) -> p j c", j=4)[:, :, g * 64:(g + 1) * 64],
                    in0=t1[:, g * 256:(g + 1) * 256].rearrange("p (j d) -> p j d", j=4),
                    in1=t2[:, g * 256:(g + 1) * 256].rearrange("p (j d) -> p j d", j=4), op=ALU.add),
                    ["M_t1", "M_t2"], [qrk], partial=(g > 0))
            self.V(lambda e, qr=qr: e.tensor_tensor(out=qr[:, 512:640], in0=t1[:, 512:640], in1=t2[:, 512:640], op=ALU.add),
                   ["M_t1", "M_t2"], [qrk], partial=True)
            if "stopA5" in self.debug:
                continue
            for j in range(4):
                if "noq" in self.debug:
                    break
                self.tr(self.psT[:, j, :], qr[:, j * 128:(j + 1) * 128], self.idb[:], [qrk], "psT", partial=(j > 0))
            self.tr(self.psT[:, 4, :], qr[:, 512:640], self.idb[:], [qrk], "psT", partial=True)
            if "noq" not in self.debug:
                self.V(lambda e, tt=tt: e.tensor_copy(out=qT[:, tt, :], in_=self.psT[:, 0:4, :].rearrange("p j n -> p (j n)")),
                       ["psT"], [("M_qT", tt)])
            self.V(lambda e, tt=tt: e.tensor_copy(out=kT[:, 512 + tt * 128:512 + (tt + 1) * 128], in_=self.psT[:, 4, :]),
                   ["psT"], [("M_kT", tt)])

        if "qk" in self.debug and l == 0:
            d = self.dbg("qT", [128, NT, 512], BF16)
            self.store(d, qT, ("M_qT", 0))
            for tt in range(1, NT):
                self.S.ops[-1].deps += [(w, "raw") for w in self.S.writers[("M_qT", tt)]]
            d = self.dbg("kT", [128, 512 + T], BF16)
            self.store(d, kT, ("M_kT", 0))
            for tt in range(1, NT):
                self.S.ops[-1].deps += [(w, "raw") for w in self.S.writers[("M_kT", tt)]]
            self.S.ops[-1].deps += [(w, "raw") for w in self.S.writers["M_kT_ctx"]]

        if "stop_att_proj" in self.debug:
            return
        allk = ["M_kT_ctx", "M_vA_ctx", "M_vA1"] + [("M_kT", t) for t in range(NT)] + [("M_vA", t) for t in range(NT)]
        it = 0
        for qb in range(NT):
            qs = slice(qb * 128, (qb + 1) * 128)
            for g in range(2):
                po = self.ps[4 + (it % 2)]
                pok = "ps%d" % (4 + (it % 2))
                rc = rec[it % 2]
                rck = "M_rec%d" % (it % 2)
                it += 1
                pov = po[:, 0:512].rearrange("p (j c) -> p j c", j=4)
                for kc in range(12):
                    x = (kc % 3)
                    psc, psk = self.ps[x], "ps%d" % x
                    self.mm(psc[:], kT[g * 64:(g + 1) * 64, kc * 128:(kc + 1) * 128],
                            qT[g * 64:(g + 1) * 64, qb, :], True, True, allk + [("M_qT", qb)], psk, partial=False)
                    pt, ptk = pTs[x], "M_pT%d" % x
                    self.act(pt, psc[:], AF.Exp, [psk, "abias_s"], [ptk], scale=0.125,
                             bias=self.abias_s[:, kc * NT + qb:kc * NT + qb + 1])
                    for j in range(4):
                        self.mm(pov[:, j, 0:65], pt[:, j * 128:(j + 1) * 128], vA[:, kc, g, 0:65], kc == 0 and j == 0, kc == 11 and j == 3,
                                [ptk] + allk, pok, partial=not (kc == 0 and j == 0))
                self.V(lambda e, rc=rc, pov=pov: e.reciprocal(out=rc, in_=pov[:, :, 64]), [pok], [rck])
                dst = self.cat[:, qb, g * 256:(g + 1) * 256].rearrange("p (j d) -> p j d", j=4)
                self.V(lambda e, rc=rc, pov=pov, dst=dst: e.tensor_tensor(
                    out=dst, in0=pov[:, :, 0:64], in1=rc.unsqueeze(2).to_broadcast([128, 4, 64]), op=ALU.mult),
                    [pok, rck], [("cat", qb, "a%d" % g)])


    def gla(self, l):
        self.ar_reset()
        stB = [self.ar([800]) for _ in range(2)]
        gcT = self.ar([T])
        wgg_s = self.ar([256])
        gnrep = self.ar([256])
        sp = [self.ar([256]) for _ in range(2)]
        E = [self.ar([3, 256]) for _ in range(2)]
        qkt = [self.ar([6, 128], BF16) for _ in range(2)]
        khat = self.ar([NT, 256], BF16)
        qtT = self.ar([2, T], BF16)
        ktT = self.ar([2, T], BF16)
        vb_s = self.ar([NT, 256], BF16)
        srb = self.ar([NT, 256], BF16)
        dl = self.ar([NT, 2])
        ob = self.ar([NT, 256])
        attm = [self.ar([512], BF16) for _ in range(2)]
        qmsk = [self.ar([512], BF16) for _ in range(2)]
        Snew = [self.ar([64]) for _ in range(2)]
        Scur = self.ar([64])
        Sbf = self.ar([64], BF16)
        st4 = self.ar([8])
        sq = self.ar([256])

        self.load(wgg_s[0:33, :], self.wgg[l], "M_wgg")
        self.load(gnrep[:, 0:64], self.gla_norm[l:l + 1, :].partition_broadcast(128), "M_gn0")
        for h in range(1, 4):
            self.V(lambda e, h=h: e.tensor_copy(out=gnrep[:, h * 64:(h + 1) * 64], in_=gnrep[:, 0:64]), ["M_gn0"], [("M_gn", h)])
        self.V(lambda e: e.memset(gcT[32:64, :], 1.0), [], ["M_gcT1"])

        r0, w0 = self.load_w_piece(self.w_in[l], 768, 1280)
        r1, w1 = self.load_w_piece(self.w_in[l], 1280, 1568)
        for half in range(2):
            hs = slice(half * 512, (half + 1) * 512)
            pg = self.ps[4 + half]
            pgk = "ps%d" % (4 + half)
            for kc in range(8):
                self.mm(pg[0:32, :], w1[:, kc, 256:288], self.actT[:, kc, hs], kc == 0, kc == 7,
                        [("actT", t) for t in range(half * 4, half * 4 + 4)] + ["ring%d" % r1], pgk)
            self.V(lambda e, pg=pg, hs=hs: e.tensor_copy(out=gcT[0:32, hs], in_=pg[0:32, :]), [pgk], [("M_gcT", half)])

        if "stopB0" in self.debug:
            return
        for tt in range(NT):
            b = tt % 2
            ts = slice(tt * 128, (tt + 1) * 128)
            pa, pb = self.ps[0 + b], self.ps[2 + b]
            pak, pbk = "ps%d" % b, "ps%d" % (2 + b)
            for kc in range(8):
                self.mm(pa[:], self.actT[:, kc, ts], w0[:, kc, :], kc == 0, kc == 7, [("actT", tt), "ring%d" % r0], pak)
            for kc in range(8):
                self.mm(pb[:, 0:256], self.actT[:, kc, ts], w1[:, kc, 0:256], kc == 0, kc == 7, [("actT", tt), "ring%d" % r1], pbk)
            stg, sk = stB[b], "M_stB%d" % b
            self.V(lambda e, stg=stg, pa=pa: e.tensor_scalar(out=stg[:, 0:128], in0=pa[:, 0:128], scalar1=32.0 ** -0.5, scalar2=None,
                                                             op0=ALU.mult), [pak], [sk])
            self.V(lambda e, stg=stg, pa=pa: e.tensor_copy(out=stg[:, 128:256], in_=pa[:, 128:256]), [pak], [sk], partial=True)
            if "noB_vb" not in self.debug:
                self.V(lambda e, pa=pa, tt=tt: e.tensor_copy(out=vb_s[:, tt, :], in_=pa[:, 256:512]), [pak], [("M_vb", tt)])
            if "noB_srb" not in self.debug:
                self.act(srb[:, tt, :], pb[:, 0:256], AF.Silu, [pbk], [("M_srb", tt)])
            if "stopB2" in self.debug:
                continue
            px = self.ps[6]
            self.mm(px[:, 0:256], gcT[0:33, ts], wgg_s[0:33, :], True, True,
                    [("M_gcT", tt // 4), "M_gcT1", "M_wgg"], "ps6", partial=False)
            spt, spk = sp[b], "M_sp%d" % b
            self.act(spt, px[:, 0:256], AF.Exp, ["ps6"], [spk], scale=-1.0)
            self.act(spt, spt, AF.Ln, [spk], [spk], bias=1.0)
            if "stopB2a" in self.debug:
                continue
            pc = self.ps[4 + b]
            pck = "ps%d" % (4 + b)
            for d in range(2):
                cs = slice(d * 128, (d + 1) * 128)
                self.mm(pc[:, d * 128:(d + 1) * 128], self.trif[:, d, :], spt[:, cs], True, True,
                        [spk, "trif"], pck, partial=(d > 0))
            for d in range(2):
                cs = slice(d * 128, (d + 1) * 128)
                self.mm(pc[:, 256 + d * 128:256 + (d + 1) * 128], self.trif[:, 2 + d, :], spt[:, cs], True, True,
                        [spk, "trif"], pck, partial=True)
            Et, Ek = E[b], "M_E%d" % b
            self.act(Et[:, 0, :], pc[:, 0:256], AF.Exp, [pck], [Ek])
            self.act(Et[:, 1, :], pc[:, 0:256], AF.Exp, [pck], [Ek], scale=-1.0, partial=True)
            self.act(Et[:, 2, :], pc[:, 256:512], AF.Exp, [pck], [Ek], partial=True)
            if "stopB2b" in self.debug:
                continue
            pd = self.ps[6]
            for d in range(2):
                self.mm(pd[:, 256 + 8 * d:264 + 8 * d], spt[:, d * 128:(d + 1) * 128], self.trif[:, 4, 0:8], True, True,
                        [spk, "trif"], "ps6", partial=(d > 0))
            self.act(dl[:, tt, :], pd[:, 256:272].rearrange("p (d e) -> p d e", d=2)[:, :, 0], AF.Exp, ["ps6"], [("M_dl", tt)])
            if "stopB3" in self.debug:
                continue
            qt, qk_ = qkt[b], "M_qkt%d" % b
            for d in range(2):
                cs = slice(d * 128, (d + 1) * 128)
                self.V(lambda e, qt=qt, stg=stg, Et=Et, d=d, cs=cs: e.tensor_tensor(
                    out=qt[:, 2 * d, :], in0=stg[:, 0:128], in1=Et[:, 0, cs], op=ALU.mult), [sk, Ek], [qk_], partial=(d > 0))
                self.V(lambda e, qt=qt, stg=stg, Et=Et, d=d, cs=cs: e.tensor_tensor(
                    out=qt[:, 2 * d + 1, :], in0=stg[:, 128:256], in1=Et[:, 1, cs], op=ALU.mult), [sk, Ek], [qk_], partial=True)
                self.G(lambda e, stg=stg, Et=Et, d=d, cs=cs, tt=tt: e.tensor_tensor(
                    out=khat[:, tt, cs], in0=stg[:, 128:256], in1=Et[:, 2, cs], op=ALU.mult), [sk, Ek], [("M_khat", tt)],
                    partial=(d > 0))
            for i in range(4):
                self.tr(self.psT[:, i, :], qt[:, i, :], self.idb[:], [qk_], "psT", partial=(i > 0))
            for d in range(2):
                self.V(lambda e, d=d, ts=ts: e.tensor_copy(out=qtT[:, d, ts], in_=self.psT[:, 2 * d, :]), ["psT"], [("M_qtT", tt)],
                       partial=(d > 0))
                self.V(lambda e, d=d, ts=ts: e.tensor_copy(out=ktT[:, d, ts], in_=self.psT[:, 2 * d + 1, :]), ["psT"],
                       [("M_ktT", tt)], partial=(d > 0))

        if "stopB1" in self.debug:
            return
        it = 0
        for d in range(2):
            order = list(range(NT)) if d == 0 else list(range(NT - 1, -1, -1))
            if "noSload" not in self.debug:
                self.load(Scur, self.s0_gla[l, d], "M_Scur")
            if "noSbf" not in self.debug:
                self.act(Sbf, Scur, AF.Copy, ["M_Scur"], ["M_Sbf"])
            for n, tt in enumerate(order):
                ts = slice(tt * 128, (tt + 1) * 128)
                b = it % 2
                it += 1
                if "rB0" in self.debug:
                    continue
                pat = self.ps[0 + b]
                patk = "ps%d" % b
                rd = [("M_qtT", tt), ("M_ktT", tt)]
                qm, qmk = qmsk[b], "M_qm%d" % b
                for h in range(4):
                    self.V(lambda e, qm=qm, h=h, d=d, ts=ts: e.tensor_scalar(out=qm[:, h * 128:(h + 1) * 128], in0=qtT[:, d, ts],
                                                                             scalar1=self.hmask_s[:, h:h + 1], scalar2=None,
                                                                             op0=ALU.mult),
                           [("M_qtT", tt), "hmask_s"], [qmk], partial=(h > 0))
                for h in range(4):
                    self.mm(pat[:, h * 128:(h + 1) * 128], ktT[:, d, ts], qm[:, h * 128:(h + 1) * 128], True, True,
                            [("M_ktT", tt), qmk], patk, partial=(h > 0))
                am, amk = attm[b], "M_attm%d" % b
                self.V(lambda e, am=am, pat=pat, d=d: e.tensor_tensor(out=am, in0=pat[:], in1=self.mask4[:, d, :], op=ALU.mult),
                       [patk, "mask4"], [amk])
                po = self.ps[2 + b]
                pok = "ps%d" % (2 + b)
                for h in range(4):
                    self.mm(po[:, h * 64:(h + 1) * 64], am[:, h * 128:(h + 1) * 128], vb_s[:, tt, h * 64:(h + 1) * 64],
                            True, False, [amk, ("M_vb", tt)], pok, partial=(h > 0))
                    self.mm(po[:, h * 64:(h + 1) * 64], qm[:, h * 128:(h + 1) * 128], Sbf, False, True,
                            [qmk, "M_Sbf"], pok, partial=True)
                if d == 0:
                    self.act(ob[:, tt, :], po[:, 0:256], AF.Copy, [pok], [("M_ob", tt)])
                else:
                    self.V(lambda e, tt=tt, po=po: e.tensor_tensor(out=ob[:, tt, :], in0=ob[:, tt, :], in1=po[:, 0:256], op=ALU.add),
                           [pok, ("M_ob", tt)], [("M_ob", tt)])
                if "noSupd" in self.debug:
                    continue
                pS = self.ps[4 + b]
                pSk = "ps%d" % (4 + b)
                self.mm(pS[:, 0:256], khat[:, tt, d * 128:(d + 1) * 128], vb_s[:, tt, :], True, True,
                        [("M_khat", tt), ("M_vb", tt)], pSk, partial=False)
                sn, snk = Snew[b], "M_Snew%d" % b
                for h in range(4):
                    hp = slice(32 * h, 32 * h + 32)
                    self.V(lambda e, sn=sn, hp=hp, h=h, pS=pS, tt=tt, d=d: e.scalar_tensor_tensor(
                        out=sn[hp, :], in0=Scur[hp, :], scalar=dl[hp, tt, d:d + 1], in1=pS[hp, h * 64:(h + 1) * 64],
                        op0=ALU.mult, op1=ALU.add), ["M_Scur", ("M_dl", tt), pSk], [snk], partial=(h > 0))
                is_out = (tt % 2 == 1) if d == 0 else (tt % 2 == 0)
                if is_out:
                    self.store(self.sg_out[l, d, tt // 2], sn, snk)
                if n < NT - 1:
                    nxt = order[n + 1]
                    kcol = d * NT + nxt
                    self.V(lambda e, sn=sn, kcol=kcol: e.tensor_scalar(out=Scur, in0=sn, scalar1=self.keep_s[:, kcol:kcol + 1],
                                                                      scalar2=None, op0=ALU.mult),
                           [snk, "keep_s"], ["M_Scur"])
                    self.act(Sbf, Scur, AF.Copy, ["M_Scur"], ["M_Sbf"])

        for tt in range(NT):
            if "noBnorm" in self.debug:
                break
            self.V(lambda e, tt=tt: e.tensor_tensor(out=sq, in0=ob[:, tt, :], in1=ob[:, tt, :], op=ALU.mult), [("M_ob", tt)], ["M_sq"])
            self.V(lambda e: e.tensor_reduce(out=st4[:, 0:4], in_=sq.rearrange("p (h d) -> p h d", h=4), axis=AX.X, op=ALU.add),
                   ["M_sq"], ["M_st4"])
            self.act(st4[:, 4:8], st4[:, 0:4], AF.Ln, ["M_st4"], ["M_rs4"], bias=EPS, scale=1.0 / 64)
            self.act(st4[:, 4:8], st4[:, 4:8], AF.Exp, ["M_rs4"], ["M_rs4"], scale=-0.5)
            self.V(lambda e, tt=tt: e.tensor_tensor(out=sq.rearrange("p (h d) -> p h d", h=4),
                                                    in0=ob[:, tt, :].rearrange("p (h d) -> p h d", h=4),
                                                    in1=st4[:, 4:8].unsqueeze(2).to_broadcast([128, 4, 64]), op=ALU.mult),
                   [("M_ob", tt), "M_rs4"], ["M_sq"])
            self.V(lambda e: e.tensor_tensor(out=sq, in0=sq, in1=gnrep, op=ALU.mult),
                   ["M_sq", "M_gn0"] + [("M_gn", h) for h in range(1, 4)], ["M_sq"])
            self.V(lambda e, tt=tt: e.tensor_tensor(out=self.cat[:, tt, 512:768], in0=sq, in1=srb[:, tt, :], op=ALU.mult),
                   ["M_sq", ("M_srb", tt)], [("cat", tt, "b")])


    def ar_mark_reset(self, mark):
        self.S.barrier("gpsimd", lambda e: e.memset(self.bar_s[:], 0.0))
        self.ar_off = mark

    def delta(self, l):
        self.ar_reset()
        qTc = self.ar([2, T], BF16)
        kTc = self.ar([2, T], BF16)
        k_tok = self.ar([NT, 256], BF16)
        v_tok = self.ar([NT, 256], BF16)
        sgc = self.ar([NT, 256], BF16)
        oc = self.ar([NT, 256])
        beta = self.ar([NT, 8])
        gam = self.ar([NT, 8])
        ngam = self.ar([NT, 8])
        eg = self.ar([NT, 8])
        eglm = self.ar([NT, 8])
        egl = self.ar([NT, 8])
        bkg = self.ar([NT, 8])
        cw_s = self.ar([6, 5])
        negA = self.ar([8])
        dtb_s = self.ar([8])
        dnrep = self.ar([256])
        bones = self.ar([128])
        mark = self.ar_off

        self.load(cw_s, self.cw[l], "M_cw")
        self.load(negA, self.alog[l:l + 1, :].partition_broadcast(128), "M_negA")
        self.load(dtb_s, self.dtb[l:l + 1, :].partition_broadcast(128), "M_dtb")
        self.load(dnrep[:, 0:64], self.delta_norm[l:l + 1, :].partition_broadcast(128), "M_dn0")
        for h in range(1, 4):
            self.V(lambda e, h=h: e.tensor_copy(out=dnrep[:, h * 64:(h + 1) * 64], in_=dnrep[:, 0:64]), ["M_dn0"], [("M_dn", h)])
        self.act(negA, negA, AF.Exp, ["M_negA"], ["M_negA"])
        self.V(lambda e: e.tensor_scalar(out=negA, in0=negA, scalar1=-1.0, scalar2=None, op0=ALU.mult), ["M_negA"], ["M_negA"])
        self.V(lambda e: e.memset(bones, 0.0), [], ["M_bones"])
        self.V(lambda e: e.memset(bones[0:64, 0:64], 1.0), ["M_bones"], ["M_bones"])
        self.V(lambda e: e.memset(bones[64:128, 64:128], 1.0), ["M_bones"], ["M_bones"])

        if "stopC0" in self.debug:
            return
        xin = [self.ar([4, 260]) for _ in range(2)]
        ycv = [self.ar([T]) for _ in range(2)]
        ysl = self.ar([T])
        sqb = self.ar([T])
        vTc = self.ar([2, T], BF16)
        rn = self.ar([T])

        pieces = [(1568, 2080, 4), (2080, 2336, 2)]
        cc = 0
        for (c0, c1, nch) in pieces:
            ri, wt = self.load_w_piece(self.w_in[l], c0, c1)
            for sub in range(nch):
                b = cc % 2
                xi, xik = xin[b], "M_xin%d" % b
                for half in range(2):
                    hs = slice(half * 512, (half + 1) * 512)
                    pp = self.ps[half]
                    ppk = "ps%d" % half
                    for kc in range(8):
                        self.mm(pp[:], wt[:, kc, sub * 128:(sub + 1) * 128], self.actT[:, kc, hs], kc == 0, kc == 7,
                                [("actT", t) for t in range(half * 4, half * 4 + 4)] + ["ring%d" % ri], ppk)
                    self.act(xi[:, 2 * half:2 * half + 2, 2:258], pp[:].rearrange("p (s n) -> p s n", s=2), AF.Copy, [ppk], [xik],
                             partial=(half > 0))
                self.V(lambda e, xi=xi: e.memset(xi[:, 0, 0:2], 0.0), [], [xik], partial=True)
                self.V(lambda e, xi=xi: e.memset(xi[:, 3, 258:260], 0.0), [], [xik], partial=True)
                self.V(lambda e, xi=xi: e.tensor_scalar(out=xi[:, 1:4, 0:2], in0=xi[:, 0:3, 256:258], scalar1=self.cflag_s[:, 0:1],
                                                        scalar2=None, op0=ALU.mult), [xik, "cflag_s"], [xik])
                self.V(lambda e, xi=xi: e.tensor_scalar(out=xi[:, 0:3, 258:260], in0=xi[:, 1:4, 2:4], scalar1=self.cflag_s[:, 0:1],
                                                        scalar2=None, op0=ALU.mult), [xik, "cflag_s"], [xik])
                yc, yck = ycv[b], "M_ycv%d" % b
                yv = yc.rearrange("p (s n) -> p s n", s=4)
                self.V(lambda e, xi=xi, yv=yv, cc=cc: e.tensor_scalar(out=yv, in0=xi[:, :, 0:256], scalar1=cw_s[:, cc, 0:1],
                                                                      scalar2=None, op0=ALU.mult), [xik, "M_cw"], [yck])
                for j in range(1, 5):
                    eng = self.V
                    eng(lambda e, xi=xi, yv=yv, cc=cc, j=j: e.scalar_tensor_tensor(
                        out=yv, in0=xi[:, :, j:j + 256], scalar=cw_s[:, cc, j:j + 1], in1=yv, op0=ALU.mult, op1=ALU.add),
                        [xik, "M_cw", yck], [yck])
                self.act(ysl, yc, AF.Silu, [yck], ["M_ysl"])
                if cc < 4:
                    self.V(lambda e: e.tensor_tensor(out=sqb, in0=ysl, in1=ysl, op=ALU.mult), ["M_ysl"], ["M_sqb"])
                    for half in range(2):
                        hs = slice(half * 512, (half + 1) * 512)
                        pn = self.ps[2 + half]
                        pnk = "ps%d" % (2 + half)
                        self.mm(pn[:], bones, sqb[:, hs], True, True, ["M_bones", "M_sqb"], pnk, partial=False)
                        self.act(rn[:, hs], pn[:], AF.Ln, [pnk], [("M_rn", half)], bias=EPS)
                        self.act(rn[:, hs], rn[:, hs], AF.Exp, [("M_rn", half)], [("M_rn", half)], scale=-0.5)
                    dst = qTc if cc < 2 else kTc
                    dk_ = ("M_qTc", cc) if cc < 2 else ("M_kTc", cc - 2)
                    sc_ = 64.0 ** -0.5 if cc < 2 else 1.0
                    self.V(lambda e, dst=dst, cc=cc, sc_=sc_: e.scalar_tensor_tensor(
                        out=dst[:, cc % 2, :], in0=ysl, scalar=sc_, in1=rn, op0=ALU.mult, op1=ALU.mult),
                        ["M_ysl", ("M_rn", 0), ("M_rn", 1)], [dk_])
                else:
                    self.V(lambda e, cc=cc: e.tensor_copy(out=vTc[:, cc - 4, :], in_=ysl), ["M_ysl"], [("M_vTc", cc - 4)])
                cc += 1
        if "stopC1" in self.debug:
            return
        for tt in range(NT):
            ts = slice(tt * 128, (tt + 1) * 128)
            for c in range(2):
                self.tr(self.psT[:, c, :], kTc[:, c, ts], self.idb[:], [("M_kTc", c)], "psT", partial=(c > 0))
                self.tr(self.psT[:, 2 + c, :], vTc[:, c, ts], self.idb[:], [("M_vTc", c)], "psT", partial=True)
            self.V(lambda e, tt=tt: e.tensor_copy(out=k_tok[:, tt, :], in_=self.psT[:, 0:2, :].rearrange("p c n -> p (c n)")),
                   ["psT"], [("M_ktok", tt)])
            self.V(lambda e, tt=tt: e.tensor_copy(out=v_tok[:, tt, :], in_=self.psT[:, 2:4, :].rearrange("p c n -> p (c n)")),
                   ["psT"], [("M_vtok", tt)])
        if "stopC2" in self.debug:
            return
        ri, wt = self.load_w_piece(self.w_in[l], 2336, 2608)
        g8 = [self.ar([8]) for _ in range(2)]
        g16 = [self.ar([16]) for _ in range(2)]
        for tt in range(NT):
            b = tt % 2
            ts = slice(tt * 128, (tt + 1) * 128)
            pg = self.ps[4 + b]
            pgk = "ps%d" % (4 + b)
            for kc in range(8):
                self.mm(pg[:, 0:256], self.actT[:, kc, ts], wt[:, kc, 0:256], kc == 0, kc == 7, [("actT", tt), "ring%d" % ri], pgk)
            for kc in range(8):
                self.mm(pg[:, 256:272], self.actT[:, kc, ts], wt[:, kc, 256:272], kc == 0, kc == 7, [("actT", tt), "ring%d" % ri], pgk,
                        partial=True)
            self.act(sgc[:, tt, :], pg[:, 0:256], AF.Silu, [pgk], [("M_sgc", tt)])
            if "gCa" in self.debug:
                continue
            self.act(beta[:, tt, :], pg[:, 256:264], AF.Exp, [pgk], [("M_beta", tt)], scale=-1.0)
            self.V(lambda e, tt=tt: e.tensor_scalar(out=beta[:, tt, :], in0=beta[:, tt, :], scalar1=1.0, scalar2=None, op0=ALU.add),
                   [("M_beta", tt)], [("M_beta", tt)])
            self.V(lambda e, tt=tt: e.reciprocal(out=beta[:, tt, :], in_=beta[:, tt, :]), [("M_beta", tt)], [("M_beta", tt)])
            if "gCb" in self.debug:
                continue
            gt, gk = g8[b], "M_g8%d" % b
            self.V(lambda e, gt=gt, pg=pg: e.tensor_tensor(out=gt, in0=pg[:, 264:272], in1=dtb_s, op=ALU.add), [pgk, "M_dtb"], [gk])
            self.act(gt, gt, AF.Exp, [gk], [gk])
            self.act(gt, gt, AF.Ln, [gk], [gk], bias=1.0)
            self.V(lambda e, gt=gt: e.tensor_tensor(out=gt, in0=gt, in1=negA, op=ALU.mult), [gk, "M_negA"], [gk])
            if "gCc" in self.debug:
                continue
            pc = self.ps[6]
            gt2, gk2 = g16[b], "M_g16%d" % b
            for d in range(2):
                self.V(lambda e, gt=gt, gt2=gt2, d=d: e.tensor_tensor(out=gt2[:, d * 8:(d + 1) * 8], in0=gt,
                                                                      in1=self.sel8_s[:, d * 8:(d + 1) * 8], op=ALU.mult),
                       [gk, "sel8_s"], [gk2], partial=(d > 0))
            for d in range(2):
                self.mm(pc[:, 0:8], self.masks_s[:, d, :], gt2[:, d * 8:(d + 1) * 8], d == 0, d == 1, [gk2, "masks_s"], "ps6",
                        partial=(d > 0))
            self.mm(pc[:, 16:24], self.ones_f[:], gt, True, True, [gk, "ones_f"], "ps6", partial=True)
            self.V(lambda e, tt=tt: e.tensor_copy(out=gam[:, tt, :], in_=pc[:, 0:8]), ["ps6"], [("M_gam", tt)])
            if "gCd" in self.debug:
                continue
            self.V(lambda e, tt=tt: e.tensor_scalar(out=ngam[:, tt, :], in0=gam[:, tt, :], scalar1=-1.0, scalar2=None, op0=ALU.mult),
                   [("M_gam", tt)], [("M_ngam", tt)])
            self.act(eg[:, tt, :], gam[:, tt, :], AF.Exp, [("M_gam", tt)], [("M_eg", tt)])
            if "gCe" in self.debug:
                continue
            self.act(egl[:, tt, :], pc[:, 16:24], AF.Exp, ["ps6"], [("M_egl", tt)])
            self.V(lambda e, tt=tt: e.tensor_tensor(out=eglm[:, tt, :], in0=pc[:, 16:24], in1=ngam[:, tt, :], op=ALU.add),
                   ["ps6", ("M_ngam", tt)], [("M_eglm", tt)])
            self.act(eglm[:, tt, :], eglm[:, tt, :], AF.Exp, [("M_eglm", tt)], [("M_eglm", tt)])
            self.V(lambda e, tt=tt: e.tensor_tensor(out=bkg[:, tt, :], in0=beta[:, tt, :], in1=eg[:, tt, :], op=ALU.mult),
                   [("M_beta", tt), ("M_eg", tt)], [("M_bkg", tt)])

        if "stopC3" in self.debug:
            return
        self.ar_mark_reset(mark)
        bigm = self.ar([4, 128])
        cm = self.ar([7, 128])
        self.load(cm, self.cmasks, "M_cm")
        for i, (mi, sgn) in enumerate(((0, 1.0), (1, 1.0), (3, -1.0), (2, -1.0))):
            self.V(lambda e, i=i, mi=mi, sgn=sgn: e.tensor_scalar(out=bigm[:, i, :], in0=self.masks_s[:, mi, :], scalar1=-sgn * NEG,
                                                                  scalar2=None, op0=ALU.mult), ["masks_s"], [("M_bigm", i)])
        attm4 = self.ar([512], BF16)
        Mm4 = self.ar([512])
        dec4 = Mm4
        SD = BF16
        Cm4 = [self.ar([512], SD) for _ in range(2)]
        Ym4 = self.ar([512], SD)
        Zm4 = [self.ar([512], SD) for _ in range(2)]
        Xm4 = [self.ar([512], SD) for _ in range(2)]
        decT4 = self.ar([512])
        dg4 = decT4
        Mt4 = self.ar([512], SD)
        Ct4 = [self.ar([512], SD) for _ in range(2)]
        Yp4 = self.ar([512], SD)
        id4 = self.ar([512], SD)
        bv4 = self.ar([256], SD)
        kbg4 = self.ar([256], SD)
        wT4 = self.ar([512], SD)
        vnew4 = self.ar([256], BF16)
        khat4 = self.ar([512], BF16)
        otmp4 = self.ar([256])
        Sf4 = self.ar([256])
        Sb4 = self.ar([256], BF16)
        Snew4 = [self.ar([256]) for _ in range(2)]
        H4 = range(4)
        for h in H4:
            self.V(lambda e, h=h: e.tensor_copy(out=id4[:, h * 128:(h + 1) * 128], in_=self.idf[:]), ["idf"], ["M_id4"], partial=(h > 0))
        idS = self.idb

        def c128(t, h):
            return t[:, h * 128:(h + 1) * 128]

        def c64(t, h):
            return t[:, h * 64:(h + 1) * 64]

        P = self.ps
        itn = 0
        for d in range(2):
            order = list(range(NT)) if d == 0 else list(range(NT - 1, -1, -1))
            for h in H4:
                self.load(Sf4[0:64, h * 64:(h + 1) * 64], self.s0_delta[l, d, h * 64:(h + 1) * 64, :], "M_Sf", partial=(h > 0))
                self.load(Sf4[64:128, h * 64:(h + 1) * 64], self.s0_delta[l, d, h * 64:(h + 1) * 64, :], "M_Sf", partial=True)
            self.act(Sb4, Sf4, AF.Copy, ["M_Sf"], ["M_Sb"])
            for n, tt in enumerate(order):
                ts = slice(tt * 128, (tt + 1) * 128)
                cols = [d * 4 + h for h in H4]
                hrs = [slice((h % 2) * 64, (h % 2) * 64 + 64) for h in H4]
                chs = [h // 2 for h in H4]
                kcs = [slice(h * 64, (h + 1) * 64) for h in H4]
                if "dS0" in self.debug:
                    continue
                def Gr(h):
                    return P[h % 2][:, (h // 2) * 256:(h // 2) * 256 + 128]

                def Ar(h):
                    return P[h % 2][:, (h // 2) * 256 + 128:(h // 2) * 256 + 256]
                for h in (0, 2, 1, 3):
                    hr, ch = hrs[h], chs[h]
                    bk = "ps%d" % (h % 2)
                    self.mm(Gr(h), kTc[hr, ch, ts], kTc[hr, ch, ts], True, True, [("M_kTc", ch)], bk, partial=(h >= 2))
                    self.mm(Ar(h), kTc[hr, ch, ts], qTc[hr, ch, ts], True, True, [("M_kTc", ch), ("M_qTc", ch)], bk, partial=True)
                if "dS1" in self.debug:
                    continue
                for h in H4:
                    col = cols[h]
                    self.V(lambda e, h=h, tt=tt, col=col: e.tensor_scalar(out=c128(dg4, h), in0=self.idf[:],
                                                                          scalar1=gam[:, tt, col:col + 1], scalar2=None, op0=ALU.mult),
                           ["idf", ("M_gam", tt)], ["M_decT"], partial=(h > 0))
                if "dS2" in self.debug:
                    continue
                for h in H4:
                    self.mm(c128(P[2], h), self.ones_f[:], c128(dg4, h), True, False, ["ones_f", "M_decT"], "ps2", partial=(h > 0))
                    self.mm(c128(P[2], h), self.idf[:], bigm[:, d, :], False, True, ["idf", ("M_bigm", d)], "ps2", partial=True)
                for h in H4:
                    self.mm(c128(P[3], h), self.ones_f[:], c128(dg4, h), True, False, ["ones_f", "M_decT"], "ps3", partial=(h > 0))
                    self.mm(c128(P[3], h), self.idf[:], bigm[:, 2 + d, :], False, True, ["idf", ("M_bigm", 2 + d)], "ps3", partial=True)
                if "dS3" in self.debug:
                    continue
                for h in H4:
                    col = cols[h]
                    self.act(c128(dec4, h), c128(P[2], h), AF.Exp, ["ps2", ("M_gam", tt)], ["M_M"], scale=-1.0,
                             bias=gam[:, tt, col:col + 1], partial=(h > 0))
                for h in H4:
                    col = cols[h]
                    self.act(c128(decT4, h), c128(P[3], h), AF.Exp, ["ps3", ("M_ngam", tt)], ["M_decT"], scale=1.0,
                             bias=ngam[:, tt, col:col + 1], partial=(h > 0))
                for h in H4:
                    col = cols[h]
                    self.V(lambda e, h=h, tt=tt, col=col: e.scalar_tensor_tensor(
                        out=c128(Mm4, h), in0=Gr(h), scalar=beta[:, tt, col:col + 1], in1=c128(dec4, h),
                        op0=ALU.mult, op1=ALU.mult), ["ps%d" % (h % 2), ("M_beta", tt), "M_M"], ["M_M"], partial=(h > 0))
                for h in H4:
                    self.V(lambda e, h=h: e.tensor_tensor(out=c128(attm4, h), in0=Ar(h), in1=c128(decT4, h), op=ALU.mult),
                           ["ps%d" % (h % 2), "M_decT"], ["M_attm"], partial=(h > 0))
                if "dS5" in self.debug:
                    continue
                for h in H4:
                    self.S.op("tensor", lambda e, h=h: e.transpose(out=c128(P[4], h), in_=c128(Mm4, h), identity=self.idf[:]),
                              reads=["M_M", "idf"], writes=["ps4"], partial=(h > 0))
                self.act(Mt4, P[4][:], AF.Copy, ["ps4"], ["M_Mt"])
                Zc, Zk = id4, "M_id4"
                Xc, Xk = id4, "M_id4"
                for k in range(7):
                    Ck, Ckk = Cm4[k % 2], "M_C%d" % (k % 2)
                    Ctk, Ctkk = Ct4[k % 2], "M_Ct%d" % (k % 2)
                    for h in H4:
                        self.G(lambda e, h=h, k=k, Ck=Ck: e.tensor_tensor(out=c128(Ck, h), in0=c128(Mm4, h), in1=cm[:, k, :], op=ALU.mult),
                               ["M_M", "M_cm"], [Ckk], partial=(h > 0))
                    for h in H4:
                        self.G(lambda e, h=h, k=k, Ctk=Ctk: e.tensor_tensor(out=c128(Ctk, h), in0=c128(Mt4, h), in1=cm[:, k, :],
                                                                            op=ALU.mult), ["M_Mt", "M_cm"], [Ctkk], partial=(h > 0))
                    for h in H4:
                        self.mm(c128(P[2], h), c128(Ck, h), c128(Zc, h), True, True, [Ckk, Zk], "ps2", partial=(h > 0))
                    last = (k == 6)
                    if not last:
                        for h in H4:
                            self.mm(c128(P[0], h), c128(Ctk, h), c128(Xc, h), True, True, [Ctkk, Xk], "ps0", partial=(h > 0))
                    self.act(Ym4, P[2][:], AF.Copy, ["ps2"], ["M_Y"])
                    if not last:
                        self.V(lambda e: e.tensor_copy(out=Yp4, in_=P[0][:]), ["ps0"], ["M_Yp"])
                    for h in H4:
                        self.mm(c128(P[3], h), c128(Xc, h), c128(Ym4, h), True, True, [Xk, "M_Y"], "ps3", partial=(h > 0))
                    if not last:
                        for h in H4:
                            self.mm(c128(P[1], h), c128(Zc, h), c128(Yp4, h), True, True, [Zk, "M_Yp"], "ps1", partial=(h > 0))
                    Zn, Znk = Zm4[k % 2], "M_Z%d" % (k % 2)
                    self.V(lambda e, Zn=Zn, Zo=Zc: e.scalar_tensor_tensor(out=Zn, in0=P[3][:], scalar=-1.0, in1=Zo, op0=ALU.mult,
                                                                          op1=ALU.add), ["ps3", Zk], [Znk])
                    if not last:
                        Xn, Xnk = Xm4[k % 2], "M_X%d" % (k % 2)
                        self.V(lambda e, Xn=Xn, Xo=Xc: e.scalar_tensor_tensor(out=Xn, in0=P[1][:], scalar=-1.0, in1=Xo, op0=ALU.mult,
                                                                              op1=ALU.add), ["ps1", Xk], [Xnk])
                        Xc, Xk = Xn, Xnk
                    Zc, Zk = Zn, Znk
                if "dS6" in self.debug:
                    continue
                for h in H4:
                    col, kcols = cols[h], kcs[h]
                    self.V(lambda e, h=h, tt=tt, col=col, kcols=kcols: e.tensor_scalar(
                        out=c64(bv4, h), in0=v_tok[:, tt, kcols], scalar1=beta[:, tt, col:col + 1], scalar2=None, op0=ALU.mult),
                        [("M_vtok", tt), ("M_beta", tt)], ["M_bv"], partial=(h > 0))
                    self.V(lambda e, h=h, tt=tt, col=col, kcols=kcols: e.tensor_scalar(
                        out=c64(kbg4, h), in0=k_tok[:, tt, kcols], scalar1=bkg[:, tt, col:col + 1], scalar2=None, op0=ALU.mult),
                        [("M_ktok", tt), ("M_bkg", tt)], ["M_kbg"], partial=(h > 0))
                    for half in range(2):
                        self.G(lambda e, h=h, tt=tt, col=col, kcols=kcols, half=half: e.tensor_scalar(
                            out=khat4[:, h * 128 + half * 64:h * 128 + (half + 1) * 64], in0=k_tok[:, tt, kcols],
                            scalar1=eglm[:, tt, col:col + 1], scalar2=None, op0=ALU.mult), [("M_ktok", tt), ("M_eglm", tt)],
                            ["M_khat"], partial=not (h == 0 and half == 0))
                for h in H4:
                    self.mm(P[5][0:64, h * 128:(h + 1) * 128], c64(kbg4, h), c128(Zc, h), True, True, ["M_kbg", Zk], "ps5", partial=(h > 0))
                self.V(lambda e: e.tensor_scalar(out=wT4[0:64, :], in0=P[5][0:64, :], scalar1=-1.0, scalar2=None, op0=ALU.mult),
                       ["ps5"], ["M_wT"])
                for h in H4:
                    self.mm(c64(P[6], h), c128(Zc, h), c64(bv4, h), True, False, [Zk, "M_bv"], "ps6", partial=(h > 0))
                    self.mm(c64(P[6], h), wT4[0:64, h * 128:(h + 1) * 128], Sb4[0:64, h * 64:(h + 1) * 64], False, True,
                            ["M_wT", "M_Sb"], "ps6", partial=True)
                self.act(vnew4, P[6][:, 0:256], AF.Copy, ["ps6"], ["M_vnew"])
                if "dS9" in self.debug:
                    continue
                def Qr(h):
                    return (P[0] if h % 2 == 0 else P[5])[:, h * 64:(h + 1) * 64]
                for h in (0, 2):
                    hr, ch = hrs[h], chs[h]
                    self.mm(Qr(h), qTc[hr, ch, ts], Sb4[hr, h * 64:(h + 1) * 64], True, True, [("M_qTc", ch), "M_Sb"], "ps0",
                            partial=(h > 0))
                for h in H4:
                    self.mm(c64(P[1], h), c128(attm4, h), c64(vnew4, h), True, True, ["M_attm", "M_vnew"], "ps1", partial=(h > 0))
                for h in (1, 3):
                    hr, ch = hrs[h], chs[h]
                    self.mm(Qr(h), qTc[hr, ch, ts], Sb4[hr, h * 64:(h + 1) * 64], True, True, [("M_qTc", ch), "M_Sb"], "ps5",
                            partial=(h > 1))
                for h in H4:
                    self.mm(c64(P[4], h), c128(khat4, h), c64(vnew4, h), True, True, ["M_khat", "M_vnew"], "ps4", partial=(h > 0))
                for h in H4:
                    col = cols[h]
                    self.V(lambda e, h=h, tt=tt, col=col: e.tensor_scalar(
                        out=c64(otmp4, h), in0=Qr(h), scalar1=eg[:, tt, col:col + 1], scalar2=None, op0=ALU.mult),
                        ["ps0" if h % 2 == 0 else "ps5", ("M_eg", tt)], ["M_otmp"], partial=(h > 0))
                if d == 0:
                    self.V(lambda e, tt=tt: e.tensor_tensor(out=oc[:, tt, :], in0=otmp4, in1=P[1][:, 0:256], op=ALU.add),
                           ["M_otmp", "ps1"], [("M_oc", tt)])
                else:
                    self.V(lambda e: e.tensor_tensor(out=otmp4, in0=otmp4, in1=P[1][:, 0:256], op=ALU.add), ["M_otmp", "ps1"], ["M_otmp"])
                    self.G(lambda e, tt=tt: e.tensor_tensor(out=oc[:, tt, :], in0=oc[:, tt, :], in1=otmp4, op=ALU.add),
                           ["M_otmp", ("M_oc", tt)], [("M_oc", tt)])
                sn, snk = Snew4[itn % 2], "M_Snew%d" % (itn % 2)
                itn += 1
                for h in H4:
                    col = cols[h]
                    self.V(lambda e, sn=sn, h=h, tt=tt, col=col: e.scalar_tensor_tensor(
                        out=c64(sn, h), in0=c64(Sf4, h), scalar=egl[:, tt, col:col + 1], in1=c64(P[4], h), op0=ALU.mult, op1=ALU.add),
                        ["M_Sf", ("M_egl", tt), "ps4"], [snk], partial=(h > 0))
                is_out = (tt % 2 == 1) if d == 0 else (tt % 2 == 0)
                if is_out:
                    for h in H4:
                        self.store(self.sd_out[l, d, tt // 2, h * 64:(h + 1) * 64, :], sn[0:64, h * 64:(h + 1) * 64], snk)
                if n < NT - 1:
                    nxt = order[n + 1]
                    kcol = d * NT + nxt
                    self.V(lambda e, sn=sn, kcol=kcol: e.tensor_scalar(out=Sf4, in0=sn, scalar1=self.keep_s[:, kcol:kcol + 1],
                                                                      scalar2=None, op0=ALU.mult), [snk, "keep_s"], ["M_Sf"])
                    self.act(Sb4, Sf4, AF.Copy, ["M_Sf"], ["M_Sb"])

        sq = otmp4
        st4 = self.ar([8])
        for tt in range(NT):
            ock = [("M_oc", tt)]
            self.V(lambda e, tt=tt: e.tensor_tensor(out=sq, in0=oc[:, tt, :], in1=oc[:, tt, :], op=ALU.mult), ock, ["M_otmp"])
            self.V(lambda e: e.tensor_reduce(out=st4[:, 0:4], in_=sq.rearrange("p (h d) -> p h d", h=4), axis=AX.X, op=ALU.add),
                   ["M_otmp"], ["M_st4"])
            self.act(st4[:, 4:8], st4[:, 0:4], AF.Ln, ["M_st4"], ["M_rs4"], bias=EPS, scale=1.0 / 64)
            self.act(st4[:, 4:8], st4[:, 4:8], AF.Exp, ["M_rs4"], ["M_rs4"], scale=-0.5)
            self.V(lambda e, tt=tt: e.tensor_tensor(out=sq.rearrange("p (h d) -> p h d", h=4),
                                                    in0=oc[:, tt, :].rearrange("p (h d) -> p h d", h=4),
                                                    in1=st4[:, 4:8].unsqueeze(2).to_broadcast([128, 4, 64]), op=ALU.mult),
                   ock + ["M_rs4"], ["M_otmp"])
            self.V(lambda e: e.tensor_tensor(out=sq, in0=sq, in1=dnrep, op=ALU.mult),
                   ["M_otmp", "M_dn0"] + [("M_dn", h) for h in range(1, 4)], ["M_otmp"])
            self.V(lambda e, tt=tt: e.tensor_tensor(out=self.cat[:, tt, 768:1024], in0=sq, in1=sgc[:, tt, :], op=ALU.mult),
                   ["M_otmp", ("M_sgc", tt)], [("cat", tt, "c")])

    def resid_update(self, tt, ps_lo, lo_key, ps_hi, hi_key, GG, ggkeys, lo_is_sbuf=False):
        st = self.ssq2
        self.act(self.junk[:, 0:512], ps_lo, AF.Square, [lo_key], ["junk", "ssq2"], accum_out=st[:, 0:1])
        self.act(self.junk[:, 512:1024], ps_hi, AF.Square, [hi_key], ["junk", "ssq2"], accum_out=st[:, 1:2], partial=True)
        self.V(lambda e: e.tensor_tensor(out=st[:, 2:3], in0=st[:, 0:1], in1=st[:, 1:2], op=ALU.add), ["ssq2"], ["ssq2s"])
        self.act(st[:, 3:4], st[:, 2:3], AF.Ln, ["ssq2s"], ["rstd2"], bias=EPS, scale=1.0 / D)
        self.act(st[:, 3:4], st[:, 3:4], AF.Exp, ["rstd2"], ["rstd2"], scale=-0.5)
        tmp = self.tmpf[0]
        for h, (src, key) in enumerate(((ps_lo, lo_key), (ps_hi, hi_key))):
            hs = slice(h * 512, (h + 1) * 512)
            self.V(lambda e, src=src, hs=hs: e.scalar_tensor_tensor(out=tmp[:, hs], in0=src, scalar=st[:, 3:4], in1=GG[:, hs],
                                                                   op0=ALU.mult, op1=ALU.mult),
                   [key, "rstd2"] + list(ggkeys), ["tmpf0"], partial=(h > 0))
        self.V(lambda e: e.tensor_tensor(out=self.xs[:, tt, :], in0=self.xs[:, tt, :], in1=tmp[:], op=ALU.add),
               ["tmpf0", ("xs", tt)], [("xs", tt)])

    def out_proj(self, l):
        for tt in range(NT):
            self.transpose_tile_to_actT(self.cat[:, tt, :], [("cat", tt, "a0"), ("cat", tt, "a1"), ("cat", tt, "b"), ("cat", tt, "c")], tt)
        r0, w0 = self.load_w_piece(self.w_out[l], 0, 512)
        r1, w1 = self.load_w_piece(self.w_out[l], 512, 1024)
        ggk = [("GG", 0, 0), ("GG", 0, 1)]
        for tt in range(NT):
            b = tt % 2
            pa, pb = self.ps[0 + b], self.ps[2 + b]
            pak, pbk = "ps%d" % b, "ps%d" % (2 + b)
            ts = slice(tt * 128, (tt + 1) * 128)
            for kc in range(8):
                self.mm(pa[:], self.actT[:, kc, ts], w0[:, kc, :], kc == 0, kc == 7, [("actT", tt), "ring%d" % r0], pak)
            for kc in range(8):
                self.mm(pb[:], self.actT[:, kc, ts], w1[:, kc, :], kc == 0, kc == 7, [("actT", tt), "ring%d" % r1], pbk)
            self.resid_update(tt, pa[:], pak, pb[:], pbk, self.GGm, ggk)

    def ffn(self, l):
        self.norm_to_actT(1)
        self.ar_reset()
        aT = self.ar([NFC, T], BF16)
        sg = [self.ar([512]) for _ in range(2)]
        fbuf = self.ar([NT, 512])
        it = 0
        for c0 in range(0, DFF, 512):
            c1 = min(c0 + 512, DFF)
            rg, wg = self.load_w_piece(self.w_gate[l], c0, c1)
            ru, wu = self.load_w_piece(self.w_up[l], c0, c1)
            for sub in range((c1 - c0) // 128):
                fc = c0 // 128 + sub
                for half in range(2):
                    hs = slice(half * 512, (half + 1) * 512)
                    b = it % 2
                    it += 1
                    pg, pu = self.ps[0 + b], self.ps[2 + b]
                    pgk, puk = "ps%d" % b, "ps%d" % (2 + b)
                    rd = [("actT", t) for t in range(half * 4, half * 4 + 4)]
                    for kc in range(8):
                        self.mm(pg[:], wg[:, kc, sub * 128:(sub + 1) * 128], self.actT[:, kc, hs], kc == 0, kc == 7,
                                rd + ["ring%d" % rg], pgk)
                    for kc in range(8):
                        self.mm(pu[:], wu[:, kc, sub * 128:(sub + 1) * 128], self.actT[:, kc, hs], kc == 0, kc == 7,
                                rd + ["ring%d" % ru], puk)
                    sgt, sgk = sg[b], "F_sg%d" % b
                    self.act(sgt, pg[:], AF.Silu, [pgk], [sgk])
                    self.V(lambda e, sgt=sgt, pu=pu, fc=fc, hs=hs: e.tensor_tensor(out=aT[:, fc, hs], in0=sgt, in1=pu[:],
                                                                                    op=ALU.mult),
                           [sgk, puk], [("F_aT", fc, half)])
        ggk = [("GG", 1, 0), ("GG", 1, 1)]
        groups = [(0, 8), (8, 8), (16, 6)]
        allaT = [("F_aT", fc, h) for fc in range(NFC) for h in range(2)]
        for half in range(2):
            pieces = []
            for (f0, nk) in groups:
                ri, wt = self.load_w_piece(self.w_down[l], half * 512, (half + 1) * 512, r0=f0 * 128, nk=nk)
                pieces.append((ri, wt, f0, nk))
            for tt in range(NT):
                b = tt % 2
                pf = self.ps[4 + b]
                pfk = "ps%d" % (4 + b)
                ts = slice(tt * 128, (tt + 1) * 128)
                n = 0
                for (ri, wt, f0, nk) in pieces:
                    for k in range(nk):
                        self.mm(pf[:], aT[:, f0 + k, ts], wt[:, k, :], n == 0, n == NFC - 1, allaT + ["ring%d" % ri], pfk)
                        n += 1
                if half == 0:
                    self.V(lambda e, tt=tt, pf=pf: e.tensor_copy(out=fbuf[:, tt, :], in_=pf[:]), [pfk], [("F_fbuf", tt)])
                else:
                    self.resid_update(tt, fbuf[:, tt, :], ("F_fbuf", tt), pf[:], pfk, self.GGf, ggk)

    def build(self):
        self.setup()
        for l in range(self.depth):
            self.layer(l)
        self.finish()
        st = self.S.emit()
        self.stats = st
        return self.nc

    def finish(self):
        for tt in range(NT):
            self.store(self.y[tt * 128:(tt + 1) * 128, :], self.xs[:, tt, :], ("xs", tt))

    def layer(self, l):
        self.mod_stage(l)
        self.norm_to_actT(0)
        self.attention(l)
        if "stop_after_att" in self.debug:
            return
        if "nogla" not in self.debug:
            self.gla(l)
        if "nodelta" not in self.debug:
            self.delta(l)
        self.out_proj(l)
        self.ffn(l)
        if "cat" in self.debug and l == 0:
            d = self.dbg("cat", [128, NT, D])
            self.S.dma("gpsimd", lambda e: e.dma_start(out=d, in_=self.cat[:]), "st_cat",
                       reads=[("cat", qb, k) for qb in range(NT) for k in ("a0", "a1", "b", "c")], store=True)
        if "actT0" in self.debug and l == 0:
            d = self.dbg("actT0", [128, 8, T])
            tmp = self.ar([8, T])
            self.V(lambda e: e.tensor_copy(out=tmp, in_=self.actT[:]), [("actT", t) for t in range(NT)], ["M_dbg_actT"])
            self.store(d, tmp, "M_dbg_actT")
            d2 = self.dbg("GG", [128, 2, D])
            self.store(d2[:, 0, :], self.GGm[:], ("GG", 0, 0))
            self.store(d2[:, 0, :], self.GGm[:], ("GG", 0, 1))
            self.store(d2[:, 1, :], self.GGf[:], ("GG", 1, 0))
            self.store(d2[:, 1, :], self.GGf[:], ("GG", 1, 1))


def rope_tables(sample):
    cos = np.ones((T, 64), np.float32)
    sin = np.zeros((T, 64), np.float32)
    if sample:
        t = np.arange(T)
        row = (t // 64).astype(np.float32)
        col = (t % 64).astype(np.float32)
        inv = (np.float32(10000.0) ** (-np.arange(16, dtype=np.float32) / np.float32(16))).astype(np.float32)
        ar = row[:, None] * inv
        ac = col[:, None] * inv
        ang = np.concatenate([ar, ar, ac, ac], axis=-1).astype(np.float32)
        cos = np.cos(ang).astype(np.float32)
        sin = np.sin(ang).astype(np.float32)
    sgn = np.concatenate([-np.ones(16), np.ones(16), -np.ones(16), np.ones(16)]).astype(np.float32)
    sins = sin * sgn
    tab = np.stack([np.tile(cos, (1, 10)), np.tile(sins, (1, 10))], 1)
    return np.ascontiguousarray(tab.astype(np.float32))


def core_tables(sample):
    ab = np.zeros((12, NT), np.float32)
    keep = np.ones((2, NT), np.float32)
    if not sample:
        ab[:] = NEG
        for kc in range(4, 12):
            for qb in range(NT):
                if (kc - 4) // 2 == qb // 2:
                    ab[kc, qb] = 0.0
        for tt in range(NT):
            if tt % 2 == 0:
                keep[0, tt] = 0.0
            if tt % 2 == 1:
                keep[1, tt] = 0.0
    abias = np.ascontiguousarray(np.broadcast_to(ab.reshape(1, -1), (128, 12 * NT))).astype(np.float32)
    keepr = np.ascontiguousarray(np.broadcast_to(keep.reshape(1, -1), (128, 2 * NT))).astype(np.float32)
    cflag = np.full((128, 1), 1.0 if sample else 0.0, np.float32)
    return abias, keepr, cflag


def make_masks():
    m = np.zeros((4, 128, 128), np.float32)
    i = np.arange(128)
    m[0] = (i[:, None] <= i[None, :])
    m[1] = (i[:, None] >= i[None, :])
    m[2] = (i[:, None] < i[None, :])
    m[3] = (i[:, None] > i[None, :])
    return np.ascontiguousarray(m.transpose(1, 0, 2))


def prep_shared(inp, L=DEPTH):
    sh = {}
    sh["ident"] = np.eye(128, dtype=np.float32)
    sh["w_mod"] = np.ascontiguousarray(inp["w_mod"], dtype=np.float32)
    bm = np.asarray(inp["b_mod"], np.float32)
    sh["bmod"] = np.ascontiguousarray(bm)
    sh["bmodT"] = np.ascontiguousarray(bm.reshape(L, 48, 128).transpose(0, 2, 1))
    ng = np.asarray(inp["norm_gains"], np.float32)
    sh["ng"] = np.ascontiguousarray(ng)
    sh["ngT"] = np.ascontiguousarray(ng.reshape(L, 4, 8, 128).transpose(0, 3, 1, 2))
    sh["w_in"] = np.ascontiguousarray(inp["w_in"], dtype=np.float32)
    qg = np.asarray(inp["qk_gain"], np.float32)
    sh["qkg"] = np.ascontiguousarray(np.concatenate([np.tile(qg[:, 0], (1, 8)), np.tile(qg[:, 1], (1, 2))], axis=1))
    wgg = np.zeros((L, 33, 256), np.float32)
    w = np.asarray(inp["w_gla_gate"], np.float32)
    b = np.asarray(inp["b_gla_gate"], np.float32)
    wgg[:, 0:16, 0:128] = w[:, 0]
    wgg[:, 16:32, 128:256] = w[:, 1]
    wgg[:, 32, :] = b.reshape(L, 256)
    sh["wgg"] = wgg
    sh["gla_norm"] = np.ascontiguousarray(inp["gla_norm"], dtype=np.float32)
    cwv = np.asarray(inp["conv_w"], np.float32)
    sh["cw"] = np.ascontiguousarray(cwv.reshape(L, 5, 6, 128).transpose(0, 3, 2, 1))
    sh["alog"] = np.ascontiguousarray(np.asarray(inp["a_log"], np.float32).reshape(L, 8))
    sh["dtb"] = np.ascontiguousarray(np.asarray(inp["dt_bias"], np.float32).reshape(L, 8))
    sh["delta_norm"] = np.ascontiguousarray(inp["delta_norm"], dtype=np.float32)
    for k in ("w_out", "w_gate", "w_up", "w_down"):
        sh[k] = np.ascontiguousarray(inp[k], dtype=np.float32)
    sh["masks"] = make_masks()
    sh["sel8"] = np.ascontiguousarray(np.broadcast_to(
        np.array([1, 1, 1, 1, 0, 0, 0, 0, 0, 0, 0, 0, 1, 1, 1, 1], np.float32)[None, :], (128, 16)))
    sh["hmask"] = (np.arange(128)[:, None] // 32 == np.arange(4)[None, :]).astype(np.float32)
    i = np.arange(128)
    cmk = np.zeros((7, 128, 128), np.float32)
    for k in range(7):
        bs, bb = 2 ** k, 2 ** (k + 1)
        cmk[k] = ((i[:, None] // bb == i[None, :] // bb) & (i[:, None] // bs != i[None, :] // bs)).astype(np.float32)
    sh["cmasks"] = np.ascontiguousarray(cmk.transpose(1, 0, 2))
    return sh


PER_LAYER = ("w_mod", "b_mod", "norm_gains", "w_in", "qk_gain", "w_gla_gate", "b_gla_gate", "gla_norm", "conv_w",
             "a_log", "dt_bias", "delta_norm", "w_out", "w_gate", "w_up", "w_down")
PER_LAYER1 = ("cache_k", "cache_v", "state_gla", "state_delta")


def make_in_maps(inp, L=DEPTH):
    if L != DEPTH:
        inp = dict(inp)
        for k in PER_LAYER:
            inp[k] = np.asarray(inp[k])[:L]
        for k in PER_LAYER1:
            inp[k] = np.asarray(inp[k])[:, :L]
    sh = prep_shared(inp, L)
    xs = np.asarray(inp["x_sample"], np.float32)
    xp = np.asarray(inp["x_prompt"], np.float32)
    maps = []
    tabs = {True: (rope_tables(True),) + core_tables(True), False: (rope_tables(False),) + core_tables(False)}
    for c in range(8):
        m = dict(sh)
        sample = c < 4
        if sample:
            b = c
            m["x"] = np.ascontiguousarray(xs[b])
            cond = np.asarray(inp["c"], np.float32)[b]
            m["ctx_k"] = np.ascontiguousarray(np.asarray(inp["cache_k"], np.float32)[b].reshape(L, 512, 128))
            m["ctx_v"] = np.ascontiguousarray(np.asarray(inp["cache_v"], np.float32)[b].reshape(L, 512, 128))
            m["s0_gla"] = np.ascontiguousarray(np.asarray(inp["state_gla"], np.float32)[b].reshape(L, 2, 128, 64))
            m["s0_delta"] = np.ascontiguousarray(np.asarray(inp["state_delta"], np.float32)[b].reshape(L, 2, 256, 64))
        else:
            j = c - 4
            m["x"] = np.ascontiguousarray(xp[4 * j:4 * j + 4].reshape(T, D))
            cond = np.asarray(inp["c_ctx"], np.float32)
            m["ctx_k"] = np.zeros((L, 512, 128), np.float32)
            m["ctx_v"] = np.zeros((L, 512, 128), np.float32)
            m["s0_gla"] = np.zeros((L, 2, 128, 64), np.float32)
            m["s0_delta"] = np.zeros((L, 2, 256, 64), np.float32)
        m["condT"] = np.ascontiguousarray(cond.reshape(8, 128).T)
        rope, abias, keep, cflag = tabs[sample]
        m["rope"] = rope
        m["abias"] = abias
        m["keep"] = keep
        m["cflag"] = cflag
        maps.append(m)
    return maps


_NC_CACHE = {}


def kernel(**inputs):
    maps = make_in_maps(inputs)
    if "nc" not in _NC_CACHE:
        _NC_CACHE["nc"] = Builder().build()
    nc = _NC_CACHE["nc"]
    res = run_bass_kernel_spmd(nc, maps, core_ids=list(range(8)))
    r = res.results
    L = DEPTH
    y_sample = np.stack([r[c]["y"] for c in range(4)], 0)
    y_prompt = np.concatenate([r[c]["y"].reshape(4, 256, D) for c in range(4, 8)], 0)
    nk = np.concatenate([r[c]["kout"].reshape(L, 4, 256, 2, 64).transpose(1, 0, 2, 3, 4) for c in range(4, 8)], 0)
    nv = np.concatenate([r[c]["vout"].reshape(L, 4, 256, 2, 64).transpose(1, 0, 2, 3, 4) for c in range(4, 8)], 0)
    sg = np.concatenate([r[c]["sg_out"].reshape(L, 2, 4, 4, 32, 64).transpose(2, 0, 1, 3, 4, 5) for c in range(4, 8)], 0)
    sd = np.concatenate([r[c]["sd_out"].reshape(L, 2, 4, 4, 64, 64).transpose(2, 0, 1, 3, 4, 5) for c in range(4, 8)], 0)
    return (y_prompt.astype(np.float32), y_sample.astype(np.float32), np.ascontiguousarray(nk, dtype=np.float32),
            np.ascontiguousarray(nv, dtype=np.float32), np.ascontiguousarray(sg, dtype=np.float32),
            np.ascontiguousarray(sd, dtype=np.float32))
```

```python
import numpy as np
import concourse.bass as bass
import concourse.mybir as mybir
from concourse.bass_utils import run_bass_kernel_spmd

F32 = mybir.dt.float32
BF16 = mybir.dt.bfloat16
AF = mybir.ActivationFunctionType
ALU = mybir.AluOpType
AX = mybir.AxisListType

DEPTH = 4
D = 1024
T = 1024
NT = 8
DFF = 2816
NFC = 22
PROJ = 2608
EPS = 1e-6
NEG = -30000.0


class Op:
    __slots__ = ("eng", "fn", "deps", "sig", "sigval", "dma", "dsem", "dval", "name")

    def __init__(self, eng, fn, dma, name):
        self.eng = eng
        self.fn = fn
        self.dma = dma
        self.deps = []
        self.sig = False
        self.sigval = 0
        self.dsem = None
        self.dval = 0
        self.name = name


class Sched:
    ENGS = ("tensor", "vector", "scalar", "gpsimd", "sync")

    def __init__(self, nc):
        self.nc = nc
        self.ops = []
        self.writers = {}
        self.readers = {}
        self.prev_readers = {}
        self.dsems = {}
        self.store_ops = []

    def _add(self, op, reads, writes, partial):
        reads = list(reads)
        if op.name != "barrier":
            for k in list(reads) + list(writes):
                nm = k[0] if isinstance(k, tuple) else k
                if nm.startswith("M_") or nm.startswith("F_"):
                    reads.append("ARENA")
                    break
        for r in reads:
            for w in self.writers.get(r, ()):
                op.deps.append((w, "raw"))
            self.readers.setdefault(r, []).append(op)
        for r in writes:
            rd = self.readers.get(r)
            if rd:
                for x in rd:
                    if x is not op:
                        op.deps.append((x, "war"))
                for x in self.writers.get(r, ()):
                    if x is not op:
                        op.deps.append((x, "war"))
                self.prev_readers[r] = [x for x in rd if x is not op] + list(self.writers.get(r, ()))
                self.readers[r] = []
                self.writers[r] = [op]
            else:
                if partial:
                    for x in self.prev_readers.get(r, ()):
                        op.deps.append((x, "war"))
                    self.writers.setdefault(r, []).append(op)
                else:
                    for x in self.writers.get(r, ()):
                        op.deps.append((x, "war"))
                    self.prev_readers[r] = list(self.writers.get(r, ()))
                    self.writers[r] = [op]
        self.ops.append(op)
        return op

    def op(self, eng, fn, reads=(), writes=(), partial=False, name=""):
        return self._add(Op(eng, fn, False, name), reads, writes, partial)

    def dma(self, eng, fn, semkey, reads=(), writes=(), partial=False, store=False, name=""):
        o = Op(eng, fn, True, name)
        ent = self.dsems.setdefault(semkey, [None, 0])
        ent[1] += 16
        o.dsem = semkey
        o.dval = ent[1]
        if store:
            self.store_ops.append(o)
        return self._add(o, reads, writes, partial)

    def barrier(self, eng, fn):
        return self._add(Op(eng, fn, False, "barrier"), (), ["ARENA"], False)

    def emit(self):
        nc = self.nc
        per_eng = {e: [] for e in self.ENGS}
        for o in self.ops:
            per_eng[o.eng].append(o)
        for o in self.ops:
            for (p, kind) in o.deps:
                if p.dma:
                    continue
                if p.eng == o.eng and p.eng == "tensor":
                    continue
                p.sig = True
        for e in self.ENGS:
            c = 0
            for o in per_eng[e]:
                if o.sig:
                    c += 1
                    o.sigval = c
        esem = {e: nc.alloc_semaphore(name="es_" + e) for e in self.ENGS}
        for i, (k, ent) in enumerate(self.dsems.items()):
            ent[0] = nc.alloc_semaphore(name="ds_%d" % i)
        stats = {e: [0, 0] for e in self.ENGS}

        def emit_engine(ename, eng):
            seen = {}
            for o in per_eng[ename]:
                need = {}
                for (p, kind) in o.deps:
                    if p.dma:
                        key = ("d", p.dsem)
                        sem = self.dsems[p.dsem][0]
                        val = p.dval
                    else:
                        if p.eng == ename and ename == "tensor":
                            continue
                        key = ("e", p.eng)
                        sem = esem[p.eng]
                        val = p.sigval
                    if seen.get(key, 0) >= val:
                        continue
                    if key not in need or need[key][1] < val:
                        need[key] = (sem, val)
                for key, (sem, val) in need.items():
                    eng.wait_ge(sem, val)
                    seen[key] = val
                    stats[ename][1] += 1
                ins = o.fn(eng)
                stats[ename][0] += 1
                if o.dma:
                    ins.then_inc(self.dsems[o.dsem][0], 16)
                elif o.sig:
                    ins.then_inc(esem[ename], 1)
            if ename == "sync":
                fin = {}
                for o in self.store_ops:
                    fin[o.dsem] = max(fin.get(o.dsem, 0), o.dval)
                for k, v in fin.items():
                    if seen.get(("d", k), 0) < v:
                        eng.wait_ge(self.dsems[k][0], v)

        with nc.Block() as block:
            @block.tensor
            def _(eng):
                emit_engine("tensor", eng)

            @block.vector
            def _(eng):
                emit_engine("vector", eng)

            @block.scalar
            def _(eng):
                emit_engine("scalar", eng)

            @block.gpsimd
            def _(eng):
                emit_engine("gpsimd", eng)

            @block.sync
            def _(eng):
                emit_engine("sync", eng)
        self.stats = stats
        return stats


W_IN_PIECES = [(0, 512), (512, 768), (768, 1280), (1280, 1568), (1568, 2080), (2080, 2336), (2336, 2608)]


class Builder:
    def __init__(self, depth=DEPTH, debug=()):
        self.depth = depth
        self.debug = set(debug)
        nc = bass.Bass("TRN2", target_bir_lowering=False)
        self.nc = nc
        self.S = Sched(nc)
        self.ring_i = 0
        self.ps_i = 0
        self.dbg_out = {}
        self.declare_io()
        self.alloc()

    def din(self, name, shape, dt=F32):
        return self.nc.dram_tensor(name, list(shape), dt, kind="ExternalInput").ap()

    def dout(self, name, shape, dt=F32):
        return self.nc.dram_tensor(name, list(shape), dt, kind="ExternalOutput").ap()

    def declare_io(self):
        L = self.depth
        self.x_in = self.din("x", [T, D])
        self.condT = self.din("condT", [128, 8])
        self.ident = self.din("ident", [128, 128])
        self.ctx_k = self.din("ctx_k", [L, 512, 128])
        self.ctx_v = self.din("ctx_v", [L, 512, 128])
        self.s0_gla = self.din("s0_gla", [L, 2, 128, 64])
        self.s0_delta = self.din("s0_delta", [L, 2, 256, 64])
        self.w_mod = self.din("w_mod", [L, D, 6 * D])
        self.bmodT = self.din("bmodT", [L, 128, 48])
        self.bmod = self.din("bmod", [L, 6 * D])
        self.ngT = self.din("ngT", [L, 128, 4, 8])
        self.ng = self.din("ng", [L, 4, D])
        self.w_in = self.din("w_in", [L, D, PROJ])
        self.qkg = self.din("qkg", [L, 640])
        self.wgg = self.din("wgg", [L, 33, 256])
        self.gla_norm = self.din("gla_norm", [L, 64])
        self.cw = self.din("cw", [L, 128, 6, 5])
        self.alog = self.din("alog", [L, 8])
        self.dtb = self.din("dtb", [L, 8])
        self.delta_norm = self.din("delta_norm", [L, 64])
        self.w_out = self.din("w_out", [L, D, D])
        self.w_gate = self.din("w_gate", [L, D, DFF])
        self.w_up = self.din("w_up", [L, D, DFF])
        self.w_down = self.din("w_down", [L, DFF, D])
        self.rope = self.din("rope", [T, 2, 640])
        self.abias = self.din("abias", [128, 12 * NT])
        self.keep = self.din("keep", [128, 2 * NT])
        self.cflag = self.din("cflag", [128, 1])
        self.hmask = self.din("hmask", [128, 4])
        self.sel8 = self.din("sel8", [128, 16])
        self.masks = self.din("masks", [128, 4, 128])
        self.cmasks = self.din("cmasks", [128, 7, 128])
        self.y = self.dout("y", [T, D])
        self.kout = self.dout("kout", [L, T, 128])
        self.vout = self.dout("vout", [L, T, 128])
        self.sg_out = self.dout("sg_out", [L, 2, 4, 128, 64])
        self.sd_out = self.dout("sd_out", [L, 2, 4, 256, 64])

    def dbg(self, name, shape, dt=F32):
        t = self.dout("dbg_" + name, shape, dt)
        self.dbg_out[name] = t
        return t

    def sb(self, name, shape, dt=F32):
        return self.nc.alloc_sbuf_tensor(name, list(shape), dt)

    def alloc(self):
        nc = self.nc
        self.xs = self.sb("xs", [128, NT, D])
        self.actT = self.sb("actT", [128, 8, T], BF16)
        self.idf = self.sb("idf", [128, 128])
        self.idb = self.sb("idb", [128, 128], BF16)
        self.ones_f = self.sb("ones_f", [128, 128])
        self.condT_s = self.sb("condT_s", [128, 8])
        self.scond = self.sb("scond", [128, 8], BF16)
        self.screp = self.sb("screp", [128, 8, 128], BF16)
        self.RING = 4
        self.ring = [self.sb("ring%d" % i, [128, 8 * 512], BF16) for i in range(self.RING)]
        self.GGm = self.sb("GGm", [128, D])
        self.GGf = self.sb("GGf", [128, D])
        self.ngrep = self.sb("ngrep", [128, D])
        self.brep = self.sb("brep", [128, D])
        self.bmodT_s = self.sb("bmodT_s", [128, 48])
        self.ngT_s = self.sb("ngT_s", [128, 4, 8])
        self.modT = self.sb("modT", [128, 48])
        self.AB = self.sb("AB", [128, 4, 8])
        self.ssq = self.sb("ssq", [128, NT])
        self.ssq2 = self.sb("ssq2", [128, 4])
        self.rstd = self.sb("rstd", [128, NT])
        self.xn = [self.sb("xn%d" % i, [128, D], BF16) for i in range(2)]
        self.tmpf = [self.sb("tmpf%d" % i, [128, D]) for i in range(2)]
        self.junk = self.tmpf[1]
        self.abias_s = self.sb("abias_s", [128, 12 * NT])
        self.keep_s = self.sb("keep_s", [128, 2 * NT])
        self.cflag_s = self.sb("cflag_s", [128, 1])
        self.hmask_s = self.sb("hmask_s", [128, 4])
        self.sel8_s = self.sb("sel8_s", [128, 16])
        self.masks_s = self.sb("masks_s", [128, 4, 128])
        self.bar_s = self.sb("bar_s", [128, 1])
        self.trif = self.sb("trif", [128, 5, 128])
        self.mask4 = self.sb("mask4", [128, 2, 512])
        self.ps = [nc.alloc_psum_tensor("ps%d" % i, [128, 512], F32) for i in range(7)]
        self.psT = nc.alloc_psum_tensor("psT", [128, 8, 128], BF16)
        self.cat = self.sb("cat", [128, NT, D], BF16)
        self.ARENA_W = 16896
        self.arena = self.sb("arena", [128, self.ARENA_W])
        self.ar_off = 0

    def ar_reset(self):
        self.S.barrier("gpsimd", lambda e: e.memset(self.bar_s[:], 0.0))
        self.ar_off = 0

    def ar(self, shape, dt=F32):
        n = int(np.prod(shape))
        words = n if dt == F32 else (n + 1) // 2
        words = (words + 31) // 32 * 32
        assert self.ar_off + words <= self.ARENA_W, ("arena overflow", self.ar_off, words)
        v = self.arena[:, self.ar_off:self.ar_off + words]
        self.ar_off += words
        if dt != F32:
            v = v.bitcast(dt)[:, 0:n]
        else:
            v = v[:, 0:n]
        if len(shape) == 2:
            v = v.rearrange("p (a b) -> p a b", a=shape[0])
        elif len(shape) == 3:
            v = v.rearrange("p (a b c) -> p a b c", a=shape[0], b=shape[1])
        return v

    def next_ring(self):
        i = self.ring_i % self.RING
        self.ring_i += 1
        return i

    def load_w_piece(self, w_l, c0, c1, r0=0, nk=8):
        i = self.next_ring()
        n = c1 - c0
        dst = self.ring[i][:, 0:nk * n].rearrange("p (k n) -> p k n", k=nk)
        src = w_l[r0:r0 + nk * 128, c0:c1].rearrange("(k p) n -> p k n", p=128)
        self.S.dma("gpsimd", lambda e: e.dma_start(out=dst, in_=src), "ring%d" % i, writes=["ring%d" % i])
        return i, dst

    def load(self, dst_ap, src_ap, key, eng="sync", partial=False):
        self.S.dma(eng, lambda e: e.dma_start(out=dst_ap, in_=src_ap), ("ld", key), writes=[key], partial=partial)

    def store(self, dst_ap, src_ap, key):
        self.S.dma("sync", lambda e: e.dma_start(out=dst_ap, in_=src_ap), ("st", key), reads=[key], store=True)

    def mm(self, out, lhsT, rhs, start, stop, reads, wkey, partial=None):
        if partial is None:
            partial = not start
        self.S.op("tensor", lambda e: e.matmul(out, lhsT=lhsT, rhs=rhs, start=start, stop=stop),
                  reads=reads, writes=[wkey], partial=partial)

    def tr(self, out, in_, ident, reads, wkey, partial):
        self.S.op("tensor", lambda e: e.transpose(out=out, in_=in_, identity=ident),
                  reads=reads + ["ident"], writes=[wkey], partial=partial)

    def V(self, fn, reads, writes, partial=False):
        self.S.op("vector", fn, reads=reads, writes=writes, partial=partial)

    def A(self, fn, reads, writes, partial=False):
        self.S.op("scalar", fn, reads=reads, writes=writes, partial=partial)

    def G(self, fn, reads, writes, partial=False):
        self.S.op("gpsimd", fn, reads=reads, writes=writes, partial=partial)

    def act(self, out, in_, func, reads, writes, bias=0.0, scale=1.0, accum_out=None, partial=False):
        assert not (func == AF.Copy and not (isinstance(scale, float) and scale == 1.0)), "scaled ACT copy faults on HW"
        if accum_out is None:
            self.A(lambda e: e.activation(out=out, in_=in_, func=func, bias=bias, scale=scale), reads, writes, partial)
        else:
            self.A(lambda e: e.activation(out=out, in_=in_, func=func, bias=bias, scale=scale, accum_out=accum_out),
                   reads, writes, partial)

    def setup(self):
        S = self.S
        for tt in range(NT):
            self.load(self.xs[:, tt, :], self.x_in[tt * 128:(tt + 1) * 128, :], ("xs", tt))
        self.load(self.idf[:], self.ident, "idf")
        self.load(self.condT_s[:], self.condT, "condT_s")
        self.load(self.abias_s[:], self.abias, "abias_s")
        self.load(self.keep_s[:], self.keep, "keep_s")
        self.load(self.cflag_s[:], self.cflag, "cflag_s")
        self.load(self.hmask_s[:], self.hmask, "hmask_s")
        self.load(self.sel8_s[:], self.sel8, "sel8_s")
        self.load(self.masks_s[:], self.masks, "masks_s")
        self.V(lambda e: e.tensor_copy(out=self.idb[:], in_=self.idf[:]), ["idf"], ["ident"])
        self.V(lambda e: e.memset(self.ones_f[:], 1.0), [], ["ones_f"])
        for i, mi in enumerate((0, 1, 3, 2)):
            self.V(lambda e, i=i, mi=mi: e.tensor_scalar(out=self.trif[:, i, :], in0=self.masks_s[:, mi, :], scalar1=-1.0 / 16,
                                                        scalar2=None, op0=ALU.mult), ["masks_s"], ["trif"], partial=(i > 0))
        self.V(lambda e: e.memset(self.trif[:, 4, :], -1.0 / 16), [], ["trif"], partial=True)
        for d in range(2):
            for h in range(4):
                self.V(lambda e, d=d, h=h: e.tensor_copy(out=self.mask4[:, d, h * 128:(h + 1) * 128], in_=self.masks_s[:, d, :]),
                       ["masks_s"], ["mask4"], partial=not (d == 0 and h == 0))
        for tt in range(NT):
            self.G(lambda e, tt=tt: e.memset(self.cat[:, tt, 512:768], 0.0), [], [("cat", tt, "b")])
            self.G(lambda e, tt=tt: e.memset(self.cat[:, tt, 768:1024], 0.0), [], [("cat", tt, "c")])
        self.act(self.scond[:], self.condT_s[:], AF.Silu, ["condT_s"], ["scond"])
        for kc in range(8):
            self.V(lambda e, kc=kc: e.tensor_copy(out=self.screp[:, kc, :],
                                                  in_=self.scond[:, kc:kc + 1].to_broadcast([128, 128])),
                   ["scond"], ["screp"], partial=(kc > 0))

    def mod_stage(self, l):
        S = self.S
        self.load(self.bmodT_s[:], self.bmodT[l], "bmodT_s")
        self.load(self.ngT_s[:], self.ngT[l], "ngT_s")
        pm = self.ps[6]
        for p in range(12):
            j = p // 2
            ri, wt = self.load_w_piece(self.w_mod[l], p * 512, (p + 1) * 512)
            rk = "ring%d" % ri
            if j in (2, 5):
                pb = self.ps[p % 2]
                pk = "ps%d" % (p % 2)
                GG = self.GGm if j == 2 else self.GGf
                gi = 0 if j == 2 else 1
                half = p % 2
                if half == 0:
                    self.load(self.ngrep[:], self.ng[l, 1 + 2 * gi:2 + 2 * gi, :].partition_broadcast(128), "ngrep")
                    self.load(self.brep[:], self.bmod[l:l + 1, j * D:(j + 1) * D].partition_broadcast(128), "brep")
                for kc in range(8):
                    self.mm(pb[:], self.screp[:, kc, :], wt[:, kc, :], kc == 0, kc == 7, [rk, "screp"], pk)
                hs = slice(half * 512, (half + 1) * 512)
                self.V(lambda e, GG=GG, hs=hs, pb=pb: e.tensor_tensor(
                    out=GG[:, hs], in0=pb[:], in1=self.brep[:, hs], op=ALU.add), [pk, "brep"], [("GG", gi, half)])
                self.G(lambda e, GG=GG, hs=hs: e.tensor_tensor(
                    out=GG[:, hs], in0=GG[:, hs], in1=self.ngrep[:, hs], op=ALU.mult),
                    [("GG", gi, half), "ngrep"], [("GG", gi, half)])
            else:
                for sub in range(4):
                    c = p * 4 + sub
                    for kc in range(8):
                        self.mm(pm[:, c:c + 1], wt[:, kc, sub * 128:(sub + 1) * 128], self.scond[:, kc:kc + 1],
                                kc == 0, kc == 7, [rk, "scond"], "ps6", partial=not (p == 0 and sub == 0 and kc == 0))
        for (a, b) in ((0, 16), (24, 40)):
            self.V(lambda e, a=a, b=b: e.tensor_tensor(out=self.modT[:, a:b], in0=pm[:, a:b], in1=self.bmodT_s[:, a:b],
                                                       op=ALU.add), ["ps6", "bmodT_s"], ["modT"], partial=(a > 0))
        for which, (jsh, jsc, gi) in enumerate(((0, 1, 0), (3, 4, 2))):
            self.V(lambda e, which=which, jsc=jsc, gi=gi: e.scalar_tensor_tensor(
                out=self.AB[:, 2 * which, :], in0=self.modT[:, jsc * 8:(jsc + 1) * 8], scalar=1.0,
                in1=self.ngT_s[:, gi, :], op0=ALU.add, op1=ALU.mult), ["modT", "ngT_s"], [("AB", 2 * which)])
            self.V(lambda e, which=which, jsh=jsh: e.tensor_copy(
                out=self.AB[:, 2 * which + 1, :], in_=self.modT[:, jsh * 8:(jsh + 1) * 8]), ["modT"],
                [("AB", 2 * which + 1)])

    def norm_to_actT(self, which):
        for tt in range(NT):
            self.act(self.junk[:], self.xs[:, tt, :], AF.Square, [("xs", tt)], ["junk", ("ssq", tt)],
                     accum_out=self.ssq[:, tt:tt + 1])
        self.act(self.rstd[:], self.ssq[:], AF.Ln, [("ssq", t) for t in range(NT)], ["rstd"], bias=EPS, scale=1.0 / D)
        self.act(self.rstd[:], self.rstd[:], AF.Exp, ["rstd"], ["rstd"], scale=-0.5)
        for tt in range(NT):
            xn = self.xn[tt % 2]
            xk = "xn%d" % (tt % 2)
            self.V(lambda e, tt=tt, xn=xn: e.tensor_scalar(out=xn[:], in0=self.xs[:, tt, :],
                                                           scalar1=self.rstd[:, tt:tt + 1], scalar2=None, op0=ALU.mult),
                   [("xs", tt), "rstd"], [xk])
            self.transpose_tile_to_actT(xn, xk, tt, A=self.AB[:, 2 * which, :], B=self.AB[:, 2 * which + 1, :],
                                        abkeys=[("AB", 2 * which), ("AB", 2 * which + 1)])

    def transpose_tile_to_actT(self, src, srckey, tt, A=None, B=None, abkeys=()):
        for kc in range(8):
            self.tr(self.psT[:, kc, :], src[:, kc * 128:(kc + 1) * 128], self.idb[:],
                    list(srckey) if isinstance(srckey, list) else [srckey], "psT", partial=(kc > 0))
        dst = self.actT[:, :, tt * 128:(tt + 1) * 128]
        if A is None:
            self.V(lambda e: e.tensor_copy(out=dst, in_=self.psT[:]), ["psT"], [("actT", tt)])
        else:
            tmp = self.tmpf[tt % 2]
            tk = "tmpf%d" % (tt % 2)
            tv = tmp[:].rearrange("p (k n) -> p k n", k=8)
            self.V(lambda e: e.tensor_tensor(out=tv, in0=self.psT[:], in1=A.unsqueeze(2).to_broadcast([128, 8, 128]),
                                             op=ALU.mult), ["psT"] + list(abkeys), [tk])
            self.G(lambda e: e.tensor_tensor(out=dst, in0=tv, in1=B.unsqueeze(2).to_broadcast([128, 8, 128]),
                                             op=ALU.add), [tk] + list(abkeys), [("actT", tt)])


    def attention(self, l):
        S = self.S
        self.ar_reset()
        stage = [self.ar([768]) for _ in range(2)]
        qkn = [self.ar([640]) for _ in range(2)]
        t1 = self.ar([640])
        t2 = self.ar([640])
        qkr = [self.ar([640], BF16) for _ in range(2)]
        qkgrep = self.ar([640])
        ropet = [self.ar([2, 640]) for _ in range(2)]
        qT = self.ar([NT, 512], BF16)
        kT = self.ar([512 + T], BF16)
        vA = self.ar([12, 2, 80], BF16)
        ctxk = self.ar([4, 128])
        ctxv = self.ar([4, 128])
        ctxkb = self.ar([4, 128], BF16)
        pTs = [self.ar([512], BF16) for _ in range(3)]
        st10 = self.ar([16])
        rs10 = self.ar([16])
        rec = [self.ar([4]) for _ in range(2)]

        self.load(qkgrep, self.qkg[l:l + 1, :].partition_broadcast(128), "M_qkgrep")
        self.load(ctxk, self.ctx_k[l].rearrange("(c p) n -> p c n", p=128), "M_ctxk")
        self.load(ctxv, self.ctx_v[l].rearrange("(c p) n -> p c n", p=128), "M_ctxv")
        self.V(lambda e: e.memset(vA[:, :, :, 64:80], 1.0), [], ["M_vA1"])
        self.V(lambda e: e.tensor_copy(out=ctxkb, in_=ctxk), ["M_ctxk"], ["M_ctxkb"])
        self.V(lambda e: e.tensor_copy(out=vA[:, 0:4, :, 0:64], in_=ctxv.rearrange("p c (g d) -> p c g d", g=2)),
               ["M_ctxv"], ["M_vA_ctx"])
        for c in range(4):
            self.tr(self.psT[:, c, :], ctxkb[:, c, :], self.idb[:], ["M_ctxkb"], "psT", partial=(c > 0))
        self.V(lambda e: e.tensor_copy(out=kT[:, 0:512].rearrange("p (c n) -> p c n", c=4), in_=self.psT[:, 0:4, :]),
               ["psT"], ["M_kT_ctx"])

        if "stopA1" in self.debug:
            return
        r0, w0 = self.load_w_piece(self.w_in[l], 0, 512)
        r1, w1 = self.load_w_piece(self.w_in[l], 512, 768)
        for tt in range(NT):
            b = tt % 2
            pa, pb = self.ps[0 + b], self.ps[2 + b]
            pak, pbk = "ps%d" % b, "ps%d" % (2 + b)
            ts = slice(tt * 128, (tt + 1) * 128)
            for kc in range(8):
                self.mm(pa[:], self.actT[:, kc, ts], w0[:, kc, :], kc == 0, kc == 7, [("actT", tt), "ring%d" % r0], pak)
            for kc in range(8):
                self.mm(pb[:, 0:256], self.actT[:, kc, ts], w1[:, kc, :], kc == 0, kc == 7,
                        [("actT", tt), "ring%d" % r1], pbk)
            stg, sk = stage[b], "M_stage%d" % b
            self.act(stg[:, 0:512], pa[:], AF.Copy, [pak], [sk])
            self.V(lambda e, stg=stg, pb=pb: e.tensor_copy(out=stg[:, 512:768], in_=pb[:, 0:256]), [pbk], [sk], partial=True)
            if "stopA2" in self.debug:
                continue
            qn, qnk = qkn[b], "M_qkn%d" % b
            sv = stg[:, 0:640].rearrange("p (h d) -> p h d", h=10)
            qv = qn.rearrange("p (h d) -> p h d", h=10)
            self.V(lambda e, qn=qn, stg=stg: e.tensor_tensor(out=qn, in0=stg[:, 0:640], in1=stg[:, 0:640], op=ALU.mult),
                   [sk], [qnk])
            self.V(lambda e, qv=qv: e.tensor_reduce(out=st10[:, 0:10], in_=qv, axis=AX.X, op=ALU.add), [qnk], ["M_st10"])
            self.act(rs10[:, 0:10], st10[:, 0:10], AF.Ln, ["M_st10"], ["M_rs10"], bias=EPS, scale=1.0 / 64)
            self.act(rs10[:, 0:10], rs10[:, 0:10], AF.Exp, ["M_rs10"], ["M_rs10"], scale=-0.5)
            self.V(lambda e, qv=qv, sv=sv: e.tensor_tensor(out=qv, in0=sv, in1=rs10[:, 0:10].unsqueeze(2).to_broadcast([128, 10, 64]),
                                                          op=ALU.mult), [sk, "M_rs10"], [qnk])
            self.V(lambda e, qn=qn: e.tensor_tensor(out=qn, in0=qn, in1=qkgrep, op=ALU.mult), [qnk, "M_qkgrep"], [qnk])
            if "stopA3" in self.debug:
                continue
            self.store(self.kout[l, ts, :], qn[:, 512:640], qnk)
            self.store(self.vout[l, ts, :], stg[:, 640:768], sk)
            self.V(lambda e, stg=stg, tt=tt: e.tensor_copy(out=vA[:, 4 + tt, :, 0:64],
                                                          in_=stg[:, 640:768].rearrange("p (g d) -> p g d", g=2)),
                   [sk], [("M_vA", tt)])
            if "stopA4" in self.debug:
                continue
            rp, rpk = ropet[b], "M_rope%d" % b
            self.load(rp, self.rope[ts, :, :], rpk)
            self.V(lambda e, qn=qn, rp=rp: e.tensor_tensor(out=t1, in0=qn, in1=rp[:, 0, :], op=ALU.mult), [qnk, rpk], ["M_t1"])
            q3 = qn.rearrange("p (h two s) -> p h two s", h=20, two=2)
            t23 = t2.rearrange("p (h two s) -> p h two s", h=20, two=2)
            sn3 = rp[:, 1, :].rearrange("p (h two s) -> p h two s", h=20, two=2)
            for two in range(2):
                self.V(lambda e, two=two, q3=q3, t23=t23, sn3=sn3: e.tensor_tensor(
                    out=t23[:, :, two, :], in0=q3[:, :, 1 - two, :], in1=sn3[:, :, two, :], op=ALU.mult),
                    [qnk, rpk], ["M_t2"], partial=(two > 0))
            qr, qrk = qkr[b], "M_qkr%d" % b
            for g in range(2):
                self.V(lambda e, qr=qr, g=g: e.tensor_tensor(
                    out=qr[:, 0:512].rearrange("p (j c) -> p j c", j=4)[:, :, g * 64:(g + 1) * 64],
                    in0=t1[:, g * 256:(g + 1) * 256].rearrange("p (j d) -> p j d", j=4),
                    in1=t2[:, g * 256:(g + 1) * 256].rearrange("p (j d) -> p j d", j=4), op=ALU.add),
                    ["M_t1", "M_t2"], [qrk], partial=(g > 0))
            self.V(lambda e, qr=qr: e.tensor_tensor(out=qr[:, 512:640], in0=t1[:, 512:640], in1=t2[:, 512:640], op=ALU.add),
                   ["M_t1", "M_t2"], [qrk], partial=True)
            if "stopA5" in self.debug:
                continue
            for j in range(4):
                if "noq" in self.debug:
                    break
                self.tr(self.psT[:, j, :], qr[:, j * 128:(j + 1) * 128], self.idb[:], [qrk], "psT", partial=(j > 0))
            self.tr(self.psT[:, 4, :], qr[:, 512:640], self.idb[:], [qrk], "psT", partial=True)
            if "noq" not in self.debug:
                self.V(lambda e, tt=tt: e.tensor_copy(out=qT[:, tt, :], in_=self.psT[:, 0:4, :].rearrange("p j n -> p (j n)")),
                       ["psT"], [("M_qT", tt)])
            self.V(lambda e, tt=tt: e.tensor_copy(out=kT[:, 512 + tt * 128:512 + (tt + 1) * 128], in_=self.psT[:, 4, :]),
                   ["psT"], [("M_kT", tt)])

        if "qk" in self.debug and l == 0:
            d = self.dbg("qT", [128, NT, 512], BF16)
            self.store(d, qT, ("M_qT", 0))
            for tt in range(1, NT):
                self.S.ops[-1].deps += [(w, "raw") for w in self.S.writers[("M_qT", tt)]]
            d = self.dbg("kT", [128, 512 + T], BF16)
            self.store(d, kT, ("M_kT", 0))
            for tt in range(1, NT):
                self.S.ops[-1].deps += [(w, "raw") for w in self.S.writers[("M_kT", tt)]]
            self.S.ops[-1].deps += [(w, "raw") for w in self.S.writers["M_kT_ctx"]]

        if "stop_att_proj" in self.debug:
            return
        allk = ["M_kT_ctx", "M_vA_ctx", "M_vA1"] + [("M_kT", t) for t in range(NT)] + [("M_vA", t) for t in range(NT)]
        it = 0
        for qb in range(NT):
            qs = slice(qb * 128, (qb + 1) * 128)
            for g in range(2):
                po = self.ps[4 + (it % 2)]
                pok = "ps%d" % (4 + (it % 2))
                rc = rec[it % 2]
                rck = "M_rec%d" % (it % 2)
                it += 1
                pov = po[:, 0:512].rearrange("p (j c) -> p j c", j=4)
                for kc in range(12):
                    x = (kc % 3)
                    psc, psk = self.ps[x], "ps%d" % x
                    self.mm(psc[:], kT[g * 64:(g + 1) * 64, kc * 128:(kc + 1) * 128],
                            qT[g * 64:(g + 1) * 64, qb, :], True, True, allk + [("M_qT", qb)], psk, partial=False)
                    pt, ptk = pTs[x], "M_pT%d" % x
                    self.act(pt, psc[:], AF.Exp, [psk, "abias_s"], [ptk], scale=0.125,
                             bias=self.abias_s[:, kc * NT + qb:kc * NT + qb + 1])
                    for j in range(4):
                        self.mm(pov[:, j, 0:65], pt[:, j * 128:(j + 1) * 128], vA[:, kc, g, 0:65], kc == 0 and j == 0, kc == 11 and j == 3,
                                [ptk] + allk, pok, partial=not (kc == 0 and j == 0))
                self.V(lambda e, rc=rc, pov=pov: e.reciprocal(out=rc, in_=pov[:, :, 64]), [pok], [rck])
                dst = self.cat[:, qb, g * 256:(g + 1) * 256].rearrange("p (j d) -> p j d", j=4)
                self.V(lambda e, rc=rc, pov=pov, dst=dst: e.tensor_tensor(
                    out=dst, in0=pov[:, :, 0:64], in1=rc.unsqueeze(2).to_broadcast([128, 4, 64]), op=ALU.mult),
                    [pok, rck], [("cat", qb, "a%d" % g)])


    def gla(self, l):
        self.ar_reset()
        stB = [self.ar([800]) for _ in range(2)]
        gcT = self.ar([T])
        wgg_s = self.ar([256])
        gnrep = self.ar([256])
        sp = [self.ar([256]) for _ in range(2)]
        E = [self.ar([3, 256]) for _ in range(2)]
        qkt = [self.ar([6, 128], BF16) for _ in range(2)]
        khat = self.ar([NT, 256], BF16)
        qtT = self.ar([2, T], BF16)
        ktT = self.ar([2, T], BF16)
        vb_s = self.ar([NT, 256], BF16)
        srb = self.ar([NT, 256], BF16)
        dl = self.ar([NT, 2])
        ob = self.ar([NT, 256])
        attm = [self.ar([512], BF16) for _ in range(2)]
        qmsk = [self.ar([512], BF16) for _ in range(2)]
        Snew = [self.ar([64]) for _ in range(2)]
        Scur = self.ar([64])
        Sbf = self.ar([64], BF16)
        st4 = self.ar([8])
        sq = self.ar([256])

        self.load(wgg_s[0:33, :], self.wgg[l], "M_wgg")
        self.load(gnrep[:, 0:64], self.gla_norm[l:l + 1, :].partition_broadcast(128), "M_gn0")
        for h in range(1, 4):
            self.V(lambda e, h=h: e.tensor_copy(out=gnrep[:, h * 64:(h + 1) * 64], in_=gnrep[:, 0:64]), ["M_gn0"], [("M_gn", h)])
        self.V(lambda e: e.memset(gcT[32:64, :], 1.0), [], ["M_gcT1"])

        r0, w0 = self.load_w_piece(self.w_in[l], 768, 1280)
        r1, w1 = self.load_w_piece(self.w_in[l], 1280, 1568)
        for half in range(2):
            hs = slice(half * 512, (half + 1) * 512)
            pg = self.ps[4 + half]
            pgk = "ps%d" % (4 + half)
            for kc in range(8):
                self.mm(pg[0:32, :], w1[:, kc, 256:288], self.actT[:, kc, hs], kc == 0, kc == 7,
                        [("actT", t) for t in range(half * 4, half * 4 + 4)] + ["ring%d" % r1], pgk)
            self.V(lambda e, pg=pg, hs=hs: e.tensor_copy(out=gcT[0:32, hs], in_=pg[0:32, :]), [pgk], [("M_gcT", half)])

        if "stopB0" in self.debug:
            return
        for tt in range(NT):
            b = tt % 2
            ts = slice(tt * 128, (tt + 1) * 128)
            pa, pb = self.ps[0 + b], self.ps[2 + b]
            pak, pbk = "ps%d" % b, "ps%d" % (2 + b)
            for kc in range(8):
                self.mm(pa[:], self.actT[:, kc, ts], w0[:, kc, :], kc == 0, kc == 7, [("actT", tt), "ring%d" % r0], pak)
            for kc in range(8):
                self.mm(pb[:, 0:256], self.actT[:, kc, ts], w1[:, kc, 0:256], kc == 0, kc == 7, [("actT", tt), "ring%d" % r1], pbk)
            stg, sk = stB[b], "M_stB%d" % b
            self.V(lambda e, stg=stg, pa=pa: e.tensor_scalar(out=stg[:, 0:128], in0=pa[:, 0:128], scalar1=32.0 ** -0.5, scalar2=None,
                                                             op0=ALU.mult), [pak], [sk])
            self.V(lambda e, stg=stg, pa=pa: e.tensor_copy(out=stg[:, 128:256], in_=pa[:, 128:256]), [pak], [sk], partial=True)
            if "noB_vb" not in self.debug:
                self.V(lambda e, pa=pa, tt=tt: e.tensor_copy(out=vb_s[:, tt, :], in_=pa[:, 256:512]), [pak], [("M_vb", tt)])
            if "noB_srb" not in self.debug:
                self.act(srb[:, tt, :], pb[:, 0:256], AF.Silu, [pbk], [("M_srb", tt)])
            if "stopB2" in self.debug:
                continue
            px = self.ps[6]
            self.mm(px[:, 0:256], gcT[0:33, ts], wgg_s[0:33, :], True, True,
                    [("M_gcT", tt // 4), "M_gcT1", "M_wgg"], "ps6", partial=False)
            spt, spk = sp[b], "M_sp%d" % b
            self.act(spt, px[:, 0:256], AF.Exp, ["ps6"], [spk], scale=-1.0)
            self.act(spt, spt, AF.Ln, [spk], [spk], bias=1.0)
            if "stopB2a" in self.debug:
                continue
            pc = self.ps[4 + b]
            pck = "ps%d" % (4 + b)
            for d in range(2):
                cs = slice(d * 128, (d + 1) * 128)
                self.mm(pc[:, d * 128:(d + 1) * 128], self.trif[:, d, :], spt[:, cs], True, True,
                        [spk, "trif"], pck, partial=(d > 0))
            for d in range(2):
                cs = slice(d * 128, (d + 1) * 128)
                self.mm(pc[:, 256 + d * 128:256 + (d + 1) * 128], self.trif[:, 2 + d, :], spt[:, cs], True, True,
                        [spk, "trif"], pck, partial=True)
            Et, Ek = E[b], "M_E%d" % b
            self.act(Et[:, 0, :], pc[:, 0:256], AF.Exp, [pck], [Ek])
            self.act(Et[:, 1, :], pc[:, 0:256], AF.Exp, [pck], [Ek], scale=-1.0, partial=True)
            self.act(Et[:, 2, :], pc[:, 256:512], AF.Exp, [pck], [Ek], partial=True)
            if "stopB2b" in self.debug:
                continue
            pd = self.ps[6]
            for d in range(2):
                self.mm(pd[:, 256 + 8 * d:264 + 8 * d], spt[:, d * 128:(d + 1) * 128], self.trif[:, 4, 0:8], True, True,
                        [spk, "trif"], "ps6", partial=(d > 0))
            self.act(dl[:, tt, :], pd[:, 256:272].rearrange("p (d e) -> p d e", d=2)[:, :, 0], AF.Exp, ["ps6"], [("M_dl", tt)])
            if "stopB3" in self.debug:
                continue
            qt, qk_ = qkt[b], "M_qkt%d" % b
            for d in range(2):
                cs = slice(d * 128, (d + 1) * 128)
                self.V(lambda e, qt=qt, stg=stg, Et=Et, d=d, cs=cs: e.tensor_tensor(
                    out=qt[:, 2 * d, :], in0=stg[:, 0:128], in1=Et[:, 0, cs], op=ALU.mult), [sk, Ek], [qk_], partial=(d > 0))
                self.V(lambda e, qt=qt, stg=stg, Et=Et, d=d, cs=cs: e.tensor_tensor(
                    out=qt[:, 2 * d + 1, :], in0=stg[:, 128:256], in1=Et[:, 1, cs], op=ALU.mult), [sk, Ek], [qk_], partial=True)
                self.G(lambda e, stg=stg, Et=Et, d=d, cs=cs, tt=tt: e.tensor_tensor(
                    out=khat[:, tt, cs], in0=stg[:, 128:256], in1=Et[:, 2, cs], op=ALU.mult), [sk, Ek], [("M_khat", tt)],
                    partial=(d > 0))
            for i in range(4):
                self.tr(self.psT[:, i, :], qt[:, i, :], self.idb[:], [qk_], "psT", partial=(i > 0))
            for d in range(2):
                self.V(lambda e, d=d, ts=ts: e.tensor_copy(out=qtT[:, d, ts], in_=self.psT[:, 2 * d, :]), ["psT"], [("M_qtT", tt)],
                       partial=(d > 0))
                self.V(lambda e, d=d, ts=ts: e.tensor_copy(out=ktT[:, d, ts], in_=self.psT[:, 2 * d + 1, :]), ["psT"],
                       [("M_ktT", tt)], partial=(d > 0))

        if "stopB1" in self.debug:
            return
        it = 0
        for d in range(2):
            order = list(range(NT)) if d == 0 else list(range(NT - 1, -1, -1))
            if "noSload" not in self.debug:
                self.load(Scur, self.s0_gla[l, d], "M_Scur")
            if "noSbf" not in self.debug:
                self.act(Sbf, Scur, AF.Copy, ["M_Scur"], ["M_Sbf"])
            for n, tt in enumerate(order):
                ts = slice(tt * 128, (tt + 1) * 128)
                b = it % 2
                it += 1
                if "rB0" in self.debug:
                    continue
                pat = self.ps[0 + b]
                patk = "ps%d" % b
                rd = [("M_qtT", tt), ("M_ktT", tt)]
                qm, qmk = qmsk[b], "M_qm%d" % b
                for h in range(4):
                    self.V(lambda e, qm=qm, h=h, d=d, ts=ts: e.tensor_scalar(out=qm[:, h * 128:(h + 1) * 128], in0=qtT[:, d, ts],
                                                                             scalar1=self.hmask_s[:, h:h + 1], scalar2=None,
                                                                             op0=ALU.mult),
                           [("M_qtT", tt), "hmask_s"], [qmk], partial=(h > 0))
                for h in range(4):
                    self.mm(pat[:, h * 128:(h + 1) * 128], ktT[:, d, ts], qm[:, h * 128:(h + 1) * 128], True, True,
                            [("M_ktT", tt), qmk], patk, partial=(h > 0))
                am, amk = attm[b], "M_attm%d" % b
                self.V(lambda e, am=am, pat=pat, d=d: e.tensor_tensor(out=am, in0=pat[:], in1=self.mask4[:, d, :], op=ALU.mult),
                       [patk, "mask4"], [amk])
                po = self.ps[2 + b]
                pok = "ps%d" % (2 + b)
                for h in range(4):
                    self.mm(po[:, h * 64:(h + 1) * 64], am[:, h * 128:(h + 1) * 128], vb_s[:, tt, h * 64:(h + 1) * 64],
                            True, False, [amk, ("M_vb", tt)], pok, partial=(h > 0))
                    self.mm(po[:, h * 64:(h + 1) * 64], qm[:, h * 128:(h + 1) * 128], Sbf, False, True,
                            [qmk, "M_Sbf"], pok, partial=True)
                if d == 0:
                    self.act(ob[:, tt, :], po[:, 0:256], AF.Copy, [pok], [("M_ob", tt)])
                else:
                    self.V(lambda e, tt=tt, po=po: e.tensor_tensor(out=ob[:, tt, :], in0=ob[:, tt, :], in1=po[:, 0:256], op=ALU.add),
                           [pok, ("M_ob", tt)], [("M_ob", tt)])
                if "noSupd" in self.debug:
                    continue
                pS = self.ps[4 + b]
                pSk = "ps%d" % (4 + b)
                self.mm(pS[:, 0:256], khat[:, tt, d * 128:(d + 1) * 128], vb_s[:, tt, :], True, True,
                        [("M_khat", tt), ("M_vb", tt)], pSk, partial=False)
                sn, snk = Snew[b], "M_Snew%d" % b
                for h in range(4):
                    hp = slice(32 * h, 32 * h + 32)
                    self.V(lambda e, sn=sn, hp=hp, h=h, pS=pS, tt=tt, d=d: e.scalar_tensor_tensor(
                        out=sn[hp, :], in0=Scur[hp, :], scalar=dl[hp, tt, d:d + 1], in1=pS[hp, h * 64:(h + 1) * 64],
                        op0=ALU.mult, op1=ALU.add), ["M_Scur", ("M_dl", tt), pSk], [snk], partial=(h > 0))
                is_out = (tt % 2 == 1) if d == 0 else (tt % 2 == 0)
                if is_out:
                    self.store(self.sg_out[l, d, tt // 2], sn, snk)
                if n < NT - 1:
                    nxt = order[n + 1]
                    kcol = d * NT + nxt
                    self.V(lambda e, sn=sn, kcol=kcol: e.tensor_scalar(out=Scur, in0=sn, scalar1=self.keep_s[:, kcol:kcol + 1],
                                                                      scalar2=None, op0=ALU.mult),
                           [snk, "keep_s"], ["M_Scur"])
                    self.act(Sbf, Scur, AF.Copy, ["M_Scur"], ["M_Sbf"])

        for tt in range(NT):
            if "noBnorm" in self.debug:
                break
            self.V(lambda e, tt=tt: e.tensor_tensor(out=sq, in0=ob[:, tt, :], in1=ob[:, tt, :], op=ALU.mult), [("M_ob", tt)], ["M_sq"])
            self.V(lambda e: e.tensor_reduce(out=st4[:, 0:4], in_=sq.rearrange("p (h d) -> p h d", h=4), axis=AX.X, op=ALU.add),
                   ["M_sq"], ["M_st4"])
            self.act(st4[:, 4:8], st4[:, 0:4], AF.Ln, ["M_st4"], ["M_rs4"], bias=EPS, scale=1.0 / 64)
            self.act(st4[:, 4:8], st4[:, 4:8], AF.Exp, ["M_rs4"], ["M_rs4"], scale=-0.5)
            self.V(lambda e, tt=tt: e.tensor_tensor(out=sq.rearrange("p (h d) -> p h d", h=4),
                                                    in0=ob[:, tt, :].rearrange("p (h d) -> p h d", h=4),
                                                    in1=st4[:, 4:8].unsqueeze(2).to_broadcast([128, 4, 64]), op=ALU.mult),
                   [("M_ob", tt), "M_rs4"], ["M_sq"])
            self.V(lambda e: e.tensor_tensor(out=sq, in0=sq, in1=gnrep, op=ALU.mult),
                   ["M_sq", "M_gn0"] + [("M_gn", h) for h in range(1, 4)], ["M_sq"])
            self.V(lambda e, tt=tt: e.tensor_tensor(out=self.cat[:, tt, 512:768], in0=sq, in1=srb[:, tt, :], op=ALU.mult),
                   ["M_sq", ("M_srb", tt)], [("cat", tt, "b")])


    def ar_mark_reset(self, mark):
        self.S.barrier("gpsimd", lambda e: e.memset(self.bar_s[:], 0.0))
        self.ar_off = mark

    def delta(self, l):
        self.ar_reset()
        qTc = self.ar([2, T], BF16)
        kTc = self.ar([2, T], BF16)
        k_tok = self.ar([NT, 256], BF16)
        v_tok = self.ar([NT, 256], BF16)
        sgc = self.ar([NT, 256], BF16)
        oc = self.ar([NT, 256])
        beta = self.ar([NT, 8])
        gam = self.ar([NT, 8])
        ngam = self.ar([NT, 8])
        eg = self.ar([NT, 8])
        eglm = self.ar([NT, 8])
        egl = self.ar([NT, 8])
        bkg = self.ar([NT, 8])
        cw_s = self.ar([6, 5])
        negA = self.ar([8])
        dtb_s = self.ar([8])
        dnrep = self.ar([256])
        bones = self.ar([128])
        mark = self.ar_off

        self.load(cw_s, self.cw[l], "M_cw")
        self.load(negA, self.alog[l:l + 1, :].partition_broadcast(128), "M_negA")
        self.load(dtb_s, self.dtb[l:l + 1, :].partition_broadcast(128), "M_dtb")
        self.load(dnrep[:, 0:64], self.delta_norm[l:l + 1, :].partition_broadcast(128), "M_dn0")
        for h in range(1, 4):
            self.V(lambda e, h=h: e.tensor_copy(out=dnrep[:, h * 64:(h + 1) * 64], in_=dnrep[:, 0:64]), ["M_dn0"], [("M_dn", h)])
        self.act(negA, negA, AF.Exp, ["M_negA"], ["M_negA"])
        self.V(lambda e: e.tensor_scalar(out=negA, in0=negA, scalar1=-1.0, scalar2=None, op0=ALU.mult), ["M_negA"], ["M_negA"])
        self.V(lambda e: e.memset(bones, 0.0), [], ["M_bones"])
        self.V(lambda e: e.memset(bones[0:64, 0:64], 1.0), ["M_bones"], ["M_bones"])
        self.V(lambda e: e.memset(bones[64:128, 64:128], 1.0), ["M_bones"], ["M_bones"])

        if "stopC0" in self.debug:
            return
        xin = [self.ar([4, 260]) for _ in range(2)]
        ycv = [self.ar([T]) for _ in range(2)]
        ysl = self.ar([T])
        sqb = self.ar([T])
        vTc = self.ar([2, T], BF16)
        rn = self.ar([T])

        pieces = [(1568, 2080, 4), (2080, 2336, 2)]
        cc = 0
        for (c0, c1, nch) in pieces:
            ri, wt = self.load_w_piece(self.w_in[l], c0, c1)
            for sub in range(nch):
                b = cc % 2
                xi, xik = xin[b], "M_xin%d" % b
                for half in range(2):
                    hs = slice(half * 512, (half + 1) * 512)
                    pp = self.ps[half]
                    ppk = "ps%d" % half
                    for kc in range(8):
                        self.mm(pp[:], wt[:, kc, sub * 128:(sub + 1) * 128], self.actT[:, kc, hs], kc == 0, kc == 7,
                                [("actT", t) for t in range(half * 4, half * 4 + 4)] + ["ring%d" % ri], ppk)
                    self.act(xi[:, 2 * half:2 * half + 2, 2:258], pp[:].rearrange("p (s n) -> p s n", s=2), AF.Copy, [ppk], [xik],
                             partial=(half > 0))
                self.V(lambda e, xi=xi: e.memset(xi[:, 0, 0:2], 0.0), [], [xik], partial=True)
                self.V(lambda e, xi=xi: e.memset(xi[:, 3, 258:260], 0.0), [], [xik], partial=True)
                self.V(lambda e, xi=xi: e.tensor_scalar(out=xi[:, 1:4, 0:2], in0=xi[:, 0:3, 256:258], scalar1=self.cflag_s[:, 0:1],
                                                        scalar2=None, op0=ALU.mult), [xik, "cflag_s"], [xik])
                self.V(lambda e, xi=xi: e.tensor_scalar(out=xi[:, 0:3, 258:260], in0=xi[:, 1:4, 2:4], scalar1=self.cflag_s[:, 0:1],
                                                        scalar2=None, op0=ALU.mult), [xik, "cflag_s"], [xik])
                yc, yck = ycv[b], "M_ycv%d" % b
                yv = yc.rearrange("p (s n) -> p s n", s=4)
                self.V(lambda e, xi=xi, yv=yv, cc=cc: e.tensor_scalar(out=yv, in0=xi[:, :, 0:256], scalar1=cw_s[:, cc, 0:1],
                                                                      scalar2=None, op0=ALU.mult), [xik, "M_cw"], [yck])
                for j in range(1, 5):
                    eng = self.V
                    eng(lambda e, xi=xi, yv=yv, cc=cc, j=j: e.scalar_tensor_tensor(
                        out=yv, in0=xi[:, :, j:j + 256], scalar=cw_s[:, cc, j:j + 1], in1=yv, op0=ALU.mult, op1=ALU.add),
                        [xik, "M_cw", yck], [yck])
                self.act(ysl, yc, AF.Silu, [yck], ["M_ysl"])
                if cc < 4:
                    self.V(lambda e: e.tensor_tensor(out=sqb, in0=ysl, in1=ysl, op=ALU.mult), ["M_ysl"], ["M_sqb"])
                    for half in range(2):
                        hs = slice(half * 512, (half + 1) * 512)
                        pn = self.ps[2 + half]
                        pnk = "ps%d" % (2 + half)
                        self.mm(pn[:], bones, sqb[:, hs], True, True, ["M_bones", "M_sqb"], pnk, partial=False)
                        self.act(rn[:, hs], pn[:], AF.Ln, [pnk], [("M_rn", half)], bias=EPS)
                        self.act(rn[:, hs], rn[:, hs], AF.Exp, [("M_rn", half)], [("M_rn", half)], scale=-0.5)
                    dst = qTc if cc < 2 else kTc
                    dk_ = ("M_qTc", cc) if cc < 2 else ("M_kTc", cc - 2)
                    sc_ = 64.0 ** -0.5 if cc < 2 else 1.0
                    self.V(lambda e, dst=dst, cc=cc, sc_=sc_: e.scalar_tensor_tensor(
                        out=dst[:, cc % 2, :], in0=ysl, scalar=sc_, in1=rn, op0=ALU.mult, op1=ALU.mult),
                        ["M_ysl", ("M_rn", 0), ("M_rn", 1)], [dk_])
                else:
                    self.V(lambda e, cc=cc: e.tensor_copy(out=vTc[:, cc - 4, :], in_=ysl), ["M_ysl"], [("M_vTc", cc - 4)])
                cc += 1
        if "stopC1" in self.debug:
            return
        for tt in range(NT):
            ts = slice(tt * 128, (tt + 1) * 128)
            for c in range(2):
                self.tr(self.psT[:, c, :], kTc[:, c, ts], self.idb[:], [("M_kTc", c)], "psT", partial=(c > 0))
                self.tr(self.psT[:, 2 + c, :], vTc[:, c, ts], self.idb[:], [("M_vTc", c)], "psT", partial=True)
            self.V(lambda e, tt=tt: e.tensor_copy(out=k_tok[:, tt, :], in_=self.psT[:, 0:2, :].rearrange("p c n -> p (c n)")),
                   ["psT"], [("M_ktok", tt)])
            self.V(lambda e, tt=tt: e.tensor_copy(out=v_tok[:, tt, :], in_=self.psT[:, 2:4, :].rearrange("p c n -> p (c n)")),
                   ["psT"], [("M_vtok", tt)])
        if "stopC2" in self.debug:
            return
        ri, wt = self.load_w_piece(self.w_in[l], 2336, 2608)
        g8 = [self.ar([8]) for _ in range(2)]
        g16 = [self.ar([16]) for _ in range(2)]
        for tt in range(NT):
            b = tt % 2
            ts = slice(tt * 128, (tt + 1) * 128)
            pg = self.ps[4 + b]
            pgk = "ps%d" % (4 + b)
            for kc in range(8):
                self.mm(pg[:, 0:256], self.actT[:, kc, ts], wt[:, kc, 0:256], kc == 0, kc == 7, [("actT", tt), "ring%d" % ri], pgk)
            for kc in range(8):
                self.mm(pg[:, 256:272], self.actT[:, kc, ts], wt[:, kc, 256:272], kc == 0, kc == 7, [("actT", tt), "ring%d" % ri], pgk,
                        partial=True)
            self.act(sgc[:, tt, :], pg[:, 0:256], AF.Silu, [pgk], [("M_sgc", tt)])
            if "gCa" in self.debug:
                continue
            self.act(beta[:, tt, :], pg[:, 256:264], AF.Exp, [pgk], [("M_beta", tt)], scale=-1.0)
            self.V(lambda e, tt=tt: e.tensor_scalar(out=beta[:, tt, :], in0=beta[:, tt, :], scalar1=1.0, scalar2=None, op0=ALU.add),
                   [("M_beta", tt)], [("M_beta", tt)])
            self.V(lambda e, tt=tt: e.reciprocal(out=beta[:, tt, :], in_=beta[:, tt, :]), [("M_beta", tt)], [("M_beta", tt)])
            if "gCb" in self.debug:
                continue
            gt, gk = g8[b], "M_g8%d" % b
            self.V(lambda e, gt=gt, pg=pg: e.tensor_tensor(out=gt, in0=pg[:, 264:272], in1=dtb_s, op=ALU.add), [pgk, "M_dtb"], [gk])
            self.act(gt, gt, AF.Exp, [gk], [gk])
            self.act(gt, gt, AF.Ln, [gk], [gk], bias=1.0)
            self.V(lambda e, gt=gt: e.tensor_tensor(out=gt, in0=gt, in1=negA, op=ALU.mult), [gk, "M_negA"], [gk])
            if "gCc" in self.debug:
                continue
            pc = self.ps[6]
            gt2, gk2 = g16[b], "M_g16%d" % b
            for d in range(2):
                self.V(lambda e, gt=gt, gt2=gt2, d=d: e.tensor_tensor(out=gt2[:, d * 8:(d + 1) * 8], in0=gt,
                                                                      in1=self.sel8_s[:, d * 8:(d + 1) * 8], op=ALU.mult),
                       [gk, "sel8_s"], [gk2], partial=(d > 0))
            for d in range(2):
                self.mm(pc[:, 0:8], self.masks_s[:, d, :], gt2[:, d * 8:(d + 1) * 8], d == 0, d == 1, [gk2, "masks_s"], "ps6",
                        partial=(d > 0))
            self.mm(pc[:, 16:24], self.ones_f[:], gt, True, True, [gk, "ones_f"], "ps6", partial=True)
            self.V(lambda e, tt=tt: e.tensor_copy(out=gam[:, tt, :], in_=pc[:, 0:8]), ["ps6"], [("M_gam", tt)])
            if "gCd" in self.debug:
                continue
            self.V(lambda e, tt=tt: e.tensor_scalar(out=ngam[:, tt, :], in0=gam[:, tt, :], scalar1=-1.0, scalar2=None, op0=ALU.mult),
                   [("M_gam", tt)], [("M_ngam", tt)])
            self.act(eg[:, tt, :], gam[:, tt, :], AF.Exp, [("M_gam", tt)], [("M_eg", tt)])
            if "gCe" in self.debug:
                continue
            self.act(egl[:, tt, :], pc[:, 16:24], AF.Exp, ["ps6"], [("M_egl", tt)])
            self.V(lambda e, tt=tt: e.tensor_tensor(out=eglm[:, tt, :], in0=pc[:, 16:24], in1=ngam[:, tt, :], op=ALU.add),
                   ["ps6", ("M_ngam", tt)], [("M_eglm", tt)])
            self.act(eglm[:, tt, :], eglm[:, tt, :], AF.Exp, [("M_eglm", tt)], [("M_eglm", tt)])
            self.V(lambda e, tt=tt: e.tensor_tensor(out=bkg[:, tt, :], in0=beta[:, tt, :], in1=eg[:, tt, :], op=ALU.mult),
                   [("M_beta", tt), ("M_eg", tt)], [("M_bkg", tt)])

        if "stopC3" in self.debug:
            return
        self.ar_mark_reset(mark)
        bigm = self.ar([4, 128])
        cm = self.ar([7, 128])
        self.load(cm, self.cmasks, "M_cm")
        for i, (mi, sgn) in enumerate(((0, 1.0), (1, 1.0), (3, -1.0), (2, -1.0))):
            self.V(lambda e, i=i, mi=mi, sgn=sgn: e.tensor_scalar(out=bigm[:, i, :], in0=self.masks_s[:, mi, :], scalar1=-sgn * NEG,
                                                                  scalar2=None, op0=ALU.mult), ["masks_s"], [("M_bigm", i)])
        attm2 = [self.ar([512], BF16) for _ in range(2)]
        Mm4 = self.ar([512])
        SD = BF16
        Cm4 = [self.ar([512], SD) for _ in range(2)]
        Ym4 = self.ar([512], SD)
        Zm4 = [self.ar([512], SD) for _ in range(2)]
        Xm4 = [self.ar([512], SD) for _ in range(2)]
        dec4 = self.ar([512])
        dg4 = self.ar([512])
        decT4 = self.ar([512])
        id4 = self.ar([512], SD)
        bv4 = self.ar([256], SD)
        kbg4 = self.ar([256], SD)
        wT4 = self.ar([512], SD)
        vnew4 = self.ar([256], BF16)
        khat4 = self.ar([512], BF16)
        otmp4 = self.ar([256])
        Sf4 = self.ar([256])
        Sb4 = self.ar([256], BF16)
        Snew4 = [self.ar([256]) for _ in range(2)]
        H4 = range(4)
        for h in H4:
            self.V(lambda e, h=h: e.tensor_copy(out=id4[:, h * 128:(h + 1) * 128], in_=self.idf[:]), ["idf"], ["M_id4"], partial=(h > 0))
        idS = self.idb

        def c128(t, h):
            return t[:, h * 128:(h + 1) * 128]

        def c64(t, h):
            return t[:, h * 64:(h + 1) * 64]

        P = self.ps
        itn = 0
        for d in range(2):
            order = list(range(NT)) if d == 0 else list(range(NT - 1, -1, -1))
            for h in H4:
                self.load(Sf4[0:64, h * 64:(h + 1) * 64], self.s0_delta[l, d, h * 64:(h + 1) * 64, :], "M_Sf", partial=(h > 0))
                self.load(Sf4[64:128, h * 64:(h + 1) * 64], self.s0_delta[l, d, h * 64:(h + 1) * 64, :], "M_Sf", partial=True)
            self.act(Sb4, Sf4, AF.Copy, ["M_Sf"], ["M_Sb"])
            def ctx(tt):
                ts = slice(tt * 128, (tt + 1) * 128)
                cols = [d * 4 + h for h in H4]
                hrs = [slice((h % 2) * 64, (h % 2) * 64 + 64) for h in H4]
                chs = [h // 2 for h in H4]
                kcs = [slice(h * 64, (h + 1) * 64) for h in H4]
                return ts, cols, hrs, chs, kcs

            def Gr(h):
                return P[h % 2][:, (h // 2) * 256:(h // 2) * 256 + 128]

            def Ar(h):
                return P[h % 2][:, (h // 2) * 256 + 128:(h // 2) * 256 + 256]

            def Qr(h):
                return (P[4] if h % 2 == 0 else P[5])[:, h * 64:(h + 1) * 64]

            zstate = {}

            def emit_pre(tt, pn):
                ts, cols, hrs, chs, kcs = ctx(tt)
                for h in (0, 2, 1, 3):
                    hr, ch = hrs[h], chs[h]
                    bk = "ps%d" % (h % 2)
                    self.mm(Gr(h), kTc[hr, ch, ts], kTc[hr, ch, ts], True, True, [("M_kTc", ch)], bk, partial=(h >= 2))
                    self.mm(Ar(h), kTc[hr, ch, ts], qTc[hr, ch, ts], True, True, [("M_kTc", ch), ("M_qTc", ch)], bk, partial=True)
                for h in H4:
                    col = cols[h]
                    self.V(lambda e, h=h, tt=tt, col=col: e.tensor_scalar(out=c128(dg4, h), in0=self.idf[:],
                                                                          scalar1=gam[:, tt, col:col + 1], scalar2=None, op0=ALU.mult),
                           ["idf", ("M_gam", tt)], ["M_dg"], partial=(h > 0))
                for h in H4:
                    self.mm(c128(P[2], h), self.ones_f[:], c128(dg4, h), True, False, ["ones_f", "M_dg"], "ps2", partial=(h > 0))
                    self.mm(c128(P[2], h), self.idf[:], bigm[:, d, :], False, True, ["idf", ("M_bigm", d)], "ps2", partial=True)
                for h in H4:
                    self.mm(c128(P[3], h), self.ones_f[:], c128(dg4, h), True, False, ["ones_f", "M_dg"], "ps3", partial=(h > 0))
                    self.mm(c128(P[3], h), self.idf[:], bigm[:, 2 + d, :], False, True, ["idf", ("M_bigm", 2 + d)], "ps3", partial=True)
                for h in H4:
                    col = cols[h]
                    self.act(c128(dec4, h), c128(P[2], h), AF.Exp, ["ps2", ("M_gam", tt)], ["M_dec"], scale=-1.0,
                             bias=gam[:, tt, col:col + 1], partial=(h > 0))
                for h in H4:
                    col = cols[h]
                    self.act(c128(decT4, h), c128(P[3], h), AF.Exp, ["ps3", ("M_ngam", tt)], ["M_decT"], scale=1.0,
                             bias=ngam[:, tt, col:col + 1], partial=(h > 0))
                for h in H4:
                    col = cols[h]
                    self.V(lambda e, h=h, tt=tt, col=col: e.scalar_tensor_tensor(
                        out=c128(Mm4, h), in0=Gr(h), scalar=beta[:, tt, col:col + 1], in1=c128(dec4, h),
                        op0=ALU.mult, op1=ALU.mult), ["ps%d" % (h % 2), ("M_beta", tt), "M_dec"], ["M_M"], partial=(h > 0))
                for h in H4:
                    self.V(lambda e, h=h: e.tensor_tensor(out=c128(attm2[pn % 2], h), in0=Ar(h), in1=c128(decT4, h), op=ALU.mult),
                           ["ps%d" % (h % 2), "M_decT"], ["M_attm%d" % (pn % 2)], partial=(h > 0))

            def emit_inv(tt, pn):
                ts, cols, hrs, chs, kcs = ctx(tt)
                Zc, Zk = id4, "M_id4"
                Xc, Xk = id4, "M_id4"
                for k in range(7):
                    Ck, Ckk = Cm4[k % 2], "M_C%d" % (k % 2)
                    for h in H4:
                        self.G(lambda e, h=h, k=k, Ck=Ck: e.tensor_tensor(out=c128(Ck, h), in0=c128(Mm4, h), in1=cm[:, k, :], op=ALU.mult),
                               ["M_M", "M_cm"], [Ckk], partial=(h > 0))
                    for h in H4:
                        self.mm(c128(P[2], h), c128(Ck, h), c128(Zc, h), True, True, [Ckk, Zk], "ps2", partial=(h > 0))
                    self.act(Ym4, P[2][:], AF.Copy, ["ps2"], ["M_Y"])
                    for h in H4:
                        self.mm(c128(P[3], h), c128(Xc, h), c128(Ym4, h), True, True, [Xk, "M_Y"], "ps3", partial=(h > 0))
                    Zn, Znk = Zm4[k % 2], "M_Z%d" % (k % 2)
                    self.V(lambda e, Zn=Zn, Zo=Zc: e.scalar_tensor_tensor(out=Zn, in0=P[3][:], scalar=-1.0, in1=Zo, op0=ALU.mult,
                                                                          op1=ALU.add), ["ps3", Zk], [Znk])
                    Zc, Zk = Zn, Znk
                    if k < 6:
                        for h in H4:
                            self.S.op("tensor", lambda e, h=h, Zc=Zc: e.transpose(out=self.psT[:, h, :], in_=c128(Zc, h), identity=idS[:]),
                                      reads=[Zk, "ident"], writes=["psT"], partial=(h > 0))
                        Xn, Xnk = Xm4[k % 2], "M_X%d" % (k % 2)
                        self.act(Xn, self.psT[:, 0:4, :].rearrange("p a b -> p (a b)"), AF.Copy, ["psT"], [Xnk])
                        Xc, Xk = Xn, Xnk
                zstate['Z'] = (Zc, Zk)

            def emit_post(tt, pn, n):
                ts, cols, hrs, chs, kcs = ctx(tt)
                Zc, Zk = zstate['Z']
                for h in H4:
                    col, kcols = cols[h], kcs[h]
                    self.V(lambda e, h=h, tt=tt, col=col, kcols=kcols: e.tensor_scalar(
                        out=c64(bv4, h), in0=v_tok[:, tt, kcols], scalar1=beta[:, tt, col:col + 1], scalar2=None, op0=ALU.mult),
                        [("M_vtok", tt), ("M_beta", tt)], ["M_bv"], partial=(h > 0))
                    self.V(lambda e, h=h, tt=tt, col=col, kcols=kcols: e.tensor_scalar(
                        out=c64(kbg4, h), in0=k_tok[:, tt, kcols], scalar1=bkg[:, tt, col:col + 1], scalar2=None, op0=ALU.mult),
                        [("M_ktok", tt), ("M_bkg", tt)], ["M_kbg"], partial=(h > 0))
                    for half in range(2):
                        self.G(lambda e, h=h, tt=tt, col=col, kcols=kcols, half=half: e.tensor_scalar(
                            out=khat4[:, h * 128 + half * 64:h * 128 + (half + 1) * 64], in0=k_tok[:, tt, kcols],
                            scalar1=eglm[:, tt, col:col + 1], scalar2=None, op0=ALU.mult), [("M_ktok", tt), ("M_eglm", tt)],
                            ["M_khat"], partial=not (h == 0 and half == 0))
                for h in H4:
                    self.mm(P[5][0:64, h * 128:(h + 1) * 128], c64(kbg4, h), c128(Zc, h), True, True, ["M_kbg", Zk], "ps5", partial=(h > 0))
                self.V(lambda e: e.tensor_scalar(out=wT4[0:64, :], in0=P[5][0:64, :], scalar1=-1.0, scalar2=None, op0=ALU.mult),
                       ["ps5"], ["M_wT"])
                for h in H4:
                    self.mm(c64(P[6], h), c128(Zc, h), c64(bv4, h), True, False, [Zk, "M_bv"], "ps6", partial=(h > 0))
                    self.mm(c64(P[6], h), wT4[0:64, h * 128:(h + 1) * 128], Sb4[0:64, h * 64:(h + 1) * 64], False, True,
                            ["M_wT", "M_Sb"], "ps6", partial=True)
                self.act(vnew4, P[6][:, 0:256], AF.Copy, ["ps6"], ["M_vnew"])
                for h in (0, 2):
                    hr, ch = hrs[h], chs[h]
                    self.mm(Qr(h), qTc[hr, ch, ts], Sb4[hr, h * 64:(h + 1) * 64], True, True, [("M_qTc", ch), "M_Sb"], "ps4",
                            partial=(h > 0))
                for h in H4:
                    self.mm(P[6][:, 256 + h * 64:256 + (h + 1) * 64], c128(attm2[pn % 2], h), c64(vnew4, h), True, True,
                            ["M_attm%d" % (pn % 2), "M_vnew"], "ps6", partial=True)
                for h in (1, 3):
                    hr, ch = hrs[h], chs[h]
                    self.mm(Qr(h), qTc[hr, ch, ts], Sb4[hr, h * 64:(h + 1) * 64], True, True, [("M_qTc", ch), "M_Sb"], "ps5",
                            partial=(h > 1))
                for h in H4:
                    self.mm(P[4][:, 256 + h * 64:256 + (h + 1) * 64], c128(khat4, h), c64(vnew4, h), True, True, ["M_khat", "M_vnew"], "ps4",
                            partial=True)
                for h in H4:
                    col = cols[h]
                    self.V(lambda e, h=h, tt=tt, col=col: e.tensor_scalar(
                        out=c64(otmp4, h), in0=Qr(h), scalar1=eg[:, tt, col:col + 1], scalar2=None, op0=ALU.mult),
                        ["ps4" if h % 2 == 0 else "ps5", ("M_eg", tt)], ["M_otmp"], partial=(h > 0))
                if d == 0:
                    self.V(lambda e, tt=tt: e.tensor_tensor(out=oc[:, tt, :], in0=otmp4, in1=P[6][:, 256:512], op=ALU.add),
                           ["M_otmp", "ps6"], [("M_oc", tt)])
                else:
                    self.V(lambda e: e.tensor_tensor(out=otmp4, in0=otmp4, in1=P[6][:, 256:512], op=ALU.add), ["M_otmp", "ps6"], ["M_otmp"])
                    self.G(lambda e, tt=tt: e.tensor_tensor(out=oc[:, tt, :], in0=oc[:, tt, :], in1=otmp4, op=ALU.add),
                           ["M_otmp", ("M_oc", tt)], [("M_oc", tt)])
                sn, snk = Snew4[pn % 2], "M_Snew%d" % (pn % 2)
                for h in H4:
                    col = cols[h]
                    self.V(lambda e, sn=sn, h=h, tt=tt, col=col: e.scalar_tensor_tensor(
                        out=c64(sn, h), in0=c64(Sf4, h), scalar=egl[:, tt, col:col + 1], in1=P[4][:, 256 + h * 64:256 + (h + 1) * 64], op0=ALU.mult, op1=ALU.add),
                        ["M_Sf", ("M_egl", tt), "ps4"], [snk], partial=(h > 0))
                is_out = (tt % 2 == 1) if d == 0 else (tt % 2 == 0)
                if is_out:
                    for h in H4:
                        self.store(self.sd_out[l, d, tt // 2, h * 64:(h + 1) * 64, :], sn[0:64, h * 64:(h + 1) * 64], snk)
                if n < NT - 1:
                    nxt = order[n + 1]
                    kcol = d * NT + nxt
                    self.V(lambda e, sn=sn, kcol=kcol: e.tensor_scalar(out=Sf4, in0=sn, scalar1=self.keep_s[:, kcol:kcol + 1],
                                                                      scalar2=None, op0=ALU.mult), [snk, "keep_s"], ["M_Sf"])
                    self.act(Sb4, Sf4, AF.Copy, ["M_Sf"], ["M_Sb"])


            emit_pre(order[0], 0)
            for n, tt in enumerate(order):
                emit_inv(tt, n)
                if n + 1 < NT:
                    emit_pre(order[n + 1], n + 1)
                emit_post(tt, n, n)

        sq = otmp4
        st4 = self.ar([8])
        for tt in range(NT):
            ock = [("M_oc", tt)]
            self.V(lambda e, tt=tt: e.tensor_tensor(out=sq, in0=oc[:, tt, :], in1=oc[:, tt, :], op=ALU.mult), ock, ["M_otmp"])
            self.V(lambda e: e.tensor_reduce(out=st4[:, 0:4], in_=sq.rearrange("p (h d) -> p h d", h=4), axis=AX.X, op=ALU.add),
                   ["M_otmp"], ["M_st4"])
            self.act(st4[:, 4:8], st4[:, 0:4], AF.Ln, ["M_st4"], ["M_rs4"], bias=EPS, scale=1.0 / 64)
            self.act(st4[:, 4:8], st4[:, 4:8], AF.Exp, ["M_rs4"], ["M_rs4"], scale=-0.5)
            self.V(lambda e, tt=tt: e.tensor_tensor(out=sq.rearrange("p (h d) -> p h d", h=4),
                                                    in0=oc[:, tt, :].rearrange("p (h d) -> p h d", h=4),
                                                    in1=st4[:, 4:8].unsqueeze(2).to_broadcast([128, 4, 64]), op=ALU.mult),
                   ock + ["M_rs4"], ["M_otmp"])
            self.V(lambda e: e.tensor_tensor(out=sq, in0=sq, in1=dnrep, op=ALU.mult),
                   ["M_otmp", "M_dn0"] + [("M_dn", h) for h in range(1, 4)], ["M_otmp"])
            self.V(lambda e, tt=tt: e.tensor_tensor(out=self.cat[:, tt, 768:1024], in0=sq, in1=sgc[:, tt, :], op=ALU.mult),
                   ["M_otmp", ("M_sgc", tt)], [("cat", tt, "c")])

    def resid_update(self, tt, ps_lo, lo_key, ps_hi, hi_key, GG, ggkeys, lo_is_sbuf=False):
        st = self.ssq2
        self.act(self.junk[:, 0:512], ps_lo, AF.Square, [lo_key], ["junk", "ssq2"], accum_out=st[:, 0:1])
        self.act(self.junk[:, 512:1024], ps_hi, AF.Square, [hi_key], ["junk", "ssq2"], accum_out=st[:, 1:2], partial=True)
        self.V(lambda e: e.tensor_tensor(out=st[:, 2:3], in0=st[:, 0:1], in1=st[:, 1:2], op=ALU.add), ["ssq2"], ["ssq2s"])
        self.act(st[:, 3:4], st[:, 2:3], AF.Ln, ["ssq2s"], ["rstd2"], bias=EPS, scale=1.0 / D)
        self.act(st[:, 3:4], st[:, 3:4], AF.Exp, ["rstd2"], ["rstd2"], scale=-0.5)
        tmp = self.tmpf[0]
        for h, (src, key) in enumerate(((ps_lo, lo_key), (ps_hi, hi_key))):
            hs = slice(h * 512, (h + 1) * 512)
            self.V(lambda e, src=src, hs=hs: e.scalar_tensor_tensor(out=tmp[:, hs], in0=src, scalar=st[:, 3:4], in1=GG[:, hs],
                                                                   op0=ALU.mult, op1=ALU.mult),
                   [key, "rstd2"] + list(ggkeys), ["tmpf0"], partial=(h > 0))
        self.V(lambda e: e.tensor_tensor(out=self.xs[:, tt, :], in0=self.xs[:, tt, :], in1=tmp[:], op=ALU.add),
               ["tmpf0", ("xs", tt)], [("xs", tt)])

    def out_proj(self, l):
        for tt in range(NT):
            self.transpose_tile_to_actT(self.cat[:, tt, :], [("cat", tt, "a0"), ("cat", tt, "a1"), ("cat", tt, "b"), ("cat", tt, "c")], tt)
        r0, w0 = self.load_w_piece(self.w_out[l], 0, 512)
        r1, w1 = self.load_w_piece(self.w_out[l], 512, 1024)
        ggk = [("GG", 0, 0), ("GG", 0, 1)]
        for tt in range(NT):
            b = tt % 2
            pa, pb = self.ps[0 + b], self.ps[2 + b]
            pak, pbk = "ps%d" % b, "ps%d" % (2 + b)
            ts = slice(tt * 128, (tt + 1) * 128)
            for kc in range(8):
                self.mm(pa[:], self.actT[:, kc, ts], w0[:, kc, :], kc == 0, kc == 7, [("actT", tt), "ring%d" % r0], pak)
            for kc in range(8):
                self.mm(pb[:], self.actT[:, kc, ts], w1[:, kc, :], kc == 0, kc == 7, [("actT", tt), "ring%d" % r1], pbk)
            self.resid_update(tt, pa[:], pak, pb[:], pbk, self.GGm, ggk)

    def ffn(self, l):
        self.norm_to_actT(1)
        self.ar_reset()
        aT = self.ar([NFC, T], BF16)
        sg = [self.ar([512]) for _ in range(2)]
        fbuf = self.ar([NT, 512])
        it = 0
        for c0 in range(0, DFF, 512):
            c1 = min(c0 + 512, DFF)
            rg, wg = self.load_w_piece(self.w_gate[l], c0, c1)
            ru, wu = self.load_w_piece(self.w_up[l], c0, c1)
            for sub in range((c1 - c0) // 128):
                fc = c0 // 128 + sub
                for half in range(2):
                    hs = slice(half * 512, (half + 1) * 512)
                    b = it % 2
                    it += 1
                    pg, pu = self.ps[0 + b], self.ps[2 + b]
                    pgk, puk = "ps%d" % b, "ps%d" % (2 + b)
                    rd = [("actT", t) for t in range(half * 4, half * 4 + 4)]
                    for kc in range(8):
                        self.mm(pg[:], wg[:, kc, sub * 128:(sub + 1) * 128], self.actT[:, kc, hs], kc == 0, kc == 7,
                                rd + ["ring%d" % rg], pgk)
                    for kc in range(8):
                        self.mm(pu[:], wu[:, kc, sub * 128:(sub + 1) * 128], self.actT[:, kc, hs], kc == 0, kc == 7,
                                rd + ["ring%d" % ru], puk)
                    sgt, sgk = sg[b], "F_sg%d" % b
                    self.act(sgt, pg[:], AF.Silu, [pgk], [sgk])
                    self.V(lambda e, sgt=sgt, pu=pu, fc=fc, hs=hs: e.tensor_tensor(out=aT[:, fc, hs], in0=sgt, in1=pu[:],
                                                                                    op=ALU.mult),
                           [sgk, puk], [("F_aT", fc, half)])
        ggk = [("GG", 1, 0), ("GG", 1, 1)]
        groups = [(0, 8), (8, 8), (16, 6)]
        allaT = [("F_aT", fc, h) for fc in range(NFC) for h in range(2)]
        for half in range(2):
            pieces = []
            for (f0, nk) in groups:
                ri, wt = self.load_w_piece(self.w_down[l], half * 512, (half + 1) * 512, r0=f0 * 128, nk=nk)
                pieces.append((ri, wt, f0, nk))
            for tt in range(NT):
                b = tt % 2
                pf = self.ps[4 + b]
                pfk = "ps%d" % (4 + b)
                ts = slice(tt * 128, (tt + 1) * 128)
                n = 0
                for (ri, wt, f0, nk) in pieces:
                    for k in range(nk):
                        self.mm(pf[:], aT[:, f0 + k, ts], wt[:, k, :], n == 0, n == NFC - 1, allaT + ["ring%d" % ri], pfk)
                        n += 1
                if half == 0:
                    self.V(lambda e, tt=tt, pf=pf: e.tensor_copy(out=fbuf[:, tt, :], in_=pf[:]), [pfk], [("F_fbuf", tt)])
                else:
                    self.resid_update(tt, fbuf[:, tt, :], ("F_fbuf", tt), pf[:], pfk, self.GGf, ggk)

    def build(self):
        self.setup()
        for l in range(self.depth):
            self.layer(l)
        self.finish()
        st = self.S.emit()
        self.stats = st
        return self.nc

    def finish(self):
        for tt in range(NT):
            self.store(self.y[tt * 128:(tt + 1) * 128, :], self.xs[:, tt, :], ("xs", tt))

    def layer(self, l):
        self.mod_stage(l)
        self.norm_to_actT(0)
        self.attention(l)
        if "stop_after_att" in self.debug:
            return
        if "nogla" not in self.debug:
            self.gla(l)
        if "nodelta" not in self.debug:
            self.delta(l)
        self.out_proj(l)
        self.ffn(l)
        if "cat" in self.debug and l == 0:
            d = self.dbg("cat", [128, NT, D])
            self.S.dma("gpsimd", lambda e: e.dma_start(out=d, in_=self.cat[:]), "st_cat",
                       reads=[("cat", qb, k) for qb in range(NT) for k in ("a0", "a1", "b", "c")], store=True)
        if "actT0" in self.debug and l == 0:
            d = self.dbg("actT0", [128, 8, T])
            tmp = self.ar([8, T])
            self.V(lambda e: e.tensor_copy(out=tmp, in_=self.actT[:]), [("actT", t) for t in range(NT)], ["M_dbg_actT"])
            self.store(d, tmp, "M_dbg_actT")
            d2 = self.dbg("GG", [128, 2, D])
            self.store(d2[:, 0, :], self.GGm[:], ("GG", 0, 0))
            self.store(d2[:, 0, :], self.GGm[:], ("GG", 0, 1))
            self.store(d2[:, 1, :], self.GGf[:], ("GG", 1, 0))
            self.store(d2[:, 1, :], self.GGf[:], ("GG", 1, 1))


def rope_tables(sample):
    cos = np.ones((T, 64), np.float32)
    sin = np.zeros((T, 64), np.float32)
    if sample:
        t = np.arange(T)
        row = (t // 64).astype(np.float32)
        col = (t % 64).astype(np.float32)
        inv = (np.float32(10000.0) ** (-np.arange(16, dtype=np.float32) / np.float32(16))).astype(np.float32)
        ar = row[:, None] * inv
        ac = col[:, None] * inv
        ang = np.concatenate([ar, ar, ac, ac], axis=-1).astype(np.float32)
        cos = np.cos(ang).astype(np.float32)
        sin = np.sin(ang).astype(np.float32)
    sgn = np.concatenate([-np.ones(16), np.ones(16), -np.ones(16), np.ones(16)]).astype(np.float32)
    sins = sin * sgn
    tab = np.stack([np.tile(cos, (1, 10)), np.tile(sins, (1, 10))], 1)
    return np.ascontiguousarray(tab.astype(np.float32))


def core_tables(sample):
    ab = np.zeros((12, NT), np.float32)
    keep = np.ones((2, NT), np.float32)
    if not sample:
        ab[:] = NEG
        for kc in range(4, 12):
            for qb in range(NT):
                if (kc - 4) // 2 == qb // 2:
                    ab[kc, qb] = 0.0
        for tt in range(NT):
            if tt % 2 == 0:
                keep[0, tt] = 0.0
            if tt % 2 == 1:
                keep[1, tt] = 0.0
    abias = np.ascontiguousarray(np.broadcast_to(ab.reshape(1, -1), (128, 12 * NT))).astype(np.float32)
    keepr = np.ascontiguousarray(np.broadcast_to(keep.reshape(1, -1), (128, 2 * NT))).astype(np.float32)
    cflag = np.full((128, 1), 1.0 if sample else 0.0, np.float32)
    return abias, keepr, cflag


def make_masks():
    m = np.zeros((4, 128, 128), np.float32)
    i = np.arange(128)
    m[0] = (i[:, None] <= i[None, :])
    m[1] = (i[:, None] >= i[None, :])
    m[2] = (i[:, None] < i[None, :])
    m[3] = (i[:, None] > i[None, :])
    return np.ascontiguousarray(m.transpose(1, 0, 2))


def prep_shared(inp, L=DEPTH):
    sh = {}
    sh["ident"] = np.eye(128, dtype=np.float32)
    sh["w_mod"] = np.ascontiguousarray(inp["w_mod"], dtype=np.float32)
    bm = np.asarray(inp["b_mod"], np.float32)
    sh["bmod"] = np.ascontiguousarray(bm)
    sh["bmodT"] = np.ascontiguousarray(bm.reshape(L, 48, 128).transpose(0, 2, 1))
    ng = np.asarray(inp["norm_gains"], np.float32)
    sh["ng"] = np.ascontiguousarray(ng)
    sh["ngT"] = np.ascontiguousarray(ng.reshape(L, 4, 8, 128).transpose(0, 3, 1, 2))
    sh["w_in"] = np.ascontiguousarray(inp["w_in"], dtype=np.float32)
    qg = np.asarray(inp["qk_gain"], np.float32)
    sh["qkg"] = np.ascontiguousarray(np.concatenate([np.tile(qg[:, 0], (1, 8)), np.tile(qg[:, 1], (1, 2))], axis=1))
    wgg = np.zeros((L, 33, 256), np.float32)
    w = np.asarray(inp["w_gla_gate"], np.float32)
    b = np.asarray(inp["b_gla_gate"], np.float32)
    wgg[:, 0:16, 0:128] = w[:, 0]
    wgg[:, 16:32, 128:256] = w[:, 1]
    wgg[:, 32, :] = b.reshape(L, 256)
    sh["wgg"] = wgg
    sh["gla_norm"] = np.ascontiguousarray(inp["gla_norm"], dtype=np.float32)
    cwv = np.asarray(inp["conv_w"], np.float32)
    sh["cw"] = np.ascontiguousarray(cwv.reshape(L, 5, 6, 128).transpose(0, 3, 2, 1))
    sh["alog"] = np.ascontiguousarray(np.asarray(inp["a_log"], np.float32).reshape(L, 8))
    sh["dtb"] = np.ascontiguousarray(np.asarray(inp["dt_bias"], np.float32).reshape(L, 8))
    sh["delta_norm"] = np.ascontiguousarray(inp["delta_norm"], dtype=np.float32)
    for k in ("w_out", "w_gate", "w_up", "w_down"):
        sh[k] = np.ascontiguousarray(inp[k], dtype=np.float32)
    sh["masks"] = make_masks()
    sh["sel8"] = np.ascontiguousarray(np.broadcast_to(
        np.array([1, 1, 1, 1, 0, 0, 0, 0, 0, 0, 0, 0, 1, 1, 1, 1], np.float32)[None, :], (128, 16)))
    sh["hmask"] = (np.arange(128)[:, None] // 32 == np.arange(4)[None, :]).astype(np.float32)
    i = np.arange(128)
    cmk = np.zeros((7, 128, 128), np.float32)
    for k in range(7):
        bs, bb = 2 ** k, 2 ** (k + 1)
        cmk[k] = ((i[:, None] // bb == i[None, :] // bb) & (i[:, None] // bs != i[None, :] // bs)).astype(np.float32)
    sh["cmasks"] = np.ascontiguousarray(cmk.transpose(1, 0, 2))
    return sh


PER_LAYER = ("w_mod", "b_mod", "norm_gains", "w_in", "qk_gain", "w_gla_gate", "b_gla_gate", "gla_norm", "conv_w",
             "a_log", "dt_bias", "delta_norm", "w_out", "w_gate", "w_up", "w_down")
PER_LAYER1 = ("cache_k", "cache_v", "state_gla", "state_delta")


def make_in_maps(inp, L=DEPTH):
    if L != DEPTH:
        inp = dict(inp)
        for k in PER_LAYER:
            inp[k] = np.asarray(inp[k])[:L]
        for k in PER_LAYER1:
            inp[k] = np.asarray(inp[k])[:, :L]
    sh = prep_shared(inp, L)
    xs = np.asarray(inp["x_sample"], np.float32)
    xp = np.asarray(inp["x_prompt"], np.float32)
    maps = []
    tabs = {True: (rope_tables(True),) + core_tables(True), False: (rope_tables(False),) + core_tables(False)}
    for c in range(8):
        m = dict(sh)
        sample = c < 4
        if sample:
            b = c
            m["x"] = np.ascontiguousarray(xs[b])
            cond = np.asarray(inp["c"], np.float32)[b]
            m["ctx_k"] = np.ascontiguousarray(np.asarray(inp["cache_k"], np.float32)[b].reshape(L, 512, 128))
            m["ctx_v"] = np.ascontiguousarray(np.asarray(inp["cache_v"], np.float32)[b].reshape(L, 512, 128))
            m["s0_gla"] = np.ascontiguousarray(np.asarray(inp["state_gla"], np.float32)[b].reshape(L, 2, 128, 64))
            m["s0_delta"] = np.ascontiguousarray(np.asarray(inp["state_delta"], np.float32)[b].reshape(L, 2, 256, 64))
        else:
            j = c - 4
            m["x"] = np.ascontiguousarray(xp[4 * j:4 * j + 4].reshape(T, D))
            cond = np.asarray(inp["c_ctx"], np.float32)
            m["ctx_k"] = np.zeros((L, 512, 128), np.float32)
            m["ctx_v"] = np.zeros((L, 512, 128), np.float32)
            m["s0_gla"] = np.zeros((L, 2, 128, 64), np.float32)
            m["s0_delta"] = np.zeros((L, 2, 256, 64), np.float32)
        m["condT"] = np.ascontiguousarray(cond.reshape(8, 128).T)
        rope, abias, keep, cflag = tabs[sample]
        m["rope"] = rope
        m["abias"] = abias
        m["keep"] = keep
        m["cflag"] = cflag
        maps.append(m)
    return maps


_NC_CACHE = {}


def kernel(**inputs):
    maps = make_in_maps(inputs)
    if "nc" not in _NC_CACHE:
        _NC_CACHE["nc"] = Builder().build()
    nc = _NC_CACHE["nc"]
    res = run_bass_kernel_spmd(nc, maps, core_ids=list(range(8)))
    r = res.results
    L = DEPTH
    y_sample = np.stack([r[c]["y"] for c in range(4)], 0)
    y_prompt = np.concatenate([r[c]["y"].reshape(4, 256, D) for c in range(4, 8)], 0)
    nk = np.concatenate([r[c]["kout"].reshape(L, 4, 256, 2, 64).transpose(1, 0, 2, 3, 4) for c in range(4, 8)], 0)
    nv = np.concatenate([r[c]["vout"].reshape(L, 4, 256, 2, 64).transpose(1, 0, 2, 3, 4) for c in range(4, 8)], 0)
    sg = np.concatenate([r[c]["sg_out"].reshape(L, 2, 4, 4, 32, 64).transpose(2, 0, 1, 3, 4, 5) for c in range(4, 8)], 0)
    sd = np.concatenate([r[c]["sd_out"].reshape(L, 2, 4, 4, 64, 64).transpose(2, 0, 1, 3, 4, 5) for c in range(4, 8)], 0)
    return (y_prompt.astype(np.float32), y_sample.astype(np.float32), np.ascontiguousarray(nk, dtype=np.float32),
            np.ascontiguousarray(nv, dtype=np.float32), np.ascontiguousarray(sg, dtype=np.float32),
            np.ascontiguousarray(sd, dtype=np.float32))
```

```python
import numpy as np
import concourse.bass as bass
import concourse.mybir as mybir
from concourse.bass_utils import run_bass_kernel_spmd

F32 = mybir.dt.float32
BF16 = mybir.dt.bfloat16
AF = mybir.ActivationFunctionType
ALU = mybir.AluOpType
AX = mybir.AxisListType

DEPTH = 4
D = 1024
T = 1024
NT = 8
DFF = 2816
NFC = 22
PROJ = 2608
EPS = 1e-6
NEG = -30000.0


class Op:
    __slots__ = ("eng", "fn", "deps", "sig", "sigval", "dma", "dsem", "dval", "name")

    def __init__(self, eng, fn, dma, name):
        self.eng = eng
        self.fn = fn
        self.dma = dma
        self.deps = []
        self.sig = False
        self.sigval = 0
        self.dsem = None
        self.dval = 0
        self.name = name


class Sched:
    ENGS = ("tensor", "vector", "scalar", "gpsimd", "sync")

    def __init__(self, nc):
        self.nc = nc
        self.ops = []
        self.writers = {}
        self.readers = {}
        self.prev_readers = {}
        self.dsems = {}
        self.store_ops = []

    def _add(self, op, reads, writes, partial):
        reads = list(reads)
        if op.name != "barrier":
            for k in list(reads) + list(writes):
                nm = k[0] if isinstance(k, tuple) else k
                if nm.startswith("M_") or nm.startswith("F_"):
                    reads.append("ARENA")
                    break
        for r in reads:
            for w in self.writers.get(r, ()):
                op.deps.append((w, "raw"))
            self.readers.setdefault(r, []).append(op)
        for r in writes:
            rd = self.readers.get(r)
            if rd:
                for x in rd:
                    if x is not op:
                        op.deps.append((x, "war"))
                for x in self.writers.get(r, ()):
                    if x is not op:
                        op.deps.append((x, "war"))
                self.prev_readers[r] = [x for x in rd if x is not op] + list(self.writers.get(r, ()))
                self.readers[r] = []
                self.writers[r] = [op]
            else:
                if partial:
                    for x in self.prev_readers.get(r, ()):
                        op.deps.append((x, "war"))
                    self.writers.setdefault(r, []).append(op)
                else:
                    for x in self.writers.get(r, ()):
                        op.deps.append((x, "war"))
                    self.prev_readers[r] = list(self.writers.get(r, ()))
                    self.writers[r] = [op]
        self.ops.append(op)
        return op

    def op(self, eng, fn, reads=(), writes=(), partial=False, name=""):
        return self._add(Op(eng, fn, False, name), reads, writes, partial)

    def dma(self, eng, fn, semkey, reads=(), writes=(), partial=False, store=False, name=""):
        o = Op(eng, fn, True, name)
        ent = self.dsems.setdefault(semkey, [None, 0])
        ent[1] += 16
        o.dsem = semkey
        o.dval = ent[1]
        if store:
            self.store_ops.append(o)
        return self._add(o, reads, writes, partial)

    def barrier(self, eng, fn):
        return self._add(Op(eng, fn, False, "barrier"), (), ["ARENA"], False)

    def emit(self):
        nc = self.nc
        per_eng = {e: [] for e in self.ENGS}
        for o in self.ops:
            per_eng[o.eng].append(o)
        for o in self.ops:
            for (p, kind) in o.deps:
                if p.dma:
                    continue
                if p.eng == o.eng and p.eng == "tensor":
                    continue
                p.sig = True
        for e in self.ENGS:
            c = 0
            for o in per_eng[e]:
                if o.sig:
                    c += 1
                    o.sigval = c
        esem = {e: nc.alloc_semaphore(name="es_" + e) for e in self.ENGS}
        for i, (k, ent) in enumerate(self.dsems.items()):
            ent[0] = nc.alloc_semaphore(name="ds_%d" % i)
        stats = {e: [0, 0] for e in self.ENGS}

        def emit_engine(ename, eng):
            seen = {}
            for o in per_eng[ename]:
                need = {}
                for (p, kind) in o.deps:
                    if p.dma:
                        key = ("d", p.dsem)
                        sem = self.dsems[p.dsem][0]
                        val = p.dval
                    else:
                        if p.eng == ename and ename == "tensor":
                            continue
                        key = ("e", p.eng)
                        sem = esem[p.eng]
                        val = p.sigval
                    if seen.get(key, 0) >= val:
                        continue
                    if key not in need or need[key][1] < val:
                        need[key] = (sem, val)
                for key, (sem, val) in need.items():
                    eng.wait_ge(sem, val)
                    seen[key] = val
                    stats[ename][1] += 1
                ins = o.fn(eng)
                stats[ename][0] += 1
                if o.dma:
                    ins.then_inc(self.dsems[o.dsem][0], 16)
                elif o.sig:
                    ins.then_inc(esem[ename], 1)
            if ename == "sync":
                fin = {}
                for o in self.store_ops:
                    fin[o.dsem] = max(fin.get(o.dsem, 0), o.dval)
                for k, v in fin.items():
                    if seen.get(("d", k), 0) < v:
                        eng.wait_ge(self.dsems[k][0], v)

        with nc.Block() as block:
            @block.tensor
            def _(eng):
                emit_engine("tensor", eng)

            @block.vector
            def _(eng):
                emit_engine("vector", eng)

            @block.scalar
            def _(eng):
                emit_engine("scalar", eng)

            @block.gpsimd
            def _(eng):
                emit_engine("gpsimd", eng)

            @block.sync
            def _(eng):
                emit_engine("sync", eng)
        self.stats = stats
        return stats


W_IN_PIECES = [(0, 512), (512, 768), (768, 1280), (1280, 1568), (1568, 2080), (2080, 2336), (2336, 2608)]


class Builder:
    def __init__(self, depth=DEPTH, debug=()):
        self.depth = depth
        self.debug = set(debug)
        nc = bass.Bass("TRN2", target_bir_lowering=False)
        self.nc = nc
        self.S = Sched(nc)
        self.ring_i = 0
        self.ps_i = 0
        self.dbg_out = {}
        self.declare_io()
        self.alloc()

    def din(self, name, shape, dt=F32):
        return self.nc.dram_tensor(name, list(shape), dt, kind="ExternalInput").ap()

    def dout(self, name, shape, dt=F32):
        return self.nc.dram_tensor(name, list(shape), dt, kind="ExternalOutput").ap()

    def declare_io(self):
        L = self.depth
        self.x_in = self.din("x", [T, D])
        self.condT = self.din("condT", [128, 8])
        self.ident = self.din("ident", [128, 128])
        self.ctx_k = self.din("ctx_k", [L, 512, 128])
        self.ctx_v = self.din("ctx_v", [L, 512, 128])
        self.s0_gla = self.din("s0_gla", [L, 2, 128, 64])
        self.s0_delta = self.din("s0_delta", [L, 2, 256, 64])
        self.w_mod = self.din("w_mod", [L, D, 6 * D])
        self.bmodT = self.din("bmodT", [L, 128, 48])
        self.bmod = self.din("bmod", [L, 6 * D])
        self.ngT = self.din("ngT", [L, 128, 4, 8])
        self.ng = self.din("ng", [L, 4, D])
        self.w_in = self.din("w_in", [L, D, PROJ])
        self.qkg = self.din("qkg", [L, 640])
        self.wgg = self.din("wgg", [L, 33, 256])
        self.gla_norm = self.din("gla_norm", [L, 64])
        self.cw = self.din("cw", [L, 128, 6, 5])
        self.alog = self.din("alog", [L, 8])
        self.dtb = self.din("dtb", [L, 8])
        self.delta_norm = self.din("delta_norm", [L, 64])
        self.w_out = self.din("w_out", [L, D, D])
        self.w_gate = self.din("w_gate", [L, D, DFF])
        self.w_up = self.din("w_up", [L, D, DFF])
        self.w_down = self.din("w_down", [L, DFF, D])
        self.rope = self.din("rope", [T, 2, 640])
        self.abias = self.din("abias", [128, 12 * NT])
        self.keep = self.din("keep", [128, 2 * NT])
        self.cflag = self.din("cflag", [128, 1])
        self.hmask = self.din("hmask", [128, 4])
        self.sel8 = self.din("sel8", [128, 16])
        self.masks = self.din("masks", [128, 4, 128])
        self.cmasks = self.din("cmasks", [128, 7, 128])
        self.y = self.dout("y", [T, D])
        self.kout = self.dout("kout", [L, T, 128])
        self.vout = self.dout("vout", [L, T, 128])
        self.sg_out = self.dout("sg_out", [L, 2, 4, 128, 64])
        self.sd_out = self.dout("sd_out", [L, 2, 4, 256, 64])

    def dbg(self, name, shape, dt=F32):
        t = self.dout("dbg_" + name, shape, dt)
        self.dbg_out[name] = t
        return t

    def sb(self, name, shape, dt=F32):
        return self.nc.alloc_sbuf_tensor(name, list(shape), dt)

    def alloc(self):
        nc = self.nc
        self.xs = self.sb("xs", [128, NT, D])
        self.actT = self.sb("actT", [128, 8, T], BF16)
        self.idf = self.sb("idf", [128, 128])
        self.idb = self.sb("idb", [128, 128], BF16)
        self.ones_f = self.sb("ones_f", [128, 128])
        self.condT_s = self.sb("condT_s", [128, 8])
        self.scond = self.sb("scond", [128, 8], BF16)
        self.screp = self.sb("screp", [128, 8, 128], BF16)
        self.RING = 4
        self.ring = [self.sb("ring%d" % i, [128, 8 * 512], BF16) for i in range(self.RING)]
        self.GGm = self.sb("GGm", [128, D])
        self.GGf = self.sb("GGf", [128, D])
        self.ngrep = self.sb("ngrep", [128, D])
        self.brep = self.sb("brep", [128, D])
        self.bmodT_s = self.sb("bmodT_s", [128, 48])
        self.ngT_s = self.sb("ngT_s", [128, 4, 8])
        self.modT = self.sb("modT", [128, 48])
        self.AB = self.sb("AB", [128, 4, 8])
        self.ssq = self.sb("ssq", [128, NT])
        self.ssq2 = self.sb("ssq2", [128, 4])
        self.rstd = self.sb("rstd", [128, NT])
        self.xn = [self.sb("xn%d" % i, [128, D], BF16) for i in range(2)]
        self.tmpf = [self.sb("tmpf%d" % i, [128, D]) for i in range(2)]
        self.junk = self.tmpf[1]
        self.abias_s = self.sb("abias_s", [128, 12 * NT])
        self.keep_s = self.sb("keep_s", [128, 2 * NT])
        self.cflag_s = self.sb("cflag_s", [128, 1])
        self.hmask_s = self.sb("hmask_s", [128, 4])
        self.sel8_s = self.sb("sel8_s", [128, 16])
        self.masks_s = self.sb("masks_s", [128, 4, 128])
        self.bar_s = self.sb("bar_s", [128, 1])
        self.trif = self.sb("trif", [128, 5, 128])
        self.mask4 = self.sb("mask4", [128, 2, 512])
        self.ps = [nc.alloc_psum_tensor("ps%d" % i, [128, 512], F32) for i in range(7)]
        self.psT = nc.alloc_psum_tensor("psT", [128, 8, 128], BF16)
        self.cat = self.sb("cat", [128, NT, D], BF16)
        self.ARENA_W = 16896
        self.arena = self.sb("arena", [128, self.ARENA_W])
        self.ar_off = 0

    def ar_reset(self):
        self.S.barrier("gpsimd", lambda e: e.memset(self.bar_s[:], 0.0))
        self.ar_off = 0

    def ar(self, shape, dt=F32):
        n = int(np.prod(shape))
        words = n if dt == F32 else (n + 1) // 2
        words = (words + 31) // 32 * 32
        assert self.ar_off + words <= self.ARENA_W, ("arena overflow", self.ar_off, words)
        v = self.arena[:, self.ar_off:self.ar_off + words]
        self.ar_off += words
        if dt != F32:
            v = v.bitcast(dt)[:, 0:n]
        else:
            v = v[:, 0:n]
        if len(shape) == 2:
            v = v.rearrange("p (a b) -> p a b", a=shape[0])
        elif len(shape) == 3:
            v = v.rearrange("p (a b c) -> p a b c", a=shape[0], b=shape[1])
        return v

    def next_ring(self):
        i = self.ring_i % self.RING
        self.ring_i += 1
        return i

    def load_w_piece(self, w_l, c0, c1, r0=0, nk=8):
        i = self.next_ring()
        n = c1 - c0
        dst = self.ring[i][:, 0:nk * n].rearrange("p (k n) -> p k n", k=nk)
        src = w_l[r0:r0 + nk * 128, c0:c1].rearrange("(k p) n -> p k n", p=128)
        self.S.dma("gpsimd", lambda e: e.dma_start(out=dst, in_=src), "ring%d" % i, writes=["ring%d" % i])
        return i, dst

    def load(self, dst_ap, src_ap, key, eng="sync", partial=False):
        self.S.dma(eng, lambda e: e.dma_start(out=dst_ap, in_=src_ap), ("ld", key), writes=[key], partial=partial)

    def store(self, dst_ap, src_ap, key):
        self.S.dma("sync", lambda e: e.dma_start(out=dst_ap, in_=src_ap), ("st", key), reads=[key], store=True)

    def mm(self, out, lhsT, rhs, start, stop, reads, wkey, partial=None):
        if partial is None:
            partial = not start
        self.S.op("tensor", lambda e: e.matmul(out, lhsT=lhsT, rhs=rhs, start=start, stop=stop),
                  reads=reads, writes=[wkey], partial=partial)

    def tr(self, out, in_, ident, reads, wkey, partial):
        self.S.op("tensor", lambda e: e.transpose(out=out, in_=in_, identity=ident),
                  reads=reads + ["ident"], writes=[wkey], partial=partial)

    def V(self, fn, reads, writes, partial=False):
        self.S.op("vector", fn, reads=reads, writes=writes, partial=partial)

    def A(self, fn, reads, writes, partial=False):
        self.S.op("scalar", fn, reads=reads, writes=writes, partial=partial)

    def G(self, fn, reads, writes, partial=False):
        self.S.op("gpsimd", fn, reads=reads, writes=writes, partial=partial)

    def act(self, out, in_, func, reads, writes, bias=0.0, scale=1.0, accum_out=None, partial=False):
        assert not (func == AF.Copy and not (isinstance(scale, float) and scale == 1.0)), "scaled ACT copy faults on HW"
        if accum_out is None:
            self.A(lambda e: e.activation(out=out, in_=in_, func=func, bias=bias, scale=scale), reads, writes, partial)
        else:
            self.A(lambda e: e.activation(out=out, in_=in_, func=func, bias=bias, scale=scale, accum_out=accum_out),
                   reads, writes, partial)

    def setup(self):
        S = self.S
        for tt in range(NT):
            self.load(self.xs[:, tt, :], self.x_in[tt * 128:(tt + 1) * 128, :], ("xs", tt))
        self.load(self.idf[:], self.ident, "idf")
        self.load(self.condT_s[:], self.condT, "condT_s")
        self.load(self.abias_s[:], self.abias, "abias_s")
        self.load(self.keep_s[:], self.keep, "keep_s")
        self.load(self.cflag_s[:], self.cflag, "cflag_s")
        self.load(self.hmask_s[:], self.hmask, "hmask_s")
        self.load(self.sel8_s[:], self.sel8, "sel8_s")
        self.load(self.masks_s[:], self.masks, "masks_s")
        self.V(lambda e: e.tensor_copy(out=self.idb[:], in_=self.idf[:]), ["idf"], ["ident"])
        self.V(lambda e: e.memset(self.ones_f[:], 1.0), [], ["ones_f"])
        for i, mi in enumerate((0, 1, 3, 2)):
            self.V(lambda e, i=i, mi=mi: e.tensor_scalar(out=self.trif[:, i, :], in0=self.masks_s[:, mi, :], scalar1=-1.0 / 16,
                                                        scalar2=None, op0=ALU.mult), ["masks_s"], ["trif"], partial=(i > 0))
        self.V(lambda e: e.memset(self.trif[:, 4, :], -1.0 / 16), [], ["trif"], partial=True)
        for d in range(2):
            for h in range(4):
                self.V(lambda e, d=d, h=h: e.tensor_copy(out=self.mask4[:, d, h * 128:(h + 1) * 128], in_=self.masks_s[:, d, :]),
                       ["masks_s"], ["mask4"], partial=not (d == 0 and h == 0))
        for tt in range(NT):
            self.G(lambda e, tt=tt: e.memset(self.cat[:, tt, 512:768], 0.0), [], [("cat", tt, "b")])
            self.G(lambda e, tt=tt: e.memset(self.cat[:, tt, 768:1024], 0.0), [], [("cat", tt, "c")])
        self.act(self.scond[:], self.condT_s[:], AF.Silu, ["condT_s"], ["scond"])
        for kc in range(8):
            self.V(lambda e, kc=kc: e.tensor_copy(out=self.screp[:, kc, :],
                                                  in_=self.scond[:, kc:kc + 1].to_broadcast([128, 128])),
                   ["scond"], ["screp"], partial=(kc > 0))

    def mod_stage(self, l):
        S = self.S
        self.load(self.bmodT_s[:], self.bmodT[l], "bmodT_s")
        self.load(self.ngT_s[:], self.ngT[l], "ngT_s")
        pm = self.ps[6]
        for p in range(12):
            j = p // 2
            ri, wt = self.load_w_piece(self.w_mod[l], p * 512, (p + 1) * 512)
            rk = "ring%d" % ri
            if j in (2, 5):
                pb = self.ps[p % 2]
                pk = "ps%d" % (p % 2)
                GG = self.GGm if j == 2 else self.GGf
                gi = 0 if j == 2 else 1
                half = p % 2
                if half == 0:
                    self.load(self.ngrep[:], self.ng[l, 1 + 2 * gi:2 + 2 * gi, :].partition_broadcast(128), "ngrep")
                    self.load(self.brep[:], self.bmod[l:l + 1, j * D:(j + 1) * D].partition_broadcast(128), "brep")
                for kc in range(8):
                    self.mm(pb[:], self.screp[:, kc, :], wt[:, kc, :], kc == 0, kc == 7, [rk, "screp"], pk)
                hs = slice(half * 512, (half + 1) * 512)
                self.V(lambda e, GG=GG, hs=hs, pb=pb: e.tensor_tensor(
                    out=GG[:, hs], in0=pb[:], in1=self.brep[:, hs], op=ALU.add), [pk, "brep"], [("GG", gi, half)])
                self.G(lambda e, GG=GG, hs=hs: e.tensor_tensor(
                    out=GG[:, hs], in0=GG[:, hs], in1=self.ngrep[:, hs], op=ALU.mult),
                    [("GG", gi, half), "ngrep"], [("GG", gi, half)])
            else:
                for sub in range(4):
                    c = p * 4 + sub
                    for kc in range(8):
                        self.mm(pm[:, c:c + 1], wt[:, kc, sub * 128:(sub + 1) * 128], self.scond[:, kc:kc + 1],
                                kc == 0, kc == 7, [rk, "scond"], "ps6", partial=not (p == 0 and sub == 0 and kc == 0))
        for (a, b) in ((0, 16), (24, 40)):
            self.V(lambda e, a=a, b=b: e.tensor_tensor(out=self.modT[:, a:b], in0=pm[:, a:b], in1=self.bmodT_s[:, a:b],
                                                       op=ALU.add), ["ps6", "bmodT_s"], ["modT"], partial=(a > 0))
        for which, (jsh, jsc, gi) in enumerate(((0, 1, 0), (3, 4, 2))):
            self.V(lambda e, which=which, jsc=jsc, gi=gi: e.scalar_tensor_tensor(
                out=self.AB[:, 2 * which, :], in0=self.modT[:, jsc * 8:(jsc + 1) * 8], scalar=1.0,
                in1=self.ngT_s[:, gi, :], op0=ALU.add, op1=ALU.mult), ["modT", "ngT_s"], [("AB", 2 * which)])
            self.V(lambda e, which=which, jsh=jsh: e.tensor_copy(
                out=self.AB[:, 2 * which + 1, :], in_=self.modT[:, jsh * 8:(jsh + 1) * 8]), ["modT"],
                [("AB", 2 * which + 1)])

    def norm_to_actT(self, which):
        for tt in range(NT):
            self.act(self.junk[:], self.xs[:, tt, :], AF.Square, [("xs", tt)], ["junk", ("ssq", tt)],
                     accum_out=self.ssq[:, tt:tt + 1])
        self.act(self.rstd[:], self.ssq[:], AF.Ln, [("ssq", t) for t in range(NT)], ["rstd"], bias=EPS, scale=1.0 / D)
        self.act(self.rstd[:], self.rstd[:], AF.Exp, ["rstd"], ["rstd"], scale=-0.5)
        for tt in range(NT):
            xn = self.xn[tt % 2]
            xk = "xn%d" % (tt % 2)
            self.V(lambda e, tt=tt, xn=xn: e.tensor_scalar(out=xn[:], in0=self.xs[:, tt, :],
                                                           scalar1=self.rstd[:, tt:tt + 1], scalar2=None, op0=ALU.mult),
                   [("xs", tt), "rstd"], [xk])
            self.transpose_tile_to_actT(xn, xk, tt, A=self.AB[:, 2 * which, :], B=self.AB[:, 2 * which + 1, :],
                                        abkeys=[("AB", 2 * which), ("AB", 2 * which + 1)])

    def transpose_tile_to_actT(self, src, srckey, tt, A=None, B=None, abkeys=()):
        for kc in range(8):
            self.tr(self.psT[:, kc, :], src[:, kc * 128:(kc + 1) * 128], self.idb[:],
                    list(srckey) if isinstance(srckey, list) else [srckey], "psT", partial=(kc > 0))
        dst = self.actT[:, :, tt * 128:(tt + 1) * 128]
        if A is None:
            self.V(lambda e: e.tensor_copy(out=dst, in_=self.psT[:]), ["psT"], [("actT", tt)])
        else:
            tmp = self.tmpf[tt % 2]
            tk = "tmpf%d" % (tt % 2)
            tv = tmp[:].rearrange("p (k n) -> p k n", k=8)
            self.V(lambda e: e.tensor_tensor(out=tv, in0=self.psT[:], in1=A.unsqueeze(2).to_broadcast([128, 8, 128]),
                                             op=ALU.mult), ["psT"] + list(abkeys), [tk])
            self.G(lambda e: e.tensor_tensor(out=dst, in0=tv, in1=B.unsqueeze(2).to_broadcast([128, 8, 128]),
                                             op=ALU.add), [tk] + list(abkeys), [("actT", tt)])


    def attention(self, l):
        S = self.S
        self.ar_reset()
        stage = [self.ar([768]) for _ in range(2)]
        qkn = [self.ar([640]) for _ in range(2)]
        t1 = self.ar([640])
        t2 = self.ar([640])
        qkr = [self.ar([640], BF16) for _ in range(2)]
        qkgrep = self.ar([640])
        ropet = [self.ar([2, 640]) for _ in range(2)]
        qT = self.ar([NT, 512], BF16)
        kT = self.ar([512 + T], BF16)
        vA = self.ar([12, 2, 80], BF16)
        ctxk = self.ar([4, 128])
        ctxv = self.ar([4, 128])
        ctxkb = self.ar([4, 128], BF16)
        pTs = [self.ar([512], BF16) for _ in range(3)]
        st10 = self.ar([16])
        rs10 = self.ar([16])
        rec = [self.ar([4]) for _ in range(2)]

        self.load(qkgrep, self.qkg[l:l + 1, :].partition_broadcast(128), "M_qkgrep")
        self.load(ctxk, self.ctx_k[l].rearrange("(c p) n -> p c n", p=128), "M_ctxk")
        self.load(ctxv, self.ctx_v[l].rearrange("(c p) n -> p c n", p=128), "M_ctxv")
        self.V(lambda e: e.memset(vA[:, :, :, 64:80], 1.0), [], ["M_vA1"])
        self.V(lambda e: e.tensor_copy(out=ctxkb, in_=ctxk), ["M_ctxk"], ["M_ctxkb"])
        self.V(lambda e: e.tensor_copy(out=vA[:, 0:4, :, 0:64], in_=ctxv.rearrange("p c (g d) -> p c g d", g=2)),
               ["M_ctxv"], ["M_vA_ctx"])
        for c in range(4):
            self.tr(self.psT[:, c, :], ctxkb[:, c, :], self.idb[:], ["M_ctxkb"], "psT", partial=(c > 0))
        self.V(lambda e: e.tensor_copy(out=kT[:, 0:512].rearrange("p (c n) -> p c n", c=4), in_=self.psT[:, 0:4, :]),
               ["psT"], ["M_kT_ctx"])

        if "stopA1" in self.debug:
            return
        r0, w0 = self.load_w_piece(self.w_in[l], 0, 512)
        r1, w1 = self.load_w_piece(self.w_in[l], 512, 768)
        for tt in range(NT):
            b = tt % 2
            pa, pb = self.ps[0 + b], self.ps[2 + b]
            pak, pbk = "ps%d" % b, "ps%d" % (2 + b)
            ts = slice(tt * 128, (tt + 1) * 128)
            for kc in range(8):
                self.mm(pa[:], self.actT[:, kc, ts], w0[:, kc, :], kc == 0, kc == 7, [("actT", tt), "ring%d" % r0], pak)
            for kc in range(8):
                self.mm(pb[:, 0:256], self.actT[:, kc, ts], w1[:, kc, :], kc == 0, kc == 7,
                        [("actT", tt), "ring%d" % r1], pbk)
            stg, sk = stage[b], "M_stage%d" % b
            self.act(stg[:, 0:512], pa[:], AF.Copy, [pak], [sk])
            self.V(lambda e, stg=stg, pb=pb: e.tensor_copy(out=stg[:, 512:768], in_=pb[:, 0:256]), [pbk], [sk], partial=True)
            if "stopA2" in self.debug:
                continue
            qn, qnk = qkn[b], "M_qkn%d" % b
            sv = stg[:, 0:640].rearrange("p (h d) -> p h d", h=10)
            qv = qn.rearrange("p (h d) -> p h d", h=10)
            self.V(lambda e, qn=qn, stg=stg: e.tensor_tensor(out=qn, in0=stg[:, 0:640], in1=stg[:, 0:640], op=ALU.mult),
                   [sk], [qnk])
            self.V(lambda e, qv=qv: e.tensor_reduce(out=st10[:, 0:10], in_=qv, axis=AX.X, op=ALU.add), [qnk], ["M_st10"])
            self.act(rs10[:, 0:10], st10[:, 0:10], AF.Ln, ["M_st10"], ["M_rs10"], bias=EPS, scale=1.0 / 64)
            self.act(rs10[:, 0:10], rs10[:, 0:10], AF.Exp, ["M_rs10"], ["M_rs10"], scale=-0.5)
            self.V(lambda e, qv=qv, sv=sv: e.tensor_tensor(out=qv, in0=sv, in1=rs10[:, 0:10].unsqueeze(2).to_broadcast([128, 10, 64]),
                                                          op=ALU.mult), [sk, "M_rs10"], [qnk])
            self.V(lambda e, qn=qn: e.tensor_tensor(out=qn, in0=qn, in1=qkgrep, op=ALU.mult), [qnk, "M_qkgrep"], [qnk])
            if "stopA3" in self.debug:
                continue
            self.store(self.kout[l, ts, :], qn[:, 512:640], qnk)
            self.store(self.vout[l, ts, :], stg[:, 640:768], sk)
            self.V(lambda e, stg=stg, tt=tt: e.tensor_copy(out=vA[:, 4 + tt, :, 0:64],
                                                          in_=stg[:, 640:768].rearrange("p (g d) -> p g d", g=2)),
                   [sk], [("M_vA", tt)])
            if "stopA4" in self.debug:
                continue
            rp, rpk = ropet[b], "M_rope%d" % b
            self.load(rp, self.rope[ts, :, :], rpk)
            self.V(lambda e, qn=qn, rp=rp: e.tensor_tensor(out=t1, in0=qn, in1=rp[:, 0, :], op=ALU.mult), [qnk, rpk], ["M_t1"])
            q3 = qn.rearrange("p (h two s) -> p h two s", h=20, two=2)
            t23 = t2.rearrange("p (h two s) -> p h two s", h=20, two=2)
            sn3 = rp[:, 1, :].rearrange("p (h two s) -> p h two s", h=20, two=2)
            for two in range(2):
                self.V(lambda e, two=two, q3=q3, t23=t23, sn3=sn3: e.tensor_tensor(
                    out=t23[:, :, two, :], in0=q3[:, :, 1 - two, :], in1=sn3[:, :, two, :], op=ALU.mult),
                    [qnk, rpk], ["M_t2"], partial=(two > 0))
            qr, qrk = qkr[b], "M_qkr%d" % b
            for g in range(2):
                self.V(lambda e, qr=qr, g=g: e.tensor_tensor(
                    out=qr[:, 0:512].rearrange("p (j c) -> p j c", j=4)[:, :, g * 64:(g + 1) * 64],
                    in0=t1[:, g * 256:(g + 1) * 256].rearrange("p (j d) -> p j d", j=4),
                    in1=t2[:, g * 256:(g + 1) * 256].rearrange("p (j d) -> p j d", j=4), op=ALU.add),
                    ["M_t1", "M_t2"], [qrk], partial=(g > 0))
            self.V(lambda e, qr=qr: e.tensor_tensor(out=qr[:, 512:640], in0=t1[:, 512:640], in1=t2[:, 512:640], op=ALU.add),
                   ["M_t1", "M_t2"], [qrk], partial=True)
            if "stopA5" in self.debug:
                continue
            for j in range(4):
                if "noq" in self.debug:
                    break
                self.tr(self.psT[:, j, :], qr[:, j * 128:(j + 1) * 128], self.idb[:], [qrk], "psT", partial=(j > 0))
            self.tr(self.psT[:, 4, :], qr[:, 512:640], self.idb[:], [qrk], "psT", partial=True)
            if "noq" not in self.debug:
                self.V(lambda e, tt=tt: e.tensor_copy(out=qT[:, tt, :], in_=self.psT[:, 0:4, :].rearrange("p j n -> p (j n)")),
                       ["psT"], [("M_qT", tt)])
            self.V(lambda e, tt=tt: e.tensor_copy(out=kT[:, 512 + tt * 128:512 + (tt + 1) * 128], in_=self.psT[:, 4, :]),
                   ["psT"], [("M_kT", tt)])

        if "qk" in self.debug and l == 0:
            d = self.dbg("qT", [128, NT, 512], BF16)
            self.store(d, qT, ("M_qT", 0))
            for tt in range(1, NT):
                self.S.ops[-1].deps += [(w, "raw") for w in self.S.writers[("M_qT", tt)]]
            d = self.dbg("kT", [128, 512 + T], BF16)
            self.store(d, kT, ("M_kT", 0))
            for tt in range(1, NT):
                self.S.ops[-1].deps += [(w, "raw") for w in self.S.writers[("M_kT", tt)]]
            self.S.ops[-1].deps += [(w, "raw") for w in self.S.writers["M_kT_ctx"]]

        if "stop_att_proj" in self.debug:
            return
        allk = ["M_kT_ctx", "M_vA_ctx", "M_vA1"] + [("M_kT", t) for t in range(NT)] + [("M_vA", t) for t in range(NT)]
        iters = [(qb, g, kc) for qb in range(NT) for g in range(2) for kc in range(12)]

        def emit_score(i):
            qb, g, kc = iters[i]
            x = i % 3
            psc, psk = self.ps[x], "ps%d" % x
            self.mm(psc[:], kT[g * 64:(g + 1) * 64, kc * 128:(kc + 1) * 128], qT[g * 64:(g + 1) * 64, qb, :], True, True,
                    allk + [("M_qT", qb)], psk, partial=False)

        emit_score(0)
        for i, (qb, g, kc) in enumerate(iters):
            grp = i // 12
            po = self.ps[4 + (grp % 2)]
            pok = "ps%d" % (4 + (grp % 2))
            rc = rec[grp % 2]
            rck = "M_rec%d" % (grp % 2)
            pov = po[:, 0:512].rearrange("p (j c) -> p j c", j=4)
            x = i % 3
            psc, psk = self.ps[x], "ps%d" % x
            pt, ptk = pTs[x], "M_pT%d" % x
            self.act(pt, psc[:], AF.Exp, [psk, "abias_s"], [ptk], scale=0.125,
                     bias=self.abias_s[:, kc * NT + qb:kc * NT + qb + 1])
            if i + 1 < len(iters):
                emit_score(i + 1)
            for j in range(4):
                self.mm(pov[:, j, 0:65], pt[:, j * 128:(j + 1) * 128], vA[:, kc, g, 0:65], kc == 0 and j == 0, kc == 11 and j == 3,
                        [ptk] + allk, pok, partial=not (kc == 0 and j == 0))
            if kc == 11:
                self.V(lambda e, rc=rc, pov=pov: e.reciprocal(out=rc, in_=pov[:, :, 64]), [pok], [rck])
                dst = self.cat[:, qb, g * 256:(g + 1) * 256].rearrange("p (j d) -> p j d", j=4)
                self.V(lambda e, rc=rc, pov=pov, dst=dst: e.tensor_tensor(
                    out=dst, in0=pov[:, :, 0:64], in1=rc.unsqueeze(2).to_broadcast([128, 4, 64]), op=ALU.mult),
                    [pok, rck], [("cat", qb, "a%d" % g)])

    def gla(self, l):
        self.ar_reset()
        stB = [self.ar([800]) for _ in range(2)]
        gcT = self.ar([T])
        wgg_s = self.ar([256])
        gnrep = self.ar([256])
        sp = [self.ar([256]) for _ in range(2)]
        E = [self.ar([3, 256]) for _ in range(2)]
        qkt = [self.ar([6, 128], BF16) for _ in range(2)]
        khat = self.ar([NT, 256], BF16)
        qtT = self.ar([2, T], BF16)
        ktT = self.ar([2, T], BF16)
        vb_s = self.ar([NT, 256], BF16)
        srb = self.ar([NT, 256], BF16)
        dl = self.ar([NT, 2])
        ob = self.ar([NT, 256])
        attm = [self.ar([512], BF16) for _ in range(2)]
        qmsk = [self.ar([512], BF16) for _ in range(2)]
        Snew = [self.ar([64]) for _ in range(2)]
        Scur = self.ar([64])
        Sbf = self.ar([64], BF16)
        st4 = self.ar([8])
        sq = self.ar([256])

        self.load(wgg_s[0:33, :], self.wgg[l], "M_wgg")
        self.load(gnrep[:, 0:64], self.gla_norm[l:l + 1, :].partition_broadcast(128), "M_gn0")
        for h in range(1, 4):
            self.V(lambda e, h=h: e.tensor_copy(out=gnrep[:, h * 64:(h + 1) * 64], in_=gnrep[:, 0:64]), ["M_gn0"], [("M_gn", h)])
        self.V(lambda e: e.memset(gcT[32:64, :], 1.0), [], ["M_gcT1"])

        r0, w0 = self.load_w_piece(self.w_in[l], 768, 1280)
        r1, w1 = self.load_w_piece(self.w_in[l], 1280, 1568)
        for half in range(2):
            hs = slice(half * 512, (half + 1) * 512)
            pg = self.ps[4 + half]
            pgk = "ps%d" % (4 + half)
            for kc in range(8):
                self.mm(pg[0:32, :], w1[:, kc, 256:288], self.actT[:, kc, hs], kc == 0, kc == 7,
                        [("actT", t) for t in range(half * 4, half * 4 + 4)] + ["ring%d" % r1], pgk)
            self.V(lambda e, pg=pg, hs=hs: e.tensor_copy(out=gcT[0:32, hs], in_=pg[0:32, :]), [pgk], [("M_gcT", half)])

        if "stopB0" in self.debug:
            return
        for tt in range(NT):
            b = tt % 2
            ts = slice(tt * 128, (tt + 1) * 128)
            pa, pb = self.ps[0 + b], self.ps[2 + b]
            pak, pbk = "ps%d" % b, "ps%d" % (2 + b)
            for kc in range(8):
                self.mm(pa[:], self.actT[:, kc, ts], w0[:, kc, :], kc == 0, kc == 7, [("actT", tt), "ring%d" % r0], pak)
            for kc in range(8):
                self.mm(pb[:, 0:256], self.actT[:, kc, ts], w1[:, kc, 0:256], kc == 0, kc == 7, [("actT", tt), "ring%d" % r1], pbk)
            stg, sk = stB[b], "M_stB%d" % b
            self.V(lambda e, stg=stg, pa=pa: e.tensor_scalar(out=stg[:, 0:128], in0=pa[:, 0:128], scalar1=32.0 ** -0.5, scalar2=None,
                                                             op0=ALU.mult), [pak], [sk])
            self.V(lambda e, stg=stg, pa=pa: e.tensor_copy(out=stg[:, 128:256], in_=pa[:, 128:256]), [pak], [sk], partial=True)
            if "noB_vb" not in self.debug:
                self.V(lambda e, pa=pa, tt=tt: e.tensor_copy(out=vb_s[:, tt, :], in_=pa[:, 256:512]), [pak], [("M_vb", tt)])
            if "noB_srb" not in self.debug:
                self.act(srb[:, tt, :], pb[:, 0:256], AF.Silu, [pbk], [("M_srb", tt)])
            if "stopB2" in self.debug:
                continue
            px = self.ps[6]
            self.mm(px[:, 0:256], gcT[0:33, ts], wgg_s[0:33, :], True, True,
                    [("M_gcT", tt // 4), "M_gcT1", "M_wgg"], "ps6", partial=False)
            spt, spk = sp[b], "M_sp%d" % b
            self.act(spt, px[:, 0:256], AF.Exp, ["ps6"], [spk], scale=-1.0)
            self.act(spt, spt, AF.Ln, [spk], [spk], bias=1.0)
            if "stopB2a" in self.debug:
                continue
            pc = self.ps[4 + b]
            pck = "ps%d" % (4 + b)
            for d in range(2):
                cs = slice(d * 128, (d + 1) * 128)
                self.mm(pc[:, d * 128:(d + 1) * 128], self.trif[:, d, :], spt[:, cs], True, True,
                        [spk, "trif"], pck, partial=(d > 0))
            for d in range(2):
                cs = slice(d * 128, (d + 1) * 128)
                self.mm(pc[:, 256 + d * 128:256 + (d + 1) * 128], self.trif[:, 2 + d, :], spt[:, cs], True, True,
                        [spk, "trif"], pck, partial=True)
            Et, Ek = E[b], "M_E%d" % b
            self.act(Et[:, 0, :], pc[:, 0:256], AF.Exp, [pck], [Ek])
            self.act(Et[:, 1, :], pc[:, 0:256], AF.Exp, [pck], [Ek], scale=-1.0, partial=True)
            self.act(Et[:, 2, :], pc[:, 256:512], AF.Exp, [pck], [Ek], partial=True)
            if "stopB2b" in self.debug:
                continue
            pd = self.ps[6]
            for d in range(2):
                self.mm(pd[:, 256 + 8 * d:264 + 8 * d], spt[:, d * 128:(d + 1) * 128], self.trif[:, 4, 0:8], True, True,
                        [spk, "trif"], "ps6", partial=(d > 0))
            self.act(dl[:, tt, :], pd[:, 256:272].rearrange("p (d e) -> p d e", d=2)[:, :, 0], AF.Exp, ["ps6"], [("M_dl", tt)])
            if "stopB3" in self.debug:
                continue
            qt, qk_ = qkt[b], "M_qkt%d" % b
            for d in range(2):
                cs = slice(d * 128, (d + 1) * 128)
                self.V(lambda e, qt=qt, stg=stg, Et=Et, d=d, cs=cs: e.tensor_tensor(
                    out=qt[:, 2 * d, :], in0=stg[:, 0:128], in1=Et[:, 0, cs], op=ALU.mult), [sk, Ek], [qk_], partial=(d > 0))
                self.V(lambda e, qt=qt, stg=stg, Et=Et, d=d, cs=cs: e.tensor_tensor(
                    out=qt[:, 2 * d + 1, :], in0=stg[:, 128:256], in1=Et[:, 1, cs], op=ALU.mult), [sk, Ek], [qk_], partial=True)
                self.G(lambda e, stg=stg, Et=Et, d=d, cs=cs, tt=tt: e.tensor_tensor(
                    out=khat[:, tt, cs], in0=stg[:, 128:256], in1=Et[:, 2, cs], op=ALU.mult), [sk, Ek], [("M_khat", tt)],
                    partial=(d > 0))
            for i in range(4):
                self.tr(self.psT[:, i, :], qt[:, i, :], self.idb[:], [qk_], "psT", partial=(i > 0))
            for d in range(2):
                self.V(lambda e, d=d, ts=ts: e.tensor_copy(out=qtT[:, d, ts], in_=self.psT[:, 2 * d, :]), ["psT"], [("M_qtT", tt)],
                       partial=(d > 0))
                self.V(lambda e, d=d, ts=ts: e.tensor_copy(out=ktT[:, d, ts], in_=self.psT[:, 2 * d + 1, :]), ["psT"],
                       [("M_ktT", tt)], partial=(d > 0))

        if "stopB1" in self.debug:
            return
        it = 0
        for d in range(2):
            order = list(range(NT)) if d == 0 else list(range(NT - 1, -1, -1))
            if "noSload" not in self.debug:
                self.load(Scur, self.s0_gla[l, d], "M_Scur")
            if "noSbf" not in self.debug:
                self.act(Sbf, Scur, AF.Copy, ["M_Scur"], ["M_Sbf"])
            for n, tt in enumerate(order):
                ts = slice(tt * 128, (tt + 1) * 128)
                b = it % 2
                it += 1
                if "rB0" in self.debug:
                    continue
                pat = self.ps[0 + b]
                patk = "ps%d" % b
                rd = [("M_qtT", tt), ("M_ktT", tt)]
                qm, qmk = qmsk[b], "M_qm%d" % b
                for h in range(4):
                    self.V(lambda e, qm=qm, h=h, d=d, ts=ts: e.tensor_scalar(out=qm[:, h * 128:(h + 1) * 128], in0=qtT[:, d, ts],
                                                                             scalar1=self.hmask_s[:, h:h + 1], scalar2=None,
                                                                             op0=ALU.mult),
                           [("M_qtT", tt), "hmask_s"], [qmk], partial=(h > 0))
                for h in range(4):
                    self.mm(pat[:, h * 128:(h + 1) * 128], ktT[:, d, ts], qm[:, h * 128:(h + 1) * 128], True, True,
                            [("M_ktT", tt), qmk], patk, partial=(h > 0))
                am, amk = attm[b], "M_attm%d" % b
                self.V(lambda e, am=am, pat=pat, d=d: e.tensor_tensor(out=am, in0=pat[:], in1=self.mask4[:, d, :], op=ALU.mult),
                       [patk, "mask4"], [amk])
                po = self.ps[2 + b]
                pok = "ps%d" % (2 + b)
                for h in range(4):
                    self.mm(po[:, h * 64:(h + 1) * 64], am[:, h * 128:(h + 1) * 128], vb_s[:, tt, h * 64:(h + 1) * 64],
                            True, False, [amk, ("M_vb", tt)], pok, partial=(h > 0))
                    self.mm(po[:, h * 64:(h + 1) * 64], qm[:, h * 128:(h + 1) * 128], Sbf, False, True,
                            [qmk, "M_Sbf"], pok, partial=True)
                if d == 0:
                    self.act(ob[:, tt, :], po[:, 0:256], AF.Copy, [pok], [("M_ob", tt)])
                else:
                    self.V(lambda e, tt=tt, po=po: e.tensor_tensor(out=ob[:, tt, :], in0=ob[:, tt, :], in1=po[:, 0:256], op=ALU.add),
                           [pok, ("M_ob", tt)], [("M_ob", tt)])
                if "noSupd" in self.debug:
                    continue
                pS = self.ps[4 + b]
                pSk = "ps%d" % (4 + b)
                self.mm(pS[:, 0:256], khat[:, tt, d * 128:(d + 1) * 128], vb_s[:, tt, :], True, True,
                        [("M_khat", tt), ("M_vb", tt)], pSk, partial=False)
                sn, snk = Snew[b], "M_Snew%d" % b
                for h in range(4):
                    hp = slice(32 * h, 32 * h + 32)
                    self.V(lambda e, sn=sn, hp=hp, h=h, pS=pS, tt=tt, d=d: e.scalar_tensor_tensor(
                        out=sn[hp, :], in0=Scur[hp, :], scalar=dl[hp, tt, d:d + 1], in1=pS[hp, h * 64:(h + 1) * 64],
                        op0=ALU.mult, op1=ALU.add), ["M_Scur", ("M_dl", tt), pSk], [snk], partial=(h > 0))
                is_out = (tt % 2 == 1) if d == 0 else (tt % 2 == 0)
                if is_out:
                    self.store(self.sg_out[l, d, tt // 2], sn, snk)
                if n < NT - 1:
                    nxt = order[n + 1]
                    kcol = d * NT + nxt
                    self.V(lambda e, sn=sn, kcol=kcol: e.tensor_scalar(out=Scur, in0=sn, scalar1=self.keep_s[:, kcol:kcol + 1],
                                                                      scalar2=None, op0=ALU.mult),
                           [snk, "keep_s"], ["M_Scur"])
                    self.act(Sbf, Scur, AF.Copy, ["M_Scur"], ["M_Sbf"])

        for tt in range(NT):
            if "noBnorm" in self.debug:
                break
            self.V(lambda e, tt=tt: e.tensor_tensor(out=sq, in0=ob[:, tt, :], in1=ob[:, tt, :], op=ALU.mult), [("M_ob", tt)], ["M_sq"])
            self.V(lambda e: e.tensor_reduce(out=st4[:, 0:4], in_=sq.rearrange("p (h d) -> p h d", h=4), axis=AX.X, op=ALU.add),
                   ["M_sq"], ["M_st4"])
            self.act(st4[:, 4:8], st4[:, 0:4], AF.Ln, ["M_st4"], ["M_rs4"], bias=EPS, scale=1.0 / 64)
            self.act(st4[:, 4:8], st4[:, 4:8], AF.Exp, ["M_rs4"], ["M_rs4"], scale=-0.5)
            self.V(lambda e, tt=tt: e.tensor_tensor(out=sq.rearrange("p (h d) -> p h d", h=4),
                                                    in0=ob[:, tt, :].rearrange("p (h d) -> p h d", h=4),
                                                    in1=st4[:, 4:8].unsqueeze(2).to_broadcast([128, 4, 64]), op=ALU.mult),
                   [("M_ob", tt), "M_rs4"], ["M_sq"])
            self.V(lambda e: e.tensor_tensor(out=sq, in0=sq, in1=gnrep, op=ALU.mult),
                   ["M_sq", "M_gn0"] + [("M_gn", h) for h in range(1, 4)], ["M_sq"])
            self.V(lambda e, tt=tt: e.tensor_tensor(out=self.cat[:, tt, 512:768], in0=sq, in1=srb[:, tt, :], op=ALU.mult),
                   ["M_sq", ("M_srb", tt)], [("cat", tt, "b")])


    def ar_mark_reset(self, mark):
        self.S.barrier("gpsimd", lambda e: e.memset(self.bar_s[:], 0.0))
        self.ar_off = mark

    def delta(self, l):
        self.ar_reset()
        qTc = self.ar([2, T], BF16)
        kTc = self.ar([2, T], BF16)
        k_tok = self.ar([NT, 256], BF16)
        v_tok = self.ar([NT, 256], BF16)
        sgc = self.ar([NT, 256], BF16)
        oc = self.ar([NT, 256])
        beta = self.ar([NT, 8])
        gam = self.ar([NT, 8])
        ngam = self.ar([NT, 8])
        eg = self.ar([NT, 8])
        eglm = self.ar([NT, 8])
        egl = self.ar([NT, 8])
        bkg = self.ar([NT, 8])
        cw_s = self.ar([6, 5])
        negA = self.ar([8])
        dtb_s = self.ar([8])
        dnrep = self.ar([256])
        bones = self.ar([128])
        mark = self.ar_off

        self.load(cw_s, self.cw[l], "M_cw")
        self.load(negA, self.alog[l:l + 1, :].partition_broadcast(128), "M_negA")
        self.load(dtb_s, self.dtb[l:l + 1, :].partition_broadcast(128), "M_dtb")
        self.load(dnrep[:, 0:64], self.delta_norm[l:l + 1, :].partition_broadcast(128), "M_dn0")
        for h in range(1, 4):
            self.V(lambda e, h=h: e.tensor_copy(out=dnrep[:, h * 64:(h + 1) * 64], in_=dnrep[:, 0:64]), ["M_dn0"], [("M_dn", h)])
        self.act(negA, negA, AF.Exp, ["M_negA"], ["M_negA"])
        self.V(lambda e: e.tensor_scalar(out=negA, in0=negA, scalar1=-1.0, scalar2=None, op0=ALU.mult), ["M_negA"], ["M_negA"])
        self.V(lambda e: e.memset(bones, 0.0), [], ["M_bones"])
        self.V(lambda e: e.memset(bones[0:64, 0:64], 1.0), ["M_bones"], ["M_bones"])
        self.V(lambda e: e.memset(bones[64:128, 64:128], 1.0), ["M_bones"], ["M_bones"])

        if "stopC0" in self.debug:
            return
        xin = [self.ar([4, 260]) for _ in range(2)]
        ycv = [self.ar([T]) for _ in range(2)]
        ysl = self.ar([T])
        sqb = self.ar([T])
        vTc = self.ar([2, T], BF16)
        rn = self.ar([T])

        pieces = [(1568, 2080, 4), (2080, 2336, 2)]
        cc = 0
        for (c0, c1, nch) in pieces:
            ri, wt = self.load_w_piece(self.w_in[l], c0, c1)
            for sub in range(nch):
                b = cc % 2
                xi, xik = xin[b], "M_xin%d" % b
                for half in range(2):
                    hs = slice(half * 512, (half + 1) * 512)
                    pp = self.ps[half]
                    ppk = "ps%d" % half
                    for kc in range(8):
                        self.mm(pp[:], wt[:, kc, sub * 128:(sub + 1) * 128], self.actT[:, kc, hs], kc == 0, kc == 7,
                                [("actT", t) for t in range(half * 4, half * 4 + 4)] + ["ring%d" % ri], ppk)
                    self.act(xi[:, 2 * half:2 * half + 2, 2:258], pp[:].rearrange("p (s n) -> p s n", s=2), AF.Copy, [ppk], [xik],
                             partial=(half > 0))
                self.V(lambda e, xi=xi: e.memset(xi[:, 0, 0:2], 0.0), [], [xik], partial=True)
                self.V(lambda e, xi=xi: e.memset(xi[:, 3, 258:260], 0.0), [], [xik], partial=True)
                self.V(lambda e, xi=xi: e.tensor_scalar(out=xi[:, 1:4, 0:2], in0=xi[:, 0:3, 256:258], scalar1=self.cflag_s[:, 0:1],
                                                        scalar2=None, op0=ALU.mult), [xik, "cflag_s"], [xik])
                self.V(lambda e, xi=xi: e.tensor_scalar(out=xi[:, 0:3, 258:260], in0=xi[:, 1:4, 2:4], scalar1=self.cflag_s[:, 0:1],
                                                        scalar2=None, op0=ALU.mult), [xik, "cflag_s"], [xik])
                yc, yck = ycv[b], "M_ycv%d" % b
                yv = yc.rearrange("p (s n) -> p s n", s=4)
                self.V(lambda e, xi=xi, yv=yv, cc=cc: e.tensor_scalar(out=yv, in0=xi[:, :, 0:256], scalar1=cw_s[:, cc, 0:1],
                                                                      scalar2=None, op0=ALU.mult), [xik, "M_cw"], [yck])
                for j in range(1, 5):
                    eng = self.V
                    eng(lambda e, xi=xi, yv=yv, cc=cc, j=j: e.scalar_tensor_tensor(
                        out=yv, in0=xi[:, :, j:j + 256], scalar=cw_s[:, cc, j:j + 1], in1=yv, op0=ALU.mult, op1=ALU.add),
                        [xik, "M_cw", yck], [yck])
                self.act(ysl, yc, AF.Silu, [yck], ["M_ysl"])
                if cc < 4:
                    self.V(lambda e: e.tensor_tensor(out=sqb, in0=ysl, in1=ysl, op=ALU.mult), ["M_ysl"], ["M_sqb"])
                    for half in range(2):
                        hs = slice(half * 512, (half + 1) * 512)
                        pn = self.ps[2 + half]
                        pnk = "ps%d" % (2 + half)
                        self.mm(pn[:], bones, sqb[:, hs], True, True, ["M_bones", "M_sqb"], pnk, partial=False)
                        self.act(rn[:, hs], pn[:], AF.Ln, [pnk], [("M_rn", half)], bias=EPS)
                        self.act(rn[:, hs], rn[:, hs], AF.Exp, [("M_rn", half)], [("M_rn", half)], scale=-0.5)
                    dst = qTc if cc < 2 else kTc
                    dk_ = ("M_qTc", cc) if cc < 2 else ("M_kTc", cc - 2)
                    sc_ = 64.0 ** -0.5 if cc < 2 else 1.0
                    self.V(lambda e, dst=dst, cc=cc, sc_=sc_: e.scalar_tensor_tensor(
                        out=dst[:, cc % 2, :], in0=ysl, scalar=sc_, in1=rn, op0=ALU.mult, op1=ALU.mult),
                        ["M_ysl", ("M_rn", 0), ("M_rn", 1)], [dk_])
                else:
                    self.V(lambda e, cc=cc: e.tensor_copy(out=vTc[:, cc - 4, :], in_=ysl), ["M_ysl"], [("M_vTc", cc - 4)])
                cc += 1
        if "stopC1" in self.debug:
            return
        for tt in range(NT):
            ts = slice(tt * 128, (tt + 1) * 128)
            for c in range(2):
                self.tr(self.psT[:, c, :], kTc[:, c, ts], self.idb[:], [("M_kTc", c)], "psT", partial=(c > 0))
                self.tr(self.psT[:, 2 + c, :], vTc[:, c, ts], self.idb[:], [("M_vTc", c)], "psT", partial=True)
            self.V(lambda e, tt=tt: e.tensor_copy(out=k_tok[:, tt, :], in_=self.psT[:, 0:2, :].rearrange("p c n -> p (c n)")),
                   ["psT"], [("M_ktok", tt)])
            self.V(lambda e, tt=tt: e.tensor_copy(out=v_tok[:, tt, :], in_=self.psT[:, 2:4, :].rearrange("p c n -> p (c n)")),
                   ["psT"], [("M_vtok", tt)])
        if "stopC2" in self.debug:
            return
        ri, wt = self.load_w_piece(self.w_in[l], 2336, 2608)
        g8 = [self.ar([8]) for _ in range(2)]
        g16 = [self.ar([16]) for _ in range(2)]
        for tt in range(NT):
            b = tt % 2
            ts = slice(tt * 128, (tt + 1) * 128)
            pg = self.ps[4 + b]
            pgk = "ps%d" % (4 + b)
            for kc in range(8):
                self.mm(pg[:, 0:256], self.actT[:, kc, ts], wt[:, kc, 0:256], kc == 0, kc == 7, [("actT", tt), "ring%d" % ri], pgk)
            for kc in range(8):
                self.mm(pg[:, 256:272], self.actT[:, kc, ts], wt[:, kc, 256:272], kc == 0, kc == 7, [("actT", tt), "ring%d" % ri], pgk,
                        partial=True)
            self.act(sgc[:, tt, :], pg[:, 0:256], AF.Silu, [pgk], [("M_sgc", tt)])
            if "gCa" in self.debug:
                continue
            self.act(beta[:, tt, :], pg[:, 256:264], AF.Exp, [pgk], [("M_beta", tt)], scale=-1.0)
            self.V(lambda e, tt=tt: e.tensor_scalar(out=beta[:, tt, :], in0=beta[:, tt, :], scalar1=1.0, scalar2=None, op0=ALU.add),
                   [("M_beta", tt)], [("M_beta", tt)])
            self.V(lambda e, tt=tt: e.reciprocal(out=beta[:, tt, :], in_=beta[:, tt, :]), [("M_beta", tt)], [("M_beta", tt)])
            if "gCb" in self.debug:
                continue
            gt, gk = g8[b], "M_g8%d" % b
            self.V(lambda e, gt=gt, pg=pg: e.tensor_tensor(out=gt, in0=pg[:, 264:272], in1=dtb_s, op=ALU.add), [pgk, "M_dtb"], [gk])
            self.act(gt, gt, AF.Exp, [gk], [gk])
            self.act(gt, gt, AF.Ln, [gk], [gk], bias=1.0)
            self.V(lambda e, gt=gt: e.tensor_tensor(out=gt, in0=gt, in1=negA, op=ALU.mult), [gk, "M_negA"], [gk])
            if "gCc" in self.debug:
                continue
            pc = self.ps[6]
            gt2, gk2 = g16[b], "M_g16%d" % b
            for d in range(2):
                self.V(lambda e, gt=gt, gt2=gt2, d=d: e.tensor_tensor(out=gt2[:, d * 8:(d + 1) * 8], in0=gt,
                                                                      in1=self.sel8_s[:, d * 8:(d + 1) * 8], op=ALU.mult),
                       [gk, "sel8_s"], [gk2], partial=(d > 0))
            for d in range(2):
                self.mm(pc[:, 0:8], self.masks_s[:, d, :], gt2[:, d * 8:(d + 1) * 8], d == 0, d == 1, [gk2, "masks_s"], "ps6",
                        partial=(d > 0))
            self.mm(pc[:, 16:24], self.ones_f[:], gt, True, True, [gk, "ones_f"], "ps6", partial=True)
            self.V(lambda e, tt=tt: e.tensor_copy(out=gam[:, tt, :], in_=pc[:, 0:8]), ["ps6"], [("M_gam", tt)])
            if "gCd" in self.debug:
                continue
            self.V(lambda e, tt=tt: e.tensor_scalar(out=ngam[:, tt, :], in0=gam[:, tt, :], scalar1=-1.0, scalar2=None, op0=ALU.mult),
                   [("M_gam", tt)], [("M_ngam", tt)])
            self.act(eg[:, tt, :], gam[:, tt, :], AF.Exp, [("M_gam", tt)], [("M_eg", tt)])
            if "gCe" in self.debug:
                continue
            self.act(egl[:, tt, :], pc[:, 16:24], AF.Exp, ["ps6"], [("M_egl", tt)])
            self.V(lambda e, tt=tt: e.tensor_tensor(out=eglm[:, tt, :], in0=pc[:, 16:24], in1=ngam[:, tt, :], op=ALU.add),
                   ["ps6", ("M_ngam", tt)], [("M_eglm", tt)])
            self.act(eglm[:, tt, :], eglm[:, tt, :], AF.Exp, [("M_eglm", tt)], [("M_eglm", tt)])
            self.V(lambda e, tt=tt: e.tensor_tensor(out=bkg[:, tt, :], in0=beta[:, tt, :], in1=eg[:, tt, :], op=ALU.mult),
                   [("M_beta", tt), ("M_eg", tt)], [("M_bkg", tt)])

        if "stopC3" in self.debug:
            return
        self.ar_mark_reset(mark)
        bigm = self.ar([4, 128])
        cm = self.ar([7, 128])
        self.load(cm, self.cmasks, "M_cm")
        for i, (mi, sgn) in enumerate(((0, 1.0), (1, 1.0), (3, -1.0), (2, -1.0))):
            self.V(lambda e, i=i, mi=mi, sgn=sgn: e.tensor_scalar(out=bigm[:, i, :], in0=self.masks_s[:, mi, :], scalar1=-sgn * NEG,
                                                                  scalar2=None, op0=ALU.mult), ["masks_s"], [("M_bigm", i)])
        attm4 = self.ar([512], BF16)
        Mm4 = self.ar([512])
        SD = BF16
        Cm4 = [self.ar([512], SD) for _ in range(2)]
        Ym4 = self.ar([512], SD)
        Zm4 = [self.ar([512], SD) for _ in range(2)]
        Xm4 = [self.ar([512], SD) for _ in range(2)]
        dec4 = self.ar([512])
        dg4 = self.ar([512])
        decT4 = self.ar([512])
        id4 = self.ar([512], SD)
        bv4 = self.ar([256], SD)
        kbg4 = self.ar([256], SD)
        wT4 = self.ar([512], SD)
        vnew4 = self.ar([256], BF16)
        khat4 = self.ar([512], BF16)
        otmp4 = self.ar([256])
        Sf4 = self.ar([256])
        Sb4 = self.ar([256], BF16)
        Snew4 = [self.ar([256]) for _ in range(2)]
        H4 = range(4)
        for h in H4:
            self.V(lambda e, h=h: e.tensor_copy(out=id4[:, h * 128:(h + 1) * 128], in_=self.idf[:]), ["idf"], ["M_id4"], partial=(h > 0))
        idS = self.idb

        def c128(t, h):
            return t[:, h * 128:(h + 1) * 128]

        def c64(t, h):
            return t[:, h * 64:(h + 1) * 64]

        P = self.ps
        itn = 0
        for d in range(2):
            order = list(range(NT)) if d == 0 else list(range(NT - 1, -1, -1))
            for h in H4:
                self.load(Sf4[0:64, h * 64:(h + 1) * 64], self.s0_delta[l, d, h * 64:(h + 1) * 64, :], "M_Sf", partial=(h > 0))
                self.load(Sf4[64:128, h * 64:(h + 1) * 64], self.s0_delta[l, d, h * 64:(h + 1) * 64, :], "M_Sf", partial=True)
            self.act(Sb4, Sf4, AF.Copy, ["M_Sf"], ["M_Sb"])
            for n, tt in enumerate(order):
                ts = slice(tt * 128, (tt + 1) * 128)
                cols = [d * 4 + h for h in H4]
                hrs = [slice((h % 2) * 64, (h % 2) * 64 + 64) for h in H4]
                chs = [h // 2 for h in H4]
                kcs = [slice(h * 64, (h + 1) * 64) for h in H4]
                if "dS0" in self.debug:
                    continue
                def Gr(h):
                    return P[h % 2][:, (h // 2) * 256:(h // 2) * 256 + 128]

                def Ar(h):
                    return P[h % 2][:, (h // 2) * 256 + 128:(h // 2) * 256 + 256]
                for h in (0, 2, 1, 3):
                    hr, ch = hrs[h], chs[h]
                    bk = "ps%d" % (h % 2)
                    self.mm(Gr(h), kTc[hr, ch, ts], kTc[hr, ch, ts], True, True, [("M_kTc", ch)], bk, partial=(h >= 2))
                    self.mm(Ar(h), kTc[hr, ch, ts], qTc[hr, ch, ts], True, True, [("M_kTc", ch), ("M_qTc", ch)], bk, partial=True)
                if "dS1" in self.debug:
                    continue
                for h in H4:
                    col = cols[h]
                    self.V(lambda e, h=h, tt=tt, col=col: e.tensor_scalar(out=c128(dg4, h), in0=self.idf[:],
                                                                          scalar1=gam[:, tt, col:col + 1], scalar2=None, op0=ALU.mult),
                           ["idf", ("M_gam", tt)], ["M_dg"], partial=(h > 0))
                if "dS2" in self.debug:
                    continue
                for h in H4:
                    self.mm(c128(P[2], h), self.ones_f[:], c128(dg4, h), True, False, ["ones_f", "M_dg"], "ps2", partial=(h > 0))
                    self.mm(c128(P[2], h), self.idf[:], bigm[:, d, :], False, True, ["idf", ("M_bigm", d)], "ps2", partial=True)
                for h in H4:
                    self.mm(c128(P[3], h), self.ones_f[:], c128(dg4, h), True, False, ["ones_f", "M_dg"], "ps3", partial=(h > 0))
                    self.mm(c128(P[3], h), self.idf[:], bigm[:, 2 + d, :], False, True, ["idf", ("M_bigm", 2 + d)], "ps3", partial=True)
                if "dS3" in self.debug:
                    continue
                for h in H4:
                    col = cols[h]
                    self.act(c128(dec4, h), c128(P[2], h), AF.Exp, ["ps2", ("M_gam", tt)], ["M_dec"], scale=-1.0,
                             bias=gam[:, tt, col:col + 1], partial=(h > 0))
                for h in H4:
                    col = cols[h]
                    self.act(c128(decT4, h), c128(P[3], h), AF.Exp, ["ps3", ("M_ngam", tt)], ["M_decT"], scale=1.0,
                             bias=ngam[:, tt, col:col + 1], partial=(h > 0))
                for h in H4:
                    col = cols[h]
                    self.V(lambda e, h=h, tt=tt, col=col: e.scalar_tensor_tensor(
                        out=c128(Mm4, h), in0=Gr(h), scalar=beta[:, tt, col:col + 1], in1=c128(dec4, h),
                        op0=ALU.mult, op1=ALU.mult), ["ps%d" % (h % 2), ("M_beta", tt), "M_dec"], ["M_M"], partial=(h > 0))
                for h in H4:
                    self.V(lambda e, h=h: e.tensor_tensor(out=c128(attm4, h), in0=Ar(h), in1=c128(decT4, h), op=ALU.mult),
                           ["ps%d" % (h % 2), "M_decT"], ["M_attm"], partial=(h > 0))
                if "dS5" in self.debug:
                    continue
                Zc, Zk = id4, "M_id4"
                Xc, Xk = id4, "M_id4"
                for k in range(7):
                    Ck, Ckk = Cm4[k % 2], "M_C%d" % (k % 2)
                    for h in H4:
                        self.G(lambda e, h=h, k=k, Ck=Ck: e.tensor_tensor(out=c128(Ck, h), in0=c128(Mm4, h), in1=cm[:, k, :], op=ALU.mult),
                               ["M_M", "M_cm"], [Ckk], partial=(h > 0))
                    for h in H4:
                        self.mm(c128(P[2], h), c128(Ck, h), c128(Zc, h), True, True, [Ckk, Zk], "ps2", partial=(h > 0))
                    self.act(Ym4, P[2][:], AF.Copy, ["ps2"], ["M_Y"])
                    for h in H4:
                        self.mm(c128(P[3], h), c128(Xc, h), c128(Ym4, h), True, True, [Xk, "M_Y"], "ps3", partial=(h > 0))
                    Zn, Znk = Zm4[k % 2], "M_Z%d" % (k % 2)
                    self.V(lambda e, Zn=Zn, Zo=Zc: e.scalar_tensor_tensor(out=Zn, in0=P[3][:], scalar=-1.0, in1=Zo, op0=ALU.mult,
                                                                          op1=ALU.add), ["ps3", Zk], [Znk])
                    Zc, Zk = Zn, Znk
                    if k < 6:
                        for h in H4:
                            self.S.op("tensor", lambda e, h=h, Zc=Zc: e.transpose(out=self.psT[:, h, :], in_=c128(Zc, h), identity=idS[:]),
                                      reads=[Zk, "ident"], writes=["psT"], partial=(h > 0))
                        Xn, Xnk = Xm4[k % 2], "M_X%d" % (k % 2)
                        self.act(Xn, self.psT[:, 0:4, :].rearrange("p a b -> p (a b)"), AF.Copy, ["psT"], [Xnk])
                        Xc, Xk = Xn, Xnk
                if "dS6" in self.debug:
                    continue
                for h in H4:
                    col, kcols = cols[h], kcs[h]
                    self.V(lambda e, h=h, tt=tt, col=col, kcols=kcols: e.tensor_scalar(
                        out=c64(bv4, h), in0=v_tok[:, tt, kcols], scalar1=beta[:, tt, col:col + 1], scalar2=None, op0=ALU.mult),
                        [("M_vtok", tt), ("M_beta", tt)], ["M_bv"], partial=(h > 0))
                    self.V(lambda e, h=h, tt=tt, col=col, kcols=kcols: e.tensor_scalar(
                        out=c64(kbg4, h), in0=k_tok[:, tt, kcols], scalar1=bkg[:, tt, col:col + 1], scalar2=None, op0=ALU.mult),
                        [("M_ktok", tt), ("M_bkg", tt)], ["M_kbg"], partial=(h > 0))
                    for half in range(2):
                        self.G(lambda e, h=h, tt=tt, col=col, kcols=kcols, half=half: e.tensor_scalar(
                            out=khat4[:, h * 128 + half * 64:h * 128 + (half + 1) * 64], in0=k_tok[:, tt, kcols],
                            scalar1=eglm[:, tt, col:col + 1], scalar2=None, op0=ALU.mult), [("M_ktok", tt), ("M_eglm", tt)],
                            ["M_khat"], partial=not (h == 0 and half == 0))
                for h in H4:
                    self.mm(P[5][0:64, h * 128:(h + 1) * 128], c64(kbg4, h), c128(Zc, h), True, True, ["M_kbg", Zk], "ps5", partial=(h > 0))
                self.V(lambda e: e.tensor_scalar(out=wT4[0:64, :], in0=P[5][0:64, :], scalar1=-1.0, scalar2=None, op0=ALU.mult),
                       ["ps5"], ["M_wT"])
                for h in H4:
                    self.mm(c64(P[6], h), c128(Zc, h), c64(bv4, h), True, False, [Zk, "M_bv"], "ps6", partial=(h > 0))
                    self.mm(c64(P[6], h), wT4[0:64, h * 128:(h + 1) * 128], Sb4[0:64, h * 64:(h + 1) * 64], False, True,
                            ["M_wT", "M_Sb"], "ps6", partial=True)
                self.act(vnew4, P[6][:, 0:256], AF.Copy, ["ps6"], ["M_vnew"])
                if "dS9" in self.debug:
                    continue
                def Qr(h):
                    return (P[0] if h % 2 == 0 else P[5])[:, h * 64:(h + 1) * 64]
                for h in (0, 2):
                    hr, ch = hrs[h], chs[h]
                    self.mm(Qr(h), qTc[hr, ch, ts], Sb4[hr, h * 64:(h + 1) * 64], True, True, [("M_qTc", ch), "M_Sb"], "ps0",
                            partial=(h > 0))
                for h in H4:
                    self.mm(c64(P[1], h), c128(attm4, h), c64(vnew4, h), True, True, ["M_attm", "M_vnew"], "ps1", partial=(h > 0))
                for h in (1, 3):
                    hr, ch = hrs[h], chs[h]
                    self.mm(Qr(h), qTc[hr, ch, ts], Sb4[hr, h * 64:(h + 1) * 64], True, True, [("M_qTc", ch), "M_Sb"], "ps5",
                            partial=(h > 1))
                for h in H4:
                    self.mm(c64(P[4], h), c128(khat4, h), c64(vnew4, h), True, True, ["M_khat", "M_vnew"], "ps4", partial=(h > 0))
                for h in H4:
                    col = cols[h]
                    self.V(lambda e, h=h, tt=tt, col=col: e.tensor_scalar(
                        out=c64(otmp4, h), in0=Qr(h), scalar1=eg[:, tt, col:col + 1], scalar2=None, op0=ALU.mult),
                        ["ps0" if h % 2 == 0 else "ps5", ("M_eg", tt)], ["M_otmp"], partial=(h > 0))
                if d == 0:
                    self.V(lambda e, tt=tt: e.tensor_tensor(out=oc[:, tt, :], in0=otmp4, in1=P[1][:, 0:256], op=ALU.add),
                           ["M_otmp", "ps1"], [("M_oc", tt)])
                else:
                    self.V(lambda e: e.tensor_tensor(out=otmp4, in0=otmp4, in1=P[1][:, 0:256], op=ALU.add), ["M_otmp", "ps1"], ["M_otmp"])
                    self.G(lambda e, tt=tt: e.tensor_tensor(out=oc[:, tt, :], in0=oc[:, tt, :], in1=otmp4, op=ALU.add),
                           ["M_otmp", ("M_oc", tt)], [("M_oc", tt)])
                sn, snk = Snew4[itn % 2], "M_Snew%d" % (itn % 2)
                itn += 1
                for h in H4:
                    col = cols[h]
                    self.V(lambda e, sn=sn, h=h, tt=tt, col=col: e.scalar_tensor_tensor(
                        out=c64(sn, h), in0=c64(Sf4, h), scalar=egl[:, tt, col:col + 1], in1=c64(P[4], h), op0=ALU.mult, op1=ALU.add),
                        ["M_Sf", ("M_egl", tt), "ps4"], [snk], partial=(h > 0))
                is_out = (tt % 2 == 1) if d == 0 else (tt % 2 == 0)
                if is_out:
                    for h in H4:
                        self.store(self.sd_out[l, d, tt // 2, h * 64:(h + 1) * 64, :], sn[0:64, h * 64:(h + 1) * 64], snk)
                if n < NT - 1:
                    nxt = order[n + 1]
                    kcol = d * NT + nxt
                    self.V(lambda e, sn=sn, kcol=kcol: e.tensor_scalar(out=Sf4, in0=sn, scalar1=self.keep_s[:, kcol:kcol + 1],
                                                                      scalar2=None, op0=ALU.mult), [snk, "keep_s"], ["M_Sf"])
                    self.act(Sb4, Sf4, AF.Copy, ["M_Sf"], ["M_Sb"])

        sq = otmp4
        st4 = self.ar([8])
        for tt in range(NT):
            ock = [("M_oc", tt)]
            self.V(lambda e, tt=tt: e.tensor_tensor(out=sq, in0=oc[:, tt, :], in1=oc[:, tt, :], op=ALU.mult), ock, ["M_otmp"])
            self.V(lambda e: e.tensor_reduce(out=st4[:, 0:4], in_=sq.rearrange("p (h d) -> p h d", h=4), axis=AX.X, op=ALU.add),
                   ["M_otmp"], ["M_st4"])
            self.act(st4[:, 4:8], st4[:, 0:4], AF.Ln, ["M_st4"], ["M_rs4"], bias=EPS, scale=1.0 / 64)
            self.act(st4[:, 4:8], st4[:, 4:8], AF.Exp, ["M_rs4"], ["M_rs4"], scale=-0.5)
            self.V(lambda e, tt=tt: e.tensor_tensor(out=sq.rearrange("p (h d) -> p h d", h=4),
                                                    in0=oc[:, tt, :].rearrange("p (h d) -> p h d", h=4),
                                                    in1=st4[:, 4:8].unsqueeze(2).to_broadcast([128, 4, 64]), op=ALU.mult),
                   ock + ["M_rs4"], ["M_otmp"])
            self.V(lambda e: e.tensor_tensor(out=sq, in0=sq, in1=dnrep, op=ALU.mult),
                   ["M_otmp", "M_dn0"] + [("M_dn", h) for h in range(1, 4)], ["M_otmp"])
            self.V(lambda e, tt=tt: e.tensor_tensor(out=self.cat[:, tt, 768:1024], in0=sq, in1=sgc[:, tt, :], op=ALU.mult),
                   ["M_otmp", ("M_sgc", tt)], [("cat", tt, "c")])

    def resid_update(self, tt, ps_lo, lo_key, ps_hi, hi_key, GG, ggkeys, lo_is_sbuf=False):
        st = self.ssq2
        self.act(self.junk[:, 0:512], ps_lo, AF.Square, [lo_key], ["junk", "ssq2"], accum_out=st[:, 0:1])
        self.act(self.junk[:, 512:1024], ps_hi, AF.Square, [hi_key], ["junk", "ssq2"], accum_out=st[:, 1:2], partial=True)
        self.V(lambda e: e.tensor_tensor(out=st[:, 2:3], in0=st[:, 0:1], in1=st[:, 1:2], op=ALU.add), ["ssq2"], ["ssq2s"])
        self.act(st[:, 3:4], st[:, 2:3], AF.Ln, ["ssq2s"], ["rstd2"], bias=EPS, scale=1.0 / D)
        self.act(st[:, 3:4], st[:, 3:4], AF.Exp, ["rstd2"], ["rstd2"], scale=-0.5)
        tmp = self.tmpf[0]
        for h, (src, key) in enumerate(((ps_lo, lo_key), (ps_hi, hi_key))):
            hs = slice(h * 512, (h + 1) * 512)
            self.V(lambda e, src=src, hs=hs: e.scalar_tensor_tensor(out=tmp[:, hs], in0=src, scalar=st[:, 3:4], in1=GG[:, hs],
                                                                   op0=ALU.mult, op1=ALU.mult),
                   [key, "rstd2"] + list(ggkeys), ["tmpf0"], partial=(h > 0))
        self.V(lambda e: e.tensor_tensor(out=self.xs[:, tt, :], in0=self.xs[:, tt, :], in1=tmp[:], op=ALU.add),
               ["tmpf0", ("xs", tt)], [("xs", tt)])

    def out_proj(self, l):
        for tt in range(NT):
            self.transpose_tile_to_actT(self.cat[:, tt, :], [("cat", tt, "a0"), ("cat", tt, "a1"), ("cat", tt, "b"), ("cat", tt, "c")], tt)
        r0, w0 = self.load_w_piece(self.w_out[l], 0, 512)
        r1, w1 = self.load_w_piece(self.w_out[l], 512, 1024)
        ggk = [("GG", 0, 0), ("GG", 0, 1)]
        for tt in range(NT):
            b = tt % 2
            pa, pb = self.ps[0 + b], self.ps[2 + b]
            pak, pbk = "ps%d" % b, "ps%d" % (2 + b)
            ts = slice(tt * 128, (tt + 1) * 128)
            for kc in range(8):
                self.mm(pa[:], self.actT[:, kc, ts], w0[:, kc, :], kc == 0, kc == 7, [("actT", tt), "ring%d" % r0], pak)
            for kc in range(8):
                self.mm(pb[:], self.actT[:, kc, ts], w1[:, kc, :], kc == 0, kc == 7, [("actT", tt), "ring%d" % r1], pbk)
            self.resid_update(tt, pa[:], pak, pb[:], pbk, self.GGm, ggk)

    def ffn(self, l):
        self.norm_to_actT(1)
        self.ar_reset()
        aT = self.ar([NFC, T], BF16)
        sg = [self.ar([512]) for _ in range(2)]
        fbuf = self.ar([NT, 512])
        it = 0
        for c0 in range(0, DFF, 512):
            c1 = min(c0 + 512, DFF)
            rg, wg = self.load_w_piece(self.w_gate[l], c0, c1)
            ru, wu = self.load_w_piece(self.w_up[l], c0, c1)
            for sub in range((c1 - c0) // 128):
                fc = c0 // 128 + sub
                for half in range(2):
                    hs = slice(half * 512, (half + 1) * 512)
                    b = it % 2
                    it += 1
                    pg, pu = self.ps[0 + b], self.ps[2 + b]
                    pgk, puk = "ps%d" % b, "ps%d" % (2 + b)
                    rd = [("actT", t) for t in range(half * 4, half * 4 + 4)]
                    for kc in range(8):
                        self.mm(pg[:], wg[:, kc, sub * 128:(sub + 1) * 128], self.actT[:, kc, hs], kc == 0, kc == 7,
                                rd + ["ring%d" % rg], pgk)
                    for kc in range(8):
                        self.mm(pu[:], wu[:, kc, sub * 128:(sub + 1) * 128], self.actT[:, kc, hs], kc == 0, kc == 7,
                                rd + ["ring%d" % ru], puk)
                    sgt, sgk = sg[b], "F_sg%d" % b
                    self.act(sgt, pg[:], AF.Silu, [pgk], [sgk])
                    self.V(lambda e, sgt=sgt, pu=pu, fc=fc, hs=hs: e.tensor_tensor(out=aT[:, fc, hs], in0=sgt, in1=pu[:],
                                                                                    op=ALU.mult),
                           [sgk, puk], [("F_aT", fc, half)])
        ggk = [("GG", 1, 0), ("GG", 1, 1)]
        groups = [(0, 8), (8, 8), (16, 6)]
        allaT = [("F_aT", fc, h) for fc in range(NFC) for h in range(2)]
        for half in range(2):
            pieces = []
            for (f0, nk) in groups:
                ri, wt = self.load_w_piece(self.w_down[l], half * 512, (half + 1) * 512, r0=f0 * 128, nk=nk)
                pieces.append((ri, wt, f0, nk))
            for tt in range(NT):
                b = tt % 2
                pf = self.ps[4 + b]
                pfk = "ps%d" % (4 + b)
                ts = slice(tt * 128, (tt + 1) * 128)
                n = 0
                for (ri, wt, f0, nk) in pieces:
                    for k in range(nk):
                        self.mm(pf[:], aT[:, f0 + k, ts], wt[:, k, :], n == 0, n == NFC - 1, allaT + ["ring%d" % ri], pfk)
                        n += 1
                if half == 0:
                    self.V(lambda e, tt=tt, pf=pf: e.tensor_copy(out=fbuf[:, tt, :], in_=pf[:]), [pfk], [("F_fbuf", tt)])
                else:
                    self.resid_update(tt, fbuf[:, tt, :], ("F_fbuf", tt), pf[:], pfk, self.GGf, ggk)

    def build(self):
        self.setup()
        for l in range(self.depth):
            self.layer(l)
        self.finish()
        st = self.S.emit()
        self.stats = st
        return self.nc

    def finish(self):
        for tt in range(NT):
            self.store(self.y[tt * 128:(tt + 1) * 128, :], self.xs[:, tt, :], ("xs", tt))

    def layer(self, l):
        self.mod_stage(l)
        self.norm_to_actT(0)
        self.attention(l)
        if "stop_after_att" in self.debug:
            return
        if "nogla" not in self.debug:
            self.gla(l)
        if "nodelta" not in self.debug:
            self.delta(l)
        self.out_proj(l)
        self.ffn(l)
        if "cat" in self.debug and l == 0:
            d = self.dbg("cat", [128, NT, D])
            self.S.dma("gpsimd", lambda e: e.dma_start(out=d, in_=self.cat[:]), "st_cat",
                       reads=[("cat", qb, k) for qb in range(NT) for k in ("a0", "a1", "b", "c")], store=True)
        if "actT0" in self.debug and l == 0:
            d = self.dbg("actT0", [128, 8, T])
            tmp = self.ar([8, T])
            self.V(lambda e: e.tensor_copy(out=tmp, in_=self.actT[:]), [("actT", t) for t in range(NT)], ["M_dbg_actT"])
            self.store(d, tmp, "M_dbg_actT")
            d2 = self.dbg("GG", [128, 2, D])
            self.store(d2[:, 0, :], self.GGm[:], ("GG", 0, 0))
            self.store(d2[:, 0, :], self.GGm[:], ("GG", 0, 1))
            self.store(d2[:, 1, :], self.GGf[:], ("GG", 1, 0))
            self.store(d2[:, 1, :], self.GGf[:], ("GG", 1, 1))


def rope_tables(sample):
    cos = np.ones((T, 64), np.float32)
    sin = np.zeros((T, 64), np.float32)
    if sample:
        t = np.arange(T)
        row = (t // 64).astype(np.float32)
        col = (t % 64).astype(np.float32)
        inv = (np.float32(10000.0) ** (-np.arange(16, dtype=np.float32) / np.float32(16))).astype(np.float32)
        ar = row[:, None] * inv
        ac = col[:, None] * inv
        ang = np.concatenate([ar, ar, ac, ac], axis=-1).astype(np.float32)
        cos = np.cos(ang).astype(np.float32)
        sin = np.sin(ang).astype(np.float32)
    sgn = np.concatenate([-np.ones(16), np.ones(16), -np.ones(16), np.ones(16)]).astype(np.float32)
    sins = sin * sgn
    tab = np.stack([np.tile(cos, (1, 10)), np.tile(sins, (1, 10))], 1)
    return np.ascontiguousarray(tab.astype(np.float32))


def core_tables(sample):
    ab = np.zeros((12, NT), np.float32)
    keep = np.ones((2, NT), np.float32)
    if not sample:
        ab[:] = NEG
        for kc in range(4, 12):
            for qb in range(NT):
                if (kc - 4) // 2 == qb // 2:
                    ab[kc, qb] = 0.0
        for tt in range(NT):
            if tt % 2 == 0:
                keep[0, tt] = 0.0
            if tt % 2 == 1:
                keep[1, tt] = 0.0
    abias = np.ascontiguousarray(np.broadcast_to(ab.reshape(1, -1), (128, 12 * NT))).astype(np.float32)
    keepr = np.ascontiguousarray(np.broadcast_to(keep.reshape(1, -1), (128, 2 * NT))).astype(np.float32)
    cflag = np.full((128, 1), 1.0 if sample else 0.0, np.float32)
    return abias, keepr, cflag


def make_masks():
    m = np.zeros((4, 128, 128), np.float32)
    i = np.arange(128)
    m[0] = (i[:, None] <= i[None, :])
    m[1] = (i[:, None] >= i[None, :])
    m[2] = (i[:, None] < i[None, :])
    m[3] = (i[:, None] > i[None, :])
    return np.ascontiguousarray(m.transpose(1, 0, 2))


def prep_shared(inp, L=DEPTH):
    sh = {}
    sh["ident"] = np.eye(128, dtype=np.float32)
    sh["w_mod"] = np.ascontiguousarray(inp["w_mod"], dtype=np.float32)
    bm = np.asarray(inp["b_mod"], np.float32)
    sh["bmod"] = np.ascontiguousarray(bm)
    sh["bmodT"] = np.ascontiguousarray(bm.reshape(L, 48, 128).transpose(0, 2, 1))
    ng = np.asarray(inp["norm_gains"], np.float32)
    sh["ng"] = np.ascontiguousarray(ng)
    sh["ngT"] = np.ascontiguousarray(ng.reshape(L, 4, 8, 128).transpose(0, 3, 1, 2))
    sh["w_in"] = np.ascontiguousarray(inp["w_in"], dtype=np.float32)
    qg = np.asarray(inp["qk_gain"], np.float32)
    sh["qkg"] = np.ascontiguousarray(np.concatenate([np.tile(qg[:, 0], (1, 8)), np.tile(qg[:, 1], (1, 2))], axis=1))
    wgg = np.zeros((L, 33, 256), np.float32)
    w = np.asarray(inp["w_gla_gate"], np.float32)
    b = np.asarray(inp["b_gla_gate"], np.float32)
    wgg[:, 0:16, 0:128] = w[:, 0]
    wgg[:, 16:32, 128:256] = w[:, 1]
    wgg[:, 32, :] = b.reshape(L, 256)
    sh["wgg"] = wgg
    sh["gla_norm"] = np.ascontiguousarray(inp["gla_norm"], dtype=np.float32)
    cwv = np.asarray(inp["conv_w"], np.float32)
    sh["cw"] = np.ascontiguousarray(cwv.reshape(L, 5, 6, 128).transpose(0, 3, 2, 1))
    sh["alog"] = np.ascontiguousarray(np.asarray(inp["a_log"], np.float32).reshape(L, 8))
    sh["dtb"] = np.ascontiguousarray(np.asarray(inp["dt_bias"], np.float32).reshape(L, 8))
    sh["delta_norm"] = np.ascontiguousarray(inp["delta_norm"], dtype=np.float32)
    for k in ("w_out", "w_gate", "w_up", "w_down"):
        sh[k] = np.ascontiguousarray(inp[k], dtype=np.float32)
    sh["masks"] = make_masks()
    sh["sel8"] = np.ascontiguousarray(np.broadcast_to(
        np.array([1, 1, 1, 1, 0, 0, 0, 0, 0, 0, 0, 0, 1, 1, 1, 1], np.float32)[None, :], (128, 16)))
    sh["hmask"] = (np.arange(128)[:, None] // 32 == np.arange(4)[None, :]).astype(np.float32)
    i = np.arange(128)
    cmk = np.zeros((7, 128, 128), np.float32)
    for k in range(7):
        bs, bb = 2 ** k, 2 ** (k + 1)
        cmk[k] = ((i[:, None] // bb == i[None, :] // bb) & (i[:, None] // bs != i[None, :] // bs)).astype(np.float32)
    sh["cmasks"] = np.ascontiguousarray(cmk.transpose(1, 0, 2))
    return sh


PER_LAYER = ("w_mod", "b_mod", "norm_gains", "w_in", "qk_gain", "w_gla_gate", "b_gla_gate", "gla_norm", "conv_w",
             "a_log", "dt_bias", "delta_norm", "w_out", "w_gate", "w_up", "w_down")
PER_LAYER1 = ("cache_k", "cache_v", "state_gla", "state_delta")


def make_in_maps(inp, L=DEPTH):
    if L != DEPTH:
        inp = dict(inp)
        for k in PER_LAYER:
            inp[k] = np.asarray(inp[k])[:L]
        for k in PER_LAYER1:
            inp[k] = np.asarray(inp[k])[:, :L]
    sh = prep_shared(inp, L)
    xs = np.asarray(inp["x_sample"], np.float32)
    xp = np.asarray(inp["x_prompt"], np.float32)
    maps = []
    tabs = {True: (rope_tables(True),) + core_tables(True), False: (rope_tables(False),) + core_tables(False)}
    for c in range(8):
        m = dict(sh)
        sample = c < 4
        if sample:
            b = c
            m["x"] = np.ascontiguousarray(xs[b])
            cond = np.asarray(inp["c"], np.float32)[b]
            m["ctx_k"] = np.ascontiguousarray(np.asarray(inp["cache_k"], np.float32)[b].reshape(L, 512, 128))
            m["ctx_v"] = np.ascontiguousarray(np.asarray(inp["cache_v"], np.float32)[b].reshape(L, 512, 128))
            m["s0_gla"] = np.ascontiguousarray(np.asarray(inp["state_gla"], np.float32)[b].reshape(L, 2, 128, 64))
            m["s0_delta"] = np.ascontiguousarray(np.asarray(inp["state_delta"], np.float32)[b].reshape(L, 2, 256, 64))
        else:
            j = c - 4
            m["x"] = np.ascontiguousarray(xp[4 * j:4 * j + 4].reshape(T, D))
            cond = np.asarray(inp["c_ctx"], np.float32)
            m["ctx_k"] = np.zeros((L, 512, 128), np.float32)
            m["ctx_v"] = np.zeros((L, 512, 128), np.float32)
            m["s0_gla"] = np.zeros((L, 2, 128, 64), np.float32)
            m["s0_delta"] = np.zeros((L, 2, 256, 64), np.float32)
        m["condT"] = np.ascontiguousarray(cond.reshape(8, 128).T)
        rope, abias, keep, cflag = tabs[sample]
        m["rope"] = rope
        m["abias"] = abias
        m["keep"] = keep
        m["cflag"] = cflag
        maps.append(m)
    return maps


_NC_CACHE = {}


def kernel(**inputs):
    maps = make_in_maps(inputs)
    if "nc" not in _NC_CACHE:
        _NC_CACHE["nc"] = Builder().build()
    nc = _NC_CACHE["nc"]
    res = run_bass_kernel_spmd(nc, maps, core_ids=list(range(8)))
    r = res.results
    L = DEPTH
    y_sample = np.stack([r[c]["y"] for c in range(4)], 0)
    y_prompt = np.concatenate([r[c]["y"].reshape(4, 256, D) for c in range(4, 8)], 0)
    nk = np.concatenate([r[c]["kout"].reshape(L, 4, 256, 2, 64).transpose(1, 0, 2, 3, 4) for c in range(4, 8)], 0)
    nv = np.concatenate([r[c]["vout"].reshape(L, 4, 256, 2, 64).transpose(1, 0, 2, 3, 4) for c in range(4, 8)], 0)
    sg = np.concatenate([r[c]["sg_out"].reshape(L, 2, 4, 4, 32, 64).transpose(2, 0, 1, 3, 4, 5) for c in range(4, 8)], 0)
    sd = np.concatenate([r[c]["sd_out"].reshape(L, 2, 4, 4, 64, 64).transpose(2, 0, 1, 3, 4, 5) for c in range(4, 8)], 0)
    return (y_prompt.astype(np.float32), y_sample.astype(np.float32), np.ascontiguousarray(nk, dtype=np.float32),
            np.ascontiguousarray(nv, dtype=np.float32), np.ascontiguousarray(sg, dtype=np.float32),
            np.ascontiguousarray(sd, dtype=np.float32))
```

```python
import numpy as np
import concourse.bass as bass
import concourse.mybir as mybir
from concourse.bass_utils import run_bass_kernel_spmd

F32 = mybir.dt.float32
BF16 = mybir.dt.bfloat16
AF = mybir.ActivationFunctionType
ALU = mybir.AluOpType
AX = mybir.AxisListType

DEPTH = 4
D = 1024
T = 1024
NT = 8
DFF = 2816
NFC = 22
PROJ = 2608
EPS = 1e-6
NEG = -30000.0


class Op:
    __slots__ = ("eng", "fn", "deps", "sig", "sigval", "dma", "dsem", "dval", "name")

    def __init__(self, eng, fn, dma, name):
        self.eng = eng
        self.fn = fn
        self.dma = dma
        self.deps = []
        self.sig = False
        self.sigval = 0
        self.dsem = None
        self.dval = 0
        self.name = name


class Sched:
    ENGS = ("tensor", "vector", "scalar", "gpsimd", "sync")

    def __init__(self, nc):
        self.nc = nc
        self.ops = []
        self.writers = {}
        self.readers = {}
        self.prev_readers = {}
        self.dsems = {}
        self.store_ops = []

    def _add(self, op, reads, writes, partial):
        reads = list(reads)
        if op.name != "barrier":
            for k in list(reads) + list(writes):
                nm = k[0] if isinstance(k, tuple) else k
                if nm.startswith("M_") or nm.startswith("F_"):
                    reads.append("ARENA")
                    break
        for r in reads:
            for w in self.writers.get(r, ()):
                op.deps.append((w, "raw"))
            self.readers.setdefault(r, []).append(op)
        for r in writes:
            rd = self.readers.get(r)
            if rd:
                for x in rd:
                    if x is not op:
                        op.deps.append((x, "war"))
                for x in self.writers.get(r, ()):
                    if x is not op:
                        op.deps.append((x, "war"))
                self.prev_readers[r] = [x for x in rd if x is not op] + list(self.writers.get(r, ()))
                self.readers[r] = []
                self.writers[r] = [op]
            else:
                if partial:
                    for x in self.prev_readers.get(r, ()):
                        op.deps.append((x, "war"))
                    self.writers.setdefault(r, []).append(op)
                else:
                    for x in self.writers.get(r, ()):
                        op.deps.append((x, "war"))
                    self.prev_readers[r] = list(self.writers.get(r, ()))
                    self.writers[r] = [op]
        self.ops.append(op)
        return op

    def op(self, eng, fn, reads=(), writes=(), partial=False, name=""):
        return self._add(Op(eng, fn, False, name), reads, writes, partial)

    def dma(self, eng, fn, semkey, reads=(), writes=(), partial=False, store=False, name=""):
        o = Op(eng, fn, True, name)
        ent = self.dsems.setdefault(semkey, [None, 0])
        ent[1] += 16
        o.dsem = semkey
        o.dval = ent[1]
        if store:
            self.store_ops.append(o)
        return self._add(o, reads, writes, partial)

    def barrier(self, eng, fn):
        return self._add(Op(eng, fn, False, "barrier"), (), ["ARENA"], False)

    def emit(self):
        nc = self.nc
        per_eng = {e: [] for e in self.ENGS}
        for o in self.ops:
            per_eng[o.eng].append(o)
        for o in self.ops:
            for (p, kind) in o.deps:
                if p.dma:
                    continue
                if p.eng == o.eng and p.eng == "tensor":
                    continue
                p.sig = True
        for e in self.ENGS:
            c = 0
            for o in per_eng[e]:
                if o.sig:
                    c += 1
                    o.sigval = c
        esem = {e: nc.alloc_semaphore(name="es_" + e) for e in self.ENGS}
        for i, (k, ent) in enumerate(self.dsems.items()):
            ent[0] = nc.alloc_semaphore(name="ds_%d" % i)
        stats = {e: [0, 0] for e in self.ENGS}

        def emit_engine(ename, eng):
            seen = {}
            for o in per_eng[ename]:
                need = {}
                for (p, kind) in o.deps:
                    if p.dma:
                        key = ("d", p.dsem)
                        sem = self.dsems[p.dsem][0]
                        val = p.dval
                    else:
                        if p.eng == ename and ename == "tensor":
                            continue
                        key = ("e", p.eng)
                        sem = esem[p.eng]
                        val = p.sigval
                    if seen.get(key, 0) >= val:
                        continue
                    if key not in need or need[key][1] < val:
                        need[key] = (sem, val)
                for key, (sem, val) in need.items():
                    eng.wait_ge(sem, val)
                    seen[key] = val
                    stats[ename][1] += 1
                ins = o.fn(eng)
                stats[ename][0] += 1
                if o.dma:
                    ins.then_inc(self.dsems[o.dsem][0], 16)
                elif o.sig:
                    ins.then_inc(esem[ename], 1)
            if ename == "sync":
                fin = {}
                for o in self.store_ops:
                    fin[o.dsem] = max(fin.get(o.dsem, 0), o.dval)
                for k, v in fin.items():
                    if seen.get(("d", k), 0) < v:
                        eng.wait_ge(self.dsems[k][0], v)

        with nc.Block() as block:
            @block.tensor
            def _(eng):
                emit_engine("tensor", eng)

            @block.vector
            def _(eng):
                emit_engine("vector", eng)

            @block.scalar
            def _(eng):
                emit_engine("scalar", eng)

            @block.gpsimd
            def _(eng):
                emit_engine("gpsimd", eng)

            @block.sync
            def _(eng):
                emit_engine("sync", eng)
        self.stats = stats
        return stats


W_IN_PIECES = [(0, 512), (512, 768), (768, 1280), (1280, 1568), (1568, 2080), (2080, 2336), (2336, 2608)]


class Builder:
    def __init__(self, depth=DEPTH, debug=()):
        self.depth = depth
        self.debug = set(debug)
        nc = bass.Bass("TRN2", target_bir_lowering=False)
        self.nc = nc
        self.S = Sched(nc)
        self.ring_i = 0
        self.ps_i = 0
        self.dbg_out = {}
        self.declare_io()
        self.alloc()

    def din(self, name, shape, dt=F32):
        return self.nc.dram_tensor(name, list(shape), dt, kind="ExternalInput").ap()

    def dout(self, name, shape, dt=F32):
        return self.nc.dram_tensor(name, list(shape), dt, kind="ExternalOutput").ap()

    def declare_io(self):
        L = self.depth
        self.x_in = self.din("x", [T, D])
        self.condT = self.din("condT", [128, 8])
        self.ident = self.din("ident", [128, 128])
        self.ctx_k = self.din("ctx_k", [L, 512, 128])
        self.ctx_v = self.din("ctx_v", [L, 512, 128])
        self.s0_gla = self.din("s0_gla", [L, 2, 128, 64])
        self.s0_delta = self.din("s0_delta", [L, 2, 256, 64])
        self.w_mod = self.din("w_mod", [L, D, 6 * D])
        self.bmodT = self.din("bmodT", [L, 128, 48])
        self.bmod = self.din("bmod", [L, 6 * D])
        self.ngT = self.din("ngT", [L, 128, 4, 8])
        self.ng = self.din("ng", [L, 4, D])
        self.w_in = self.din("w_in", [L, D, PROJ])
        self.qkg = self.din("qkg", [L, 640])
        self.wgg = self.din("wgg", [L, 33, 256])
        self.gla_norm = self.din("gla_norm", [L, 64])
        self.cw = self.din("cw", [L, 128, 6, 5])
        self.alog = self.din("alog", [L, 8])
        self.dtb = self.din("dtb", [L, 8])
        self.delta_norm = self.din("delta_norm", [L, 64])
        self.w_out = self.din("w_out", [L, D, D])
        self.w_gate = self.din("w_gate", [L, D, DFF])
        self.w_up = self.din("w_up", [L, D, DFF])
        self.w_down = self.din("w_down", [L, DFF, D])
        self.rope = self.din("rope", [T, 2, 640])
        self.abias = self.din("abias", [128, 12 * NT])
        self.keep = self.din("keep", [128, 2 * NT])
        self.cflag = self.din("cflag", [128, 1])
        self.hmask = self.din("hmask", [128, 4])
        self.sel8 = self.din("sel8", [128, 16])
        self.masks = self.din("masks", [128, 4, 128])
        self.cmasks = self.din("cmasks", [128, 7, 128])
        self.y = self.dout("y", [T, D])
        self.kout = self.dout("kout", [L, T, 128])
        self.vout = self.dout("vout", [L, T, 128])
        self.sg_out = self.dout("sg_out", [L, 2, 4, 128, 64])
        self.sd_out = self.dout("sd_out", [L, 2, 4, 256, 64])

    def dbg(self, name, shape, dt=F32):
        t = self.dout("dbg_" + name, shape, dt)
        self.dbg_out[name] = t
        return t

    def sb(self, name, shape, dt=F32):
        return self.nc.alloc_sbuf_tensor(name, list(shape), dt)

    def alloc(self):
        nc = self.nc
        self.xs = self.sb("xs", [128, NT, D])
        self.actT = self.sb("actT", [128, 8, T], BF16)
        self.idf = self.sb("idf", [128, 128])
        self.idb = self.sb("idb", [128, 128], BF16)
        self.ones_f = self.sb("ones_f", [128, 128])
        self.condT_s = self.sb("condT_s", [128, 8])
        self.scond = self.sb("scond", [128, 8], BF16)
        self.screp = self.sb("screp", [128, 8, 128], BF16)
        self.RING = 4
        self.ring = [self.sb("ring%d" % i, [128, 8 * 512], BF16) for i in range(self.RING)]
        self.GGm = self.sb("GGm", [128, D])
        self.GGf = self.sb("GGf", [128, D])
        self.ngrep = self.sb("ngrep", [128, D])
        self.brep = self.sb("brep", [128, D])
        self.bmodT_s = self.sb("bmodT_s", [128, 48])
        self.ngT_s = self.sb("ngT_s", [128, 4, 8])
        self.modT = self.sb("modT", [128, 48])
        self.AB = self.sb("AB", [128, 4, 8])
        self.ssq = self.sb("ssq", [128, NT])
        self.ssq2 = self.sb("ssq2", [128, 4])
        self.rstd = self.sb("rstd", [128, NT])
        self.xn = [self.sb("xn%d" % i, [128, D], BF16) for i in range(2)]
        self.tmpf = [self.sb("tmpf%d" % i, [128, D]) for i in range(2)]
        self.junk = self.tmpf[1]
        self.abias_s = self.sb("abias_s", [128, 12 * NT])
        self.keep_s = self.sb("keep_s", [128, 2 * NT])
        self.cflag_s = self.sb("cflag_s", [128, 1])
        self.hmask_s = self.sb("hmask_s", [128, 4])
        self.sel8_s = self.sb("sel8_s", [128, 16])
        self.masks_s = self.sb("masks_s", [128, 4, 128])
        self.bar_s = self.sb("bar_s", [128, 1])
        self.trif = self.sb("trif", [128, 5, 128])
        self.mask4 = self.sb("mask4", [128, 2, 512])
        self.ps = [nc.alloc_psum_tensor("ps%d" % i, [128, 512], F32) for i in range(7)]
        self.psT = nc.alloc_psum_tensor("psT", [128, 8, 128], BF16)
        self.cat = self.sb("cat", [128, NT, D], BF16)
        self.ARENA_W = 16896
        self.arena = self.sb("arena", [128, self.ARENA_W])
        self.ar_off = 0

    def ar_reset(self):
        self.S.barrier("gpsimd", lambda e: e.memset(self.bar_s[:], 0.0))
        self.ar_off = 0

    def ar(self, shape, dt=F32):
        n = int(np.prod(shape))
        words = n if dt == F32 else (n + 1) // 2
        words = (words + 31) // 32 * 32
        assert self.ar_off + words <= self.ARENA_W, ("arena overflow", self.ar_off, words)
        v = self.arena[:, self.ar_off:self.ar_off + words]
        self.ar_off += words
        if dt != F32:
            v = v.bitcast(dt)[:, 0:n]
        else:
            v = v[:, 0:n]
        if len(shape) == 2:
            v = v.rearrange("p (a b) -> p a b", a=shape[0])
        elif len(shape) == 3:
            v = v.rearrange("p (a b c) -> p a b c", a=shape[0], b=shape[1])
        return v

    def next_ring(self):
        i = self.ring_i % self.RING
        self.ring_i += 1
        return i

    def load_w_piece(self, w_l, c0, c1, r0=0, nk=8):
        i = self.next_ring()
        n = c1 - c0
        dst = self.ring[i][:, 0:nk * n].rearrange("p (k n) -> p k n", k=nk)
        src = w_l[r0:r0 + nk * 128, c0:c1].rearrange("(k p) n -> p k n", p=128)
        self.S.dma("gpsimd", lambda e: e.dma_start(out=dst, in_=src), "ring%d" % i, writes=["ring%d" % i])
        return i, dst

    def load(self, dst_ap, src_ap, key, eng="sync", partial=False):
        self.S.dma(eng, lambda e: e.dma_start(out=dst_ap, in_=src_ap), ("ld", key), writes=[key], partial=partial)

    def store(self, dst_ap, src_ap, key):
        self.S.dma("sync", lambda e: e.dma_start(out=dst_ap, in_=src_ap), ("st", key), reads=[key], store=True)

    def mm(self, out, lhsT, rhs, start, stop, reads, wkey, partial=None):
        if partial is None:
            partial = not start
        self.S.op("tensor", lambda e: e.matmul(out, lhsT=lhsT, rhs=rhs, start=start, stop=stop),
                  reads=reads, writes=[wkey], partial=partial)

    def tr(self, out, in_, ident, reads, wkey, partial):
        self.S.op("tensor", lambda e: e.transpose(out=out, in_=in_, identity=ident),
                  reads=reads + ["ident"], writes=[wkey], partial=partial)

    def V(self, fn, reads, writes, partial=False):
        self.S.op("vector", fn, reads=reads, writes=writes, partial=partial)

    def A(self, fn, reads, writes, partial=False):
        self.S.op("scalar", fn, reads=reads, writes=writes, partial=partial)

    def G(self, fn, reads, writes, partial=False):
        self.S.op("gpsimd", fn, reads=reads, writes=writes, partial=partial)

    def act(self, out, in_, func, reads, writes, bias=0.0, scale=1.0, accum_out=None, partial=False):
        assert not (func == AF.Copy and not (isinstance(scale, float) and scale == 1.0)), "scaled ACT copy faults on HW"
        if accum_out is None:
            self.A(lambda e: e.activation(out=out, in_=in_, func=func, bias=bias, scale=scale), reads, writes, partial)
        else:
            self.A(lambda e: e.activation(out=out, in_=in_, func=func, bias=bias, scale=scale, accum_out=accum_out),
                   reads, writes, partial)

    def setup(self):
        S = self.S
        for tt in range(NT):
            self.load(self.xs[:, tt, :], self.x_in[tt * 128:(tt + 1) * 128, :], ("xs", tt))
        self.load(self.idf[:], self.ident, "idf")
        self.load(self.condT_s[:], self.condT, "condT_s")
        self.load(self.abias_s[:], self.abias, "abias_s")
        self.load(self.keep_s[:], self.keep, "keep_s")
        self.load(self.cflag_s[:], self.cflag, "cflag_s")
        self.load(self.hmask_s[:], self.hmask, "hmask_s")
        self.load(self.sel8_s[:], self.sel8, "sel8_s")
        self.load(self.masks_s[:], self.masks, "masks_s")
        self.V(lambda e: e.tensor_copy(out=self.idb[:], in_=self.idf[:]), ["idf"], ["ident"])
        self.V(lambda e: e.memset(self.ones_f[:], 1.0), [], ["ones_f"])
        for i, mi in enumerate((0, 1, 3, 2)):
            self.V(lambda e, i=i, mi=mi: e.tensor_scalar(out=self.trif[:, i, :], in0=self.masks_s[:, mi, :], scalar1=-1.0 / 16,
                                                        scalar2=None, op0=ALU.mult), ["masks_s"], ["trif"], partial=(i > 0))
        self.V(lambda e: e.memset(self.trif[:, 4, :], -1.0 / 16), [], ["trif"], partial=True)
        for d in range(2):
            for h in range(4):
                self.V(lambda e, d=d, h=h: e.tensor_copy(out=self.mask4[:, d, h * 128:(h + 1) * 128], in_=self.masks_s[:, d, :]),
                       ["masks_s"], ["mask4"], partial=not (d == 0 and h == 0))
        for tt in range(NT):
            self.G(lambda e, tt=tt: e.memset(self.cat[:, tt, 512:768], 0.0), [], [("cat", tt, "b")])
            self.G(lambda e, tt=tt: e.memset(self.cat[:, tt, 768:1024], 0.0), [], [("cat", tt, "c")])
        self.act(self.scond[:], self.condT_s[:], AF.Silu, ["condT_s"], ["scond"])
        for kc in range(8):
            self.V(lambda e, kc=kc: e.tensor_copy(out=self.screp[:, kc, :],
                                                  in_=self.scond[:, kc:kc + 1].to_broadcast([128, 128])),
                   ["scond"], ["screp"], partial=(kc > 0))

    def mod_stage(self, l):
        S = self.S
        self.load(self.bmodT_s[:], self.bmodT[l], "bmodT_s")
        self.load(self.ngT_s[:], self.ngT[l], "ngT_s")
        pm = self.ps[6]
        for p in range(12):
            j = p // 2
            ri, wt = self.load_w_piece(self.w_mod[l], p * 512, (p + 1) * 512)
            rk = "ring%d" % ri
            if j in (2, 5):
                pb = self.ps[p % 2]
                pk = "ps%d" % (p % 2)
                GG = self.GGm if j == 2 else self.GGf
                gi = 0 if j == 2 else 1
                half = p % 2
                if half == 0:
                    self.load(self.ngrep[:], self.ng[l, 1 + 2 * gi:2 + 2 * gi, :].partition_broadcast(128), "ngrep")
                    self.load(self.brep[:], self.bmod[l:l + 1, j * D:(j + 1) * D].partition_broadcast(128), "brep")
                for kc in range(8):
                    self.mm(pb[:], self.screp[:, kc, :], wt[:, kc, :], kc == 0, kc == 7, [rk, "screp"], pk)
                hs = slice(half * 512, (half + 1) * 512)
                self.V(lambda e, GG=GG, hs=hs, pb=pb: e.tensor_tensor(
                    out=GG[:, hs], in0=pb[:], in1=self.brep[:, hs], op=ALU.add), [pk, "brep"], [("GG", gi, half)])
                self.G(lambda e, GG=GG, hs=hs: e.tensor_tensor(
                    out=GG[:, hs], in0=GG[:, hs], in1=self.ngrep[:, hs], op=ALU.mult),
                    [("GG", gi, half), "ngrep"], [("GG", gi, half)])
            else:
                for sub in range(4):
                    c = p * 4 + sub
                    for kc in range(8):
                        self.mm(pm[:, c:c + 1], wt[:, kc, sub * 128:(sub + 1) * 128], self.scond[:, kc:kc + 1],
                                kc == 0, kc == 7, [rk, "scond"], "ps6", partial=not (p == 0 and sub == 0 and kc == 0))
        for (a, b) in ((0, 16), (24, 40)):
            self.V(lambda e, a=a, b=b: e.tensor_tensor(out=self.modT[:, a:b], in0=pm[:, a:b], in1=self.bmodT_s[:, a:b],
                                                       op=ALU.add), ["ps6", "bmodT_s"], ["modT"], partial=(a > 0))
        for which, (jsh, jsc, gi) in enumerate(((0, 1, 0), (3, 4, 2))):
            self.V(lambda e, which=which, jsc=jsc, gi=gi: e.scalar_tensor_tensor(
                out=self.AB[:, 2 * which, :], in0=self.modT[:, jsc * 8:(jsc + 1) * 8], scalar=1.0,
                in1=self.ngT_s[:, gi, :], op0=ALU.add, op1=ALU.mult), ["modT", "ngT_s"], [("AB", 2 * which)])
            self.V(lambda e, which=which, jsh=jsh: e.tensor_copy(
                out=self.AB[:, 2 * which + 1, :], in_=self.modT[:, jsh * 8:(jsh + 1) * 8]), ["modT"],
                [("AB", 2 * which + 1)])

    def norm_to_actT(self, which):
        for tt in range(NT):
            self.act(self.junk[:], self.xs[:, tt, :], AF.Square, [("xs", tt)], ["junk", ("ssq", tt)],
                     accum_out=self.ssq[:, tt:tt + 1])
        self.act(self.rstd[:], self.ssq[:], AF.Ln, [("ssq", t) for t in range(NT)], ["rstd"], bias=EPS, scale=1.0 / D)
        self.act(self.rstd[:], self.rstd[:], AF.Exp, ["rstd"], ["rstd"], scale=-0.5)
        for tt in range(NT):
            xn = self.xn[tt % 2]
            xk = "xn%d" % (tt % 2)
            self.V(lambda e, tt=tt, xn=xn: e.tensor_scalar(out=xn[:], in0=self.xs[:, tt, :],
                                                           scalar1=self.rstd[:, tt:tt + 1], scalar2=None, op0=ALU.mult),
                   [("xs", tt), "rstd"], [xk])
            self.transpose_tile_to_actT(xn, xk, tt, A=self.AB[:, 2 * which, :], B=self.AB[:, 2 * which + 1, :],
                                        abkeys=[("AB", 2 * which), ("AB", 2 * which + 1)])

    def transpose_tile_to_actT(self, src, srckey, tt, A=None, B=None, abkeys=()):
        for kc in range(8):
            self.tr(self.psT[:, kc, :], src[:, kc * 128:(kc + 1) * 128], self.idb[:],
                    list(srckey) if isinstance(srckey, list) else [srckey], "psT", partial=(kc > 0))
        dst = self.actT[:, :, tt * 128:(tt + 1) * 128]
        if A is None:
            self.V(lambda e: e.tensor_copy(out=dst, in_=self.psT[:]), ["psT"], [("actT", tt)])
        else:
            tmp = self.tmpf[tt % 2]
            tk = "tmpf%d" % (tt % 2)
            tv = tmp[:].rearrange("p (k n) -> p k n", k=8)
            self.V(lambda e: e.tensor_tensor(out=tv, in0=self.psT[:], in1=A.unsqueeze(2).to_broadcast([128, 8, 128]),
                                             op=ALU.mult), ["psT"] + list(abkeys), [tk])
            self.G(lambda e: e.tensor_tensor(out=dst, in0=tv, in1=B.unsqueeze(2).to_broadcast([128, 8, 128]),
                                             op=ALU.add), [tk] + list(abkeys), [("actT", tt)])


    def attention(self, l):
        S = self.S
        self.ar_reset()
        stage = [self.ar([768]) for _ in range(2)]
        qkn = [self.ar([640]) for _ in range(2)]
        t1 = self.ar([640])
        t2 = self.ar([640])
        qkr = [self.ar([640], BF16) for _ in range(2)]
        qkgrep = self.ar([640])
        ropet = [self.ar([2, 640]) for _ in range(2)]
        qT = self.ar([NT, 512], BF16)
        kT = self.ar([512 + T], BF16)
        vA = self.ar([12, 2, 80], BF16)
        ctxk = self.ar([4, 128])
        ctxv = self.ar([4, 128])
        ctxkb = self.ar([4, 128], BF16)
        pTs = [self.ar([512], BF16) for _ in range(3)]
        st10 = self.ar([16])
        rs10 = self.ar([16])
        rec = [self.ar([4]) for _ in range(2)]

        self.load(qkgrep, self.qkg[l:l + 1, :].partition_broadcast(128), "M_qkgrep")
        self.load(ctxk, self.ctx_k[l].rearrange("(c p) n -> p c n", p=128), "M_ctxk")
        self.load(ctxv, self.ctx_v[l].rearrange("(c p) n -> p c n", p=128), "M_ctxv")
        self.V(lambda e: e.memset(vA[:, :, :, 64:80], 1.0), [], ["M_vA1"])
        self.V(lambda e: e.tensor_copy(out=ctxkb, in_=ctxk), ["M_ctxk"], ["M_ctxkb"])
        self.V(lambda e: e.tensor_copy(out=vA[:, 0:4, :, 0:64], in_=ctxv.rearrange("p c (g d) -> p c g d", g=2)),
               ["M_ctxv"], ["M_vA_ctx"])
        for c in range(4):
            self.tr(self.psT[:, c, :], ctxkb[:, c, :], self.idb[:], ["M_ctxkb"], "psT", partial=(c > 0))
        self.V(lambda e: e.tensor_copy(out=kT[:, 0:512].rearrange("p (c n) -> p c n", c=4), in_=self.psT[:, 0:4, :]),
               ["psT"], ["M_kT_ctx"])

        if "stopA1" in self.debug:
            return
        r0, w0 = self.load_w_piece(self.w_in[l], 0, 512)
        r1, w1 = self.load_w_piece(self.w_in[l], 512, 768)
        def att_proj_mm(tt):
            b = tt % 2
            pa, pb = self.ps[0 + b], self.ps[2 + b]
            pak, pbk = "ps%d" % b, "ps%d" % (2 + b)
            ts = slice(tt * 128, (tt + 1) * 128)
            for kc in range(8):
                self.mm(pa[:], self.actT[:, kc, ts], w0[:, kc, :], kc == 0, kc == 7, [("actT", tt), "ring%d" % r0], pak)
            for kc in range(8):
                self.mm(pb[:, 0:256], self.actT[:, kc, ts], w1[:, kc, :], kc == 0, kc == 7,
                        [("actT", tt), "ring%d" % r1], pbk)

        att_proj_mm(0)
        for tt in range(NT):
            b = tt % 2
            pa, pb = self.ps[0 + b], self.ps[2 + b]
            pak, pbk = "ps%d" % b, "ps%d" % (2 + b)
            ts = slice(tt * 128, (tt + 1) * 128)
            stg, sk = stage[b], "M_stage%d" % b
            self.act(stg[:, 0:512], pa[:], AF.Copy, [pak], [sk])
            self.V(lambda e, stg=stg, pb=pb: e.tensor_copy(out=stg[:, 512:768], in_=pb[:, 0:256]), [pbk], [sk], partial=True)
            qn, qnk = qkn[b], "M_qkn%d" % b
            sv = stg[:, 0:640].rearrange("p (h d) -> p h d", h=10)
            qv = qn.rearrange("p (h d) -> p h d", h=10)
            self.V(lambda e, qn=qn, stg=stg: e.tensor_tensor(out=qn, in0=stg[:, 0:640], in1=stg[:, 0:640], op=ALU.mult),
                   [sk], [qnk])
            self.V(lambda e, qv=qv: e.tensor_reduce(out=st10[:, 0:10], in_=qv, axis=AX.X, op=ALU.add), [qnk], ["M_st10"])
            self.act(rs10[:, 0:10], st10[:, 0:10], AF.Ln, ["M_st10"], ["M_rs10"], bias=EPS, scale=1.0 / 64)
            self.act(rs10[:, 0:10], rs10[:, 0:10], AF.Exp, ["M_rs10"], ["M_rs10"], scale=-0.5)
            self.V(lambda e, qv=qv, sv=sv: e.tensor_tensor(out=qv, in0=sv, in1=rs10[:, 0:10].unsqueeze(2).to_broadcast([128, 10, 64]),
                                                          op=ALU.mult), [sk, "M_rs10"], [qnk])
            self.V(lambda e, qn=qn: e.tensor_tensor(out=qn, in0=qn, in1=qkgrep, op=ALU.mult), [qnk, "M_qkgrep"], [qnk])
            self.store(self.kout[l, ts, :], qn[:, 512:640], qnk)
            self.store(self.vout[l, ts, :], stg[:, 640:768], sk)
            self.V(lambda e, stg=stg, tt=tt: e.tensor_copy(out=vA[:, 4 + tt, :, 0:64],
                                                          in_=stg[:, 640:768].rearrange("p (g d) -> p g d", g=2)),
                   [sk], [("M_vA", tt)])
            rp, rpk = ropet[b], "M_rope%d" % b
            self.load(rp, self.rope[ts, :, :], rpk)
            self.V(lambda e, qn=qn, rp=rp: e.tensor_tensor(out=t1, in0=qn, in1=rp[:, 0, :], op=ALU.mult), [qnk, rpk], ["M_t1"])
            q3 = qn.rearrange("p (h two s) -> p h two s", h=20, two=2)
            t23 = t2.rearrange("p (h two s) -> p h two s", h=20, two=2)
            sn3 = rp[:, 1, :].rearrange("p (h two s) -> p h two s", h=20, two=2)
            for two in range(2):
                self.V(lambda e, two=two, q3=q3, t23=t23, sn3=sn3: e.tensor_tensor(
                    out=t23[:, :, two, :], in0=q3[:, :, 1 - two, :], in1=sn3[:, :, two, :], op=ALU.mult),
                    [qnk, rpk], ["M_t2"], partial=(two > 0))
            qr, qrk = qkr[b], "M_qkr%d" % b
            for g in range(2):
                self.V(lambda e, qr=qr, g=g: e.tensor_tensor(
                    out=qr[:, 0:512].rearrange("p (j c) -> p j c", j=4)[:, :, g * 64:(g + 1) * 64],
                    in0=t1[:, g * 256:(g + 1) * 256].rearrange("p (j d) -> p j d", j=4),
                    in1=t2[:, g * 256:(g + 1) * 256].rearrange("p (j d) -> p j d", j=4), op=ALU.add),
                    ["M_t1", "M_t2"], [qrk], partial=(g > 0))
            self.V(lambda e, qr=qr: e.tensor_tensor(out=qr[:, 512:640], in0=t1[:, 512:640], in1=t2[:, 512:640], op=ALU.add),
                   ["M_t1", "M_t2"], [qrk], partial=True)
            if tt + 1 < NT:
                att_proj_mm(tt + 1)
            for j in range(4):
                self.tr(self.psT[:, j, :], qr[:, j * 128:(j + 1) * 128], self.idb[:], [qrk], "psT", partial=(j > 0))
            self.tr(self.psT[:, 4, :], qr[:, 512:640], self.idb[:], [qrk], "psT", partial=True)
            if "noq" not in self.debug:
                self.V(lambda e, tt=tt: e.tensor_copy(out=qT[:, tt, :], in_=self.psT[:, 0:4, :].rearrange("p j n -> p (j n)")),
                       ["psT"], [("M_qT", tt)])
            self.V(lambda e, tt=tt: e.tensor_copy(out=kT[:, 512 + tt * 128:512 + (tt + 1) * 128], in_=self.psT[:, 4, :]),
                   ["psT"], [("M_kT", tt)])

        if "qk" in self.debug and l == 0:
            d = self.dbg("qT", [128, NT, 512], BF16)
            self.store(d, qT, ("M_qT", 0))
            for tt in range(1, NT):
                self.S.ops[-1].deps += [(w, "raw") for w in self.S.writers[("M_qT", tt)]]
            d = self.dbg("kT", [128, 512 + T], BF16)
            self.store(d, kT, ("M_kT", 0))
            for tt in range(1, NT):
                self.S.ops[-1].deps += [(w, "raw") for w in self.S.writers[("M_kT", tt)]]
            self.S.ops[-1].deps += [(w, "raw") for w in self.S.writers["M_kT_ctx"]]

        if "stop_att_proj" in self.debug:
            return
        allk = ["M_kT_ctx", "M_vA_ctx", "M_vA1"] + [("M_kT", t) for t in range(NT)] + [("M_vA", t) for t in range(NT)]
        iters = [(qb, g, kc) for qb in range(NT) for g in range(2) for kc in range(12)]

        def emit_score(i):
            qb, g, kc = iters[i]
            x = i % 3
            psc, psk = self.ps[x], "ps%d" % x
            self.mm(psc[:], kT[g * 64:(g + 1) * 64, kc * 128:(kc + 1) * 128], qT[g * 64:(g + 1) * 64, qb, :], True, True,
                    allk + [("M_qT", qb)], psk, partial=False)

        emit_score(0)
        for i, (qb, g, kc) in enumerate(iters):
            grp = i // 12
            po = self.ps[4 + (grp % 2)]
            pok = "ps%d" % (4 + (grp % 2))
            rc = rec[grp % 2]
            rck = "M_rec%d" % (grp % 2)
            pov = po[:, 0:512].rearrange("p (j c) -> p j c", j=4)
            x = i % 3
            psc, psk = self.ps[x], "ps%d" % x
            pt, ptk = pTs[x], "M_pT%d" % x
            self.act(pt, psc[:], AF.Exp, [psk, "abias_s"], [ptk], scale=0.125,
                     bias=self.abias_s[:, kc * NT + qb:kc * NT + qb + 1])
            if i + 1 < len(iters):
                emit_score(i + 1)
            for j in range(4):
                self.mm(pov[:, j, 0:65], pt[:, j * 128:(j + 1) * 128], vA[:, kc, g, 0:65], kc == 0 and j == 0, kc == 11 and j == 3,
                        [ptk] + allk, pok, partial=not (kc == 0 and j == 0))
            if kc == 11:
                self.V(lambda e, rc=rc, pov=pov: e.reciprocal(out=rc, in_=pov[:, :, 64]), [pok], [rck])
                dst = self.cat[:, qb, g * 256:(g + 1) * 256].rearrange("p (j d) -> p j d", j=4)
                self.V(lambda e, rc=rc, pov=pov, dst=dst: e.tensor_tensor(
                    out=dst, in0=pov[:, :, 0:64], in1=rc.unsqueeze(2).to_broadcast([128, 4, 64]), op=ALU.mult),
                    [pok, rck], [("cat", qb, "a%d" % g)])

    def gla(self, l):
        self.ar_reset()
        stB = [self.ar([800]) for _ in range(2)]
        gcT = self.ar([T])
        wgg_s = self.ar([256])
        gnrep = self.ar([256])
        sp = [self.ar([256]) for _ in range(2)]
        E = [self.ar([3, 256]) for _ in range(2)]
        qkt = [self.ar([6, 128], BF16) for _ in range(2)]
        khat = self.ar([NT, 256], BF16)
        qtT = self.ar([2, T], BF16)
        ktT = self.ar([2, T], BF16)
        vb_s = self.ar([NT, 256], BF16)
        srb = self.ar([NT, 256], BF16)
        dl = self.ar([NT, 2])
        ob = self.ar([NT, 256])
        attm = [self.ar([512], BF16) for _ in range(2)]
        qmsk = [self.ar([512], BF16) for _ in range(2)]
        Snew = [self.ar([64]) for _ in range(2)]
        Scur = self.ar([64])
        Sbf = self.ar([64], BF16)
        st4 = self.ar([8])
        sq = self.ar([256])

        self.load(wgg_s[0:33, :], self.wgg[l], "M_wgg")
        self.load(gnrep[:, 0:64], self.gla_norm[l:l + 1, :].partition_broadcast(128), "M_gn0")
        for h in range(1, 4):
            self.V(lambda e, h=h: e.tensor_copy(out=gnrep[:, h * 64:(h + 1) * 64], in_=gnrep[:, 0:64]), ["M_gn0"], [("M_gn", h)])
        self.V(lambda e: e.memset(gcT[32:64, :], 1.0), [], ["M_gcT1"])

        r0, w0 = self.load_w_piece(self.w_in[l], 768, 1280)
        r1, w1 = self.load_w_piece(self.w_in[l], 1280, 1568)
        for half in range(2):
            hs = slice(half * 512, (half + 1) * 512)
            pg = self.ps[4 + half]
            pgk = "ps%d" % (4 + half)
            for kc in range(8):
                self.mm(pg[0:32, :], w1[:, kc, 256:288], self.actT[:, kc, hs], kc == 0, kc == 7,
                        [("actT", t) for t in range(half * 4, half * 4 + 4)] + ["ring%d" % r1], pgk)
            self.V(lambda e, pg=pg, hs=hs: e.tensor_copy(out=gcT[0:32, hs], in_=pg[0:32, :]), [pgk], [("M_gcT", half)])

        if "stopB0" in self.debug:
            return
        for tt in range(NT):
            b = tt % 2
            ts = slice(tt * 128, (tt + 1) * 128)
            pa, pb = self.ps[0 + b], self.ps[2 + b]
            pak, pbk = "ps%d" % b, "ps%d" % (2 + b)
            for kc in range(8):
                self.mm(pa[:], self.actT[:, kc, ts], w0[:, kc, :], kc == 0, kc == 7, [("actT", tt), "ring%d" % r0], pak)
            for kc in range(8):
                self.mm(pb[:, 0:256], self.actT[:, kc, ts], w1[:, kc, 0:256], kc == 0, kc == 7, [("actT", tt), "ring%d" % r1], pbk)
            stg, sk = stB[b], "M_stB%d" % b
            self.V(lambda e, stg=stg, pa=pa: e.tensor_scalar(out=stg[:, 0:128], in0=pa[:, 0:128], scalar1=32.0 ** -0.5, scalar2=None,
                                                             op0=ALU.mult), [pak], [sk])
            self.V(lambda e, stg=stg, pa=pa: e.tensor_copy(out=stg[:, 128:256], in_=pa[:, 128:256]), [pak], [sk], partial=True)
            if "noB_vb" not in self.debug:
                self.V(lambda e, pa=pa, tt=tt: e.tensor_copy(out=vb_s[:, tt, :], in_=pa[:, 256:512]), [pak], [("M_vb", tt)])
            if "noB_srb" not in self.debug:
                self.act(srb[:, tt, :], pb[:, 0:256], AF.Silu, [pbk], [("M_srb", tt)])
            if "stopB2" in self.debug:
                continue
            px = self.ps[6]
            self.mm(px[:, 0:256], gcT[0:33, ts], wgg_s[0:33, :], True, True,
                    [("M_gcT", tt // 4), "M_gcT1", "M_wgg"], "ps6", partial=False)
            spt, spk = sp[b], "M_sp%d" % b
            self.act(spt, px[:, 0:256], AF.Exp, ["ps6"], [spk], scale=-1.0)
            self.act(spt, spt, AF.Ln, [spk], [spk], bias=1.0)
            if "stopB2a" in self.debug:
                continue
            pc = self.ps[4 + b]
            pck = "ps%d" % (4 + b)
            for d in range(2):
                cs = slice(d * 128, (d + 1) * 128)
                self.mm(pc[:, d * 128:(d + 1) * 128], self.trif[:, d, :], spt[:, cs], True, True,
                        [spk, "trif"], pck, partial=(d > 0))
            for d in range(2):
                cs = slice(d * 128, (d + 1) * 128)
                self.mm(pc[:, 256 + d * 128:256 + (d + 1) * 128], self.trif[:, 2 + d, :], spt[:, cs], True, True,
                        [spk, "trif"], pck, partial=True)
            Et, Ek = E[b], "M_E%d" % b
            self.act(Et[:, 0, :], pc[:, 0:256], AF.Exp, [pck], [Ek])
            self.act(Et[:, 1, :], pc[:, 0:256], AF.Exp, [pck], [Ek], scale=-1.0, partial=True)
            self.act(Et[:, 2, :], pc[:, 256:512], AF.Exp, [pck], [Ek], partial=True)
            if "stopB2b" in self.debug:
                continue
            pd = self.ps[6]
            for d in range(2):
                self.mm(pd[:, 256 + 8 * d:264 + 8 * d], spt[:, d * 128:(d + 1) * 128], self.trif[:, 4, 0:8], True, True,
                        [spk, "trif"], "ps6", partial=(d > 0))
            self.act(dl[:, tt, :], pd[:, 256:272].rearrange("p (d e) -> p d e", d=2)[:, :, 0], AF.Exp, ["ps6"], [("M_dl", tt)])
            if "stopB3" in self.debug:
                continue
            qt, qk_ = qkt[b], "M_qkt%d" % b
            for d in range(2):
                cs = slice(d * 128, (d + 1) * 128)
                self.V(lambda e, qt=qt, stg=stg, Et=Et, d=d, cs=cs: e.tensor_tensor(
                    out=qt[:, 2 * d, :], in0=stg[:, 0:128], in1=Et[:, 0, cs], op=ALU.mult), [sk, Ek], [qk_], partial=(d > 0))
                self.V(lambda e, qt=qt, stg=stg, Et=Et, d=d, cs=cs: e.tensor_tensor(
                    out=qt[:, 2 * d + 1, :], in0=stg[:, 128:256], in1=Et[:, 1, cs], op=ALU.mult), [sk, Ek], [qk_], partial=True)
                self.G(lambda e, stg=stg, Et=Et, d=d, cs=cs, tt=tt: e.tensor_tensor(
                    out=khat[:, tt, cs], in0=stg[:, 128:256], in1=Et[:, 2, cs], op=ALU.mult), [sk, Ek], [("M_khat", tt)],
                    partial=(d > 0))
            for i in range(4):
                self.tr(self.psT[:, i, :], qt[:, i, :], self.idb[:], [qk_], "psT", partial=(i > 0))
            for d in range(2):
                self.V(lambda e, d=d, ts=ts: e.tensor_copy(out=qtT[:, d, ts], in_=self.psT[:, 2 * d, :]), ["psT"], [("M_qtT", tt)],
                       partial=(d > 0))
                self.V(lambda e, d=d, ts=ts: e.tensor_copy(out=ktT[:, d, ts], in_=self.psT[:, 2 * d + 1, :]), ["psT"],
                       [("M_ktT", tt)], partial=(d > 0))

        if "stopB1" in self.debug:
            return
        steps = []
        for d in range(2):
            order = list(range(NT)) if d == 0 else list(range(NT - 1, -1, -1))
            for n, tt in enumerate(order):
                steps.append((d, n, tt, order))

        def gla_A(i):
            d, n, tt, order = steps[i]
            ts = slice(tt * 128, (tt + 1) * 128)
            b = i % 2
            pat = self.ps[0 + b]
            patk = "ps%d" % b
            qm, qmk = qmsk[b], "M_qm%d" % b
            for h in range(4):
                self.V(lambda e, qm=qm, h=h, d=d, ts=ts: e.tensor_scalar(out=qm[:, h * 128:(h + 1) * 128], in0=qtT[:, d, ts],
                                                                         scalar1=self.hmask_s[:, h:h + 1], scalar2=None,
                                                                         op0=ALU.mult),
                       [("M_qtT", tt), "hmask_s"], [qmk], partial=(h > 0))
            for h in range(4):
                self.mm(pat[:, h * 128:(h + 1) * 128], ktT[:, d, ts], qm[:, h * 128:(h + 1) * 128], True, True,
                        [("M_ktT", tt), qmk], patk, partial=(h > 0))
            am, amk = attm[b], "M_attm%d" % b
            self.V(lambda e, am=am, pat=pat, d=d: e.tensor_tensor(out=am, in0=pat[:], in1=self.mask4[:, d, :], op=ALU.mult),
                   [patk, "mask4"], [amk])

        def gla_B(i):
            d, n, tt, order = steps[i]
            ts = slice(tt * 128, (tt + 1) * 128)
            b = i % 2
            qm, qmk = qmsk[b], "M_qm%d" % b
            am, amk = attm[b], "M_attm%d" % b
            if n == 0:
                self.load(Scur, self.s0_gla[l, d], "M_Scur")
                self.act(Sbf, Scur, AF.Copy, ["M_Scur"], ["M_Sbf"])
            po = self.ps[2 + b]
            pok = "ps%d" % (2 + b)
            for h in range(4):
                self.mm(po[:, h * 64:(h + 1) * 64], am[:, h * 128:(h + 1) * 128], vb_s[:, tt, h * 64:(h + 1) * 64],
                        True, False, [amk, ("M_vb", tt)], pok, partial=(h > 0))
                self.mm(po[:, h * 64:(h + 1) * 64], qm[:, h * 128:(h + 1) * 128], Sbf, False, True,
                        [qmk, "M_Sbf"], pok, partial=True)
            if d == 0:
                self.act(ob[:, tt, :], po[:, 0:256], AF.Copy, [pok], [("M_ob", tt)])
            else:
                self.V(lambda e, tt=tt, po=po: e.tensor_tensor(out=ob[:, tt, :], in0=ob[:, tt, :], in1=po[:, 0:256], op=ALU.add),
                       [pok, ("M_ob", tt)], [("M_ob", tt)])
            pS = self.ps[4 + b]
            pSk = "ps%d" % (4 + b)
            self.mm(pS[:, 0:256], khat[:, tt, d * 128:(d + 1) * 128], vb_s[:, tt, :], True, True,
                    [("M_khat", tt), ("M_vb", tt)], pSk, partial=False)
            sn, snk = Snew[b], "M_Snew%d" % b
            for h in range(4):
                hp = slice(32 * h, 32 * h + 32)
                self.V(lambda e, sn=sn, hp=hp, h=h, pS=pS, tt=tt, d=d: e.scalar_tensor_tensor(
                    out=sn[hp, :], in0=Scur[hp, :], scalar=dl[hp, tt, d:d + 1], in1=pS[hp, h * 64:(h + 1) * 64],
                    op0=ALU.mult, op1=ALU.add), ["M_Scur", ("M_dl", tt), pSk], [snk], partial=(h > 0))
            is_out = (tt % 2 == 1) if d == 0 else (tt % 2 == 0)
            if is_out:
                self.store(self.sg_out[l, d, tt // 2], sn, snk)
            if n < NT - 1:
                nxt = order[n + 1]
                kcol = d * NT + nxt
                self.V(lambda e, sn=sn, kcol=kcol: e.tensor_scalar(out=Scur, in0=sn, scalar1=self.keep_s[:, kcol:kcol + 1],
                                                                  scalar2=None, op0=ALU.mult),
                       [snk, "keep_s"], ["M_Scur"])
                self.act(Sbf, Scur, AF.Copy, ["M_Scur"], ["M_Sbf"])

        gla_A(0)
        for i in range(len(steps)):
            if i + 1 < len(steps):
                gla_A(i + 1)
            gla_B(i)

        for tt in range(NT):
            if "noBnorm" in self.debug:
                break
            self.V(lambda e, tt=tt: e.tensor_tensor(out=sq, in0=ob[:, tt, :], in1=ob[:, tt, :], op=ALU.mult), [("M_ob", tt)], ["M_sq"])
            self.V(lambda e: e.tensor_reduce(out=st4[:, 0:4], in_=sq.rearrange("p (h d) -> p h d", h=4), axis=AX.X, op=ALU.add),
                   ["M_sq"], ["M_st4"])
            self.act(st4[:, 4:8], st4[:, 0:4], AF.Ln, ["M_st4"], ["M_rs4"], bias=EPS, scale=1.0 / 64)
            self.act(st4[:, 4:8], st4[:, 4:8], AF.Exp, ["M_rs4"], ["M_rs4"], scale=-0.5)
            self.V(lambda e, tt=tt: e.tensor_tensor(out=sq.rearrange("p (h d) -> p h d", h=4),
                                                    in0=ob[:, tt, :].rearrange("p (h d) -> p h d", h=4),
                                                    in1=st4[:, 4:8].unsqueeze(2).to_broadcast([128, 4, 64]), op=ALU.mult),
                   [("M_ob", tt), "M_rs4"], ["M_sq"])
            self.V(lambda e: e.tensor_tensor(out=sq, in0=sq, in1=gnrep, op=ALU.mult),
                   ["M_sq", "M_gn0"] + [("M_gn", h) for h in range(1, 4)], ["M_sq"])
            self.V(lambda e, tt=tt: e.tensor_tensor(out=self.cat[:, tt, 512:768], in0=sq, in1=srb[:, tt, :], op=ALU.mult),
                   ["M_sq", ("M_srb", tt)], [("cat", tt, "b")])


    def ar_mark_reset(self, mark):
        self.S.barrier("gpsimd", lambda e: e.memset(self.bar_s[:], 0.0))
        self.ar_off = mark

    def delta(self, l):
        self.ar_reset()
        qTc = self.ar([2, T], BF16)
        kTc = self.ar([2, T], BF16)
        k_tok = self.ar([NT, 256], BF16)
        v_tok = self.ar([NT, 256], BF16)
        sgc = self.ar([NT, 256], BF16)
        oc = self.ar([NT, 256])
        beta = self.ar([NT, 8])
        gam = self.ar([NT, 8])
        ngam = self.ar([NT, 8])
        eg = self.ar([NT, 8])
        eglm = self.ar([NT, 8])
        egl = self.ar([NT, 8])
        bkg = self.ar([NT, 8])
        cw_s = self.ar([6, 5])
        negA = self.ar([8])
        dtb_s = self.ar([8])
        dnrep = self.ar([256])
        bones = self.ar([128])
        mark = self.ar_off

        self.load(cw_s, self.cw[l], "M_cw")
        self.load(negA, self.alog[l:l + 1, :].partition_broadcast(128), "M_negA")
        self.load(dtb_s, self.dtb[l:l + 1, :].partition_broadcast(128), "M_dtb")
        self.load(dnrep[:, 0:64], self.delta_norm[l:l + 1, :].partition_broadcast(128), "M_dn0")
        for h in range(1, 4):
            self.V(lambda e, h=h: e.tensor_copy(out=dnrep[:, h * 64:(h + 1) * 64], in_=dnrep[:, 0:64]), ["M_dn0"], [("M_dn", h)])
        self.act(negA, negA, AF.Exp, ["M_negA"], ["M_negA"])
        self.V(lambda e: e.tensor_scalar(out=negA, in0=negA, scalar1=-1.0, scalar2=None, op0=ALU.mult), ["M_negA"], ["M_negA"])
        self.V(lambda e: e.memset(bones, 0.0), [], ["M_bones"])
        self.V(lambda e: e.memset(bones[0:64, 0:64], 1.0), ["M_bones"], ["M_bones"])
        self.V(lambda e: e.memset(bones[64:128, 64:128], 1.0), ["M_bones"], ["M_bones"])

        if "stopC0" in self.debug:
            return
        xin = [self.ar([4, 260]) for _ in range(2)]
        ycv = [self.ar([T]) for _ in range(2)]
        ysl = self.ar([T])
        sqb = self.ar([T])
        vTc = self.ar([2, T], BF16)
        rn = self.ar([T])

        pieces = [(1568, 2080, 4), (2080, 2336, 2)]
        cc = 0
        for (c0, c1, nch) in pieces:
            ri, wt = self.load_w_piece(self.w_in[l], c0, c1)
            for sub in range(nch):
                b = cc % 2
                xi, xik = xin[b], "M_xin%d" % b
                for half in range(2):
                    hs = slice(half * 512, (half + 1) * 512)
                    pp = self.ps[half]
                    ppk = "ps%d" % half
                    for kc in range(8):
                        self.mm(pp[:], wt[:, kc, sub * 128:(sub + 1) * 128], self.actT[:, kc, hs], kc == 0, kc == 7,
                                [("actT", t) for t in range(half * 4, half * 4 + 4)] + ["ring%d" % ri], ppk)
                    self.act(xi[:, 2 * half:2 * half + 2, 2:258], pp[:].rearrange("p (s n) -> p s n", s=2), AF.Copy, [ppk], [xik],
                             partial=(half > 0))
                self.V(lambda e, xi=xi: e.memset(xi[:, 0, 0:2], 0.0), [], [xik], partial=True)
                self.V(lambda e, xi=xi: e.memset(xi[:, 3, 258:260], 0.0), [], [xik], partial=True)
                self.V(lambda e, xi=xi: e.tensor_scalar(out=xi[:, 1:4, 0:2], in0=xi[:, 0:3, 256:258], scalar1=self.cflag_s[:, 0:1],
                                                        scalar2=None, op0=ALU.mult), [xik, "cflag_s"], [xik])
                self.V(lambda e, xi=xi: e.tensor_scalar(out=xi[:, 0:3, 258:260], in0=xi[:, 1:4, 2:4], scalar1=self.cflag_s[:, 0:1],
                                                        scalar2=None, op0=ALU.mult), [xik, "cflag_s"], [xik])
                yc, yck = ycv[b], "M_ycv%d" % b
                yv = yc.rearrange("p (s n) -> p s n", s=4)
                self.V(lambda e, xi=xi, yv=yv, cc=cc: e.tensor_scalar(out=yv, in0=xi[:, :, 0:256], scalar1=cw_s[:, cc, 0:1],
                                                                      scalar2=None, op0=ALU.mult), [xik, "M_cw"], [yck])
                for j in range(1, 5):
                    eng = self.V
                    eng(lambda e, xi=xi, yv=yv, cc=cc, j=j: e.scalar_tensor_tensor(
                        out=yv, in0=xi[:, :, j:j + 256], scalar=cw_s[:, cc, j:j + 1], in1=yv, op0=ALU.mult, op1=ALU.add),
                        [xik, "M_cw", yck], [yck])
                self.act(ysl, yc, AF.Silu, [yck], ["M_ysl"])
                if cc < 4:
                    self.V(lambda e: e.tensor_tensor(out=sqb, in0=ysl, in1=ysl, op=ALU.mult), ["M_ysl"], ["M_sqb"])
                    for half in range(2):
                        hs = slice(half * 512, (half + 1) * 512)
                        pn = self.ps[2 + half]
                        pnk = "ps%d" % (2 + half)
                        self.mm(pn[:], bones, sqb[:, hs], True, True, ["M_bones", "M_sqb"], pnk, partial=False)
                        self.act(rn[:, hs], pn[:], AF.Ln, [pnk], [("M_rn", half)], bias=EPS)
                        self.act(rn[:, hs], rn[:, hs], AF.Exp, [("M_rn", half)], [("M_rn", half)], scale=-0.5)
                    dst = qTc if cc < 2 else kTc
                    dk_ = ("M_qTc", cc) if cc < 2 else ("M_kTc", cc - 2)
                    sc_ = 64.0 ** -0.5 if cc < 2 else 1.0
                    self.V(lambda e, dst=dst, cc=cc, sc_=sc_: e.scalar_tensor_tensor(
                        out=dst[:, cc % 2, :], in0=ysl, scalar=sc_, in1=rn, op0=ALU.mult, op1=ALU.mult),
                        ["M_ysl", ("M_rn", 0), ("M_rn", 1)], [dk_])
                else:
                    self.V(lambda e, cc=cc: e.tensor_copy(out=vTc[:, cc - 4, :], in_=ysl), ["M_ysl"], [("M_vTc", cc - 4)])
                cc += 1
        if "stopC1" in self.debug:
            return
        for tt in range(NT):
            ts = slice(tt * 128, (tt + 1) * 128)
            for c in range(2):
                self.tr(self.psT[:, c, :], kTc[:, c, ts], self.idb[:], [("M_kTc", c)], "psT", partial=(c > 0))
                self.tr(self.psT[:, 2 + c, :], vTc[:, c, ts], self.idb[:], [("M_vTc", c)], "psT", partial=True)
            self.V(lambda e, tt=tt: e.tensor_copy(out=k_tok[:, tt, :], in_=self.psT[:, 0:2, :].rearrange("p c n -> p (c n)")),
                   ["psT"], [("M_ktok", tt)])
            self.V(lambda e, tt=tt: e.tensor_copy(out=v_tok[:, tt, :], in_=self.psT[:, 2:4, :].rearrange("p c n -> p (c n)")),
                   ["psT"], [("M_vtok", tt)])
        if "stopC2" in self.debug:
            return
        ri, wt = self.load_w_piece(self.w_in[l], 2336, 2608)
        g8 = [self.ar([8]) for _ in range(2)]
        g16 = [self.ar([16]) for _ in range(2)]
        for tt in range(NT):
            b = tt % 2
            ts = slice(tt * 128, (tt + 1) * 128)
            pg = self.ps[4 + b]
            pgk = "ps%d" % (4 + b)
            for kc in range(8):
                self.mm(pg[:, 0:256], self.actT[:, kc, ts], wt[:, kc, 0:256], kc == 0, kc == 7, [("actT", tt), "ring%d" % ri], pgk)
            for kc in range(8):
                self.mm(pg[:, 256:272], self.actT[:, kc, ts], wt[:, kc, 256:272], kc == 0, kc == 7, [("actT", tt), "ring%d" % ri], pgk,
                        partial=True)
            self.act(sgc[:, tt, :], pg[:, 0:256], AF.Silu, [pgk], [("M_sgc", tt)])
            if "gCa" in self.debug:
                continue
            self.act(beta[:, tt, :], pg[:, 256:264], AF.Exp, [pgk], [("M_beta", tt)], scale=-1.0)
            self.V(lambda e, tt=tt: e.tensor_scalar(out=beta[:, tt, :], in0=beta[:, tt, :], scalar1=1.0, scalar2=None, op0=ALU.add),
                   [("M_beta", tt)], [("M_beta", tt)])
            self.V(lambda e, tt=tt: e.reciprocal(out=beta[:, tt, :], in_=beta[:, tt, :]), [("M_beta", tt)], [("M_beta", tt)])
            if "gCb" in self.debug:
                continue
            gt, gk = g8[b], "M_g8%d" % b
            self.V(lambda e, gt=gt, pg=pg: e.tensor_tensor(out=gt, in0=pg[:, 264:272], in1=dtb_s, op=ALU.add), [pgk, "M_dtb"], [gk])
            self.act(gt, gt, AF.Exp, [gk], [gk])
            self.act(gt, gt, AF.Ln, [gk], [gk], bias=1.0)
            self.V(lambda e, gt=gt: e.tensor_tensor(out=gt, in0=gt, in1=negA, op=ALU.mult), [gk, "M_negA"], [gk])
            if "gCc" in self.debug:
                continue
            pc = self.ps[6]
            gt2, gk2 = g16[b], "M_g16%d" % b
            for d in range(2):
                self.V(lambda e, gt=gt, gt2=gt2, d=d: e.tensor_tensor(out=gt2[:, d * 8:(d + 1) * 8], in0=gt,
                                                                      in1=self.sel8_s[:, d * 8:(d + 1) * 8], op=ALU.mult),
                       [gk, "sel8_s"], [gk2], partial=(d > 0))
            for d in range(2):
                self.mm(pc[:, 0:8], self.masks_s[:, d, :], gt2[:, d * 8:(d + 1) * 8], d == 0, d == 1, [gk2, "masks_s"], "ps6",
                        partial=(d > 0))
            self.mm(pc[:, 16:24], self.ones_f[:], gt, True, True, [gk, "ones_f"], "ps6", partial=True)
            self.V(lambda e, tt=tt: e.tensor_copy(out=gam[:, tt, :], in_=pc[:, 0:8]), ["ps6"], [("M_gam", tt)])
            if "gCd" in self.debug:
                continue
            self.V(lambda e, tt=tt: e.tensor_scalar(out=ngam[:, tt, :], in0=gam[:, tt, :], scalar1=-1.0, scalar2=None, op0=ALU.mult),
                   [("M_gam", tt)], [("M_ngam", tt)])
            self.act(eg[:, tt, :], gam[:, tt, :], AF.Exp, [("M_gam", tt)], [("M_eg", tt)])
            if "gCe" in self.debug:
                continue
            self.act(egl[:, tt, :], pc[:, 16:24], AF.Exp, ["ps6"], [("M_egl", tt)])
            self.V(lambda e, tt=tt: e.tensor_tensor(out=eglm[:, tt, :], in0=pc[:, 16:24], in1=ngam[:, tt, :], op=ALU.add),
                   ["ps6", ("M_ngam", tt)], [("M_eglm", tt)])
            self.act(eglm[:, tt, :], eglm[:, tt, :], AF.Exp, [("M_eglm", tt)], [("M_eglm", tt)])
            self.V(lambda e, tt=tt: e.tensor_tensor(out=bkg[:, tt, :], in0=beta[:, tt, :], in1=eg[:, tt, :], op=ALU.mult),
                   [("M_beta", tt), ("M_eg", tt)], [("M_bkg", tt)])

        if "stopC3" in self.debug:
            return
        self.ar_mark_reset(mark)
        bigm = self.ar([4, 128])
        cm = self.ar([7, 128])
        self.load(cm, self.cmasks, "M_cm")
        for i, (mi, sgn) in enumerate(((0, 1.0), (1, 1.0), (3, -1.0), (2, -1.0))):
            self.V(lambda e, i=i, mi=mi, sgn=sgn: e.tensor_scalar(out=bigm[:, i, :], in0=self.masks_s[:, mi, :], scalar1=-sgn * NEG,
                                                                  scalar2=None, op0=ALU.mult), ["masks_s"], [("M_bigm", i)])
        attm4 = self.ar([512], BF16)
        Mm4 = self.ar([512])
        SD = BF16
        Cm4 = [self.ar([512], SD) for _ in range(2)]
        Ym4 = self.ar([512], SD)
        Zm4 = [self.ar([512], SD) for _ in range(2)]
        Xm4 = [self.ar([512], SD) for _ in range(2)]
        dec4 = self.ar([512])
        dg4 = self.ar([512])
        decT4 = self.ar([512])
        id4 = self.ar([512], SD)
        bv4 = self.ar([256], SD)
        kbg4 = self.ar([256], SD)
        wT4 = self.ar([512], SD)
        vnew4 = self.ar([256], BF16)
        khat4 = self.ar([512], BF16)
        otmp4 = self.ar([256])
        Sf4 = self.ar([256])
        Sb4 = self.ar([256], BF16)
        Snew4 = [self.ar([256]) for _ in range(2)]
        H4 = range(4)
        for h in H4:
            self.V(lambda e, h=h: e.tensor_copy(out=id4[:, h * 128:(h + 1) * 128], in_=self.idf[:]), ["idf"], ["M_id4"], partial=(h > 0))
        idS = self.idb

        def c128(t, h):
            return t[:, h * 128:(h + 1) * 128]

        def c64(t, h):
            return t[:, h * 64:(h + 1) * 64]

        P = self.ps
        itn = 0
        for d in range(2):
            order = list(range(NT)) if d == 0 else list(range(NT - 1, -1, -1))
            for h in H4:
                self.load(Sf4[0:64, h * 64:(h + 1) * 64], self.s0_delta[l, d, h * 64:(h + 1) * 64, :], "M_Sf", partial=(h > 0))
                self.load(Sf4[64:128, h * 64:(h + 1) * 64], self.s0_delta[l, d, h * 64:(h + 1) * 64, :], "M_Sf", partial=True)
            self.act(Sb4, Sf4, AF.Copy, ["M_Sf"], ["M_Sb"])
            for n, tt in enumerate(order):
                ts = slice(tt * 128, (tt + 1) * 128)
                cols = [d * 4 + h for h in H4]
                hrs = [slice((h % 2) * 64, (h % 2) * 64 + 64) for h in H4]
                chs = [h // 2 for h in H4]
                kcs = [slice(h * 64, (h + 1) * 64) for h in H4]
                if "dS0" in self.debug:
                    continue
                def Gr(h):
                    return P[h % 2][:, (h // 2) * 256:(h // 2) * 256 + 128]

                def Ar(h):
                    return P[h % 2][:, (h // 2) * 256 + 128:(h // 2) * 256 + 256]
                for h in (0, 2, 1, 3):
                    hr, ch = hrs[h], chs[h]
                    bk = "ps%d" % (h % 2)
                    self.mm(Gr(h), kTc[hr, ch, ts], kTc[hr, ch, ts], True, True, [("M_kTc", ch)], bk, partial=(h >= 2))
                    self.mm(Ar(h), kTc[hr, ch, ts], qTc[hr, ch, ts], True, True, [("M_kTc", ch), ("M_qTc", ch)], bk, partial=True)
                if "dS1" in self.debug:
                    continue
                for h in H4:
                    col = cols[h]
                    self.V(lambda e, h=h, tt=tt, col=col: e.tensor_scalar(out=c128(dg4, h), in0=self.idf[:],
                                                                          scalar1=gam[:, tt, col:col + 1], scalar2=None, op0=ALU.mult),
                           ["idf", ("M_gam", tt)], ["M_dg"], partial=(h > 0))
                if "dS2" in self.debug:
                    continue
                for h in H4:
                    self.mm(c128(P[2], h), self.ones_f[:], c128(dg4, h), True, False, ["ones_f", "M_dg"], "ps2", partial=(h > 0))
                    self.mm(c128(P[2], h), self.idf[:], bigm[:, d, :], False, True, ["idf", ("M_bigm", d)], "ps2", partial=True)
                for h in H4:
                    self.mm(c128(P[3], h), self.ones_f[:], c128(dg4, h), True, False, ["ones_f", "M_dg"], "ps3", partial=(h > 0))
                    self.mm(c128(P[3], h), self.idf[:], bigm[:, 2 + d, :], False, True, ["idf", ("M_bigm", 2 + d)], "ps3", partial=True)
                if "dS3" in self.debug:
                    continue
                for h in H4:
                    col = cols[h]
                    self.act(c128(dec4, h), c128(P[2], h), AF.Exp, ["ps2", ("M_gam", tt)], ["M_dec"], scale=-1.0,
                             bias=gam[:, tt, col:col + 1], partial=(h > 0))
                for h in H4:
                    col = cols[h]
                    self.act(c128(decT4, h), c128(P[3], h), AF.Exp, ["ps3", ("M_ngam", tt)], ["M_decT"], scale=1.0,
                             bias=ngam[:, tt, col:col + 1], partial=(h > 0))
                for h in H4:
                    col = cols[h]
                    self.V(lambda e, h=h, tt=tt, col=col: e.scalar_tensor_tensor(
                        out=c128(Mm4, h), in0=Gr(h), scalar=beta[:, tt, col:col + 1], in1=c128(dec4, h),
                        op0=ALU.mult, op1=ALU.mult), ["ps%d" % (h % 2), ("M_beta", tt), "M_dec"], ["M_M"], partial=(h > 0))
                for h in H4:
                    self.V(lambda e, h=h: e.tensor_tensor(out=c128(attm4, h), in0=Ar(h), in1=c128(decT4, h), op=ALU.mult),
                           ["ps%d" % (h % 2), "M_decT"], ["M_attm"], partial=(h > 0))
                if "dS5" in self.debug:
                    continue
                Zc, Zk = id4, "M_id4"
                Xc, Xk = id4, "M_id4"
                for k in range(7):
                    Ck, Ckk = Cm4[k % 2], "M_C%d" % (k % 2)
                    for h in H4:
                        self.G(lambda e, h=h, k=k, Ck=Ck: e.tensor_tensor(out=c128(Ck, h), in0=c128(Mm4, h), in1=cm[:, k, :], op=ALU.mult),
                               ["M_M", "M_cm"], [Ckk], partial=(h > 0))
                    for h in H4:
                        self.mm(c128(P[2], h), c128(Ck, h), c128(Zc, h), True, True, [Ckk, Zk], "ps2", partial=(h > 0))
                    self.act(Ym4, P[2][:], AF.Copy, ["ps2"], ["M_Y"])
                    for h in H4:
                        self.mm(c128(P[3], h), c128(Xc, h), c128(Ym4, h), True, True, [Xk, "M_Y"], "ps3", partial=(h > 0))
                    Zn, Znk = Zm4[k % 2], "M_Z%d" % (k % 2)
                    self.V(lambda e, Zn=Zn, Zo=Zc: e.scalar_tensor_tensor(out=Zn, in0=P[3][:], scalar=-1.0, in1=Zo, op0=ALU.mult,
                                                                          op1=ALU.add), ["ps3", Zk], [Znk])
                    Zc, Zk = Zn, Znk
                    if k < 6:
                        for h in H4:
                            self.S.op("tensor", lambda e, h=h, Zc=Zc: e.transpose(out=self.psT[:, h, :], in_=c128(Zc, h), identity=idS[:]),
                                      reads=[Zk, "ident"], writes=["psT"], partial=(h > 0))
                        Xn, Xnk = Xm4[k % 2], "M_X%d" % (k % 2)
                        self.act(Xn, self.psT[:, 0:4, :].rearrange("p a b -> p (a b)"), AF.Copy, ["psT"], [Xnk])
                        Xc, Xk = Xn, Xnk
                if "dS6" in self.debug:
                    continue
                for h in H4:
                    col, kcols = cols[h], kcs[h]
                    self.V(lambda e, h=h, tt=tt, col=col, kcols=kcols: e.tensor_scalar(
                        out=c64(bv4, h), in0=v_tok[:, tt, kcols], scalar1=beta[:, tt, col:col + 1], scalar2=None, op0=ALU.mult),
                        [("M_vtok", tt), ("M_beta", tt)], ["M_bv"], partial=(h > 0))
                    self.V(lambda e, h=h, tt=tt, col=col, kcols=kcols: e.tensor_scalar(
                        out=c64(kbg4, h), in0=k_tok[:, tt, kcols], scalar1=bkg[:, tt, col:col + 1], scalar2=None, op0=ALU.mult),
                        [("M_ktok", tt), ("M_bkg", tt)], ["M_kbg"], partial=(h > 0))
                    for half in range(2):
                        self.G(lambda e, h=h, tt=tt, col=col, kcols=kcols, half=half: e.tensor_scalar(
                            out=khat4[:, h * 128 + half * 64:h * 128 + (half + 1) * 64], in0=k_tok[:, tt, kcols],
                            scalar1=eglm[:, tt, col:col + 1], scalar2=None, op0=ALU.mult), [("M_ktok", tt), ("M_eglm", tt)],
                            ["M_khat"], partial=not (h == 0 and half == 0))
                for h in H4:
                    self.mm(P[5][0:64, h * 128:(h + 1) * 128], c64(kbg4, h), c128(Zc, h), True, True, ["M_kbg", Zk], "ps5", partial=(h > 0))
                self.V(lambda e: e.tensor_scalar(out=wT4[0:64, :], in0=P[5][0:64, :], scalar1=-1.0, scalar2=None, op0=ALU.mult),
                       ["ps5"], ["M_wT"])
                for h in H4:
                    self.mm(c64(P[6], h), c128(Zc, h), c64(bv4, h), True, False, [Zk, "M_bv"], "ps6", partial=(h > 0))
                    self.mm(c64(P[6], h), wT4[0:64, h * 128:(h + 1) * 128], Sb4[0:64, h * 64:(h + 1) * 64], False, True,
                            ["M_wT", "M_Sb"], "ps6", partial=True)
                self.act(vnew4, P[6][:, 0:256], AF.Copy, ["ps6"], ["M_vnew"])
                if "dS9" in self.debug:
                    continue
                def Qr(h):
                    return (P[0] if h % 2 == 0 else P[5])[:, h * 64:(h + 1) * 64]
                for h in (0, 2):
                    hr, ch = hrs[h], chs[h]
                    self.mm(Qr(h), qTc[hr, ch, ts], Sb4[hr, h * 64:(h + 1) * 64], True, True, [("M_qTc", ch), "M_Sb"], "ps0",
                            partial=(h > 0))
                for h in H4:
                    self.mm(c64(P[1], h), c128(attm4, h), c64(vnew4, h), True, True, ["M_attm", "M_vnew"], "ps1", partial=(h > 0))
                for h in (1, 3):
                    hr, ch = hrs[h], chs[h]
                    self.mm(Qr(h), qTc[hr, ch, ts], Sb4[hr, h * 64:(h + 1) * 64], True, True, [("M_qTc", ch), "M_Sb"], "ps5",
                            partial=(h > 1))
                for h in H4:
                    self.mm(c64(P[4], h), c128(khat4, h), c64(vnew4, h), True, True, ["M_khat", "M_vnew"], "ps4", partial=(h > 0))
                for h in H4:
                    col = cols[h]
                    self.V(lambda e, h=h, tt=tt, col=col: e.tensor_scalar(
                        out=c64(otmp4, h), in0=Qr(h), scalar1=eg[:, tt, col:col + 1], scalar2=None, op0=ALU.mult),
                        ["ps0" if h % 2 == 0 else "ps5", ("M_eg", tt)], ["M_otmp"], partial=(h > 0))
                if d == 0:
                    self.V(lambda e, tt=tt: e.tensor_tensor(out=oc[:, tt, :], in0=otmp4, in1=P[1][:, 0:256], op=ALU.add),
                           ["M_otmp", "ps1"], [("M_oc", tt)])
                else:
                    self.V(lambda e: e.tensor_tensor(out=otmp4, in0=otmp4, in1=P[1][:, 0:256], op=ALU.add), ["M_otmp", "ps1"], ["M_otmp"])
                    self.G(lambda e, tt=tt: e.tensor_tensor(out=oc[:, tt, :], in0=oc[:, tt, :], in1=otmp4, op=ALU.add),
                           ["M_otmp", ("M_oc", tt)], [("M_oc", tt)])
                sn, snk = Snew4[itn % 2], "M_Snew%d" % (itn % 2)
                itn += 1
                for h in H4:
                    col = cols[h]
                    self.V(lambda e, sn=sn, h=h, tt=tt, col=col: e.scalar_tensor_tensor(
                        out=c64(sn, h), in0=c64(Sf4, h), scalar=egl[:, tt, col:col + 1], in1=c64(P[4], h), op0=ALU.mult, op1=ALU.add),
                        ["M_Sf", ("M_egl", tt), "ps4"], [snk], partial=(h > 0))
                is_out = (tt % 2 == 1) if d == 0 else (tt % 2 == 0)
                if is_out:
                    for h in H4:
                        self.store(self.sd_out[l, d, tt // 2, h * 64:(h + 1) * 64, :], sn[0:64, h * 64:(h + 1) * 64], snk)
                if n < NT - 1:
                    nxt = order[n + 1]
                    kcol = d * NT + nxt
                    self.V(lambda e, sn=sn, kcol=kcol: e.tensor_scalar(out=Sf4, in0=sn, scalar1=self.keep_s[:, kcol:kcol + 1],
                                                                      scalar2=None, op0=ALU.mult), [snk, "keep_s"], ["M_Sf"])
                    self.act(Sb4, Sf4, AF.Copy, ["M_Sf"], ["M_Sb"])

        sq = otmp4
        st4 = self.ar([8])
        for tt in range(NT):
            ock = [("M_oc", tt)]
            self.V(lambda e, tt=tt: e.tensor_tensor(out=sq, in0=oc[:, tt, :], in1=oc[:, tt, :], op=ALU.mult), ock, ["M_otmp"])
            self.V(lambda e: e.tensor_reduce(out=st4[:, 0:4], in_=sq.rearrange("p (h d) -> p h d", h=4), axis=AX.X, op=ALU.add),
                   ["M_otmp"], ["M_st4"])
            self.act(st4[:, 4:8], st4[:, 0:4], AF.Ln, ["M_st4"], ["M_rs4"], bias=EPS, scale=1.0 / 64)
            self.act(st4[:, 4:8], st4[:, 4:8], AF.Exp, ["M_rs4"], ["M_rs4"], scale=-0.5)
            self.V(lambda e, tt=tt: e.tensor_tensor(out=sq.rearrange("p (h d) -> p h d", h=4),
                                                    in0=oc[:, tt, :].rearrange("p (h d) -> p h d", h=4),
                                                    in1=st4[:, 4:8].unsqueeze(2).to_broadcast([128, 4, 64]), op=ALU.mult),
                   ock + ["M_rs4"], ["M_otmp"])
            self.V(lambda e: e.tensor_tensor(out=sq, in0=sq, in1=dnrep, op=ALU.mult),
                   ["M_otmp", "M_dn0"] + [("M_dn", h) for h in range(1, 4)], ["M_otmp"])
            self.V(lambda e, tt=tt: e.tensor_tensor(out=self.cat[:, tt, 768:1024], in0=sq, in1=sgc[:, tt, :], op=ALU.mult),
                   ["M_otmp", ("M_sgc", tt)], [("cat", tt, "c")])

    def resid_update(self, tt, ps_lo, lo_key, ps_hi, hi_key, GG, ggkeys, lo_is_sbuf=False):
        st = self.ssq2
        self.act(self.junk[:, 0:512], ps_lo, AF.Square, [lo_key], ["junk", "ssq2"], accum_out=st[:, 0:1])
        self.act(self.junk[:, 512:1024], ps_hi, AF.Square, [hi_key], ["junk", "ssq2"], accum_out=st[:, 1:2], partial=True)
        self.V(lambda e: e.tensor_tensor(out=st[:, 2:3], in0=st[:, 0:1], in1=st[:, 1:2], op=ALU.add), ["ssq2"], ["ssq2s"])
        self.act(st[:, 3:4], st[:, 2:3], AF.Ln, ["ssq2s"], ["rstd2"], bias=EPS, scale=1.0 / D)
        self.act(st[:, 3:4], st[:, 3:4], AF.Exp, ["rstd2"], ["rstd2"], scale=-0.5)
        tmp = self.tmpf[0]
        for h, (src, key) in enumerate(((ps_lo, lo_key), (ps_hi, hi_key))):
            hs = slice(h * 512, (h + 1) * 512)
            self.V(lambda e, src=src, hs=hs: e.scalar_tensor_tensor(out=tmp[:, hs], in0=src, scalar=st[:, 3:4], in1=GG[:, hs],
                                                                   op0=ALU.mult, op1=ALU.mult),
                   [key, "rstd2"] + list(ggkeys), ["tmpf0"], partial=(h > 0))
        self.V(lambda e: e.tensor_tensor(out=self.xs[:, tt, :], in0=self.xs[:, tt, :], in1=tmp[:], op=ALU.add),
               ["tmpf0", ("xs", tt)], [("xs", tt)])

    def out_proj(self, l):
        for tt in range(NT):
            self.transpose_tile_to_actT(self.cat[:, tt, :], [("cat", tt, "a0"), ("cat", tt, "a1"), ("cat", tt, "b"), ("cat", tt, "c")], tt)
        r0, w0 = self.load_w_piece(self.w_out[l], 0, 512)
        r1, w1 = self.load_w_piece(self.w_out[l], 512, 1024)
        ggk = [("GG", 0, 0), ("GG", 0, 1)]
        for tt in range(NT):
            b = tt % 2
            pa, pb = self.ps[0 + b], self.ps[2 + b]
            pak, pbk = "ps%d" % b, "ps%d" % (2 + b)
            ts = slice(tt * 128, (tt + 1) * 128)
            for kc in range(8):
                self.mm(pa[:], self.actT[:, kc, ts], w0[:, kc, :], kc == 0, kc == 7, [("actT", tt), "ring%d" % r0], pak)
            for kc in range(8):
                self.mm(pb[:], self.actT[:, kc, ts], w1[:, kc, :], kc == 0, kc == 7, [("actT", tt), "ring%d" % r1], pbk)
            self.resid_update(tt, pa[:], pak, pb[:], pbk, self.GGm, ggk)

    def ffn(self, l):
        self.norm_to_actT(1)
        self.ar_reset()
        aT = self.ar([NFC, T], BF16)
        sg = [self.ar([512]) for _ in range(2)]
        fbuf = self.ar([NT, 512])
        it = 0
        for c0 in range(0, DFF, 512):
            c1 = min(c0 + 512, DFF)
            rg, wg = self.load_w_piece(self.w_gate[l], c0, c1)
            ru, wu = self.load_w_piece(self.w_up[l], c0, c1)
            for sub in range((c1 - c0) // 128):
                fc = c0 // 128 + sub
                for half in range(2):
                    hs = slice(half * 512, (half + 1) * 512)
                    b = it % 2
                    it += 1
                    pg, pu = self.ps[0 + b], self.ps[2 + b]
                    pgk, puk = "ps%d" % b, "ps%d" % (2 + b)
                    rd = [("actT", t) for t in range(half * 4, half * 4 + 4)]
                    for kc in range(8):
                        self.mm(pg[:], wg[:, kc, sub * 128:(sub + 1) * 128], self.actT[:, kc, hs], kc == 0, kc == 7,
                                rd + ["ring%d" % rg], pgk)
                    for kc in range(8):
                        self.mm(pu[:], wu[:, kc, sub * 128:(sub + 1) * 128], self.actT[:, kc, hs], kc == 0, kc == 7,
                                rd + ["ring%d" % ru], puk)
                    sgt, sgk = sg[b], "F_sg%d" % b
                    self.act(sgt, pg[:], AF.Silu, [pgk], [sgk])
                    self.V(lambda e, sgt=sgt, pu=pu, fc=fc, hs=hs: e.tensor_tensor(out=aT[:, fc, hs], in0=sgt, in1=pu[:],
                                                                                    op=ALU.mult),
                           [sgk, puk], [("F_aT", fc, half)])
        ggk = [("GG", 1, 0), ("GG", 1, 1)]
        groups = [(0, 8), (8, 8), (16, 6)]
        allaT = [("F_aT", fc, h) for fc in range(NFC) for h in range(2)]
        for half in range(2):
            pieces = []
            for (f0, nk) in groups:
                ri, wt = self.load_w_piece(self.w_down[l], half * 512, (half + 1) * 512, r0=f0 * 128, nk=nk)
                pieces.append((ri, wt, f0, nk))
            for tt in range(NT):
                b = tt % 2
                pf = self.ps[4 + b]
                pfk = "ps%d" % (4 + b)
                ts = slice(tt * 128, (tt + 1) * 128)
                n = 0
                for (ri, wt, f0, nk) in pieces:
                    for k in range(nk):
                        self.mm(pf[:], aT[:, f0 + k, ts], wt[:, k, :], n == 0, n == NFC - 1, allaT + ["ring%d" % ri], pfk)
                        n += 1
                if half == 0:
                    self.V(lambda e, tt=tt, pf=pf: e.tensor_copy(out=fbuf[:, tt, :], in_=pf[:]), [pfk], [("F_fbuf", tt)])
                else:
                    self.resid_update(tt, fbuf[:, tt, :], ("F_fbuf", tt), pf[:], pfk, self.GGf, ggk)

    def build(self):
        self.setup()
        for l in range(self.depth):
            self.layer(l)
        self.finish()
        st = self.S.emit()
        self.stats = st
        return self.nc

    def finish(self):
        for tt in range(NT):
            self.store(self.y[tt * 128:(tt + 1) * 128, :], self.xs[:, tt, :], ("xs", tt))

    def layer(self, l):
        self.mod_stage(l)
        self.norm_to_actT(0)
        self.attention(l)
        if "stop_after_att" in self.debug:
            return
        if "nogla" not in self.debug:
            self.gla(l)
        if "nodelta" not in self.debug:
            self.delta(l)
        self.out_proj(l)
        self.ffn(l)
        if "cat" in self.debug and l == 0:
            d = self.dbg("cat", [128, NT, D])
            self.S.dma("gpsimd", lambda e: e.dma_start(out=d, in_=self.cat[:]), "st_cat",
                       reads=[("cat", qb, k) for qb in range(NT) for k in ("a0", "a1", "b", "c")], store=True)
        if "actT0" in self.debug and l == 0:
            d = self.dbg("actT0", [128, 8, T])
            tmp = self.ar([8, T])
            self.V(lambda e: e.tensor_copy(out=tmp, in_=self.actT[:]), [("actT", t) for t in range(NT)], ["M_dbg_actT"])
            self.store(d, tmp, "M_dbg_actT")
            d2 = self.dbg("GG", [128, 2, D])
            self.store(d2[:, 0, :], self.GGm[:], ("GG", 0, 0))
            self.store(d2[:, 0, :], self.GGm[:], ("GG", 0, 1))
            self.store(d2[:, 1, :], self.GGf[:], ("GG", 1, 0))
            self.store(d2[:, 1, :], self.GGf[:], ("GG", 1, 1))


def rope_tables(sample):
    cos = np.ones((T, 64), np.float32)
    sin = np.zeros((T, 64), np.float32)
    if sample:
        t = np.arange(T)
        row = (t // 64).astype(np.float32)
        col = (t % 64).astype(np.float32)
        inv = (np.float32(10000.0) ** (-np.arange(16, dtype=np.float32) / np.float32(16))).astype(np.float32)
        ar = row[:, None] * inv
        ac = col[:, None] * inv
        ang = np.concatenate([ar, ar, ac, ac], axis=-1).astype(np.float32)
        cos = np.cos(ang).astype(np.float32)
        sin = np.sin(ang).astype(np.float32)
    sgn = np.concatenate([-np.ones(16), np.ones(16), -np.ones(16), np.ones(16)]).astype(np.float32)
    sins = sin * sgn
    tab = np.stack([np.tile(cos, (1, 10)), np.tile(sins, (1, 10))], 1)
    return np.ascontiguousarray(tab.astype(np.float32))


def core_tables(sample):
    ab = np.zeros((12, NT), np.float32)
    keep = np.ones((2, NT), np.float32)
    if not sample:
        ab[:] = NEG
        for kc in range(4, 12):
            for qb in range(NT):
                if (kc - 4) // 2 == qb // 2:
                    ab[kc, qb] = 0.0
        for tt in range(NT):
            if tt % 2 == 0:
                keep[0, tt] = 0.0
            if tt % 2 == 1:
                keep[1, tt] = 0.0
    abias = np.ascontiguousarray(np.broadcast_to(ab.reshape(1, -1), (128, 12 * NT))).astype(np.float32)
    keepr = np.ascontiguousarray(np.broadcast_to(keep.reshape(1, -1), (128, 2 * NT))).astype(np.float32)
    cflag = np.full((128, 1), 1.0 if sample else 0.0, np.float32)
    return abias, keepr, cflag


def make_masks():
    m = np.zeros((4, 128, 128), np.float32)
    i = np.arange(128)
    m[0] = (i[:, None] <= i[None, :])
    m[1] = (i[:, None] >= i[None, :])
    m[2] = (i[:, None] < i[None, :])
    m[3] = (i[:, None] > i[None, :])
    return np.ascontiguousarray(m.transpose(1, 0, 2))


def prep_shared(inp, L=DEPTH):
    sh = {}
    sh["ident"] = np.eye(128, dtype=np.float32)
    sh["w_mod"] = np.ascontiguousarray(inp["w_mod"], dtype=np.float32)
    bm = np.asarray(inp["b_mod"], np.float32)
    sh["bmod"] = np.ascontiguousarray(bm)
    sh["bmodT"] = np.ascontiguousarray(bm.reshape(L, 48, 128).transpose(0, 2, 1))
    ng = np.asarray(inp["norm_gains"], np.float32)
    sh["ng"] = np.ascontiguousarray(ng)
    sh["ngT"] = np.ascontiguousarray(ng.reshape(L, 4, 8, 128).transpose(0, 3, 1, 2))
    sh["w_in"] = np.ascontiguousarray(inp["w_in"], dtype=np.float32)
    qg = np.asarray(inp["qk_gain"], np.float32)
    sh["qkg"] = np.ascontiguousarray(np.concatenate([np.tile(qg[:, 0], (1, 8)), np.tile(qg[:, 1], (1, 2))], axis=1))
    wgg = np.zeros((L, 33, 256), np.float32)
    w = np.asarray(inp["w_gla_gate"], np.float32)
    b = np.asarray(inp["b_gla_gate"], np.float32)
    wgg[:, 0:16, 0:128] = w[:, 0]
    wgg[:, 16:32, 128:256] = w[:, 1]
    wgg[:, 32, :] = b.reshape(L, 256)
    sh["wgg"] = wgg
    sh["gla_norm"] = np.ascontiguousarray(inp["gla_norm"], dtype=np.float32)
    cwv = np.asarray(inp["conv_w"], np.float32)
    sh["cw"] = np.ascontiguousarray(cwv.reshape(L, 5, 6, 128).transpose(0, 3, 2, 1))
    sh["alog"] = np.ascontiguousarray(np.asarray(inp["a_log"], np.float32).reshape(L, 8))
    sh["dtb"] = np.ascontiguousarray(np.asarray(inp["dt_bias"], np.float32).reshape(L, 8))
    sh["delta_norm"] = np.ascontiguousarray(inp["delta_norm"], dtype=np.float32)
    for k in ("w_out", "w_gate", "w_up", "w_down"):
        sh[k] = np.ascontiguousarray(inp[k], dtype=np.float32)
    sh["masks"] = make_masks()
    sh["sel8"] = np.ascontiguousarray(np.broadcast_to(
        np.array([1, 1, 1, 1, 0, 0, 0, 0, 0, 0, 0, 0, 1, 1, 1, 1], np.float32)[None, :], (128, 16)))
    sh["hmask"] = (np.arange(128)[:, None] // 32 == np.arange(4)[None, :]).astype(np.float32)
    i = np.arange(128)
    cmk = np.zeros((7, 128, 128), np.float32)
    for k in range(7):
        bs, bb = 2 ** k, 2 ** (k + 1)
        cmk[k] = ((i[:, None] // bb == i[None, :] // bb) & (i[:, None] // bs != i[None, :] // bs)).astype(np.float32)
    sh["cmasks"] = np.ascontiguousarray(cmk.transpose(1, 0, 2))
    return sh


PER_LAYER = ("w_mod", "b_mod", "norm_gains", "w_in", "qk_gain", "w_gla_gate", "b_gla_gate", "gla_norm", "conv_w",
             "a_log", "dt_bias", "delta_norm", "w_out", "w_gate", "w_up", "w_down")
PER_LAYER1 = ("cache_k", "cache_v", "state_gla", "state_delta")


def make_in_maps(inp, L=DEPTH):
    if L != DEPTH:
        inp = dict(inp)
        for k in PER_LAYER:
            inp[k] = np.asarray(inp[k])[:L]
        for k in PER_LAYER1:
            inp[k] = np.asarray(inp[k])[:, :L]
    sh = prep_shared(inp, L)
    xs = np.asarray(inp["x_sample"], np.float32)
    xp = np.asarray(inp["x_prompt"], np.float32)
    maps = []
    tabs = {True: (rope_tables(True),) + core_tables(True), False: (rope_tables(False),) + core_tables(False)}
    for c in range(8):
        m = dict(sh)
        sample = c < 4
        if sample:
            b = c
            m["x"] = np.ascontiguousarray(xs[b])
            cond = np.asarray(inp["c"], np.float32)[b]
            m["ctx_k"] = np.ascontiguousarray(np.asarray(inp["cache_k"], np.float32)[b].reshape(L, 512, 128))
            m["ctx_v"] = np.ascontiguousarray(np.asarray(inp["cache_v"], np.float32)[b].reshape(L, 512, 128))
            m["s0_gla"] = np.ascontiguousarray(np.asarray(inp["state_gla"], np.float32)[b].reshape(L, 2, 128, 64))
            m["s0_delta"] = np.ascontiguousarray(np.asarray(inp["state_delta"], np.float32)[b].reshape(L, 2, 256, 64))
        else:
            j = c - 4
            m["x"] = np.ascontiguousarray(xp[4 * j:4 * j + 4].reshape(T, D))
            cond = np.asarray(inp["c_ctx"], np.float32)
            m["ctx_k"] = np.zeros((L, 512, 128), np.float32)
            m["ctx_v"] = np.zeros((L, 512, 128), np.float32)
            m["s0_gla"] = np.zeros((L, 2, 128, 64), np.float32)
            m["s0_delta"] = np.zeros((L, 2, 256, 64), np.float32)
        m["condT"] = np.ascontiguousarray(cond.reshape(8, 128).T)
        rope, abias, keep, cflag = tabs[sample]
        m["rope"] = rope
        m["abias"] = abias
        m["keep"] = keep
        m["cflag"] = cflag
        maps.append(m)
    return maps


_NC_CACHE = {}


def kernel(**inputs):
    maps = make_in_maps(inputs)
    if "nc" not in _NC_CACHE:
        _NC_CACHE["nc"] = Builder().build()
    nc = _NC_CACHE["nc"]
    res = run_bass_kernel_spmd(nc, maps, core_ids=list(range(8)))
    r = res.results
    L = DEPTH
    y_sample = np.stack([r[c]["y"] for c in range(4)], 0)
    y_prompt = np.concatenate([r[c]["y"].reshape(4, 256, D) for c in range(4, 8)], 0)
    nk = np.concatenate([r[c]["kout"].reshape(L, 4, 256, 2, 64).transpose(1, 0, 2, 3, 4) for c in range(4, 8)], 0)
    nv = np.concatenate([r[c]["vout"].reshape(L, 4, 256, 2, 64).transpose(1, 0, 2, 3, 4) for c in range(4, 8)], 0)
    sg = np.concatenate([r[c]["sg_out"].reshape(L, 2, 4, 4, 32, 64).transpose(2, 0, 1, 3, 4, 5) for c in range(4, 8)], 0)
    sd = np.concatenate([r[c]["sd_out"].reshape(L, 2, 4, 4, 64, 64).transpose(2, 0, 1, 3, 4, 5) for c in range(4, 8)], 0)
    return (y_prompt.astype(np.float32), y_sample.astype(np.float32), np.ascontiguousarray(nk, dtype=np.float32),
            np.ascontiguousarray(nv, dtype=np.float32), np.ascontiguousarray(sg, dtype=np.float32),
            np.ascontiguousarray(sd, dtype=np.float32))
```

```python
import numpy as np
import concourse.bass as bass
import concourse.mybir as mybir
from concourse.bass_utils import run_bass_kernel_spmd

F32 = mybir.dt.float32
BF16 = mybir.dt.bfloat16
AF = mybir.ActivationFunctionType
ALU = mybir.AluOpType
AX = mybir.AxisListType

DEPTH = 4
D = 1024
T = 1024
NT = 8
DFF = 2816
NFC = 22
PROJ = 2608
EPS = 1e-6
NEG = -30000.0


class Op:
    __slots__ = ("eng", "fn", "deps", "sig", "sigval", "dma", "dsem", "dval", "name")

    def __init__(self, eng, fn, dma, name):
        self.eng = eng
        self.fn = fn
        self.dma = dma
        self.deps = []
        self.sig = False
        self.sigval = 0
        self.dsem = None
        self.dval = 0
        self.name = name


class Sched:
    ENGS = ("tensor", "vector", "scalar", "gpsimd", "sync")

    def __init__(self, nc):
        self.nc = nc
        self.ops = []
        self.writers = {}
        self.readers = {}
        self.prev_readers = {}
        self.dsems = {}
        self.store_ops = []

    def _add(self, op, reads, writes, partial):
        reads = list(reads)
        if op.name != "barrier":
            for k in list(reads) + list(writes):
                nm = k[0] if isinstance(k, tuple) else k
                if nm.startswith("M_") or nm.startswith("F_"):
                    reads.append("ARENA")
                    break
        for r in reads:
            for w in self.writers.get(r, ()):
                op.deps.append((w, "raw"))
            self.readers.setdefault(r, []).append(op)
        for r in writes:
            rd = self.readers.get(r)
            if rd:
                for x in rd:
                    if x is not op:
                        op.deps.append((x, "war"))
                for x in self.writers.get(r, ()):
                    if x is not op:
                        op.deps.append((x, "war"))
                self.prev_readers[r] = [x for x in rd if x is not op] + list(self.writers.get(r, ()))
                self.readers[r] = []
                self.writers[r] = [op]
            else:
                if partial:
                    for x in self.prev_readers.get(r, ()):
                        op.deps.append((x, "war"))
                    self.writers.setdefault(r, []).append(op)
                else:
                    for x in self.writers.get(r, ()):
                        op.deps.append((x, "war"))
                    self.prev_readers[r] = list(self.writers.get(r, ()))
                    self.writers[r] = [op]
        self.ops.append(op)
        return op

    def op(self, eng, fn, reads=(), writes=(), partial=False, name=""):
        return self._add(Op(eng, fn, False, name), reads, writes, partial)

    def dma(self, eng, fn, semkey, reads=(), writes=(), partial=False, store=False, name=""):
        o = Op(eng, fn, True, name)
        ent = self.dsems.setdefault(semkey, [None, 0])
        ent[1] += 16
        o.dsem = semkey
        o.dval = ent[1]
        if store:
            self.store_ops.append(o)
        return self._add(o, reads, writes, partial)

    def barrier(self, eng, fn):
        return self._add(Op(eng, fn, False, "barrier"), (), ["ARENA"], False)

    def emit(self):
        nc = self.nc
        per_eng = {e: [] for e in self.ENGS}
        for o in self.ops:
            per_eng[o.eng].append(o)
        for o in self.ops:
            for (p, kind) in o.deps:
                if p.dma:
                    continue
                if p.eng == o.eng and p.eng == "tensor":
                    continue
                p.sig = True
        for e in self.ENGS:
            c = 0
            for o in per_eng[e]:
                if o.sig:
                    c += 1
                    o.sigval = c
        esem = {e: nc.alloc_semaphore(name="es_" + e) for e in self.ENGS}
        for i, (k, ent) in enumerate(self.dsems.items()):
            ent[0] = nc.alloc_semaphore(name="ds_%d" % i)
        stats = {e: [0, 0] for e in self.ENGS}

        def emit_engine(ename, eng):
            seen = {}
            for o in per_eng[ename]:
                need = {}
                for (p, kind) in o.deps:
                    if p.dma:
                        key = ("d", p.dsem)
                        sem = self.dsems[p.dsem][0]
                        val = p.dval
                    else:
                        if p.eng == ename and ename == "tensor":
                            continue
                        key = ("e", p.eng)
                        sem = esem[p.eng]
                        val = p.sigval
                    if seen.get(key, 0) >= val:
                        continue
                    if key not in need or need[key][1] < val:
                        need[key] = (sem, val)
                for key, (sem, val) in need.items():
                    eng.wait_ge(sem, val)
                    seen[key] = val
                    stats[ename][1] += 1
                ins = o.fn(eng)
                stats[ename][0] += 1
                if o.dma:
                    ins.then_inc(self.dsems[o.dsem][0], 16)
                elif o.sig:
                    ins.then_inc(esem[ename], 1)
            if ename == "sync":
                fin = {}
                for o in self.store_ops:
                    fin[o.dsem] = max(fin.get(o.dsem, 0), o.dval)
                for k, v in fin.items():
                    if seen.get(("d", k), 0) < v:
                        eng.wait_ge(self.dsems[k][0], v)

        with nc.Block() as block:
            @block.tensor
            def _(eng):
                emit_engine("tensor", eng)

            @block.vector
            def _(eng):
                emit_engine("vector", eng)

            @block.scalar
            def _(eng):
                emit_engine("scalar", eng)

            @block.gpsimd
            def _(eng):
                emit_engine("gpsimd", eng)

            @block.sync
            def _(eng):
                emit_engine("sync", eng)
        self.stats = stats
        return stats


W_IN_PIECES = [(0, 512), (512, 768), (768, 1280), (1280, 1568), (1568, 2080), (2080, 2336), (2336, 2608)]


class Builder:
    def __init__(self, depth=DEPTH, debug=()):
        self.depth = depth
        self.debug = set(debug)
        nc = bass.Bass("TRN2", target_bir_lowering=False)
        self.nc = nc
        self.S = Sched(nc)
        self.ring_i = 0
        self.ps_i = 0
        self.dbg_out = {}
        self.declare_io()
        self.alloc()

    def din(self, name, shape, dt=F32):
        return self.nc.dram_tensor(name, list(shape), dt, kind="ExternalInput").ap()

    def dout(self, name, shape, dt=F32):
        return self.nc.dram_tensor(name, list(shape), dt, kind="ExternalOutput").ap()

    def declare_io(self):
        L = self.depth
        self.x_in = self.din("x", [T, D])
        self.condT = self.din("condT", [128, 8])
        self.ident = self.din("ident", [128, 128])
        self.ctx_k = self.din("ctx_k", [L, 512, 128])
        self.ctx_v = self.din("ctx_v", [L, 512, 128])
        self.s0_gla = self.din("s0_gla", [L, 2, 128, 64])
        self.s0_delta = self.din("s0_delta", [L, 2, 256, 64])
        self.w_mod = self.din("w_mod", [L, D, 6 * D])
        self.bmodT = self.din("bmodT", [L, 128, 48])
        self.bmod = self.din("bmod", [L, 6 * D])
        self.ngT = self.din("ngT", [L, 128, 4, 8])
        self.ng = self.din("ng", [L, 4, D])
        self.w_in = self.din("w_in", [L, D, PROJ])
        self.qkg = self.din("qkg", [L, 640])
        self.wgg = self.din("wgg", [L, 33, 256])
        self.gla_norm = self.din("gla_norm", [L, 64])
        self.cw = self.din("cw", [L, 128, 6, 5])
        self.alog = self.din("alog", [L, 8])
        self.dtb = self.din("dtb", [L, 8])
        self.delta_norm = self.din("delta_norm", [L, 64])
        self.w_out = self.din("w_out", [L, D, D])
        self.w_gate = self.din("w_gate", [L, D, DFF])
        self.w_up = self.din("w_up", [L, D, DFF])
        self.w_down = self.din("w_down", [L, DFF, D])
        self.rope = self.din("rope", [T, 2, 640])
        self.abias = self.din("abias", [128, 12 * NT])
        self.keep = self.din("keep", [128, 2 * NT])
        self.cflag = self.din("cflag", [128, 1])
        self.hmask = self.din("hmask", [128, 4])
        self.sel8 = self.din("sel8", [128, 16])
        self.masks = self.din("masks", [128, 4, 128])
        self.cmasks = self.din("cmasks", [128, 7, 128])
        self.y = self.dout("y", [T, D])
        self.kout = self.dout("kout", [L, T, 128])
        self.vout = self.dout("vout", [L, T, 128])
        self.sg_out = self.dout("sg_out", [L, 2, 4, 128, 64])
        self.sd_out = self.dout("sd_out", [L, 2, 4, 256, 64])

    def dbg(self, name, shape, dt=F32):
        t = self.dout("dbg_" + name, shape, dt)
        self.dbg_out[name] = t
        return t

    def sb(self, name, shape, dt=F32):
        return self.nc.alloc_sbuf_tensor(name, list(shape), dt)

    def alloc(self):
        nc = self.nc
        self.xs = self.sb("xs", [128, NT, D])
        self.actT = self.sb("actT", [128, 8, T], BF16)
        self.idf = self.sb("idf", [128, 128])
        self.idb = self.sb("idb", [128, 128], BF16)
        self.ones_f = self.sb("ones_f", [128, 128])
        self.condT_s = self.sb("condT_s", [128, 8])
        self.scond = self.sb("scond", [128, 8], BF16)
        self.screp = self.sb("screp", [128, 8, 128], BF16)
        self.RING = 4
        self.ring = [self.sb("ring%d" % i, [128, 8 * 512], BF16) for i in range(self.RING)]
        self.GGm = self.sb("GGm", [128, D])
        self.GGf = self.sb("GGf", [128, D])
        self.ngrep = self.sb("ngrep", [128, D])
        self.brep = self.sb("brep", [128, D])
        self.bmodT_s = self.sb("bmodT_s", [128, 48])
        self.ngT_s = self.sb("ngT_s", [128, 4, 8])
        self.modT = self.sb("modT", [128, 48])
        self.AB = self.sb("AB", [128, 4, 8])
        self.ssq = self.sb("ssq", [128, NT])
        self.ssq2 = self.sb("ssq2", [128, 4])
        self.rstd = self.sb("rstd", [128, NT])
        self.xn = [self.sb("xn%d" % i, [128, D], BF16) for i in range(2)]
        self.tmpf = [self.sb("tmpf%d" % i, [128, D]) for i in range(2)]
        self.junk = self.tmpf[1]
        self.abias_s = self.sb("abias_s", [128, 12 * NT])
        self.keep_s = self.sb("keep_s", [128, 2 * NT])
        self.cflag_s = self.sb("cflag_s", [128, 1])
        self.hmask_s = self.sb("hmask_s", [128, 4])
        self.sel8_s = self.sb("sel8_s", [128, 16])
        self.masks_s = self.sb("masks_s", [128, 4, 128])
        self.bar_s = self.sb("bar_s", [128, 1])
        self.trif = self.sb("trif", [128, 5, 128])
        self.mask4 = self.sb("mask4", [128, 2, 512])
        self.ps = [nc.alloc_psum_tensor("ps%d" % i, [128, 512], F32) for i in range(7)]
        self.psT = nc.alloc_psum_tensor("psT", [128, 8, 128], BF16)
        self.cat = self.sb("cat", [128, NT, D], BF16)
        self.ARENA_W = 16896
        self.arena = self.sb("arena", [128, self.ARENA_W])
        self.ar_off = 0

    def ar_reset(self):
        self.S.barrier("gpsimd", lambda e: e.memset(self.bar_s[:], 0.0))
        self.ar_off = 0

    def ar(self, shape, dt=F32):
        n = int(np.prod(shape))
        words = n if dt == F32 else (n + 1) // 2
        words = (words + 31) // 32 * 32
        assert self.ar_off + words <= self.ARENA_W, ("arena overflow", self.ar_off, words)
        v = self.arena[:, self.ar_off:self.ar_off + words]
        self.ar_off += words
        if dt != F32:
            v = v.bitcast(dt)[:, 0:n]
        else:
            v = v[:, 0:n]
        if len(shape) == 2:
            v = v.rearrange("p (a b) -> p a b", a=shape[0])
        elif len(shape) == 3:
            v = v.rearrange("p (a b c) -> p a b c", a=shape[0], b=shape[1])
        return v

    def next_ring(self):
        i = self.ring_i % self.RING
        self.ring_i += 1
        return i

    def load_w_piece(self, w_l, c0, c1, r0=0, nk=8):
        i = self.next_ring()
        n = c1 - c0
        dst = self.ring[i][:, 0:nk * n].rearrange("p (k n) -> p k n", k=nk)
        src = w_l[r0:r0 + nk * 128, c0:c1].rearrange("(k p) n -> p k n", p=128)
        self.S.dma("gpsimd", lambda e: e.dma_start(out=dst, in_=src), "ring%d" % i, writes=["ring%d" % i])
        return i, dst

    def load(self, dst_ap, src_ap, key, eng="sync", partial=False):
        self.S.dma(eng, lambda e: e.dma_start(out=dst_ap, in_=src_ap), ("ld", key), writes=[key], partial=partial)

    def store(self, dst_ap, src_ap, key):
        self.S.dma("sync", lambda e: e.dma_start(out=dst_ap, in_=src_ap), ("st", key), reads=[key], store=True)

    def mm(self, out, lhsT, rhs, start, stop, reads, wkey, partial=None):
        if partial is None:
            partial = not start
        self.S.op("tensor", lambda e: e.matmul(out, lhsT=lhsT, rhs=rhs, start=start, stop=stop),
                  reads=reads, writes=[wkey], partial=partial)

    def tr(self, out, in_, ident, reads, wkey, partial):
        self.S.op("tensor", lambda e: e.transpose(out=out, in_=in_, identity=ident),
                  reads=reads + ["ident"], writes=[wkey], partial=partial)

    def V(self, fn, reads, writes, partial=False):
        self.S.op("vector", fn, reads=reads, writes=writes, partial=partial)

    def A(self, fn, reads, writes, partial=False):
        self.S.op("scalar", fn, reads=reads, writes=writes, partial=partial)

    def G(self, fn, reads, writes, partial=False):
        self.S.op("gpsimd", fn, reads=reads, writes=writes, partial=partial)

    def act(self, out, in_, func, reads, writes, bias=0.0, scale=1.0, accum_out=None, partial=False):
        assert not (func == AF.Copy and not (isinstance(scale, float) and scale == 1.0)), "scaled ACT copy faults on HW"
        if accum_out is None:
            self.A(lambda e: e.activation(out=out, in_=in_, func=func, bias=bias, scale=scale), reads, writes, partial)
        else:
            self.A(lambda e: e.activation(out=out, in_=in_, func=func, bias=bias, scale=scale, accum_out=accum_out),
                   reads, writes, partial)

    def setup(self):
        S = self.S
        for tt in range(NT):
            self.load(self.xs[:, tt, :], self.x_in[tt * 128:(tt + 1) * 128, :], ("xs", tt))
        self.load(self.idf[:], self.ident, "idf")
        self.load(self.condT_s[:], self.condT, "condT_s")
        self.load(self.abias_s[:], self.abias, "abias_s")
        self.load(self.keep_s[:], self.keep, "keep_s")
        self.load(self.cflag_s[:], self.cflag, "cflag_s")
        self.load(self.hmask_s[:], self.hmask, "hmask_s")
        self.load(self.sel8_s[:], self.sel8, "sel8_s")
        self.load(self.masks_s[:], self.masks, "masks_s")
        self.V(lambda e: e.tensor_copy(out=self.idb[:], in_=self.idf[:]), ["idf"], ["ident"])
        self.V(lambda e: e.memset(self.ones_f[:], 1.0), [], ["ones_f"])
        for i, mi in enumerate((0, 1, 3, 2)):
            self.V(lambda e, i=i, mi=mi: e.tensor_scalar(out=self.trif[:, i, :], in0=self.masks_s[:, mi, :], scalar1=-1.0 / 16,
                                                        scalar2=None, op0=ALU.mult), ["masks_s"], ["trif"], partial=(i > 0))
        self.V(lambda e: e.memset(self.trif[:, 4, :], -1.0 / 16), [], ["trif"], partial=True)
        for d in range(2):
            for h in range(4):
                self.V(lambda e, d=d, h=h: e.tensor_copy(out=self.mask4[:, d, h * 128:(h + 1) * 128], in_=self.masks_s[:, d, :]),
                       ["masks_s"], ["mask4"], partial=not (d == 0 and h == 0))
        for tt in range(NT):
            self.G(lambda e, tt=tt: e.memset(self.cat[:, tt, 512:768], 0.0), [], [("cat", tt, "b")])
            self.G(lambda e, tt=tt: e.memset(self.cat[:, tt, 768:1024], 0.0), [], [("cat", tt, "c")])
        self.act(self.scond[:], self.condT_s[:], AF.Silu, ["condT_s"], ["scond"])
        for kc in range(8):
            self.V(lambda e, kc=kc: e.tensor_copy(out=self.screp[:, kc, :],
                                                  in_=self.scond[:, kc:kc + 1].to_broadcast([128, 128])),
                   ["scond"], ["screp"], partial=(kc > 0))

    def mod_stage(self, l):
        S = self.S
        self.load(self.bmodT_s[:], self.bmodT[l], "bmodT_s")
        self.load(self.ngT_s[:], self.ngT[l], "ngT_s")
        pm = self.ps[6]
        for p in range(12):
            j = p // 2
            ri, wt = self.load_w_piece(self.w_mod[l], p * 512, (p + 1) * 512)
            rk = "ring%d" % ri
            if j in (2, 5):
                pb = self.ps[p % 2]
                pk = "ps%d" % (p % 2)
                GG = self.GGm if j == 2 else self.GGf
                gi = 0 if j == 2 else 1
                half = p % 2
                if half == 0:
                    self.load(self.ngrep[:], self.ng[l, 1 + 2 * gi:2 + 2 * gi, :].partition_broadcast(128), "ngrep")
                    self.load(self.brep[:], self.bmod[l:l + 1, j * D:(j + 1) * D].partition_broadcast(128), "brep")
                for kc in range(8):
                    self.mm(pb[:], self.screp[:, kc, :], wt[:, kc, :], kc == 0, kc == 7, [rk, "screp"], pk)
                hs = slice(half * 512, (half + 1) * 512)
                self.V(lambda e, GG=GG, hs=hs, pb=pb: e.tensor_tensor(
                    out=GG[:, hs], in0=pb[:], in1=self.brep[:, hs], op=ALU.add), [pk, "brep"], [("GG", gi, half)])
                self.G(lambda e, GG=GG, hs=hs: e.tensor_tensor(
                    out=GG[:, hs], in0=GG[:, hs], in1=self.ngrep[:, hs], op=ALU.mult),
                    [("GG", gi, half), "ngrep"], [("GG", gi, half)])
            else:
                for sub in range(4):
                    c = p * 4 + sub
                    for kc in range(8):
                        self.mm(pm[:, c:c + 1], wt[:, kc, sub * 128:(sub + 1) * 128], self.scond[:, kc:kc + 1],
                                kc == 0, kc == 7, [rk, "scond"], "ps6", partial=not (p == 0 and sub == 0 and kc == 0))
        for (a, b) in ((0, 16), (24, 40)):
            self.V(lambda e, a=a, b=b: e.tensor_tensor(out=self.modT[:, a:b], in0=pm[:, a:b], in1=self.bmodT_s[:, a:b],
                                                       op=ALU.add), ["ps6", "bmodT_s"], ["modT"], partial=(a > 0))
        for which, (jsh, jsc, gi) in enumerate(((0, 1, 0), (3, 4, 2))):
            self.V(lambda e, which=which, jsc=jsc, gi=gi: e.scalar_tensor_tensor(
                out=self.AB[:, 2 * which, :], in0=self.modT[:, jsc * 8:(jsc + 1) * 8], scalar=1.0,
                in1=self.ngT_s[:, gi, :], op0=ALU.add, op1=ALU.mult), ["modT", "ngT_s"], [("AB", 2 * which)])
            self.V(lambda e, which=which, jsh=jsh: e.tensor_copy(
                out=self.AB[:, 2 * which + 1, :], in_=self.modT[:, jsh * 8:(jsh + 1) * 8]), ["modT"],
                [("AB", 2 * which + 1)])

    def norm_to_actT(self, which):
        for tt in range(NT):
            self.act(self.junk[:], self.xs[:, tt, :], AF.Square, [("xs", tt)], ["junk", ("ssq", tt)],
                     accum_out=self.ssq[:, tt:tt + 1])
        self.act(self.rstd[:], self.ssq[:], AF.Ln, [("ssq", t) for t in range(NT)], ["rstd"], bias=EPS, scale=1.0 / D)
        self.act(self.rstd[:], self.rstd[:], AF.Exp, ["rstd"], ["rstd"], scale=-0.5)
        for tt in range(NT):
            xn = self.xn[tt % 2]
            xk = "xn%d" % (tt % 2)
            self.V(lambda e, tt=tt, xn=xn: e.tensor_scalar(out=xn[:], in0=self.xs[:, tt, :],
                                                           scalar1=self.rstd[:, tt:tt + 1], scalar2=None, op0=ALU.mult),
                   [("xs", tt), "rstd"], [xk])
            self.transpose_tile_to_actT(xn, xk, tt, A=self.AB[:, 2 * which, :], B=self.AB[:, 2 * which + 1, :],
                                        abkeys=[("AB", 2 * which), ("AB", 2 * which + 1)])

    def transpose_tile_to_actT(self, src, srckey, tt, A=None, B=None, abkeys=()):
        for kc in range(8):
            self.tr(self.psT[:, kc, :], src[:, kc * 128:(kc + 1) * 128], self.idb[:],
                    list(srckey) if isinstance(srckey, list) else [srckey], "psT", partial=(kc > 0))
        dst = self.actT[:, :, tt * 128:(tt + 1) * 128]
        if A is None:
            self.V(lambda e: e.tensor_copy(out=dst, in_=self.psT[:]), ["psT"], [("actT", tt)])
        else:
            tmp = self.tmpf[tt % 2]
            tk = "tmpf%d" % (tt % 2)
            tv = tmp[:].rearrange("p (k n) -> p k n", k=8)
            self.V(lambda e: e.tensor_tensor(out=tv, in0=self.psT[:], in1=A.unsqueeze(2).to_broadcast([128, 8, 128]),
                                             op=ALU.mult), ["psT"] + list(abkeys), [tk])
            self.G(lambda e: e.tensor_tensor(out=dst, in0=tv, in1=B.unsqueeze(2).to_broadcast([128, 8, 128]),
                                             op=ALU.add), [tk] + list(abkeys), [("actT", tt)])


    def attention(self, l):
        S = self.S
        self.ar_reset()
        stage = [self.ar([768]) for _ in range(2)]
        qkn = [self.ar([640]) for _ in range(2)]
        t1 = self.ar([640])
        t2 = self.ar([640])
        qkr = [self.ar([640], BF16) for _ in range(2)]
        qkgrep = self.ar([640])
        ropet = [self.ar([2, 640]) for _ in range(2)]
        qT = self.ar([NT, 512], BF16)
        kT = self.ar([512 + T], BF16)
        vA = self.ar([12, 2, 80], BF16)
        ctxk = self.ar([4, 128])
        ctxv = self.ar([4, 128])
        ctxkb = self.ar([4, 128], BF16)
        pTs = [self.ar([512], BF16) for _ in range(3)]
        st10 = self.ar([16])
        rs10 = self.ar([16])
        rec = [self.ar([4]) for _ in range(2)]

        self.load(qkgrep, self.qkg[l:l + 1, :].partition_broadcast(128), "M_qkgrep")
        self.load(ctxk, self.ctx_k[l].rearrange("(c p) n -> p c n", p=128), "M_ctxk")
        self.load(ctxv, self.ctx_v[l].rearrange("(c p) n -> p c n", p=128), "M_ctxv")
        self.V(lambda e: e.memset(vA[:, :, :, 64:80], 1.0), [], ["M_vA1"])
        self.V(lambda e: e.tensor_copy(out=ctxkb, in_=ctxk), ["M_ctxk"], ["M_ctxkb"])
        self.V(lambda e: e.tensor_copy(out=vA[:, 0:4, :, 0:64], in_=ctxv.rearrange("p c (g d) -> p c g d", g=2)),
               ["M_ctxv"], ["M_vA_ctx"])
        for c in range(4):
            self.tr(self.psT[:, c, :], ctxkb[:, c, :], self.idb[:], ["M_ctxkb"], "psT", partial=(c > 0))
        self.V(lambda e: e.tensor_copy(out=kT[:, 0:512].rearrange("p (c n) -> p c n", c=4), in_=self.psT[:, 0:4, :]),
               ["psT"], ["M_kT_ctx"])

        if "stopA1" in self.debug:
            return
        r0, w0 = self.load_w_piece(self.w_in[l], 0, 512)
        r1, w1 = self.load_w_piece(self.w_in[l], 512, 768)
        def att_proj_mm(tt):
            b = tt % 2
            pa, pb = self.ps[0 + b], self.ps[2 + b]
            pak, pbk = "ps%d" % b, "ps%d" % (2 + b)
            ts = slice(tt * 128, (tt + 1) * 128)
            for kc in range(8):
                self.mm(pa[:], self.actT[:, kc, ts], w0[:, kc, :], kc == 0, kc == 7, [("actT", tt), "ring%d" % r0], pak)
            for kc in range(8):
                self.mm(pb[:, 0:256], self.actT[:, kc, ts], w1[:, kc, :], kc == 0, kc == 7,
                        [("actT", tt), "ring%d" % r1], pbk)

        att_proj_mm(0)
        for tt in range(NT):
            b = tt % 2
            pa, pb = self.ps[0 + b], self.ps[2 + b]
            pak, pbk = "ps%d" % b, "ps%d" % (2 + b)
            ts = slice(tt * 128, (tt + 1) * 128)
            stg, sk = stage[b], "M_stage%d" % b
            self.act(stg[:, 0:512], pa[:], AF.Copy, [pak], [sk])
            self.V(lambda e, stg=stg, pb=pb: e.tensor_copy(out=stg[:, 512:768], in_=pb[:, 0:256]), [pbk], [sk], partial=True)
            qn, qnk = qkn[b], "M_qkn%d" % b
            sv = stg[:, 0:640].rearrange("p (h d) -> p h d", h=10)
            qv = qn.rearrange("p (h d) -> p h d", h=10)
            self.V(lambda e, qn=qn, stg=stg: e.tensor_tensor(out=qn, in0=stg[:, 0:640], in1=stg[:, 0:640], op=ALU.mult),
                   [sk], [qnk])
            self.V(lambda e, qv=qv: e.tensor_reduce(out=st10[:, 0:10], in_=qv, axis=AX.X, op=ALU.add), [qnk], ["M_st10"])
            self.act(rs10[:, 0:10], st10[:, 0:10], AF.Ln, ["M_st10"], ["M_rs10"], bias=EPS, scale=1.0 / 64)
            self.act(rs10[:, 0:10], rs10[:, 0:10], AF.Exp, ["M_rs10"], ["M_rs10"], scale=-0.5)
            self.V(lambda e, qv=qv, sv=sv: e.tensor_tensor(out=qv, in0=sv, in1=rs10[:, 0:10].unsqueeze(2).to_broadcast([128, 10, 64]),
                                                          op=ALU.mult), [sk, "M_rs10"], [qnk])
            self.V(lambda e, qn=qn: e.tensor_tensor(out=qn, in0=qn, in1=qkgrep, op=ALU.mult), [qnk, "M_qkgrep"], [qnk])
            self.store(self.kout[l, ts, :], qn[:, 512:640], qnk)
            self.store(self.vout[l, ts, :], stg[:, 640:768], sk)
            self.V(lambda e, stg=stg, tt=tt: e.tensor_copy(out=vA[:, 4 + tt, :, 0:64],
                                                          in_=stg[:, 640:768].rearrange("p (g d) -> p g d", g=2)),
                   [sk], [("M_vA", tt)])
            rp, rpk = ropet[b], "M_rope%d" % b
            self.load(rp, self.rope[ts, :, :], rpk)
            self.V(lambda e, qn=qn, rp=rp: e.tensor_tensor(out=t1, in0=qn, in1=rp[:, 0, :], op=ALU.mult), [qnk, rpk], ["M_t1"])
            q3 = qn.rearrange("p (h two s) -> p h two s", h=20, two=2)
            t23 = t2.rearrange("p (h two s) -> p h two s", h=20, two=2)
            sn3 = rp[:, 1, :].rearrange("p (h two s) -> p h two s", h=20, two=2)
            for two in range(2):
                self.V(lambda e, two=two, q3=q3, t23=t23, sn3=sn3: e.tensor_tensor(
                    out=t23[:, :, two, :], in0=q3[:, :, 1 - two, :], in1=sn3[:, :, two, :], op=ALU.mult),
                    [qnk, rpk], ["M_t2"], partial=(two > 0))
            qr, qrk = qkr[b], "M_qkr%d" % b
            for g in range(2):
                self.V(lambda e, qr=qr, g=g: e.tensor_tensor(
                    out=qr[:, 0:512].rearrange("p (j c) -> p j c", j=4)[:, :, g * 64:(g + 1) * 64],
                    in0=t1[:, g * 256:(g + 1) * 256].rearrange("p (j d) -> p j d", j=4),
                    in1=t2[:, g * 256:(g + 1) * 256].rearrange("p (j d) -> p j d", j=4), op=ALU.add),
                    ["M_t1", "M_t2"], [qrk], partial=(g > 0))
            self.V(lambda e, qr=qr: e.tensor_tensor(out=qr[:, 512:640], in0=t1[:, 512:640], in1=t2[:, 512:640], op=ALU.add),
                   ["M_t1", "M_t2"], [qrk], partial=True)
            if tt + 1 < NT:
                att_proj_mm(tt + 1)
            for j in range(4):
                self.tr(self.psT[:, j, :], qr[:, j * 128:(j + 1) * 128], self.idb[:], [qrk], "psT", partial=(j > 0))
            self.tr(self.psT[:, 4, :], qr[:, 512:640], self.idb[:], [qrk], "psT", partial=True)
            if "noq" not in self.debug:
                self.V(lambda e, tt=tt: e.tensor_copy(out=qT[:, tt, :], in_=self.psT[:, 0:4, :].rearrange("p j n -> p (j n)")),
                       ["psT"], [("M_qT", tt)])
            self.V(lambda e, tt=tt: e.tensor_copy(out=kT[:, 512 + tt * 128:512 + (tt + 1) * 128], in_=self.psT[:, 4, :]),
                   ["psT"], [("M_kT", tt)])

        if "qk" in self.debug and l == 0:
            d = self.dbg("qT", [128, NT, 512], BF16)
            self.store(d, qT, ("M_qT", 0))
            for tt in range(1, NT):
                self.S.ops[-1].deps += [(w, "raw") for w in self.S.writers[("M_qT", tt)]]
            d = self.dbg("kT", [128, 512 + T], BF16)
            self.store(d, kT, ("M_kT", 0))
            for tt in range(1, NT):
                self.S.ops[-1].deps += [(w, "raw") for w in self.S.writers[("M_kT", tt)]]
            self.S.ops[-1].deps += [(w, "raw") for w in self.S.writers["M_kT_ctx"]]

        if "stop_att_proj" in self.debug:
            return
        allk = ["M_kT_ctx", "M_vA_ctx", "M_vA1"] + [("M_kT", t) for t in range(NT)] + [("M_vA", t) for t in range(NT)]
        iters = [(qb, g, kc) for qb in range(NT) for g in range(2) for kc in range(12)]

        def emit_score(i):
            qb, g, kc = iters[i]
            x = i % 3
            psc, psk = self.ps[x], "ps%d" % x
            self.mm(psc[:], kT[g * 64:(g + 1) * 64, kc * 128:(kc + 1) * 128], qT[g * 64:(g + 1) * 64, qb, :], True, True,
                    allk + [("M_qT", qb)], psk, partial=False)

        emit_score(0)
        for i, (qb, g, kc) in enumerate(iters):
            grp = i // 12
            po = self.ps[4 + (grp % 2)]
            pok = "ps%d" % (4 + (grp % 2))
            rc = rec[grp % 2]
            rck = "M_rec%d" % (grp % 2)
            pov = po[:, 0:512].rearrange("p (j c) -> p j c", j=4)
            x = i % 3
            psc, psk = self.ps[x], "ps%d" % x
            pt, ptk = pTs[x], "M_pT%d" % x
            self.act(pt, psc[:], AF.Exp, [psk, "abias_s"], [ptk], scale=0.125,
                     bias=self.abias_s[:, kc * NT + qb:kc * NT + qb + 1])
            if i + 1 < len(iters):
                emit_score(i + 1)
            for j in range(4):
                self.mm(pov[:, j, 0:65], pt[:, j * 128:(j + 1) * 128], vA[:, kc, g, 0:65], kc == 0 and j == 0, kc == 11 and j == 3,
                        [ptk] + allk, pok, partial=not (kc == 0 and j == 0))
            if kc == 11:
                self.V(lambda e, rc=rc, pov=pov: e.reciprocal(out=rc, in_=pov[:, :, 64]), [pok], [rck])
                dst = self.cat[:, qb, g * 256:(g + 1) * 256].rearrange("p (j d) -> p j d", j=4)
                self.V(lambda e, rc=rc, pov=pov, dst=dst: e.tensor_tensor(
                    out=dst, in0=pov[:, :, 0:64], in1=rc.unsqueeze(2).to_broadcast([128, 4, 64]), op=ALU.mult),
                    [pok, rck], [("cat", qb, "a%d" % g)])

    def gla(self, l):
        self.ar_reset()
        stB = [self.ar([800]) for _ in range(2)]
        gcT = self.ar([T])
        wgg_s = self.ar([256])
        gnrep = self.ar([256])
        sp = [self.ar([256]) for _ in range(2)]
        E = [self.ar([3, 256]) for _ in range(2)]
        qkt = [self.ar([6, 128], BF16) for _ in range(2)]
        khat = self.ar([NT, 256], BF16)
        qtT = self.ar([2, T], BF16)
        ktT = self.ar([2, T], BF16)
        vb_s = self.ar([NT, 256], BF16)
        srb = self.ar([NT, 256], BF16)
        dl = self.ar([NT, 2])
        ob = self.ar([NT, 256])
        attm = [self.ar([512], BF16) for _ in range(2)]
        qmsk = [self.ar([512], BF16) for _ in range(2)]
        Snew = [self.ar([64]) for _ in range(2)]
        Scur = self.ar([64])
        Sbf = self.ar([64], BF16)
        st4 = self.ar([8])
        sq = self.ar([256])

        self.load(wgg_s[0:33, :], self.wgg[l], "M_wgg")
        self.load(gnrep[:, 0:64], self.gla_norm[l:l + 1, :].partition_broadcast(128), "M_gn0")
        for h in range(1, 4):
            self.V(lambda e, h=h: e.tensor_copy(out=gnrep[:, h * 64:(h + 1) * 64], in_=gnrep[:, 0:64]), ["M_gn0"], [("M_gn", h)])
        self.V(lambda e: e.memset(gcT[32:64, :], 1.0), [], ["M_gcT1"])

        r0, w0 = self.load_w_piece(self.w_in[l], 768, 1280)
        r1, w1 = self.load_w_piece(self.w_in[l], 1280, 1568)
        for half in range(2):
            hs = slice(half * 512, (half + 1) * 512)
            pg = self.ps[4 + half]
            pgk = "ps%d" % (4 + half)
            for kc in range(8):
                self.mm(pg[0:32, :], w1[:, kc, 256:288], self.actT[:, kc, hs], kc == 0, kc == 7,
                        [("actT", t) for t in range(half * 4, half * 4 + 4)] + ["ring%d" % r1], pgk)
            self.V(lambda e, pg=pg, hs=hs: e.tensor_copy(out=gcT[0:32, hs], in_=pg[0:32, :]), [pgk], [("M_gcT", half)])

        if "stopB0" in self.debug:
            return
        def gla_proj_mm(tt):
            b = tt % 2
            ts = slice(tt * 128, (tt + 1) * 128)
            pa, pb = self.ps[0 + b], self.ps[2 + b]
            pak, pbk = "ps%d" % b, "ps%d" % (2 + b)
            for kc in range(8):
                self.mm(pa[:], self.actT[:, kc, ts], w0[:, kc, :], kc == 0, kc == 7, [("actT", tt), "ring%d" % r0], pak)
            for kc in range(8):
                self.mm(pb[:, 0:256], self.actT[:, kc, ts], w1[:, kc, 0:256], kc == 0, kc == 7, [("actT", tt), "ring%d" % r1], pbk)

        gla_proj_mm(0)
        for tt in range(NT):
            b = tt % 2
            ts = slice(tt * 128, (tt + 1) * 128)
            pa, pb = self.ps[0 + b], self.ps[2 + b]
            pak, pbk = "ps%d" % b, "ps%d" % (2 + b)
            stg, sk = stB[b], "M_stB%d" % b
            self.V(lambda e, stg=stg, pa=pa: e.tensor_scalar(out=stg[:, 0:128], in0=pa[:, 0:128], scalar1=32.0 ** -0.5, scalar2=None,
                                                             op0=ALU.mult), [pak], [sk])
            self.V(lambda e, stg=stg, pa=pa: e.tensor_copy(out=stg[:, 128:256], in_=pa[:, 128:256]), [pak], [sk], partial=True)
            self.V(lambda e, pa=pa, tt=tt: e.tensor_copy(out=vb_s[:, tt, :], in_=pa[:, 256:512]), [pak], [("M_vb", tt)])
            self.act(srb[:, tt, :], pb[:, 0:256], AF.Silu, [pbk], [("M_srb", tt)])
            if tt + 1 < NT:
                gla_proj_mm(tt + 1)
            px = self.ps[6]
            self.mm(px[:, 0:256], gcT[0:33, ts], wgg_s[0:33, :], True, True,
                    [("M_gcT", tt // 4), "M_gcT1", "M_wgg"], "ps6", partial=False)
            spt, spk = sp[b], "M_sp%d" % b
            self.act(spt, px[:, 0:256], AF.Exp, ["ps6"], [spk], scale=-1.0)
            self.act(spt, spt, AF.Ln, [spk], [spk], bias=1.0)
            pc = self.ps[4 + b]
            pck = "ps%d" % (4 + b)
            for d in range(2):
                cs = slice(d * 128, (d + 1) * 128)
                self.mm(pc[:, d * 128:(d + 1) * 128], self.trif[:, d, :], spt[:, cs], True, True,
                        [spk, "trif"], pck, partial=(d > 0))
            for d in range(2):
                cs = slice(d * 128, (d + 1) * 128)
                self.mm(pc[:, 256 + d * 128:256 + (d + 1) * 128], self.trif[:, 2 + d, :], spt[:, cs], True, True,
                        [spk, "trif"], pck, partial=True)
            Et, Ek = E[b], "M_E%d" % b
            self.act(Et[:, 0, :], pc[:, 0:256], AF.Exp, [pck], [Ek])
            self.act(Et[:, 1, :], pc[:, 0:256], AF.Exp, [pck], [Ek], scale=-1.0, partial=True)
            self.act(Et[:, 2, :], pc[:, 256:512], AF.Exp, [pck], [Ek], partial=True)
            pd = self.ps[6]
            for d in range(2):
                self.mm(pd[:, 256 + 8 * d:264 + 8 * d], spt[:, d * 128:(d + 1) * 128], self.trif[:, 4, 0:8], True, True,
                        [spk, "trif"], "ps6", partial=(d > 0))
            self.act(dl[:, tt, :], pd[:, 256:272].rearrange("p (d e) -> p d e", d=2)[:, :, 0], AF.Exp, ["ps6"], [("M_dl", tt)])
            qt, qk_ = qkt[b], "M_qkt%d" % b
            for d in range(2):
                cs = slice(d * 128, (d + 1) * 128)
                self.V(lambda e, qt=qt, stg=stg, Et=Et, d=d, cs=cs: e.tensor_tensor(
                    out=qt[:, 2 * d, :], in0=stg[:, 0:128], in1=Et[:, 0, cs], op=ALU.mult), [sk, Ek], [qk_], partial=(d > 0))
                self.V(lambda e, qt=qt, stg=stg, Et=Et, d=d, cs=cs: e.tensor_tensor(
                    out=qt[:, 2 * d + 1, :], in0=stg[:, 128:256], in1=Et[:, 1, cs], op=ALU.mult), [sk, Ek], [qk_], partial=True)
                self.G(lambda e, stg=stg, Et=Et, d=d, cs=cs, tt=tt: e.tensor_tensor(
                    out=khat[:, tt, cs], in0=stg[:, 128:256], in1=Et[:, 2, cs], op=ALU.mult), [sk, Ek], [("M_khat", tt)],
                    partial=(d > 0))
            for i in range(4):
                self.tr(self.psT[:, i, :], qt[:, i, :], self.idb[:], [qk_], "psT", partial=(i > 0))
            for d in range(2):
                self.V(lambda e, d=d, ts=ts: e.tensor_copy(out=qtT[:, d, ts], in_=self.psT[:, 2 * d, :]), ["psT"], [("M_qtT", tt)],
                       partial=(d > 0))
                self.V(lambda e, d=d, ts=ts: e.tensor_copy(out=ktT[:, d, ts], in_=self.psT[:, 2 * d + 1, :]), ["psT"],
                       [("M_ktT", tt)], partial=(d > 0))

        if "stopB1" in self.debug:
            return
        steps = []
        for d in range(2):
            order = list(range(NT)) if d == 0 else list(range(NT - 1, -1, -1))
            for n, tt in enumerate(order):
                steps.append((d, n, tt, order))

        def gla_A(i):
            d, n, tt, order = steps[i]
            ts = slice(tt * 128, (tt + 1) * 128)
            b = i % 2
            pat = self.ps[0 + b]
            patk = "ps%d" % b
            qm, qmk = qmsk[b], "M_qm%d" % b
            for h in range(4):
                self.V(lambda e, qm=qm, h=h, d=d, ts=ts: e.tensor_scalar(out=qm[:, h * 128:(h + 1) * 128], in0=qtT[:, d, ts],
                                                                         scalar1=self.hmask_s[:, h:h + 1], scalar2=None,
                                                                         op0=ALU.mult),
                       [("M_qtT", tt), "hmask_s"], [qmk], partial=(h > 0))
            for h in range(4):
                self.mm(pat[:, h * 128:(h + 1) * 128], ktT[:, d, ts], qm[:, h * 128:(h + 1) * 128], True, True,
                        [("M_ktT", tt), qmk], patk, partial=(h > 0))
            am, amk = attm[b], "M_attm%d" % b
            self.V(lambda e, am=am, pat=pat, d=d: e.tensor_tensor(out=am, in0=pat[:], in1=self.mask4[:, d, :], op=ALU.mult),
                   [patk, "mask4"], [amk])

        def gla_B(i):
            d, n, tt, order = steps[i]
            ts = slice(tt * 128, (tt + 1) * 128)
            b = i % 2
            qm, qmk = qmsk[b], "M_qm%d" % b
            am, amk = attm[b], "M_attm%d" % b
            if n == 0:
                self.load(Scur, self.s0_gla[l, d], "M_Scur")
                self.act(Sbf, Scur, AF.Copy, ["M_Scur"], ["M_Sbf"])
            po = self.ps[2 + b]
            pok = "ps%d" % (2 + b)
            for h in range(4):
                self.mm(po[:, h * 64:(h + 1) * 64], am[:, h * 128:(h + 1) * 128], vb_s[:, tt, h * 64:(h + 1) * 64],
                        True, False, [amk, ("M_vb", tt)], pok, partial=(h > 0))
                self.mm(po[:, h * 64:(h + 1) * 64], qm[:, h * 128:(h + 1) * 128], Sbf, False, True,
                        [qmk, "M_Sbf"], pok, partial=True)
            if d == 0:
                self.act(ob[:, tt, :], po[:, 0:256], AF.Copy, [pok], [("M_ob", tt)])
            else:
                self.V(lambda e, tt=tt, po=po: e.tensor_tensor(out=ob[:, tt, :], in0=ob[:, tt, :], in1=po[:, 0:256], op=ALU.add),
                       [pok, ("M_ob", tt)], [("M_ob", tt)])
            pS = self.ps[4 + b]
            pSk = "ps%d" % (4 + b)
            self.mm(pS[:, 0:256], khat[:, tt, d * 128:(d + 1) * 128], vb_s[:, tt, :], True, True,
                    [("M_khat", tt), ("M_vb", tt)], pSk, partial=False)
            sn, snk = Snew[b], "M_Snew%d" % b
            for h in range(4):
                hp = slice(32 * h, 32 * h + 32)
                self.V(lambda e, sn=sn, hp=hp, h=h, pS=pS, tt=tt, d=d: e.scalar_tensor_tensor(
                    out=sn[hp, :], in0=Scur[hp, :], scalar=dl[hp, tt, d:d + 1], in1=pS[hp, h * 64:(h + 1) * 64],
                    op0=ALU.mult, op1=ALU.add), ["M_Scur", ("M_dl", tt), pSk], [snk], partial=(h > 0))
            is_out = (tt % 2 == 1) if d == 0 else (tt % 2 == 0)
            if is_out:
                self.store(self.sg_out[l, d, tt // 2], sn, snk)
            if n < NT - 1:
                nxt = order[n + 1]
                kcol = d * NT + nxt
                self.V(lambda e, sn=sn, kcol=kcol: e.tensor_scalar(out=Scur, in0=sn, scalar1=self.keep_s[:, kcol:kcol + 1],
                                                                  scalar2=None, op0=ALU.mult),
                       [snk, "keep_s"], ["M_Scur"])
                self.act(Sbf, Scur, AF.Copy, ["M_Scur"], ["M_Sbf"])

        gla_A(0)
        for i in range(len(steps)):
            if i + 1 < len(steps):
                gla_A(i + 1)
            gla_B(i)

        for tt in range(NT):
            if "noBnorm" in self.debug:
                break
            self.V(lambda e, tt=tt: e.tensor_tensor(out=sq, in0=ob[:, tt, :], in1=ob[:, tt, :], op=ALU.mult), [("M_ob", tt)], ["M_sq"])
            self.V(lambda e: e.tensor_reduce(out=st4[:, 0:4], in_=sq.rearrange("p (h d) -> p h d", h=4), axis=AX.X, op=ALU.add),
                   ["M_sq"], ["M_st4"])
            self.act(st4[:, 4:8], st4[:, 0:4], AF.Ln, ["M_st4"], ["M_rs4"], bias=EPS, scale=1.0 / 64)
            self.act(st4[:, 4:8], st4[:, 4:8], AF.Exp, ["M_rs4"], ["M_rs4"], scale=-0.5)
            self.V(lambda e, tt=tt: e.tensor_tensor(out=sq.rearrange("p (h d) -> p h d", h=4),
                                                    in0=ob[:, tt, :].rearrange("p (h d) -> p h d", h=4),
                                                    in1=st4[:, 4:8].unsqueeze(2).to_broadcast([128, 4, 64]), op=ALU.mult),
                   [("M_ob", tt), "M_rs4"], ["M_sq"])
            self.V(lambda e: e.tensor_tensor(out=sq, in0=sq, in1=gnrep, op=ALU.mult),
                   ["M_sq", "M_gn0"] + [("M_gn", h) for h in range(1, 4)], ["M_sq"])
            self.V(lambda e, tt=tt: e.tensor_tensor(out=self.cat[:, tt, 512:768], in0=sq, in1=srb[:, tt, :], op=ALU.mult),
                   ["M_sq", ("M_srb", tt)], [("cat", tt, "b")])


    def ar_mark_reset(self, mark):
        self.S.barrier("gpsimd", lambda e: e.memset(self.bar_s[:], 0.0))
        self.ar_off = mark

    def delta(self, l):
        self.ar_reset()
        qTc = self.ar([2, T], BF16)
        kTc = self.ar([2, T], BF16)
        k_tok = self.ar([NT, 256], BF16)
        v_tok = self.ar([NT, 256], BF16)
        sgc = self.ar([NT, 256], BF16)
        oc = self.ar([NT, 256])
        beta = self.ar([NT, 8])
        gam = self.ar([NT, 8])
        ngam = self.ar([NT, 8])
        eg = self.ar([NT, 8])
        eglm = self.ar([NT, 8])
        egl = self.ar([NT, 8])
        bkg = self.ar([NT, 8])
        cw_s = self.ar([6, 5])
        negA = self.ar([8])
        dtb_s = self.ar([8])
        dnrep = self.ar([256])
        bones = self.ar([128])
        mark = self.ar_off

        self.load(cw_s, self.cw[l], "M_cw")
        self.load(negA, self.alog[l:l + 1, :].partition_broadcast(128), "M_negA")
        self.load(dtb_s, self.dtb[l:l + 1, :].partition_broadcast(128), "M_dtb")
        self.load(dnrep[:, 0:64], self.delta_norm[l:l + 1, :].partition_broadcast(128), "M_dn0")
        for h in range(1, 4):
            self.V(lambda e, h=h: e.tensor_copy(out=dnrep[:, h * 64:(h + 1) * 64], in_=dnrep[:, 0:64]), ["M_dn0"], [("M_dn", h)])
        self.act(negA, negA, AF.Exp, ["M_negA"], ["M_negA"])
        self.V(lambda e: e.tensor_scalar(out=negA, in0=negA, scalar1=-1.0, scalar2=None, op0=ALU.mult), ["M_negA"], ["M_negA"])
        self.V(lambda e: e.memset(bones, 0.0), [], ["M_bones"])
        self.V(lambda e: e.memset(bones[0:64, 0:64], 1.0), ["M_bones"], ["M_bones"])
        self.V(lambda e: e.memset(bones[64:128, 64:128], 1.0), ["M_bones"], ["M_bones"])

        if "stopC0" in self.debug:
            return
        xin = [self.ar([4, 260]) for _ in range(2)]
        ycv = [self.ar([T]) for _ in range(2)]
        ysl = self.ar([T])
        sqb = self.ar([T])
        vTc = self.ar([2, T], BF16)
        rn = self.ar([T])

        pieces = [(1568, 2080, 4), (2080, 2336, 2)]
        cc = 0
        for (c0, c1, nch) in pieces:
            ri, wt = self.load_w_piece(self.w_in[l], c0, c1)
            for sub in range(nch):
                b = cc % 2
                xi, xik = xin[b], "M_xin%d" % b
                for half in range(2):
                    hs = slice(half * 512, (half + 1) * 512)
                    pp = self.ps[half]
                    ppk = "ps%d" % half
                    for kc in range(8):
                        self.mm(pp[:], wt[:, kc, sub * 128:(sub + 1) * 128], self.actT[:, kc, hs], kc == 0, kc == 7,
                                [("actT", t) for t in range(half * 4, half * 4 + 4)] + ["ring%d" % ri], ppk)
                    self.act(xi[:, 2 * half:2 * half + 2, 2:258], pp[:].rearrange("p (s n) -> p s n", s=2), AF.Copy, [ppk], [xik],
                             partial=(half > 0))
                self.V(lambda e, xi=xi: e.memset(xi[:, 0, 0:2], 0.0), [], [xik], partial=True)
                self.V(lambda e, xi=xi: e.memset(xi[:, 3, 258:260], 0.0), [], [xik], partial=True)
                self.V(lambda e, xi=xi: e.tensor_scalar(out=xi[:, 1:4, 0:2], in0=xi[:, 0:3, 256:258], scalar1=self.cflag_s[:, 0:1],
                                                        scalar2=None, op0=ALU.mult), [xik, "cflag_s"], [xik])
                self.V(lambda e, xi=xi: e.tensor_scalar(out=xi[:, 0:3, 258:260], in0=xi[:, 1:4, 2:4], scalar1=self.cflag_s[:, 0:1],
                                                        scalar2=None, op0=ALU.mult), [xik, "cflag_s"], [xik])
                yc, yck = ycv[b], "M_ycv%d" % b
                yv = yc.rearrange("p (s n) -> p s n", s=4)
                self.V(lambda e, xi=xi, yv=yv, cc=cc: e.tensor_scalar(out=yv, in0=xi[:, :, 0:256], scalar1=cw_s[:, cc, 0:1],
                                                                      scalar2=None, op0=ALU.mult), [xik, "M_cw"], [yck])
                for j in range(1, 5):
                    eng = self.V
                    eng(lambda e, xi=xi, yv=yv, cc=cc, j=j: e.scalar_tensor_tensor(
                        out=yv, in0=xi[:, :, j:j + 256], scalar=cw_s[:, cc, j:j + 1], in1=yv, op0=ALU.mult, op1=ALU.add),
                        [xik, "M_cw", yck], [yck])
                self.act(ysl, yc, AF.Silu, [yck], ["M_ysl"])
                if cc < 4:
                    self.V(lambda e: e.tensor_tensor(out=sqb, in0=ysl, in1=ysl, op=ALU.mult), ["M_ysl"], ["M_sqb"])
                    for half in range(2):
                        hs = slice(half * 512, (half + 1) * 512)
                        pn = self.ps[2 + half]
                        pnk = "ps%d" % (2 + half)
                        self.mm(pn[:], bones, sqb[:, hs], True, True, ["M_bones", "M_sqb"], pnk, partial=False)
                        self.act(rn[:, hs], pn[:], AF.Ln, [pnk], [("M_rn", half)], bias=EPS)
                        self.act(rn[:, hs], rn[:, hs], AF.Exp, [("M_rn", half)], [("M_rn", half)], scale=-0.5)
                    dst = qTc if cc < 2 else kTc
                    dk_ = ("M_qTc", cc) if cc < 2 else ("M_kTc", cc - 2)
                    sc_ = 64.0 ** -0.5 if cc < 2 else 1.0
                    self.V(lambda e, dst=dst, cc=cc, sc_=sc_: e.scalar_tensor_tensor(
                        out=dst[:, cc % 2, :], in0=ysl, scalar=sc_, in1=rn, op0=ALU.mult, op1=ALU.mult),
                        ["M_ysl", ("M_rn", 0), ("M_rn", 1)], [dk_])
                else:
                    self.V(lambda e, cc=cc: e.tensor_copy(out=vTc[:, cc - 4, :], in_=ysl), ["M_ysl"], [("M_vTc", cc - 4)])
                cc += 1
        if "stopC1" in self.debug:
            return
        for tt in range(NT):
            ts = slice(tt * 128, (tt + 1) * 128)
            for c in range(2):
                self.tr(self.psT[:, c, :], kTc[:, c, ts], self.idb[:], [("M_kTc", c)], "psT", partial=(c > 0))
                self.tr(self.psT[:, 2 + c, :], vTc[:, c, ts], self.idb[:], [("M_vTc", c)], "psT", partial=True)
            self.V(lambda e, tt=tt: e.tensor_copy(out=k_tok[:, tt, :], in_=self.psT[:, 0:2, :].rearrange("p c n -> p (c n)")),
                   ["psT"], [("M_ktok", tt)])
            self.V(lambda e, tt=tt: e.tensor_copy(out=v_tok[:, tt, :], in_=self.psT[:, 2:4, :].rearrange("p c n -> p (c n)")),
                   ["psT"], [("M_vtok", tt)])
        if "stopC2" in self.debug:
            return
        ri, wt = self.load_w_piece(self.w_in[l], 2336, 2608)
        g8 = [self.ar([8]) for _ in range(2)]
        g16 = [self.ar([16]) for _ in range(2)]
        for tt in range(NT):
            b = tt % 2
            ts = slice(tt * 128, (tt + 1) * 128)
            pg = self.ps[4 + b]
            pgk = "ps%d" % (4 + b)
            for kc in range(8):
                self.mm(pg[:, 0:256], self.actT[:, kc, ts], wt[:, kc, 0:256], kc == 0, kc == 7, [("actT", tt), "ring%d" % ri], pgk)
            for kc in range(8):
                self.mm(pg[:, 256:272], self.actT[:, kc, ts], wt[:, kc, 256:272], kc == 0, kc == 7, [("actT", tt), "ring%d" % ri], pgk,
                        partial=True)
            self.act(sgc[:, tt, :], pg[:, 0:256], AF.Silu, [pgk], [("M_sgc", tt)])
            if "gCa" in self.debug:
                continue
            self.act(beta[:, tt, :], pg[:, 256:264], AF.Exp, [pgk], [("M_beta", tt)], scale=-1.0)
            self.V(lambda e, tt=tt: e.tensor_scalar(out=beta[:, tt, :], in0=beta[:, tt, :], scalar1=1.0, scalar2=None, op0=ALU.add),
                   [("M_beta", tt)], [("M_beta", tt)])
            self.V(lambda e, tt=tt: e.reciprocal(out=beta[:, tt, :], in_=beta[:, tt, :]), [("M_beta", tt)], [("M_beta", tt)])
            if "gCb" in self.debug:
                continue
            gt, gk = g8[b], "M_g8%d" % b
            self.V(lambda e, gt=gt, pg=pg: e.tensor_tensor(out=gt, in0=pg[:, 264:272], in1=dtb_s, op=ALU.add), [pgk, "M_dtb"], [gk])
            self.act(gt, gt, AF.Exp, [gk], [gk])
            self.act(gt, gt, AF.Ln, [gk], [gk], bias=1.0)
            self.V(lambda e, gt=gt: e.tensor_tensor(out=gt, in0=gt, in1=negA, op=ALU.mult), [gk, "M_negA"], [gk])
            if "gCc" in self.debug:
                continue
            pc = self.ps[6]
            gt2, gk2 = g16[b], "M_g16%d" % b
            for d in range(2):
                self.V(lambda e, gt=gt, gt2=gt2, d=d: e.tensor_tensor(out=gt2[:, d * 8:(d + 1) * 8], in0=gt,
                                                                      in1=self.sel8_s[:, d * 8:(d + 1) * 8], op=ALU.mult),
                       [gk, "sel8_s"], [gk2], partial=(d > 0))
            for d in range(2):
                self.mm(pc[:, 0:8], self.masks_s[:, d, :], gt2[:, d * 8:(d + 1) * 8], d == 0, d == 1, [gk2, "masks_s"], "ps6",
                        partial=(d > 0))
            self.mm(pc[:, 16:24], self.ones_f[:], gt, True, True, [gk, "ones_f"], "ps6", partial=True)
            self.V(lambda e, tt=tt: e.tensor_copy(out=gam[:, tt, :], in_=pc[:, 0:8]), ["ps6"], [("M_gam", tt)])
            if "gCd" in self.debug:
                continue
            self.V(lambda e, tt=tt: e.tensor_scalar(out=ngam[:, tt, :], in0=gam[:, tt, :], scalar1=-1.0, scalar2=None, op0=ALU.mult),
                   [("M_gam", tt)], [("M_ngam", tt)])
            self.act(eg[:, tt, :], gam[:, tt, :], AF.Exp, [("M_gam", tt)], [("M_eg", tt)])
            if "gCe" in self.debug:
                continue
            self.act(egl[:, tt, :], pc[:, 16:24], AF.Exp, ["ps6"], [("M_egl", tt)])
            self.V(lambda e, tt=tt: e.tensor_tensor(out=eglm[:, tt, :], in0=pc[:, 16:24], in1=ngam[:, tt, :], op=ALU.add),
                   ["ps6", ("M_ngam", tt)], [("M_eglm", tt)])
            self.act(eglm[:, tt, :], eglm[:, tt, :], AF.Exp, [("M_eglm", tt)], [("M_eglm", tt)])
            self.V(lambda e, tt=tt: e.tensor_tensor(out=bkg[:, tt, :], in0=beta[:, tt, :], in1=eg[:, tt, :], op=ALU.mult),
                   [("M_beta", tt), ("M_eg", tt)], [("M_bkg", tt)])

        if "stopC3" in self.debug:
            return
        self.ar_mark_reset(mark)
        bigm = self.ar([4, 128])
        cm = self.ar([7, 128])
        self.load(cm, self.cmasks, "M_cm")
        for i, (mi, sgn) in enumerate(((0, 1.0), (1, 1.0), (3, -1.0), (2, -1.0))):
            self.V(lambda e, i=i, mi=mi, sgn=sgn: e.tensor_scalar(out=bigm[:, i, :], in0=self.masks_s[:, mi, :], scalar1=-sgn * NEG,
                                                                  scalar2=None, op0=ALU.mult), ["masks_s"], [("M_bigm", i)])
        attm4 = self.ar([512], BF16)
        Mm4 = self.ar([512])
        SD = BF16
        Cm4 = [self.ar([512], SD) for _ in range(2)]
        Ym4 = self.ar([512], SD)
        Zm4 = [self.ar([512], SD) for _ in range(2)]
        Xm4 = [self.ar([512], SD) for _ in range(2)]
        dec4 = self.ar([512])
        dg4 = self.ar([512])
        decT4 = self.ar([512])
        id4 = self.ar([512], SD)
        bv4 = self.ar([256], SD)
        kbg4 = self.ar([256], SD)
        wT4 = self.ar([512], SD)
        vnew4 = self.ar([256], BF16)
        khat4 = self.ar([512], BF16)
        otmp4 = self.ar([256])
        Sf4 = self.ar([256])
        Sb4 = self.ar([256], BF16)
        Snew4 = [self.ar([256]) for _ in range(2)]
        H4 = range(4)
        for h in H4:
            self.V(lambda e, h=h: e.tensor_copy(out=id4[:, h * 128:(h + 1) * 128], in_=self.idf[:]), ["idf"], ["M_id4"], partial=(h > 0))
        idS = self.idb

        def c128(t, h):
            return t[:, h * 128:(h + 1) * 128]

        def c64(t, h):
            return t[:, h * 64:(h + 1) * 64]

        P = self.ps
        itn = 0
        for d in range(2):
            order = list(range(NT)) if d == 0 else list(range(NT - 1, -1, -1))
            for h in H4:
                self.load(Sf4[0:64, h * 64:(h + 1) * 64], self.s0_delta[l, d, h * 64:(h + 1) * 64, :], "M_Sf", partial=(h > 0))
                self.load(Sf4[64:128, h * 64:(h + 1) * 64], self.s0_delta[l, d, h * 64:(h + 1) * 64, :], "M_Sf", partial=True)
            self.act(Sb4, Sf4, AF.Copy, ["M_Sf"], ["M_Sb"])
            for n, tt in enumerate(order):
                ts = slice(tt * 128, (tt + 1) * 128)
                cols = [d * 4 + h for h in H4]
                hrs = [slice((h % 2) * 64, (h % 2) * 64 + 64) for h in H4]
                chs = [h // 2 for h in H4]
                kcs = [slice(h * 64, (h + 1) * 64) for h in H4]
                if "dS0" in self.debug:
                    continue
                def Gr(h):
                    return P[h % 2][:, (h // 2) * 256:(h // 2) * 256 + 128]

                def Ar(h):
                    return P[h % 2][:, (h // 2) * 256 + 128:(h // 2) * 256 + 256]
                for h in (0, 2, 1, 3):
                    hr, ch = hrs[h], chs[h]
                    bk = "ps%d" % (h % 2)
                    self.mm(Gr(h), kTc[hr, ch, ts], kTc[hr, ch, ts], True, True, [("M_kTc", ch)], bk, partial=(h >= 2))
                    self.mm(Ar(h), kTc[hr, ch, ts], qTc[hr, ch, ts], True, True, [("M_kTc", ch), ("M_qTc", ch)], bk, partial=True)
                if "dS1" in self.debug:
                    continue
                for h in H4:
                    col = cols[h]
                    self.V(lambda e, h=h, tt=tt, col=col: e.tensor_scalar(out=c128(dg4, h), in0=self.idf[:],
                                                                          scalar1=gam[:, tt, col:col + 1], scalar2=None, op0=ALU.mult),
                           ["idf", ("M_gam", tt)], ["M_dg"], partial=(h > 0))
                if "dS2" in self.debug:
                    continue
                for h in H4:
                    self.mm(c128(P[2], h), self.ones_f[:], c128(dg4, h), True, False, ["ones_f", "M_dg"], "ps2", partial=(h > 0))
                    self.mm(c128(P[2], h), self.idf[:], bigm[:, d, :], False, True, ["idf", ("M_bigm", d)], "ps2", partial=True)
                for h in H4:
                    self.mm(c128(P[3], h), self.ones_f[:], c128(dg4, h), True, False, ["ones_f", "M_dg"], "ps3", partial=(h > 0))
                    self.mm(c128(P[3], h), self.idf[:], bigm[:, 2 + d, :], False, True, ["idf", ("M_bigm", 2 + d)], "ps3", partial=True)
                if "dS3" in self.debug:
                    continue
                for h in H4:
                    col = cols[h]
                    self.act(c128(dec4, h), c128(P[2], h), AF.Exp, ["ps2", ("M_gam", tt)], ["M_dec"], scale=-1.0,
                             bias=gam[:, tt, col:col + 1], partial=(h > 0))
                for h in H4:
                    col = cols[h]
                    self.act(c128(decT4, h), c128(P[3], h), AF.Exp, ["ps3", ("M_ngam", tt)], ["M_decT"], scale=1.0,
                             bias=ngam[:, tt, col:col + 1], partial=(h > 0))
                for h in H4:
                    col = cols[h]
                    self.V(lambda e, h=h, tt=tt, col=col: e.scalar_tensor_tensor(
                        out=c128(Mm4, h), in0=Gr(h), scalar=beta[:, tt, col:col + 1], in1=c128(dec4, h),
                        op0=ALU.mult, op1=ALU.mult), ["ps%d" % (h % 2), ("M_beta", tt), "M_dec"], ["M_M"], partial=(h > 0))
                for h in H4:
                    self.V(lambda e, h=h: e.tensor_tensor(out=c128(attm4, h), in0=Ar(h), in1=c128(decT4, h), op=ALU.mult),
                           ["ps%d" % (h % 2), "M_decT"], ["M_attm"], partial=(h > 0))
                if "dS5" in self.debug:
                    continue
                Zc, Zk = id4, "M_id4"
                Xc, Xk = id4, "M_id4"
                for k in range(7):
                    Ck, Ckk = Cm4[k % 2], "M_C%d" % (k % 2)
                    for h in H4:
                        self.G(lambda e, h=h, k=k, Ck=Ck: e.tensor_tensor(out=c128(Ck, h), in0=c128(Mm4, h), in1=cm[:, k, :], op=ALU.mult),
                               ["M_M", "M_cm"], [Ckk], partial=(h > 0))
                    for h in H4:
                        self.mm(c128(P[2], h), c128(Ck, h), c128(Zc, h), True, True, [Ckk, Zk], "ps2", partial=(h > 0))
                    self.act(Ym4, P[2][:], AF.Copy, ["ps2"], ["M_Y"])
                    for h in H4:
                        self.mm(c128(P[3], h), c128(Xc, h), c128(Ym4, h), True, True, [Xk, "M_Y"], "ps3", partial=(h > 0))
                    Zn, Znk = Zm4[k % 2], "M_Z%d" % (k % 2)
                    self.V(lambda e, Zn=Zn, Zo=Zc: e.scalar_tensor_tensor(out=Zn, in0=P[3][:], scalar=-1.0, in1=Zo, op0=ALU.mult,
                                                                          op1=ALU.add), ["ps3", Zk], [Znk])
                    Zc, Zk = Zn, Znk
                    if k < 6:
                        for h in H4:
                            self.S.op("tensor", lambda e, h=h, Zc=Zc: e.transpose(out=self.psT[:, h, :], in_=c128(Zc, h), identity=idS[:]),
                                      reads=[Zk, "ident"], writes=["psT"], partial=(h > 0))
                        Xn, Xnk = Xm4[k % 2], "M_X%d" % (k % 2)
                        self.act(Xn, self.psT[:, 0:4, :].rearrange("p a b -> p (a b)"), AF.Copy, ["psT"], [Xnk])
                        Xc, Xk = Xn, Xnk
                if "dS6" in self.debug:
                    continue
                for h in H4:
                    col, kcols = cols[h], kcs[h]
                    self.V(lambda e, h=h, tt=tt, col=col, kcols=kcols: e.tensor_scalar(
                        out=c64(bv4, h), in0=v_tok[:, tt, kcols], scalar1=beta[:, tt, col:col + 1], scalar2=None, op0=ALU.mult),
                        [("M_vtok", tt), ("M_beta", tt)], ["M_bv"], partial=(h > 0))
                    self.V(lambda e, h=h, tt=tt, col=col, kcols=kcols: e.tensor_scalar(
                        out=c64(kbg4, h), in0=k_tok[:, tt, kcols], scalar1=bkg[:, tt, col:col + 1], scalar2=None, op0=ALU.mult),
                        [("M_ktok", tt), ("M_bkg", tt)], ["M_kbg"], partial=(h > 0))
                    for half in range(2):
                        self.G(lambda e, h=h, tt=tt, col=col, kcols=kcols, half=half: e.tensor_scalar(
                            out=khat4[:, h * 128 + half * 64:h * 128 + (half + 1) * 64], in0=k_tok[:, tt, kcols],
                            scalar1=eglm[:, tt, col:col + 1], scalar2=None, op0=ALU.mult), [("M_ktok", tt), ("M_eglm", tt)],
                            ["M_khat"], partial=not (h == 0 and half == 0))
                for h in H4:
                    self.mm(P[5][0:64, h * 128:(h + 1) * 128], c64(kbg4, h), c128(Zc, h), True, True, ["M_kbg", Zk], "ps5", partial=(h > 0))
                self.V(lambda e: e.tensor_scalar(out=wT4[0:64, :], in0=P[5][0:64, :], scalar1=-1.0, scalar2=None, op0=ALU.mult),
                       ["ps5"], ["M_wT"])
                for h in H4:
                    self.mm(c64(P[6], h), c128(Zc, h), c64(bv4, h), True, False, [Zk, "M_bv"], "ps6", partial=(h > 0))
                    self.mm(c64(P[6], h), wT4[0:64, h * 128:(h + 1) * 128], Sb4[0:64, h * 64:(h + 1) * 64], False, True,
                            ["M_wT", "M_Sb"], "ps6", partial=True)
                self.act(vnew4, P[6][:, 0:256], AF.Copy, ["ps6"], ["M_vnew"])
                if "dS9" in self.debug:
                    continue
                def Qr(h):
                    return (P[0] if h % 2 == 0 else P[5])[:, h * 64:(h + 1) * 64]
                for h in (0, 2):
                    hr, ch = hrs[h], chs[h]
                    self.mm(Qr(h), qTc[hr, ch, ts], Sb4[hr, h * 64:(h + 1) * 64], True, True, [("M_qTc", ch), "M_Sb"], "ps0",
                            partial=(h > 0))
                for h in H4:
                    self.mm(c64(P[1], h), c128(attm4, h), c64(vnew4, h), True, True, ["M_attm", "M_vnew"], "ps1", partial=(h > 0))
                for h in (1, 3):
                    hr, ch = hrs[h], chs[h]
                    self.mm(Qr(h), qTc[hr, ch, ts], Sb4[hr, h * 64:(h + 1) * 64], True, True, [("M_qTc", ch), "M_Sb"], "ps5",
                            partial=(h > 1))
                for h in H4:
                    self.mm(c64(P[4], h), c128(khat4, h), c64(vnew4, h), True, True, ["M_khat", "M_vnew"], "ps4", partial=(h > 0))
                for h in H4:
                    col = cols[h]
                    self.V(lambda e, h=h, tt=tt, col=col: e.tensor_scalar(
                        out=c64(otmp4, h), in0=Qr(h), scalar1=eg[:, tt, col:col + 1], scalar2=None, op0=ALU.mult),
                        ["ps0" if h % 2 == 0 else "ps5", ("M_eg", tt)], ["M_otmp"], partial=(h > 0))
                if d == 0:
                    self.V(lambda e, tt=tt: e.tensor_tensor(out=oc[:, tt, :], in0=otmp4, in1=P[1][:, 0:256], op=ALU.add),
                           ["M_otmp", "ps1"], [("M_oc", tt)])
                else:
                    self.V(lambda e: e.tensor_tensor(out=otmp4, in0=otmp4, in1=P[1][:, 0:256], op=ALU.add), ["M_otmp", "ps1"], ["M_otmp"])
                    self.G(lambda e, tt=tt: e.tensor_tensor(out=oc[:, tt, :], in0=oc[:, tt, :], in1=otmp4, op=ALU.add),
                           ["M_otmp", ("M_oc", tt)], [("M_oc", tt)])
                sn, snk = Snew4[itn % 2], "M_Snew%d" % (itn % 2)
                itn += 1
                for h in H4:
                    col = cols[h]
                    self.V(lambda e, sn=sn, h=h, tt=tt, col=col: e.scalar_tensor_tensor(
                        out=c64(sn, h), in0=c64(Sf4, h), scalar=egl[:, tt, col:col + 1], in1=c64(P[4], h), op0=ALU.mult, op1=ALU.add),
                        ["M_Sf", ("M_egl", tt), "ps4"], [snk], partial=(h > 0))
                is_out = (tt % 2 == 1) if d == 0 else (tt % 2 == 0)
                if is_out:
                    for h in H4:
                        self.store(self.sd_out[l, d, tt // 2, h * 64:(h + 1) * 64, :], sn[0:64, h * 64:(h + 1) * 64], snk)
                if n < NT - 1:
                    nxt = order[n + 1]
                    kcol = d * NT + nxt
                    self.V(lambda e, sn=sn, kcol=kcol: e.tensor_scalar(out=Sf4, in0=sn, scalar1=self.keep_s[:, kcol:kcol + 1],
                                                                      scalar2=None, op0=ALU.mult), [snk, "keep_s"], ["M_Sf"])
                    self.act(Sb4, Sf4, AF.Copy, ["M_Sf"], ["M_Sb"])

        sq = otmp4
        st4 = self.ar([8])
        for tt in range(NT):
            ock = [("M_oc", tt)]
            self.V(lambda e, tt=tt: e.tensor_tensor(out=sq, in0=oc[:, tt, :], in1=oc[:, tt, :], op=ALU.mult), ock, ["M_otmp"])
            self.V(lambda e: e.tensor_reduce(out=st4[:, 0:4], in_=sq.rearrange("p (h d) -> p h d", h=4), axis=AX.X, op=ALU.add),
                   ["M_otmp"], ["M_st4"])
            self.act(st4[:, 4:8], st4[:, 0:4], AF.Ln, ["M_st4"], ["M_rs4"], bias=EPS, scale=1.0 / 64)
            self.act(st4[:, 4:8], st4[:, 4:8], AF.Exp, ["M_rs4"], ["M_rs4"], scale=-0.5)
            self.V(lambda e, tt=tt: e.tensor_tensor(out=sq.rearrange("p (h d) -> p h d", h=4),
                                                    in0=oc[:, tt, :].rearrange("p (h d) -> p h d", h=4),
                                                    in1=st4[:, 4:8].unsqueeze(2).to_broadcast([128, 4, 64]), op=ALU.mult),
                   ock + ["M_rs4"], ["M_otmp"])
            self.V(lambda e: e.tensor_tensor(out=sq, in0=sq, in1=dnrep, op=ALU.mult),
                   ["M_otmp", "M_dn0"] + [("M_dn", h) for h in range(1, 4)], ["M_otmp"])
            self.V(lambda e, tt=tt: e.tensor_tensor(out=self.cat[:, tt, 768:1024], in0=sq, in1=sgc[:, tt, :], op=ALU.mult),
                   ["M_otmp", ("M_sgc", tt)], [("cat", tt, "c")])

    def resid_update(self, tt, ps_lo, lo_key, ps_hi, hi_key, GG, ggkeys, lo_is_sbuf=False):
        st = self.ssq2
        self.act(self.junk[:, 0:512], ps_lo, AF.Square, [lo_key], ["junk", "ssq2"], accum_out=st[:, 0:1])
        self.act(self.junk[:, 512:1024], ps_hi, AF.Square, [hi_key], ["junk", "ssq2"], accum_out=st[:, 1:2], partial=True)
        self.V(lambda e: e.tensor_tensor(out=st[:, 2:3], in0=st[:, 0:1], in1=st[:, 1:2], op=ALU.add), ["ssq2"], ["ssq2s"])
        self.act(st[:, 3:4], st[:, 2:3], AF.Ln, ["ssq2s"], ["rstd2"], bias=EPS, scale=1.0 / D)
        self.act(st[:, 3:4], st[:, 3:4], AF.Exp, ["rstd2"], ["rstd2"], scale=-0.5)
        tmp = self.tmpf[0]
        for h, (src, key) in enumerate(((ps_lo, lo_key), (ps_hi, hi_key))):
            hs = slice(h * 512, (h + 1) * 512)
            self.V(lambda e, src=src, hs=hs: e.scalar_tensor_tensor(out=tmp[:, hs], in0=src, scalar=st[:, 3:4], in1=GG[:, hs],
                                                                   op0=ALU.mult, op1=ALU.mult),
                   [key, "rstd2"] + list(ggkeys), ["tmpf0"], partial=(h > 0))
        self.V(lambda e: e.tensor_tensor(out=self.xs[:, tt, :], in0=self.xs[:, tt, :], in1=tmp[:], op=ALU.add),
               ["tmpf0", ("xs", tt)], [("xs", tt)])

    def out_proj(self, l):
        for tt in range(NT):
            self.transpose_tile_to_actT(self.cat[:, tt, :], [("cat", tt, "a0"), ("cat", tt, "a1"), ("cat", tt, "b"), ("cat", tt, "c")], tt)
        r0, w0 = self.load_w_piece(self.w_out[l], 0, 512)
        r1, w1 = self.load_w_piece(self.w_out[l], 512, 1024)
        ggk = [("GG", 0, 0), ("GG", 0, 1)]
        for tt in range(NT):
            b = tt % 2
            pa, pb = self.ps[0 + b], self.ps[2 + b]
            pak, pbk = "ps%d" % b, "ps%d" % (2 + b)
            ts = slice(tt * 128, (tt + 1) * 128)
            for kc in range(8):
                self.mm(pa[:], self.actT[:, kc, ts], w0[:, kc, :], kc == 0, kc == 7, [("actT", tt), "ring%d" % r0], pak)
            for kc in range(8):
                self.mm(pb[:], self.actT[:, kc, ts], w1[:, kc, :], kc == 0, kc == 7, [("actT", tt), "ring%d" % r1], pbk)
            self.resid_update(tt, pa[:], pak, pb[:], pbk, self.GGm, ggk)

    def ffn(self, l):
        self.norm_to_actT(1)
        self.ar_reset()
        aT = self.ar([NFC, T], BF16)
        sg = [self.ar([512]) for _ in range(2)]
        fbuf = self.ar([NT, 512])
        it = 0
        for c0 in range(0, DFF, 512):
            c1 = min(c0 + 512, DFF)
            rg, wg = self.load_w_piece(self.w_gate[l], c0, c1)
            ru, wu = self.load_w_piece(self.w_up[l], c0, c1)
            for sub in range((c1 - c0) // 128):
                fc = c0 // 128 + sub
                for half in range(2):
                    hs = slice(half * 512, (half + 1) * 512)
                    b = it % 2
                    it += 1
                    pg, pu = self.ps[0 + b], self.ps[2 + b]
                    pgk, puk = "ps%d" % b, "ps%d" % (2 + b)
                    rd = [("actT", t) for t in range(half * 4, half * 4 + 4)]
                    for kc in range(8):
                        self.mm(pg[:], wg[:, kc, sub * 128:(sub + 1) * 128], self.actT[:, kc, hs], kc == 0, kc == 7,
                                rd + ["ring%d" % rg], pgk)
                    for kc in range(8):
                        self.mm(pu[:], wu[:, kc, sub * 128:(sub + 1) * 128], self.actT[:, kc, hs], kc == 0, kc == 7,
                                rd + ["ring%d" % ru], puk)
                    sgt, sgk = sg[b], "F_sg%d" % b
                    self.act(sgt, pg[:], AF.Silu, [pgk], [sgk])
                    self.V(lambda e, sgt=sgt, pu=pu, fc=fc, hs=hs: e.tensor_tensor(out=aT[:, fc, hs], in0=sgt, in1=pu[:],
                                                                                    op=ALU.mult),
                           [sgk, puk], [("F_aT", fc, half)])
        ggk = [("GG", 1, 0), ("GG", 1, 1)]
        groups = [(0, 8), (8, 8), (16, 6)]
        allaT = [("F_aT", fc, h) for fc in range(NFC) for h in range(2)]
        for half in range(2):
            pieces = []
            for (f0, nk) in groups:
                ri, wt = self.load_w_piece(self.w_down[l], half * 512, (half + 1) * 512, r0=f0 * 128, nk=nk)
                pieces.append((ri, wt, f0, nk))
            for tt in range(NT):
                b = tt % 2
                pf = self.ps[4 + b]
                pfk = "ps%d" % (4 + b)
                ts = slice(tt * 128, (tt + 1) * 128)
                n = 0
                for (ri, wt, f0, nk) in pieces:
                    for k in range(nk):
                        self.mm(pf[:], aT[:, f0 + k, ts], wt[:, k, :], n == 0, n == NFC - 1, allaT + ["ring%d" % ri], pfk)
                        n += 1
                if half == 0:
                    self.V(lambda e, tt=tt, pf=pf: e.tensor_copy(out=fbuf[:, tt, :], in_=pf[:]), [pfk], [("F_fbuf", tt)])
                else:
                    self.resid_update(tt, fbuf[:, tt, :], ("F_fbuf", tt), pf[:], pfk, self.GGf, ggk)

    def build(self):
        self.setup()
        for l in range(self.depth):
            self.layer(l)
        self.finish()
        st = self.S.emit()
        self.stats = st
        return self.nc

    def finish(self):
        for tt in range(NT):
            self.store(self.y[tt * 128:(tt + 1) * 128, :], self.xs[:, tt, :], ("xs", tt))

    def layer(self, l):
        self.mod_stage(l)
        self.norm_to_actT(0)
        self.attention(l)
        if "stop_after_att" in self.debug:
            return
        if "nogla" not in self.debug:
            self.gla(l)
        if "nodelta" not in self.debug:
            self.delta(l)
        self.out_proj(l)
        self.ffn(l)
        if "cat" in self.debug and l == 0:
            d = self.dbg("cat", [128, NT, D])
            self.S.dma("gpsimd", lambda e: e.dma_start(out=d, in_=self.cat[:]), "st_cat",
                       reads=[("cat", qb, k) for qb in range(NT) for k in ("a0", "a1", "b", "c")], store=True)
        if "actT0" in self.debug and l == 0:
            d = self.dbg("actT0", [128, 8, T])
            tmp = self.ar([8, T])
            self.V(lambda e: e.tensor_copy(out=tmp, in_=self.actT[:]), [("actT", t) for t in range(NT)], ["M_dbg_actT"])
            self.store(d, tmp, "M_dbg_actT")
            d2 = self.dbg("GG", [128, 2, D])
            self.store(d2[:, 0, :], self.GGm[:], ("GG", 0, 0))
            self.store(d2[:, 0, :], self.GGm[:], ("GG", 0, 1))
            self.store(d2[:, 1, :], self.GGf[:], ("GG", 1, 0))
            self.store(d2[:, 1, :], self.GGf[:], ("GG", 1, 1))


def rope_tables(sample):
    cos = np.ones((T, 64), np.float32)
    sin = np.zeros((T, 64), np.float32)
    if sample:
        t = np.arange(T)
        row = (t // 64).astype(np.float32)
        col = (t % 64).astype(np.float32)
        inv = (np.float32(10000.0) ** (-np.arange(16, dtype=np.float32) / np.float32(16))).astype(np.float32)
        ar = row[:, None] * inv
        ac = col[:, None] * inv
        ang = np.concatenate([ar, ar, ac, ac], axis=-1).astype(np.float32)
        cos = np.cos(ang).astype(np.float32)
        sin = np.sin(ang).astype(np.float32)
    sgn = np.concatenate([-np.ones(16), np.ones(16), -np.ones(16), np.ones(16)]).astype(np.float32)
    sins = sin * sgn
    tab = np.stack([np.tile(cos, (1, 10)), np.tile(sins, (1, 10))], 1)
    return np.ascontiguousarray(tab.astype(np.float32))


def core_tables(sample):
    ab = np.zeros((12, NT), np.float32)
    keep = np.ones((2, NT), np.float32)
    if not sample:
        ab[:] = NEG
        for kc in range(4, 12):
            for qb in range(NT):
                if (kc - 4) // 2 == qb // 2:
                    ab[kc, qb] = 0.0
        for tt in range(NT):
            if tt % 2 == 0:
                keep[0, tt] = 0.0
            if tt % 2 == 1:
                keep[1, tt] = 0.0
    abias = np.ascontiguousarray(np.broadcast_to(ab.reshape(1, -1), (128, 12 * NT))).astype(np.float32)
    keepr = np.ascontiguousarray(np.broadcast_to(keep.reshape(1, -1), (128, 2 * NT))).astype(np.float32)
    cflag = np.full((128, 1), 1.0 if sample else 0.0, np.float32)
    return abias, keepr, cflag


def make_masks():
    m = np.zeros((4, 128, 128), np.float32)
    i = np.arange(128)
    m[0] = (i[:, None] <= i[None, :])
    m[1] = (i[:, None] >= i[None, :])
    m[2] = (i[:, None] < i[None, :])
    m[3] = (i[:, None] > i[None, :])
    return np.ascontiguousarray(m.transpose(1, 0, 2))


def prep_shared(inp, L=DEPTH):
    sh = {}
    sh["ident"] = np.eye(128, dtype=np.float32)
    sh["w_mod"] = np.ascontiguousarray(inp["w_mod"], dtype=np.float32)
    bm = np.asarray(inp["b_mod"], np.float32)
    sh["bmod"] = np.ascontiguousarray(bm)
    sh["bmodT"] = np.ascontiguousarray(bm.reshape(L, 48, 128).transpose(0, 2, 1))
    ng = np.asarray(inp["norm_gains"], np.float32)
    sh["ng"] = np.ascontiguousarray(ng)
    sh["ngT"] = np.ascontiguousarray(ng.reshape(L, 4, 8, 128).transpose(0, 3, 1, 2))
    sh["w_in"] = np.ascontiguousarray(inp["w_in"], dtype=np.float32)
    qg = np.asarray(inp["qk_gain"], np.float32)
    sh["qkg"] = np.ascontiguousarray(np.concatenate([np.tile(qg[:, 0], (1, 8)), np.tile(qg[:, 1], (1, 2))], axis=1))
    wgg = np.zeros((L, 33, 256), np.float32)
    w = np.asarray(inp["w_gla_gate"], np.float32)
    b = np.asarray(inp["b_gla_gate"], np.float32)
    wgg[:, 0:16, 0:128] = w[:, 0]
    wgg[:, 16:32, 128:256] = w[:, 1]
    wgg[:, 32, :] = b.reshape(L, 256)
    sh["wgg"] = wgg
    sh["gla_norm"] = np.ascontiguousarray(inp["gla_norm"], dtype=np.float32)
    cwv = np.asarray(inp["conv_w"], np.float32)
    sh["cw"] = np.ascontiguousarray(cwv.reshape(L, 5, 6, 128).transpose(0, 3, 2, 1))
    sh["alog"] = np.ascontiguousarray(np.asarray(inp["a_log"], np.float32).reshape(L, 8))
    sh["dtb"] = np.ascontiguousarray(np.asarray(inp["dt_bias"], np.float32).reshape(L, 8))
    sh["delta_norm"] = np.ascontiguousarray(inp["delta_norm"], dtype=np.float32)
    for k in ("w_out", "w_gate", "w_up", "w_down"):
        sh[k] = np.ascontiguousarray(inp[k], dtype=np.float32)
    sh["masks"] = make_masks()
    sh["sel8"] = np.ascontiguousarray(np.broadcast_to(
        np.array([1, 1, 1, 1, 0, 0, 0, 0, 0, 0, 0, 0, 1, 1, 1, 1], np.float32)[None, :], (128, 16)))
    sh["hmask"] = (np.arange(128)[:, None] // 32 == np.arange(4)[None, :]).astype(np.float32)
    i = np.arange(128)
    cmk = np.zeros((7, 128, 128), np.float32)
    for k in range(7):
        bs, bb = 2 ** k, 2 ** (k + 1)
        cmk[k] = ((i[:, None] // bb == i[None, :] // bb) & (i[:, None] // bs != i[None, :] // bs)).astype(np.float32)
    sh["cmasks"] = np.ascontiguousarray(cmk.transpose(1, 0, 2))
    return sh


PER_LAYER = ("w_mod", "b_mod", "norm_gains", "w_in", "qk_gain", "w_gla_gate", "b_gla_gate", "gla_norm", "conv_w",
             "a_log", "dt_bias", "delta_norm", "w_out", "w_gate", "w_up", "w_down")
PER_LAYER1 = ("cache_k", "cache_v", "state_gla", "state_delta")


def make_in_maps(inp, L=DEPTH):
    if L != DEPTH:
        inp = dict(inp)
        for k in PER_LAYER:
            inp[k] = np.asarray(inp[k])[:L]
        for k in PER_LAYER1:
            inp[k] = np.asarray(inp[k])[:, :L]
    sh = prep_shared(inp, L)
    xs = np.asarray(inp["x_sample"], np.float32)
    xp = np.asarray(inp["x_prompt"], np.float32)
    maps = []
    tabs = {True: (rope_tables(True),) + core_tables(True), False: (rope_tables(False),) + core_tables(False)}
    for c in range(8):
        m = dict(sh)
        sample = c < 4
        if sample:
            b = c
            m["x"] = np.ascontiguousarray(xs[b])
            cond = np.asarray(inp["c"], np.float32)[b]
            m["ctx_k"] = np.ascontiguousarray(np.asarray(inp["cache_k"], np.float32)[b].reshape(L, 512, 128))
            m["ctx_v"] = np.ascontiguousarray(np.asarray(inp["cache_v"], np.float32)[b].reshape(L, 512, 128))
            m["s0_gla"] = np.ascontiguousarray(np.asarray(inp["state_gla"], np.float32)[b].reshape(L, 2, 128, 64))
            m["s0_delta"] = np.ascontiguousarray(np.asarray(inp["state_delta"], np.float32)[b].reshape(L, 2, 256, 64))
        else:
            j = c - 4
            m["x"] = np.ascontiguousarray(xp[4 * j:4 * j + 4].reshape(T, D))
            cond = np.asarray(inp["c_ctx"], np.float32)
            m["ctx_k"] = np.zeros((L, 512, 128), np.float32)
            m["ctx_v"] = np.zeros((L, 512, 128), np.float32)
            m["s0_gla"] = np.zeros((L, 2, 128, 64), np.float32)
            m["s0_delta"] = np.zeros((L, 2, 256, 64), np.float32)
        m["condT"] = np.ascontiguousarray(cond.reshape(8, 128).T)
        rope, abias, keep, cflag = tabs[sample]
        m["rope"] = rope
        m["abias"] = abias
        m["keep"] = keep
        m["cflag"] = cflag
        maps.append(m)
    return maps


_NC_CACHE = {}


def kernel(**inputs):
    maps = make_in_maps(inputs)
    if "nc" not in _NC_CACHE:
        _NC_CACHE["nc"] = Builder().build()
    nc = _NC_CACHE["nc"]
    res = run_bass_kernel_spmd(nc, maps, core_ids=list(range(8)))
    r = res.results
    L = DEPTH
    y_sample = np.stack([r[c]["y"] for c in range(4)], 0)
    y_prompt = np.concatenate([r[c]["y"].reshape(4, 256, D) for c in range(4, 8)], 0)
    nk = np.concatenate([r[c]["kout"].reshape(L, 4, 256, 2, 64).transpose(1, 0, 2, 3, 4) for c in range(4, 8)], 0)
    nv = np.concatenate([r[c]["vout"].reshape(L, 4, 256, 2, 64).transpose(1, 0, 2, 3, 4) for c in range(4, 8)], 0)
    sg = np.concatenate([r[c]["sg_out"].reshape(L, 2, 4, 4, 32, 64).transpose(2, 0, 1, 3, 4, 5) for c in range(4, 8)], 0)
    sd = np.concatenate([r[c]["sd_out"].reshape(L, 2, 4, 4, 64, 64).transpose(2, 0, 1, 3, 4, 5) for c in range(4, 8)], 0)
    return (y_prompt.astype(np.float32), y_sample.astype(np.float32), np.ascontiguousarray(nk, dtype=np.float32),
            np.ascontiguousarray(nv, dtype=np.float32), np.ascontiguousarray(sg, dtype=np.float32),
            np.ascontiguousarray(sd, dtype=np.float32))
```

```python
import numpy as np
import concourse.bass as bass
import concourse.mybir as mybir
from concourse.bass_utils import run_bass_kernel_spmd

F32 = mybir.dt.float32
BF16 = mybir.dt.bfloat16
AF = mybir.ActivationFunctionType
ALU = mybir.AluOpType
AX = mybir.AxisListType

DEPTH = 4
D = 1024
T = 1024
NT = 8
DFF = 2816
NFC = 22
PROJ = 2608
EPS = 1e-6
NEG = -30000.0


class Op:
    __slots__ = ("eng", "fn", "deps", "sig", "sigval", "dma", "dsem", "dval", "name")

    def __init__(self, eng, fn, dma, name):
        self.eng = eng
        self.fn = fn
        self.dma = dma
        self.deps = []
        self.sig = False
        self.sigval = 0
        self.dsem = None
        self.dval = 0
        self.name = name


class Sched:
    ENGS = ("tensor", "vector", "scalar", "gpsimd", "sync")

    def __init__(self, nc):
        self.nc = nc
        self.ops = []
        self.writers = {}
        self.readers = {}
        self.prev_readers = {}
        self.dsems = {}
        self.store_ops = []

    def _add(self, op, reads, writes, partial):
        reads = list(reads)
        if op.name != "barrier":
            for k in list(reads) + list(writes):
                nm = k[0] if isinstance(k, tuple) else k
                if nm.startswith("M_") or nm.startswith("F_"):
                    reads.append("ARENA")
                    break
        for r in reads:
            for w in self.writers.get(r, ()):
                op.deps.append((w, "raw"))
            self.readers.setdefault(r, []).append(op)
        for r in writes:
            rd = self.readers.get(r)
            if rd:
                for x in rd:
                    if x is not op:
                        op.deps.append((x, "war"))
                for x in self.writers.get(r, ()):
                    if x is not op:
                        op.deps.append((x, "war"))
                self.prev_readers[r] = [x for x in rd if x is not op] + list(self.writers.get(r, ()))
                self.readers[r] = []
                self.writers[r] = [op]
            else:
                if partial:
                    for x in self.prev_readers.get(r, ()):
                        op.deps.append((x, "war"))
                    self.writers.setdefault(r, []).append(op)
                else:
                    for x in self.writers.get(r, ()):
                        op.deps.append((x, "war"))
                    self.prev_readers[r] = list(self.writers.get(r, ()))
                    self.writers[r] = [op]
        self.ops.append(op)
        return op

    def op(self, eng, fn, reads=(), writes=(), partial=False, name=""):
        return self._add(Op(eng, fn, False, name), reads, writes, partial)

    def dma(self, eng, fn, semkey, reads=(), writes=(), partial=False, store=False, name=""):
        o = Op(eng, fn, True, name)
        ent = self.dsems.setdefault(semkey, [None, 0])
        ent[1] += 16
        o.dsem = semkey
        o.dval = ent[1]
        if store:
            self.store_ops.append(o)
        return self._add(o, reads, writes, partial)

    def barrier(self, eng, fn):
        return self._add(Op(eng, fn, False, "barrier"), (), ["ARENA"], False)

    def emit(self):
        nc = self.nc
        per_eng = {e: [] for e in self.ENGS}
        for o in self.ops:
            per_eng[o.eng].append(o)
        for o in self.ops:
            for (p, kind) in o.deps:
                if p.dma:
                    continue
                if p.eng == o.eng and not o.dma and (p.eng == "tensor" or kind != "raw"):
                    continue
                p.sig = True
        for e in self.ENGS:
            c = 0
            for o in per_eng[e]:
                if o.sig:
                    c += 1
                    o.sigval = c
        esem = {e: nc.alloc_semaphore(name="es_" + e) for e in self.ENGS}
        for i, (k, ent) in enumerate(self.dsems.items()):
            ent[0] = nc.alloc_semaphore(name="ds_%d" % i)
        stats = {e: [0, 0] for e in self.ENGS}

        def emit_engine(ename, eng):
            seen = {}
            for o in per_eng[ename]:
                need = {}
                for (p, kind) in o.deps:
                    if p.dma:
                        key = ("d", p.dsem)
                        sem = self.dsems[p.dsem][0]
                        val = p.dval
                    else:
                        if p.eng == ename and not o.dma and (ename == "tensor" or kind != "raw"):
                            continue
                        key = ("e", p.eng)
                        sem = esem[p.eng]
                        val = p.sigval
                    if seen.get(key, 0) >= val:
                        continue
                    if key not in need or need[key][1] < val:
                        need[key] = (sem, val)
                for key, (sem, val) in need.items():
                    eng.wait_ge(sem, val)
                    seen[key] = val
                    stats[ename][1] += 1
                ins = o.fn(eng)
                stats[ename][0] += 1
                if o.dma:
                    ins.then_inc(self.dsems[o.dsem][0], 16)
                elif o.sig:
                    ins.then_inc(esem[ename], 1)
            if ename == "sync":
                fin = {}
                for o in self.store_ops:
                    fin[o.dsem] = max(fin.get(o.dsem, 0), o.dval)
                for k, v in fin.items():
                    if seen.get(("d", k), 0) < v:
                        eng.wait_ge(self.dsems[k][0], v)

        with nc.Block() as block:
            @block.tensor
            def _(eng):
                emit_engine("tensor", eng)

            @block.vector
            def _(eng):
                emit_engine("vector", eng)

            @block.scalar
            def _(eng):
                emit_engine("scalar", eng)

            @block.gpsimd
            def _(eng):
                emit_engine("gpsimd", eng)

            @block.sync
            def _(eng):
                emit_engine("sync", eng)
        self.stats = stats
        return stats


W_IN_PIECES = [(0, 512), (512, 768), (768, 1280), (1280, 1568), (1568, 2080), (2080, 2336), (2336, 2608)]


class Builder:
    def __init__(self, depth=DEPTH, debug=()):
        self.depth = depth
        self.debug = set(debug)
        nc = bass.Bass("TRN2", target_bir_lowering=False)
        self.nc = nc
        self.S = Sched(nc)
        self.ring_i = 0
        self.ps_i = 0
        self.dbg_out = {}
        self.declare_io()
        self.alloc()

    def din(self, name, shape, dt=F32):
        return self.nc.dram_tensor(name, list(shape), dt, kind="ExternalInput").ap()

    def dout(self, name, shape, dt=F32):
        return self.nc.dram_tensor(name, list(shape), dt, kind="ExternalOutput").ap()

    def declare_io(self):
        L = self.depth
        self.x_in = self.din("x", [T, D])
        self.condT = self.din("condT", [128, 8])
        self.ident = self.din("ident", [128, 128])
        self.ctx_k = self.din("ctx_k", [L, 512, 128])
        self.ctx_v = self.din("ctx_v", [L, 512, 128])
        self.s0_gla = self.din("s0_gla", [L, 2, 128, 64])
        self.s0_delta = self.din("s0_delta", [L, 2, 256, 64])
        self.w_mod = self.din("w_mod", [L, D, 6 * D])
        self.bmodT = self.din("bmodT", [L, 128, 48])
        self.bmod = self.din("bmod", [L, 6 * D])
        self.ngT = self.din("ngT", [L, 128, 4, 8])
        self.ng = self.din("ng", [L, 4, D])
        self.w_in = self.din("w_in", [L, D, PROJ])
        self.qkg = self.din("qkg", [L, 640])
        self.wgg = self.din("wgg", [L, 33, 256])
        self.gla_norm = self.din("gla_norm", [L, 64])
        self.cw = self.din("cw", [L, 128, 6, 5])
        self.alog = self.din("alog", [L, 8])
        self.dtb = self.din("dtb", [L, 8])
        self.delta_norm = self.din("delta_norm", [L, 64])
        self.w_out = self.din("w_out", [L, D, D])
        self.w_gate = self.din("w_gate", [L, D, DFF])
        self.w_up = self.din("w_up", [L, D, DFF])
        self.w_down = self.din("w_down", [L, DFF, D])
        self.rope = self.din("rope", [T, 2, 640])
        self.abias = self.din("abias", [128, 12 * NT])
        self.keep = self.din("keep", [128, 2 * NT])
        self.cflag = self.din("cflag", [128, 1])
        self.hmask = self.din("hmask", [128, 4])
        self.sel8 = self.din("sel8", [128, 16])
        self.masks = self.din("masks", [128, 4, 128])
        self.cmasks = self.din("cmasks", [128, 7, 128])
        self.y = self.dout("y", [T, D])
        self.kout = self.dout("kout", [L, T, 128])
        self.vout = self.dout("vout", [L, T, 128])
        self.sg_out = self.dout("sg_out", [L, 2, 4, 128, 64])
        self.sd_out = self.dout("sd_out", [L, 2, 4, 256, 64])

    def dbg(self, name, shape, dt=F32):
        t = self.dout("dbg_" + name, shape, dt)
        self.dbg_out[name] = t
        return t

    def sb(self, name, shape, dt=F32):
        return self.nc.alloc_sbuf_tensor(name, list(shape), dt)

    def alloc(self):
        nc = self.nc
        self.xs = self.sb("xs", [128, NT, D])
        self.actT = self.sb("actT", [128, 8, T], BF16)
        self.idf = self.sb("idf", [128, 128])
        self.idb = self.sb("idb", [128, 128], BF16)
        self.ones_f = self.sb("ones_f", [128, 128])
        self.condT_s = self.sb("condT_s", [128, 8])
        self.scond = self.sb("scond", [128, 8], BF16)
        self.screp = self.sb("screp", [128, 8, 128], BF16)
        self.RING = 4
        self.ring = [self.sb("ring%d" % i, [128, 8 * 512], BF16) for i in range(self.RING)]
        self.GGm = self.sb("GGm", [128, D])
        self.GGf = self.sb("GGf", [128, D])
        self.ngrep = self.sb("ngrep", [128, D])
        self.brep = self.sb("brep", [128, D])
        self.bmodT_s = self.sb("bmodT_s", [128, 48])
        self.ngT_s = self.sb("ngT_s", [128, 4, 8])
        self.modT = self.sb("modT", [128, 48])
        self.AB = self.sb("AB", [128, 4, 8])
        self.ssq = self.sb("ssq", [128, NT])
        self.ssq2 = self.sb("ssq2", [128, 4])
        self.rstd = self.sb("rstd", [128, NT])
        self.xn = [self.sb("xn%d" % i, [128, D], BF16) for i in range(2)]
        self.tmpf = [self.sb("tmpf%d" % i, [128, D]) for i in range(2)]
        self.junk = self.tmpf[1]
        self.abias_s = self.sb("abias_s", [128, 12 * NT])
        self.keep_s = self.sb("keep_s", [128, 2 * NT])
        self.cflag_s = self.sb("cflag_s", [128, 1])
        self.hmask_s = self.sb("hmask_s", [128, 4])
        self.sel8_s = self.sb("sel8_s", [128, 16])
        self.masks_s = self.sb("masks_s", [128, 4, 128])
        self.bar_s = self.sb("bar_s", [128, 1])
        self.trif = self.sb("trif", [128, 5, 128])
        self.mask4 = self.sb("mask4", [128, 2, 512])
        self.ps = [nc.alloc_psum_tensor("ps%d" % i, [128, 512], F32) for i in range(7)]
        self.psT = nc.alloc_psum_tensor("psT", [128, 8, 128], BF16)
        self.cat = self.sb("cat", [128, NT, D], BF16)
        self.ARENA_W = 16896
        self.arena = self.sb("arena", [128, self.ARENA_W])
        self.ar_off = 0

    def ar_reset(self):
        self.S.barrier("gpsimd", lambda e: e.memset(self.bar_s[:], 0.0))
        self.ar_off = 0

    def ar(self, shape, dt=F32):
        n = int(np.prod(shape))
        words = n if dt == F32 else (n + 1) // 2
        words = (words + 31) // 32 * 32
        assert self.ar_off + words <= self.ARENA_W, ("arena overflow", self.ar_off, words)
        v = self.arena[:, self.ar_off:self.ar_off + words]
        self.ar_off += words
        if dt != F32:
            v = v.bitcast(dt)[:, 0:n]
        else:
            v = v[:, 0:n]
        if len(shape) == 2:
            v = v.rearrange("p (a b) -> p a b", a=shape[0])
        elif len(shape) == 3:
            v = v.rearrange("p (a b c) -> p a b c", a=shape[0], b=shape[1])
        return v

    def next_ring(self):
        i = self.ring_i % self.RING
        self.ring_i += 1
        return i

    def load_w_piece(self, w_l, c0, c1, r0=0, nk=8):
        i = self.next_ring()
        n = c1 - c0
        dst = self.ring[i][:, 0:nk * n].rearrange("p (k n) -> p k n", k=nk)
        src = w_l[r0:r0 + nk * 128, c0:c1].rearrange("(k p) n -> p k n", p=128)
        self.S.dma("gpsimd", lambda e: e.dma_start(out=dst, in_=src), "ring%d" % i, writes=["ring%d" % i])
        return i, dst

    def load(self, dst_ap, src_ap, key, eng="sync", partial=False):
        self.S.dma(eng, lambda e: e.dma_start(out=dst_ap, in_=src_ap), ("ld", key), writes=[key], partial=partial)

    def store(self, dst_ap, src_ap, key):
        self.S.dma("sync", lambda e: e.dma_start(out=dst_ap, in_=src_ap), ("st", key), reads=[key], store=True)

    def mm(self, out, lhsT, rhs, start, stop, reads, wkey, partial=None):
        if partial is None:
            partial = not start
        self.S.op("tensor", lambda e: e.matmul(out, lhsT=lhsT, rhs=rhs, start=start, stop=stop),
                  reads=reads, writes=[wkey], partial=partial)

    def tr(self, out, in_, ident, reads, wkey, partial):
        self.S.op("tensor", lambda e: e.transpose(out=out, in_=in_, identity=ident),
                  reads=reads + ["ident"], writes=[wkey], partial=partial)

    def V(self, fn, reads, writes, partial=False):
        self.S.op("vector", fn, reads=reads, writes=writes, partial=partial)

    def A(self, fn, reads, writes, partial=False):
        self.S.op("scalar", fn, reads=reads, writes=writes, partial=partial)

    def G(self, fn, reads, writes, partial=False):
        self.S.op("gpsimd", fn, reads=reads, writes=writes, partial=partial)

    def act(self, out, in_, func, reads, writes, bias=0.0, scale=1.0, accum_out=None, partial=False):
        assert not (func == AF.Copy and not (isinstance(scale, float) and scale == 1.0)), "scaled ACT copy faults on HW"
        if accum_out is None:
            self.A(lambda e: e.activation(out=out, in_=in_, func=func, bias=bias, scale=scale), reads, writes, partial)
        else:
            self.A(lambda e: e.activation(out=out, in_=in_, func=func, bias=bias, scale=scale, accum_out=accum_out),
                   reads, writes, partial)

    def setup(self):
        S = self.S
        for tt in range(NT):
            self.load(self.xs[:, tt, :], self.x_in[tt * 128:(tt + 1) * 128, :], ("xs", tt))
        self.load(self.idf[:], self.ident, "idf")
        self.load(self.condT_s[:], self.condT, "condT_s")
        self.load(self.abias_s[:], self.abias, "abias_s")
        self.load(self.keep_s[:], self.keep, "keep_s")
        self.load(self.cflag_s[:], self.cflag, "cflag_s")
        self.load(self.hmask_s[:], self.hmask, "hmask_s")
        self.load(self.sel8_s[:], self.sel8, "sel8_s")
        self.load(self.masks_s[:], self.masks, "masks_s")
        self.V(lambda e: e.tensor_copy(out=self.idb[:], in_=self.idf[:]), ["idf"], ["ident"])
        self.V(lambda e: e.memset(self.ones_f[:], 1.0), [], ["ones_f"])
        for i, mi in enumerate((0, 1, 3, 2)):
            self.V(lambda e, i=i, mi=mi: e.tensor_scalar(out=self.trif[:, i, :], in0=self.masks_s[:, mi, :], scalar1=-1.0 / 16,
                                                        scalar2=None, op0=ALU.mult), ["masks_s"], ["trif"], partial=(i > 0))
        self.V(lambda e: e.memset(self.trif[:, 4, :], -1.0 / 16), [], ["trif"], partial=True)
        for d in range(2):
            for h in range(4):
                self.V(lambda e, d=d, h=h: e.tensor_copy(out=self.mask4[:, d, h * 128:(h + 1) * 128], in_=self.masks_s[:, d, :]),
                       ["masks_s"], ["mask4"], partial=not (d == 0 and h == 0))
        for tt in range(NT):
            self.G(lambda e, tt=tt: e.memset(self.cat[:, tt, 512:768], 0.0), [], [("cat", tt, "b")])
            self.G(lambda e, tt=tt: e.memset(self.cat[:, tt, 768:1024], 0.0), [], [("cat", tt, "c")])
        self.act(self.scond[:], self.condT_s[:], AF.Silu, ["condT_s"], ["scond"])
        for kc in range(8):
            self.V(lambda e, kc=kc: e.tensor_copy(out=self.screp[:, kc, :],
                                                  in_=self.scond[:, kc:kc + 1].to_broadcast([128, 128])),
                   ["scond"], ["screp"], partial=(kc > 0))

    def mod_stage(self, l):
        S = self.S
        self.load(self.bmodT_s[:], self.bmodT[l], "bmodT_s")
        self.load(self.ngT_s[:], self.ngT[l], "ngT_s")
        pm = self.ps[6]
        for p in range(12):
            j = p // 2
            ri, wt = self.load_w_piece(self.w_mod[l], p * 512, (p + 1) * 512)
            rk = "ring%d" % ri
            if j in (2, 5):
                pb = self.ps[p % 2]
                pk = "ps%d" % (p % 2)
                GG = self.GGm if j == 2 else self.GGf
                gi = 0 if j == 2 else 1
                half = p % 2
                if half == 0:
                    self.load(self.ngrep[:], self.ng[l, 1 + 2 * gi:2 + 2 * gi, :].partition_broadcast(128), "ngrep")
                    self.load(self.brep[:], self.bmod[l:l + 1, j * D:(j + 1) * D].partition_broadcast(128), "brep")
                for kc in range(8):
                    self.mm(pb[:], self.screp[:, kc, :], wt[:, kc, :], kc == 0, kc == 7, [rk, "screp"], pk)
                hs = slice(half * 512, (half + 1) * 512)
                self.V(lambda e, GG=GG, hs=hs, pb=pb: e.tensor_tensor(
                    out=GG[:, hs], in0=pb[:], in1=self.brep[:, hs], op=ALU.add), [pk, "brep"], [("GG", gi, half)])
                self.G(lambda e, GG=GG, hs=hs: e.tensor_tensor(
                    out=GG[:, hs], in0=GG[:, hs], in1=self.ngrep[:, hs], op=ALU.mult),
                    [("GG", gi, half), "ngrep"], [("GG", gi, half)])
            else:
                for sub in range(4):
                    c = p * 4 + sub
                    for kc in range(8):
                        self.mm(pm[:, c:c + 1], wt[:, kc, sub * 128:(sub + 1) * 128], self.scond[:, kc:kc + 1],
                                kc == 0, kc == 7, [rk, "scond"], "ps6", partial=not (p == 0 and sub == 0 and kc == 0))
        for (a, b) in ((0, 16), (24, 40)):
            self.V(lambda e, a=a, b=b: e.tensor_tensor(out=self.modT[:, a:b], in0=pm[:, a:b], in1=self.bmodT_s[:, a:b],
                                                       op=ALU.add), ["ps6", "bmodT_s"], ["modT"], partial=(a > 0))
        for which, (jsh, jsc, gi) in enumerate(((0, 1, 0), (3, 4, 2))):
            self.V(lambda e, which=which, jsc=jsc, gi=gi: e.scalar_tensor_tensor(
                out=self.AB[:, 2 * which, :], in0=self.modT[:, jsc * 8:(jsc + 1) * 8], scalar=1.0,
                in1=self.ngT_s[:, gi, :], op0=ALU.add, op1=ALU.mult), ["modT", "ngT_s"], [("AB", 2 * which)])
            self.V(lambda e, which=which, jsh=jsh: e.tensor_copy(
                out=self.AB[:, 2 * which + 1, :], in_=self.modT[:, jsh * 8:(jsh + 1) * 8]), ["modT"],
                [("AB", 2 * which + 1)])

    def norm_to_actT(self, which):
        for tt in range(NT):
            self.act(self.junk[:], self.xs[:, tt, :], AF.Square, [("xs", tt)], ["junk", ("ssq", tt)],
                     accum_out=self.ssq[:, tt:tt + 1])
        self.act(self.rstd[:], self.ssq[:], AF.Ln, [("ssq", t) for t in range(NT)], ["rstd"], bias=EPS, scale=1.0 / D)
        self.act(self.rstd[:], self.rstd[:], AF.Exp, ["rstd"], ["rstd"], scale=-0.5)
        for tt in range(NT):
            xn = self.xn[tt % 2]
            xk = "xn%d" % (tt % 2)
            self.V(lambda e, tt=tt, xn=xn: e.tensor_scalar(out=xn[:], in0=self.xs[:, tt, :],
                                                           scalar1=self.rstd[:, tt:tt + 1], scalar2=None, op0=ALU.mult),
                   [("xs", tt), "rstd"], [xk])
            self.transpose_tile_to_actT(xn, xk, tt, A=self.AB[:, 2 * which, :], B=self.AB[:, 2 * which + 1, :],
                                        abkeys=[("AB", 2 * which), ("AB", 2 * which + 1)])

    def transpose_tile_to_actT(self, src, srckey, tt, A=None, B=None, abkeys=()):
        for kc in range(8):
            self.tr(self.psT[:, kc, :], src[:, kc * 128:(kc + 1) * 128], self.idb[:],
                    list(srckey) if isinstance(srckey, list) else [srckey], "psT", partial=(kc > 0))
        dst = self.actT[:, :, tt * 128:(tt + 1) * 128]
        if A is None:
            self.V(lambda e: e.tensor_copy(out=dst, in_=self.psT[:]), ["psT"], [("actT", tt)])
        else:
            tmp = self.tmpf[tt % 2]
            tk = "tmpf%d" % (tt % 2)
            tv = tmp[:].rearrange("p (k n) -> p k n", k=8)
            self.V(lambda e: e.tensor_tensor(out=tv, in0=self.psT[:], in1=A.unsqueeze(2).to_broadcast([128, 8, 128]),
                                             op=ALU.mult), ["psT"] + list(abkeys), [tk])
            self.G(lambda e: e.tensor_tensor(out=dst, in0=tv, in1=B.unsqueeze(2).to_broadcast([128, 8, 128]),
                                             op=ALU.add), [tk] + list(abkeys), [("actT", tt)])


    def attention(self, l):
        S = self.S
        self.ar_reset()
        stage = [self.ar([768]) for _ in range(2)]
        qkn = [self.ar([640]) for _ in range(2)]
        t1 = self.ar([640])
        t2 = self.ar([640])
        qkr = [self.ar([640], BF16) for _ in range(2)]
        qkgrep = self.ar([640])
        ropet = [self.ar([2, 640]) for _ in range(2)]
        qT = self.ar([NT, 512], BF16)
        kT = self.ar([512 + T], BF16)
        vA = self.ar([12, 2, 80], BF16)
        ctxk = self.ar([4, 128])
        ctxv = self.ar([4, 128])
        ctxkb = self.ar([4, 128], BF16)
        pTs = [self.ar([512], BF16) for _ in range(3)]
        st10 = self.ar([16])
        rs10 = self.ar([16])
        rec = [self.ar([4]) for _ in range(2)]

        self.load(qkgrep, self.qkg[l:l + 1, :].partition_broadcast(128), "M_qkgrep")
        self.load(ctxk, self.ctx_k[l].rearrange("(c p) n -> p c n", p=128), "M_ctxk")
        self.load(ctxv, self.ctx_v[l].rearrange("(c p) n -> p c n", p=128), "M_ctxv")
        self.V(lambda e: e.memset(vA[:, :, :, 64:80], 1.0), [], ["M_vA1"])
        self.V(lambda e: e.tensor_copy(out=ctxkb, in_=ctxk), ["M_ctxk"], ["M_ctxkb"])
        self.V(lambda e: e.tensor_copy(out=vA[:, 0:4, :, 0:64], in_=ctxv.rearrange("p c (g d) -> p c g d", g=2)),
               ["M_ctxv"], ["M_vA_ctx"])
        for c in range(4):
            self.tr(self.psT[:, c, :], ctxkb[:, c, :], self.idb[:], ["M_ctxkb"], "psT", partial=(c > 0))
        self.V(lambda e: e.tensor_copy(out=kT[:, 0:512].rearrange("p (c n) -> p c n", c=4), in_=self.psT[:, 0:4, :]),
               ["psT"], ["M_kT_ctx"])

        if "stopA1" in self.debug:
            return
        r0, w0 = self.load_w_piece(self.w_in[l], 0, 512)
        r1, w1 = self.load_w_piece(self.w_in[l], 512, 768)
        def att_proj_mm(tt):
            b = tt % 2
            pa, pb = self.ps[0 + b], self.ps[2 + b]
            pak, pbk = "ps%d" % b, "ps%d" % (2 + b)
            ts = slice(tt * 128, (tt + 1) * 128)
            for kc in range(8):
                self.mm(pa[:], self.actT[:, kc, ts], w0[:, kc, :], kc == 0, kc == 7, [("actT", tt), "ring%d" % r0], pak)
            for kc in range(8):
                self.mm(pb[:, 0:256], self.actT[:, kc, ts], w1[:, kc, :], kc == 0, kc == 7,
                        [("actT", tt), "ring%d" % r1], pbk)

        att_proj_mm(0)
        for tt in range(NT):
            b = tt % 2
            pa, pb = self.ps[0 + b], self.ps[2 + b]
            pak, pbk = "ps%d" % b, "ps%d" % (2 + b)
            ts = slice(tt * 128, (tt + 1) * 128)
            stg, sk = stage[b], "M_stage%d" % b
            self.act(stg[:, 0:512], pa[:], AF.Copy, [pak], [sk])
            self.V(lambda e, stg=stg, pb=pb: e.tensor_copy(out=stg[:, 512:768], in_=pb[:, 0:256]), [pbk], [sk], partial=True)
            qn, qnk = qkn[b], "M_qkn%d" % b
            sv = stg[:, 0:640].rearrange("p (h d) -> p h d", h=10)
            qv = qn.rearrange("p (h d) -> p h d", h=10)
            self.V(lambda e, qn=qn, stg=stg: e.tensor_tensor(out=qn, in0=stg[:, 0:640], in1=stg[:, 0:640], op=ALU.mult),
                   [sk], [qnk])
            self.V(lambda e, qv=qv: e.tensor_reduce(out=st10[:, 0:10], in_=qv, axis=AX.X, op=ALU.add), [qnk], ["M_st10"])
            self.act(rs10[:, 0:10], st10[:, 0:10], AF.Ln, ["M_st10"], ["M_rs10"], bias=EPS, scale=1.0 / 64)
            self.act(rs10[:, 0:10], rs10[:, 0:10], AF.Exp, ["M_rs10"], ["M_rs10"], scale=-0.5)
            self.V(lambda e, qv=qv, sv=sv: e.tensor_tensor(out=qv, in0=sv, in1=rs10[:, 0:10].unsqueeze(2).to_broadcast([128, 10, 64]),
                                                          op=ALU.mult), [sk, "M_rs10"], [qnk])
            self.V(lambda e, qn=qn: e.tensor_tensor(out=qn, in0=qn, in1=qkgrep, op=ALU.mult), [qnk, "M_qkgrep"], [qnk])
            self.store(self.kout[l, ts, :], qn[:, 512:640], qnk)
            self.store(self.vout[l, ts, :], stg[:, 640:768], sk)
            self.V(lambda e, stg=stg, tt=tt: e.tensor_copy(out=vA[:, 4 + tt, :, 0:64],
                                                          in_=stg[:, 640:768].rearrange("p (g d) -> p g d", g=2)),
                   [sk], [("M_vA", tt)])
            rp, rpk = ropet[b], "M_rope%d" % b
            self.load(rp, self.rope[ts, :, :], rpk)
            self.V(lambda e, qn=qn, rp=rp: e.tensor_tensor(out=t1, in0=qn, in1=rp[:, 0, :], op=ALU.mult), [qnk, rpk], ["M_t1"])
            q3 = qn.rearrange("p (h two s) -> p h two s", h=20, two=2)
            t23 = t2.rearrange("p (h two s) -> p h two s", h=20, two=2)
            sn3 = rp[:, 1, :].rearrange("p (h two s) -> p h two s", h=20, two=2)
            for two in range(2):
                self.V(lambda e, two=two, q3=q3, t23=t23, sn3=sn3: e.tensor_tensor(
                    out=t23[:, :, two, :], in0=q3[:, :, 1 - two, :], in1=sn3[:, :, two, :], op=ALU.mult),
                    [qnk, rpk], ["M_t2"], partial=(two > 0))
            qr, qrk = qkr[b], "M_qkr%d" % b
            for g in range(2):
                self.V(lambda e, qr=qr, g=g: e.tensor_tensor(
                    out=qr[:, 0:512].rearrange("p (j c) -> p j c", j=4)[:, :, g * 64:(g + 1) * 64],
                    in0=t1[:, g * 256:(g + 1) * 256].rearrange("p (j d) -> p j d", j=4),
                    in1=t2[:, g * 256:(g + 1) * 256].rearrange("p (j d) -> p j d", j=4), op=ALU.add),
                    ["M_t1", "M_t2"], [qrk], partial=(g > 0))
            self.V(lambda e, qr=qr: e.tensor_tensor(out=qr[:, 512:640], in0=t1[:, 512:640], in1=t2[:, 512:640], op=ALU.add),
                   ["M_t1", "M_t2"], [qrk], partial=True)
            if tt + 1 < NT:
                att_proj_mm(tt + 1)
            for j in range(4):
                self.tr(self.psT[:, j, :], qr[:, j * 128:(j + 1) * 128], self.idb[:], [qrk], "psT", partial=(j > 0))
            self.tr(self.psT[:, 4, :], qr[:, 512:640], self.idb[:], [qrk], "psT", partial=True)
            if "noq" not in self.debug:
                self.V(lambda e, tt=tt: e.tensor_copy(out=qT[:, tt, :], in_=self.psT[:, 0:4, :].rearrange("p j n -> p (j n)")),
                       ["psT"], [("M_qT", tt)])
            self.V(lambda e, tt=tt: e.tensor_copy(out=kT[:, 512 + tt * 128:512 + (tt + 1) * 128], in_=self.psT[:, 4, :]),
                   ["psT"], [("M_kT", tt)])

        if "qk" in self.debug and l == 0:
            d = self.dbg("qT", [128, NT, 512], BF16)
            self.store(d, qT, ("M_qT", 0))
            for tt in range(1, NT):
                self.S.ops[-1].deps += [(w, "raw") for w in self.S.writers[("M_qT", tt)]]
            d = self.dbg("kT", [128, 512 + T], BF16)
            self.store(d, kT, ("M_kT", 0))
            for tt in range(1, NT):
                self.S.ops[-1].deps += [(w, "raw") for w in self.S.writers[("M_kT", tt)]]
            self.S.ops[-1].deps += [(w, "raw") for w in self.S.writers["M_kT_ctx"]]

        if "stop_att_proj" in self.debug:
            return
        allk = ["M_kT_ctx", "M_vA_ctx", "M_vA1"] + [("M_kT", t) for t in range(NT)] + [("M_vA", t) for t in range(NT)]
        iters = [(qb, g, kc) for qb in range(NT) for g in range(2) for kc in range(12)]

        def emit_score(i):
            qb, g, kc = iters[i]
            x = i % 3
            psc, psk = self.ps[x], "ps%d" % x
            self.mm(psc[:], kT[g * 64:(g + 1) * 64, kc * 128:(kc + 1) * 128], qT[g * 64:(g + 1) * 64, qb, :], True, True,
                    allk + [("M_qT", qb)], psk, partial=False)

        emit_score(0)
        for i, (qb, g, kc) in enumerate(iters):
            grp = i // 12
            po = self.ps[4 + (grp % 2)]
            pok = "ps%d" % (4 + (grp % 2))
            rc = rec[grp % 2]
            rck = "M_rec%d" % (grp % 2)
            pov = po[:, 0:512].rearrange("p (j c) -> p j c", j=4)
            x = i % 3
            psc, psk = self.ps[x], "ps%d" % x
            pt, ptk = pTs[x], "M_pT%d" % x
            self.act(pt, psc[:], AF.Exp, [psk, "abias_s"], [ptk], scale=0.125,
                     bias=self.abias_s[:, kc * NT + qb:kc * NT + qb + 1])
            if i + 1 < len(iters):
                emit_score(i + 1)
            for j in range(4):
                self.mm(pov[:, j, 0:65], pt[:, j * 128:(j + 1) * 128], vA[:, kc, g, 0:65], kc == 0 and j == 0, kc == 11 and j == 3,
                        [ptk] + allk, pok, partial=not (kc == 0 and j == 0))
            if kc == 11:
                self.V(lambda e, rc=rc, pov=pov: e.reciprocal(out=rc, in_=pov[:, :, 64]), [pok], [rck])
                dst = self.cat[:, qb, g * 256:(g + 1) * 256].rearrange("p (j d) -> p j d", j=4)
                self.V(lambda e, rc=rc, pov=pov, dst=dst: e.tensor_tensor(
                    out=dst, in0=pov[:, :, 0:64], in1=rc.unsqueeze(2).to_broadcast([128, 4, 64]), op=ALU.mult),
                    [pok, rck], [("cat", qb, "a%d" % g)])

    def gla(self, l):
        self.ar_reset()
        stB = [self.ar([800]) for _ in range(2)]
        gcT = self.ar([T])
        wgg_s = self.ar([256])
        gnrep = self.ar([256])
        sp = [self.ar([256]) for _ in range(2)]
        E = [self.ar([3, 256]) for _ in range(2)]
        qkt = [self.ar([6, 128], BF16) for _ in range(2)]
        khat = self.ar([NT, 256], BF16)
        qtT = self.ar([2, T], BF16)
        ktT = self.ar([2, T], BF16)
        vb_s = self.ar([NT, 256], BF16)
        srb = self.ar([NT, 256], BF16)
        dl = self.ar([NT, 2])
        ob = self.ar([NT, 256])
        attm = [self.ar([512], BF16) for _ in range(2)]
        qmsk = [self.ar([512], BF16) for _ in range(2)]
        Snew = [self.ar([64]) for _ in range(2)]
        Scur = self.ar([64])
        Sbf = self.ar([64], BF16)
        st4 = self.ar([8])
        sq = self.ar([256])

        self.load(wgg_s[0:33, :], self.wgg[l], "M_wgg")
        self.load(gnrep[:, 0:64], self.gla_norm[l:l + 1, :].partition_broadcast(128), "M_gn0")
        for h in range(1, 4):
            self.V(lambda e, h=h: e.tensor_copy(out=gnrep[:, h * 64:(h + 1) * 64], in_=gnrep[:, 0:64]), ["M_gn0"], [("M_gn", h)])
        self.V(lambda e: e.memset(gcT[32:64, :], 1.0), [], ["M_gcT1"])

        r0, w0 = self.load_w_piece(self.w_in[l], 768, 1280)
        r1, w1 = self.load_w_piece(self.w_in[l], 1280, 1568)
        for half in range(2):
            hs = slice(half * 512, (half + 1) * 512)
            pg = self.ps[4 + half]
            pgk = "ps%d" % (4 + half)
            for kc in range(8):
                self.mm(pg[0:32, :], w1[:, kc, 256:288], self.actT[:, kc, hs], kc == 0, kc == 7,
                        [("actT", t) for t in range(half * 4, half * 4 + 4)] + ["ring%d" % r1], pgk)
            self.V(lambda e, pg=pg, hs=hs: e.tensor_copy(out=gcT[0:32, hs], in_=pg[0:32, :]), [pgk], [("M_gcT", half)])

        if "stopB0" in self.debug:
            return
        def gla_proj_mm(tt):
            b = tt % 2
            ts = slice(tt * 128, (tt + 1) * 128)
            pa, pb = self.ps[0 + b], self.ps[2 + b]
            pak, pbk = "ps%d" % b, "ps%d" % (2 + b)
            for kc in range(8):
                self.mm(pa[:], self.actT[:, kc, ts], w0[:, kc, :], kc == 0, kc == 7, [("actT", tt), "ring%d" % r0], pak)
            for kc in range(8):
                self.mm(pb[:, 0:256], self.actT[:, kc, ts], w1[:, kc, 0:256], kc == 0, kc == 7, [("actT", tt), "ring%d" % r1], pbk)

        gla_proj_mm(0)
        for tt in range(NT):
            b = tt % 2
            ts = slice(tt * 128, (tt + 1) * 128)
            pa, pb = self.ps[0 + b], self.ps[2 + b]
            pak, pbk = "ps%d" % b, "ps%d" % (2 + b)
            stg, sk = stB[b], "M_stB%d" % b
            self.V(lambda e, stg=stg, pa=pa: e.tensor_scalar(out=stg[:, 0:128], in0=pa[:, 0:128], scalar1=32.0 ** -0.5, scalar2=None,
                                                             op0=ALU.mult), [pak], [sk])
            self.V(lambda e, stg=stg, pa=pa: e.tensor_copy(out=stg[:, 128:256], in_=pa[:, 128:256]), [pak], [sk], partial=True)
            self.V(lambda e, pa=pa, tt=tt: e.tensor_copy(out=vb_s[:, tt, :], in_=pa[:, 256:512]), [pak], [("M_vb", tt)])
            self.act(srb[:, tt, :], pb[:, 0:256], AF.Silu, [pbk], [("M_srb", tt)])
            if tt + 1 < NT:
                gla_proj_mm(tt + 1)
            px = self.ps[6]
            self.mm(px[:, 0:256], gcT[0:33, ts], wgg_s[0:33, :], True, True,
                    [("M_gcT", tt // 4), "M_gcT1", "M_wgg"], "ps6", partial=False)
            spt, spk = sp[b], "M_sp%d" % b
            self.act(spt, px[:, 0:256], AF.Exp, ["ps6"], [spk], scale=-1.0)
            self.act(spt, spt, AF.Ln, [spk], [spk], bias=1.0)
            pc = self.ps[4 + b]
            pck = "ps%d" % (4 + b)
            for d in range(2):
                cs = slice(d * 128, (d + 1) * 128)
                self.mm(pc[:, d * 128:(d + 1) * 128], self.trif[:, d, :], spt[:, cs], True, True,
                        [spk, "trif"], pck, partial=(d > 0))
            for d in range(2):
                cs = slice(d * 128, (d + 1) * 128)
                self.mm(pc[:, 256 + d * 128:256 + (d + 1) * 128], self.trif[:, 2 + d, :], spt[:, cs], True, True,
                        [spk, "trif"], pck, partial=True)
            Et, Ek = E[b], "M_E%d" % b
            self.act(Et[:, 0, :], pc[:, 0:256], AF.Exp, [pck], [Ek])
            self.act(Et[:, 1, :], pc[:, 0:256], AF.Exp, [pck], [Ek], scale=-1.0, partial=True)
            self.act(Et[:, 2, :], pc[:, 256:512], AF.Exp, [pck], [Ek], partial=True)
            pd = self.ps[6]
            for d in range(2):
                self.mm(pd[:, 256 + 8 * d:264 + 8 * d], spt[:, d * 128:(d + 1) * 128], self.trif[:, 4, 0:8], True, True,
                        [spk, "trif"], "ps6", partial=(d > 0))
            self.act(dl[:, tt, :], pd[:, 256:272].rearrange("p (d e) -> p d e", d=2)[:, :, 0], AF.Exp, ["ps6"], [("M_dl", tt)])
            qt, qk_ = qkt[b], "M_qkt%d" % b
            for d in range(2):
                cs = slice(d * 128, (d + 1) * 128)
                self.V(lambda e, qt=qt, stg=stg, Et=Et, d=d, cs=cs: e.tensor_tensor(
                    out=qt[:, 2 * d, :], in0=stg[:, 0:128], in1=Et[:, 0, cs], op=ALU.mult), [sk, Ek], [qk_], partial=(d > 0))
                self.V(lambda e, qt=qt, stg=stg, Et=Et, d=d, cs=cs: e.tensor_tensor(
                    out=qt[:, 2 * d + 1, :], in0=stg[:, 128:256], in1=Et[:, 1, cs], op=ALU.mult), [sk, Ek], [qk_], partial=True)
                self.G(lambda e, stg=stg, Et=Et, d=d, cs=cs, tt=tt: e.tensor_tensor(
                    out=khat[:, tt, cs], in0=stg[:, 128:256], in1=Et[:, 2, cs], op=ALU.mult), [sk, Ek], [("M_khat", tt)],
                    partial=(d > 0))
            for i in range(4):
                self.tr(self.psT[:, i, :], qt[:, i, :], self.idb[:], [qk_], "psT", partial=(i > 0))
            for d in range(2):
                self.V(lambda e, d=d, ts=ts: e.tensor_copy(out=qtT[:, d, ts], in_=self.psT[:, 2 * d, :]), ["psT"], [("M_qtT", tt)],
                       partial=(d > 0))
                self.V(lambda e, d=d, ts=ts: e.tensor_copy(out=ktT[:, d, ts], in_=self.psT[:, 2 * d + 1, :]), ["psT"],
                       [("M_ktT", tt)], partial=(d > 0))

        if "stopB1" in self.debug:
            return
        steps = []
        for d in range(2):
            order = list(range(NT)) if d == 0 else list(range(NT - 1, -1, -1))
            for n, tt in enumerate(order):
                steps.append((d, n, tt, order))

        def gla_A(i):
            d, n, tt, order = steps[i]
            ts = slice(tt * 128, (tt + 1) * 128)
            b = i % 2
            pat = self.ps[0 + b]
            patk = "ps%d" % b
            qm, qmk = qmsk[b], "M_qm%d" % b
            for h in range(4):
                self.V(lambda e, qm=qm, h=h, d=d, ts=ts: e.tensor_scalar(out=qm[:, h * 128:(h + 1) * 128], in0=qtT[:, d, ts],
                                                                         scalar1=self.hmask_s[:, h:h + 1], scalar2=None,
                                                                         op0=ALU.mult),
                       [("M_qtT", tt), "hmask_s"], [qmk], partial=(h > 0))
            for h in range(4):
                self.mm(pat[:, h * 128:(h + 1) * 128], ktT[:, d, ts], qm[:, h * 128:(h + 1) * 128], True, True,
                        [("M_ktT", tt), qmk], patk, partial=(h > 0))
            am, amk = attm[b], "M_attm%d" % b
            self.V(lambda e, am=am, pat=pat, d=d: e.tensor_tensor(out=am, in0=pat[:], in1=self.mask4[:, d, :], op=ALU.mult),
                   [patk, "mask4"], [amk])

        def gla_B(i):
            d, n, tt, order = steps[i]
            ts = slice(tt * 128, (tt + 1) * 128)
            b = i % 2
            qm, qmk = qmsk[b], "M_qm%d" % b
            am, amk = attm[b], "M_attm%d" % b
            if n == 0:
                self.load(Scur, self.s0_gla[l, d], "M_Scur")
                self.act(Sbf, Scur, AF.Copy, ["M_Scur"], ["M_Sbf"])
            po = self.ps[2 + b]
            pok = "ps%d" % (2 + b)
            for h in range(4):
                self.mm(po[:, h * 64:(h + 1) * 64], am[:, h * 128:(h + 1) * 128], vb_s[:, tt, h * 64:(h + 1) * 64],
                        True, False, [amk, ("M_vb", tt)], pok, partial=(h > 0))
                self.mm(po[:, h * 64:(h + 1) * 64], qm[:, h * 128:(h + 1) * 128], Sbf, False, True,
                        [qmk, "M_Sbf"], pok, partial=True)
            if d == 0:
                self.act(ob[:, tt, :], po[:, 0:256], AF.Copy, [pok], [("M_ob", tt)])
            else:
                self.V(lambda e, tt=tt, po=po: e.tensor_tensor(out=ob[:, tt, :], in0=ob[:, tt, :], in1=po[:, 0:256], op=ALU.add),
                       [pok, ("M_ob", tt)], [("M_ob", tt)])
            pS = self.ps[4 + b]
            pSk = "ps%d" % (4 + b)
            self.mm(pS[:, 0:256], khat[:, tt, d * 128:(d + 1) * 128], vb_s[:, tt, :], True, True,
                    [("M_khat", tt), ("M_vb", tt)], pSk, partial=False)
            sn, snk = Snew[b], "M_Snew%d" % b
            for h in range(4):
                hp = slice(32 * h, 32 * h + 32)
                self.V(lambda e, sn=sn, hp=hp, h=h, pS=pS, tt=tt, d=d: e.scalar_tensor_tensor(
                    out=sn[hp, :], in0=Scur[hp, :], scalar=dl[hp, tt, d:d + 1], in1=pS[hp, h * 64:(h + 1) * 64],
                    op0=ALU.mult, op1=ALU.add), ["M_Scur", ("M_dl", tt), pSk], [snk], partial=(h > 0))
            is_out = (tt % 2 == 1) if d == 0 else (tt % 2 == 0)
            if is_out:
                self.store(self.sg_out[l, d, tt // 2], sn, snk)
            if n < NT - 1:
                nxt = order[n + 1]
                kcol = d * NT + nxt
                self.V(lambda e, sn=sn, kcol=kcol: e.tensor_scalar(out=Scur, in0=sn, scalar1=self.keep_s[:, kcol:kcol + 1],
                                                                  scalar2=None, op0=ALU.mult),
                       [snk, "keep_s"], ["M_Scur"])
                self.act(Sbf, Scur, AF.Copy, ["M_Scur"], ["M_Sbf"])

        gla_A(0)
        for i in range(len(steps)):
            if i + 1 < len(steps):
                gla_A(i + 1)
            gla_B(i)

        for tt in range(NT):
            if "noBnorm" in self.debug:
                break
            self.V(lambda e, tt=tt: e.tensor_tensor(out=sq, in0=ob[:, tt, :], in1=ob[:, tt, :], op=ALU.mult), [("M_ob", tt)], ["M_sq"])
            self.V(lambda e: e.tensor_reduce(out=st4[:, 0:4], in_=sq.rearrange("p (h d) -> p h d", h=4), axis=AX.X, op=ALU.add),
                   ["M_sq"], ["M_st4"])
            self.act(st4[:, 4:8], st4[:, 0:4], AF.Ln, ["M_st4"], ["M_rs4"], bias=EPS, scale=1.0 / 64)
            self.act(st4[:, 4:8], st4[:, 4:8], AF.Exp, ["M_rs4"], ["M_rs4"], scale=-0.5)
            self.V(lambda e, tt=tt: e.tensor_tensor(out=sq.rearrange("p (h d) -> p h d", h=4),
                                                    in0=ob[:, tt, :].rearrange("p (h d) -> p h d", h=4),
                                                    in1=st4[:, 4:8].unsqueeze(2).to_broadcast([128, 4, 64]), op=ALU.mult),
                   [("M_ob", tt), "M_rs4"], ["M_sq"])
            self.V(lambda e: e.tensor_tensor(out=sq, in0=sq, in1=gnrep, op=ALU.mult),
                   ["M_sq", "M_gn0"] + [("M_gn", h) for h in range(1, 4)], ["M_sq"])
            self.V(lambda e, tt=tt: e.tensor_tensor(out=self.cat[:, tt, 512:768], in0=sq, in1=srb[:, tt, :], op=ALU.mult),
                   ["M_sq", ("M_srb", tt)], [("cat", tt, "b")])


    def ar_mark_reset(self, mark):
        self.S.barrier("gpsimd", lambda e: e.memset(self.bar_s[:], 0.0))
        self.ar_off = mark

    def delta(self, l):
        self.ar_reset()
        qTc = self.ar([2, T], BF16)
        kTc = self.ar([2, T], BF16)
        k_tok = self.ar([NT, 256], BF16)
        v_tok = self.ar([NT, 256], BF16)
        sgc = self.ar([NT, 256], BF16)
        oc = self.ar([NT, 256])
        beta = self.ar([NT, 8])
        gam = self.ar([NT, 8])
        ngam = self.ar([NT, 8])
        eg = self.ar([NT, 8])
        eglm = self.ar([NT, 8])
        egl = self.ar([NT, 8])
        bkg = self.ar([NT, 8])
        cw_s = self.ar([6, 5])
        negA = self.ar([8])
        dtb_s = self.ar([8])
        dnrep = self.ar([256])
        bones = self.ar([128])
        mark = self.ar_off

        self.load(cw_s, self.cw[l], "M_cw")
        self.load(negA, self.alog[l:l + 1, :].partition_broadcast(128), "M_negA")
        self.load(dtb_s, self.dtb[l:l + 1, :].partition_broadcast(128), "M_dtb")
        self.load(dnrep[:, 0:64], self.delta_norm[l:l + 1, :].partition_broadcast(128), "M_dn0")
        for h in range(1, 4):
            self.V(lambda e, h=h: e.tensor_copy(out=dnrep[:, h * 64:(h + 1) * 64], in_=dnrep[:, 0:64]), ["M_dn0"], [("M_dn", h)])
        self.act(negA, negA, AF.Exp, ["M_negA"], ["M_negA"])
        self.V(lambda e: e.tensor_scalar(out=negA, in0=negA, scalar1=-1.0, scalar2=None, op0=ALU.mult), ["M_negA"], ["M_negA"])
        self.V(lambda e: e.memset(bones, 0.0), [], ["M_bones"])
        self.V(lambda e: e.memset(bones[0:64, 0:64], 1.0), ["M_bones"], ["M_bones"])
        self.V(lambda e: e.memset(bones[64:128, 64:128], 1.0), ["M_bones"], ["M_bones"])

        if "stopC0" in self.debug:
            return
        xin = [self.ar([4, 260]) for _ in range(2)]
        ycv = [self.ar([T]) for _ in range(2)]
        ysl = self.ar([T])
        sqb = self.ar([T])
        vTc = self.ar([2, T], BF16)
        rn = self.ar([T])

        pieces = [(1568, 2080, 4), (2080, 2336, 2)]
        cc = 0
        for (c0, c1, nch) in pieces:
            ri, wt = self.load_w_piece(self.w_in[l], c0, c1)
            for sub in range(nch):
                b = cc % 2
                xi, xik = xin[b], "M_xin%d" % b
                for half in range(2):
                    hs = slice(half * 512, (half + 1) * 512)
                    pp = self.ps[half]
                    ppk = "ps%d" % half
                    for kc in range(8):
                        self.mm(pp[:], wt[:, kc, sub * 128:(sub + 1) * 128], self.actT[:, kc, hs], kc == 0, kc == 7,
                                [("actT", t) for t in range(half * 4, half * 4 + 4)] + ["ring%d" % ri], ppk)
                    self.act(xi[:, 2 * half:2 * half + 2, 2:258], pp[:].rearrange("p (s n) -> p s n", s=2), AF.Copy, [ppk], [xik],
                             partial=(half > 0))
                self.V(lambda e, xi=xi: e.memset(xi[:, 0, 0:2], 0.0), [], [xik], partial=True)
                self.V(lambda e, xi=xi: e.memset(xi[:, 3, 258:260], 0.0), [], [xik], partial=True)
                self.V(lambda e, xi=xi: e.tensor_scalar(out=xi[:, 1:4, 0:2], in0=xi[:, 0:3, 256:258], scalar1=self.cflag_s[:, 0:1],
                                                        scalar2=None, op0=ALU.mult), [xik, "cflag_s"], [xik])
                self.V(lambda e, xi=xi: e.tensor_scalar(out=xi[:, 0:3, 258:260], in0=xi[:, 1:4, 2:4], scalar1=self.cflag_s[:, 0:1],
                                                        scalar2=None, op0=ALU.mult), [xik, "cflag_s"], [xik])
                yc, yck = ycv[b], "M_ycv%d" % b
                yv = yc.rearrange("p (s n) -> p s n", s=4)
                self.V(lambda e, xi=xi, yv=yv, cc=cc: e.tensor_scalar(out=yv, in0=xi[:, :, 0:256], scalar1=cw_s[:, cc, 0:1],
                                                                      scalar2=None, op0=ALU.mult), [xik, "M_cw"], [yck])
                for j in range(1, 5):
                    eng = self.V
                    eng(lambda e, xi=xi, yv=yv, cc=cc, j=j: e.scalar_tensor_tensor(
                        out=yv, in0=xi[:, :, j:j + 256], scalar=cw_s[:, cc, j:j + 1], in1=yv, op0=ALU.mult, op1=ALU.add),
                        [xik, "M_cw", yck], [yck])
                self.act(ysl, yc, AF.Silu, [yck], ["M_ysl"])
                if cc < 4:
                    self.V(lambda e: e.tensor_tensor(out=sqb, in0=ysl, in1=ysl, op=ALU.mult), ["M_ysl"], ["M_sqb"])
                    for half in range(2):
                        hs = slice(half * 512, (half + 1) * 512)
                        pn = self.ps[2 + half]
                        pnk = "ps%d" % (2 + half)
                        self.mm(pn[:], bones, sqb[:, hs], True, True, ["M_bones", "M_sqb"], pnk, partial=False)
                        self.act(rn[:, hs], pn[:], AF.Ln, [pnk], [("M_rn", half)], bias=EPS)
                        self.act(rn[:, hs], rn[:, hs], AF.Exp, [("M_rn", half)], [("M_rn", half)], scale=-0.5)
                    dst = qTc if cc < 2 else kTc
                    dk_ = ("M_qTc", cc) if cc < 2 else ("M_kTc", cc - 2)
                    sc_ = 64.0 ** -0.5 if cc < 2 else 1.0
                    self.V(lambda e, dst=dst, cc=cc, sc_=sc_: e.scalar_tensor_tensor(
                        out=dst[:, cc % 2, :], in0=ysl, scalar=sc_, in1=rn, op0=ALU.mult, op1=ALU.mult),
                        ["M_ysl", ("M_rn", 0), ("M_rn", 1)], [dk_])
                else:
                    self.V(lambda e, cc=cc: e.tensor_copy(out=vTc[:, cc - 4, :], in_=ysl), ["M_ysl"], [("M_vTc", cc - 4)])
                cc += 1
        if "stopC1" in self.debug:
            return
        for tt in range(NT):
            ts = slice(tt * 128, (tt + 1) * 128)
            for c in range(2):
                self.tr(self.psT[:, c, :], kTc[:, c, ts], self.idb[:], [("M_kTc", c)], "psT", partial=(c > 0))
                self.tr(self.psT[:, 2 + c, :], vTc[:, c, ts], self.idb[:], [("M_vTc", c)], "psT", partial=True)
            self.V(lambda e, tt=tt: e.tensor_copy(out=k_tok[:, tt, :], in_=self.psT[:, 0:2, :].rearrange("p c n -> p (c n)")),
                   ["psT"], [("M_ktok", tt)])
            self.V(lambda e, tt=tt: e.tensor_copy(out=v_tok[:, tt, :], in_=self.psT[:, 2:4, :].rearrange("p c n -> p (c n)")),
                   ["psT"], [("M_vtok", tt)])
        if "stopC2" in self.debug:
            return
        ri, wt = self.load_w_piece(self.w_in[l], 2336, 2608)
        g8 = [self.ar([8]) for _ in range(2)]
        g16 = [self.ar([16]) for _ in range(2)]
        for tt in range(NT):
            b = tt % 2
            ts = slice(tt * 128, (tt + 1) * 128)
            pg = self.ps[4 + b]
            pgk = "ps%d" % (4 + b)
            for kc in range(8):
                self.mm(pg[:, 0:256], self.actT[:, kc, ts], wt[:, kc, 0:256], kc == 0, kc == 7, [("actT", tt), "ring%d" % ri], pgk)
            for kc in range(8):
                self.mm(pg[:, 256:272], self.actT[:, kc, ts], wt[:, kc, 256:272], kc == 0, kc == 7, [("actT", tt), "ring%d" % ri], pgk,
                        partial=True)
            self.act(sgc[:, tt, :], pg[:, 0:256], AF.Silu, [pgk], [("M_sgc", tt)])
            if "gCa" in self.debug:
                continue
            self.act(beta[:, tt, :], pg[:, 256:264], AF.Exp, [pgk], [("M_beta", tt)], scale=-1.0)
            self.V(lambda e, tt=tt: e.tensor_scalar(out=beta[:, tt, :], in0=beta[:, tt, :], scalar1=1.0, scalar2=None, op0=ALU.add),
                   [("M_beta", tt)], [("M_beta", tt)])
            self.V(lambda e, tt=tt: e.reciprocal(out=beta[:, tt, :], in_=beta[:, tt, :]), [("M_beta", tt)], [("M_beta", tt)])
            if "gCb" in self.debug:
                continue
            gt, gk = g8[b], "M_g8%d" % b
            self.V(lambda e, gt=gt, pg=pg: e.tensor_tensor(out=gt, in0=pg[:, 264:272], in1=dtb_s, op=ALU.add), [pgk, "M_dtb"], [gk])
            self.act(gt, gt, AF.Exp, [gk], [gk])
            self.act(gt, gt, AF.Ln, [gk], [gk], bias=1.0)
            self.V(lambda e, gt=gt: e.tensor_tensor(out=gt, in0=gt, in1=negA, op=ALU.mult), [gk, "M_negA"], [gk])
            if "gCc" in self.debug:
                continue
            pc = self.ps[6]
            gt2, gk2 = g16[b], "M_g16%d" % b
            for d in range(2):
                self.V(lambda e, gt=gt, gt2=gt2, d=d: e.tensor_tensor(out=gt2[:, d * 8:(d + 1) * 8], in0=gt,
                                                                      in1=self.sel8_s[:, d * 8:(d + 1) * 8], op=ALU.mult),
                       [gk, "sel8_s"], [gk2], partial=(d > 0))
            for d in range(2):
                self.mm(pc[:, 0:8], self.masks_s[:, d, :], gt2[:, d * 8:(d + 1) * 8], d == 0, d == 1, [gk2, "masks_s"], "ps6",
                        partial=(d > 0))
            self.mm(pc[:, 16:24], self.ones_f[:], gt, True, True, [gk, "ones_f"], "ps6", partial=True)
            self.V(lambda e, tt=tt: e.tensor_copy(out=gam[:, tt, :], in_=pc[:, 0:8]), ["ps6"], [("M_gam", tt)])
            if "gCd" in self.debug:
                continue
            self.V(lambda e, tt=tt: e.tensor_scalar(out=ngam[:, tt, :], in0=gam[:, tt, :], scalar1=-1.0, scalar2=None, op0=ALU.mult),
                   [("M_gam", tt)], [("M_ngam", tt)])
            self.act(eg[:, tt, :], gam[:, tt, :], AF.Exp, [("M_gam", tt)], [("M_eg", tt)])
            if "gCe" in self.debug:
                continue
            self.act(egl[:, tt, :], pc[:, 16:24], AF.Exp, ["ps6"], [("M_egl", tt)])
            self.V(lambda e, tt=tt: e.tensor_tensor(out=eglm[:, tt, :], in0=pc[:, 16:24], in1=ngam[:, tt, :], op=ALU.add),
                   ["ps6", ("M_ngam", tt)], [("M_eglm", tt)])
            self.act(eglm[:, tt, :], eglm[:, tt, :], AF.Exp, [("M_eglm", tt)], [("M_eglm", tt)])
            self.V(lambda e, tt=tt: e.tensor_tensor(out=bkg[:, tt, :], in0=beta[:, tt, :], in1=eg[:, tt, :], op=ALU.mult),
                   [("M_beta", tt), ("M_eg", tt)], [("M_bkg", tt)])

        if "stopC3" in self.debug:
            return
        self.ar_mark_reset(mark)
        bigm = self.ar([4, 128])
        cm = self.ar([7, 128])
        self.load(cm, self.cmasks, "M_cm")
        for i, (mi, sgn) in enumerate(((0, 1.0), (1, 1.0), (3, -1.0), (2, -1.0))):
            self.V(lambda e, i=i, mi=mi, sgn=sgn: e.tensor_scalar(out=bigm[:, i, :], in0=self.masks_s[:, mi, :], scalar1=-sgn * NEG,
                                                                  scalar2=None, op0=ALU.mult), ["masks_s"], [("M_bigm", i)])
        attm4 = self.ar([512], BF16)
        Mm4 = self.ar([512])
        SD = BF16
        Cm4 = [self.ar([512], SD) for _ in range(2)]
        Ym4 = self.ar([512], SD)
        Zm4 = [self.ar([512], SD) for _ in range(2)]
        Xm4 = [self.ar([512], SD) for _ in range(2)]
        dec4 = self.ar([512])
        dg4 = self.ar([512])
        decT4 = self.ar([512])
        id4 = self.ar([512], SD)
        bv4 = self.ar([256], SD)
        kbg4 = self.ar([256], SD)
        wT4 = self.ar([512], SD)
        vnew4 = self.ar([256], BF16)
        khat4 = self.ar([512], BF16)
        otmp4 = self.ar([256])
        Sf4 = self.ar([256])
        Sb4 = self.ar([256], BF16)
        Snew4 = [self.ar([256]) for _ in range(2)]
        H4 = range(4)
        for h in H4:
            self.V(lambda e, h=h: e.tensor_copy(out=id4[:, h * 128:(h + 1) * 128], in_=self.idf[:]), ["idf"], ["M_id4"], partial=(h > 0))
        idS = self.idb

        def c128(t, h):
            return t[:, h * 128:(h + 1) * 128]

        def c64(t, h):
            return t[:, h * 64:(h + 1) * 64]

        P = self.ps
        itn = 0
        for d in range(2):
            order = list(range(NT)) if d == 0 else list(range(NT - 1, -1, -1))
            for h in H4:
                self.load(Sf4[0:64, h * 64:(h + 1) * 64], self.s0_delta[l, d, h * 64:(h + 1) * 64, :], "M_Sf", partial=(h > 0))
                self.load(Sf4[64:128, h * 64:(h + 1) * 64], self.s0_delta[l, d, h * 64:(h + 1) * 64, :], "M_Sf", partial=True)
            self.act(Sb4, Sf4, AF.Copy, ["M_Sf"], ["M_Sb"])
            for n, tt in enumerate(order):
                ts = slice(tt * 128, (tt + 1) * 128)
                cols = [d * 4 + h for h in H4]
                hrs = [slice((h % 2) * 64, (h % 2) * 64 + 64) for h in H4]
                chs = [h // 2 for h in H4]
                kcs = [slice(h * 64, (h + 1) * 64) for h in H4]
                if "dS0" in self.debug:
                    continue
                def Gr(h):
                    return P[h % 2][:, (h // 2) * 256:(h // 2) * 256 + 128]

                def Ar(h):
                    return P[h % 2][:, (h // 2) * 256 + 128:(h // 2) * 256 + 256]
                for h in (0, 2, 1, 3):
                    hr, ch = hrs[h], chs[h]
                    bk = "ps%d" % (h % 2)
                    self.mm(Gr(h), kTc[hr, ch, ts], kTc[hr, ch, ts], True, True, [("M_kTc", ch)], bk, partial=(h >= 2))
                    self.mm(Ar(h), kTc[hr, ch, ts], qTc[hr, ch, ts], True, True, [("M_kTc", ch), ("M_qTc", ch)], bk, partial=True)
                if "dS1" in self.debug:
                    continue
                for h in H4:
                    col = cols[h]
                    self.V(lambda e, h=h, tt=tt, col=col: e.tensor_scalar(out=c128(dg4, h), in0=self.idf[:],
                                                                          scalar1=gam[:, tt, col:col + 1], scalar2=None, op0=ALU.mult),
                           ["idf", ("M_gam", tt)], ["M_dg"], partial=(h > 0))
                if "dS2" in self.debug:
                    continue
                for h in H4:
                    self.mm(c128(P[2], h), self.ones_f[:], c128(dg4, h), True, False, ["ones_f", "M_dg"], "ps2", partial=(h > 0))
                    self.mm(c128(P[2], h), self.idf[:], bigm[:, d, :], False, True, ["idf", ("M_bigm", d)], "ps2", partial=True)
                for h in H4:
                    self.mm(c128(P[3], h), self.ones_f[:], c128(dg4, h), True, False, ["ones_f", "M_dg"], "ps3", partial=(h > 0))
                    self.mm(c128(P[3], h), self.idf[:], bigm[:, 2 + d, :], False, True, ["idf", ("M_bigm", 2 + d)], "ps3", partial=True)
                if "dS3" in self.debug:
                    continue
                for h in H4:
                    col = cols[h]
                    self.act(c128(dec4, h), c128(P[2], h), AF.Exp, ["ps2", ("M_gam", tt)], ["M_dec"], scale=-1.0,
                             bias=gam[:, tt, col:col + 1], partial=(h > 0))
                for h in H4:
                    col = cols[h]
                    self.act(c128(decT4, h), c128(P[3], h), AF.Exp, ["ps3", ("M_ngam", tt)], ["M_decT"], scale=1.0,
                             bias=ngam[:, tt, col:col + 1], partial=(h > 0))
                for h in H4:
                    col = cols[h]
                    self.V(lambda e, h=h, tt=tt, col=col: e.scalar_tensor_tensor(
                        out=c128(Mm4, h), in0=Gr(h), scalar=beta[:, tt, col:col + 1], in1=c128(dec4, h),
                        op0=ALU.mult, op1=ALU.mult), ["ps%d" % (h % 2), ("M_beta", tt), "M_dec"], ["M_M"], partial=(h > 0))
                for h in H4:
                    self.V(lambda e, h=h: e.tensor_tensor(out=c128(attm4, h), in0=Ar(h), in1=c128(decT4, h), op=ALU.mult),
                           ["ps%d" % (h % 2), "M_decT"], ["M_attm"], partial=(h > 0))
                if "dS5" in self.debug:
                    continue
                Zc, Zk = id4, "M_id4"
                Xc, Xk = id4, "M_id4"
                for k in range(7):
                    Ck, Ckk = Cm4[k % 2], "M_C%d" % (k % 2)
                    for h in H4:
                        self.G(lambda e, h=h, k=k, Ck=Ck: e.tensor_tensor(out=c128(Ck, h), in0=c128(Mm4, h), in1=cm[:, k, :], op=ALU.mult),
                               ["M_M", "M_cm"], [Ckk], partial=(h > 0))
                    for h in H4:
                        self.mm(c128(P[2], h), c128(Ck, h), c128(Zc, h), True, True, [Ckk, Zk], "ps2", partial=(h > 0))
                    self.act(Ym4, P[2][:], AF.Copy, ["ps2"], ["M_Y"])
                    for h in H4:
                        self.mm(c128(P[3], h), c128(Xc, h), c128(Ym4, h), True, True, [Xk, "M_Y"], "ps3", partial=(h > 0))
                    Zn, Znk = Zm4[k % 2], "M_Z%d" % (k % 2)
                    self.V(lambda e, Zn=Zn, Zo=Zc: e.scalar_tensor_tensor(out=Zn, in0=P[3][:], scalar=-1.0, in1=Zo, op0=ALU.mult,
                                                                          op1=ALU.add), ["ps3", Zk], [Znk])
                    Zc, Zk = Zn, Znk
                    if k < 6:
                        for h in H4:
                            self.S.op("tensor", lambda e, h=h, Zc=Zc: e.transpose(out=self.psT[:, h, :], in_=c128(Zc, h), identity=idS[:]),
                                      reads=[Zk, "ident"], writes=["psT"], partial=(h > 0))
                        Xn, Xnk = Xm4[k % 2], "M_X%d" % (k % 2)
                        self.act(Xn, self.psT[:, 0:4, :].rearrange("p a b -> p (a b)"), AF.Copy, ["psT"], [Xnk])
                        Xc, Xk = Xn, Xnk
                if "dS6" in self.debug:
                    continue
                for h in H4:
                    col, kcols = cols[h], kcs[h]
                    self.V(lambda e, h=h, tt=tt, col=col, kcols=kcols: e.tensor_scalar(
                        out=c64(bv4, h), in0=v_tok[:, tt, kcols], scalar1=beta[:, tt, col:col + 1], scalar2=None, op0=ALU.mult),
                        [("M_vtok", tt), ("M_beta", tt)], ["M_bv"], partial=(h > 0))
                    self.V(lambda e, h=h, tt=tt, col=col, kcols=kcols: e.tensor_scalar(
                        out=c64(kbg4, h), in0=k_tok[:, tt, kcols], scalar1=bkg[:, tt, col:col + 1], scalar2=None, op0=ALU.mult),
                        [("M_ktok", tt), ("M_bkg", tt)], ["M_kbg"], partial=(h > 0))
                    for half in range(2):
                        self.G(lambda e, h=h, tt=tt, col=col, kcols=kcols, half=half: e.tensor_scalar(
                            out=khat4[:, h * 128 + half * 64:h * 128 + (half + 1) * 64], in0=k_tok[:, tt, kcols],
                            scalar1=eglm[:, tt, col:col + 1], scalar2=None, op0=ALU.mult), [("M_ktok", tt), ("M_eglm", tt)],
                            ["M_khat"], partial=not (h == 0 and half == 0))
                for h in H4:
                    self.mm(P[5][0:64, h * 128:(h + 1) * 128], c64(kbg4, h), c128(Zc, h), True, True, ["M_kbg", Zk], "ps5", partial=(h > 0))
                self.V(lambda e: e.tensor_scalar(out=wT4[0:64, :], in0=P[5][0:64, :], scalar1=-1.0, scalar2=None, op0=ALU.mult),
                       ["ps5"], ["M_wT"])
                for h in H4:
                    self.mm(c64(P[6], h), c128(Zc, h), c64(bv4, h), True, False, [Zk, "M_bv"], "ps6", partial=(h > 0))
                    self.mm(c64(P[6], h), wT4[0:64, h * 128:(h + 1) * 128], Sb4[0:64, h * 64:(h + 1) * 64], False, True,
                            ["M_wT", "M_Sb"], "ps6", partial=True)
                self.act(vnew4, P[6][:, 0:256], AF.Copy, ["ps6"], ["M_vnew"])
                if "dS9" in self.debug:
                    continue
                def Qr(h):
                    return (P[0] if h % 2 == 0 else P[5])[:, h * 64:(h + 1) * 64]
                for h in (0, 2):
                    hr, ch = hrs[h], chs[h]
                    self.mm(Qr(h), qTc[hr, ch, ts], Sb4[hr, h * 64:(h + 1) * 64], True, True, [("M_qTc", ch), "M_Sb"], "ps0",
                            partial=(h > 0))
                for h in H4:
                    self.mm(c64(P[1], h), c128(attm4, h), c64(vnew4, h), True, True, ["M_attm", "M_vnew"], "ps1", partial=(h > 0))
                for h in (1, 3):
                    hr, ch = hrs[h], chs[h]
                    self.mm(Qr(h), qTc[hr, ch, ts], Sb4[hr, h * 64:(h + 1) * 64], True, True, [("M_qTc", ch), "M_Sb"], "ps5",
                            partial=(h > 1))
                for h in H4:
                    self.mm(c64(P[4], h), c128(khat4, h), c64(vnew4, h), True, True, ["M_khat", "M_vnew"], "ps4", partial=(h > 0))
                for h in H4:
                    col = cols[h]
                    self.V(lambda e, h=h, tt=tt, col=col: e.tensor_scalar(
                        out=c64(otmp4, h), in0=Qr(h), scalar1=eg[:, tt, col:col + 1], scalar2=None, op0=ALU.mult),
                        ["ps0" if h % 2 == 0 else "ps5", ("M_eg", tt)], ["M_otmp"], partial=(h > 0))
                if d == 0:
                    self.V(lambda e, tt=tt: e.tensor_tensor(out=oc[:, tt, :], in0=otmp4, in1=P[1][:, 0:256], op=ALU.add),
                           ["M_otmp", "ps1"], [("M_oc", tt)])
                else:
                    self.V(lambda e: e.tensor_tensor(out=otmp4, in0=otmp4, in1=P[1][:, 0:256], op=ALU.add), ["M_otmp", "ps1"], ["M_otmp"])
                    self.G(lambda e, tt=tt: e.tensor_tensor(out=oc[:, tt, :], in0=oc[:, tt, :], in1=otmp4, op=ALU.add),
                           ["M_otmp", ("M_oc", tt)], [("M_oc", tt)])
                sn, snk = Snew4[itn % 2], "M_Snew%d" % (itn % 2)
                itn += 1
                for h in H4:
                    col = cols[h]
                    self.V(lambda e, sn=sn, h=h, tt=tt, col=col: e.scalar_tensor_tensor(
                        out=c64(sn, h), in0=c64(Sf4, h), scalar=egl[:, tt, col:col + 1], in1=c64(P[4], h), op0=ALU.mult, op1=ALU.add),
                        ["M_Sf", ("M_egl", tt), "ps4"], [snk], partial=(h > 0))
                is_out = (tt % 2 == 1) if d == 0 else (tt % 2 == 0)
                if is_out:
                    for h in H4:
                        self.store(self.sd_out[l, d, tt // 2, h * 64:(h + 1) * 64, :], sn[0:64, h * 64:(h + 1) * 64], snk)
                if n < NT - 1:
                    nxt = order[n + 1]
                    kcol = d * NT + nxt
                    self.V(lambda e, sn=sn, kcol=kcol: e.tensor_scalar(out=Sf4, in0=sn, scalar1=self.keep_s[:, kcol:kcol + 1],
                                                                      scalar2=None, op0=ALU.mult), [snk, "keep_s"], ["M_Sf"])
                    self.act(Sb4, Sf4, AF.Copy, ["M_Sf"], ["M_Sb"])

        sq = otmp4
        st4 = self.ar([8])
        for tt in range(NT):
            ock = [("M_oc", tt)]
            self.V(lambda e, tt=tt: e.tensor_tensor(out=sq, in0=oc[:, tt, :], in1=oc[:, tt, :], op=ALU.mult), ock, ["M_otmp"])
            self.V(lambda e: e.tensor_reduce(out=st4[:, 0:4], in_=sq.rearrange("p (h d) -> p h d", h=4), axis=AX.X, op=ALU.add),
                   ["M_otmp"], ["M_st4"])
            self.act(st4[:, 4:8], st4[:, 0:4], AF.Ln, ["M_st4"], ["M_rs4"], bias=EPS, scale=1.0 / 64)
            self.act(st4[:, 4:8], st4[:, 4:8], AF.Exp, ["M_rs4"], ["M_rs4"], scale=-0.5)
            self.V(lambda e, tt=tt: e.tensor_tensor(out=sq.rearrange("p (h d) -> p h d", h=4),
                                                    in0=oc[:, tt, :].rearrange("p (h d) -> p h d", h=4),
                                                    in1=st4[:, 4:8].unsqueeze(2).to_broadcast([128, 4, 64]), op=ALU.mult),
                   ock + ["M_rs4"], ["M_otmp"])
            self.V(lambda e: e.tensor_tensor(out=sq, in0=sq, in1=dnrep, op=ALU.mult),
                   ["M_otmp", "M_dn0"] + [("M_dn", h) for h in range(1, 4)], ["M_otmp"])
            self.V(lambda e, tt=tt: e.tensor_tensor(out=self.cat[:, tt, 768:1024], in0=sq, in1=sgc[:, tt, :], op=ALU.mult),
                   ["M_otmp", ("M_sgc", tt)], [("cat", tt, "c")])

    def resid_update(self, tt, ps_lo, lo_key, ps_hi, hi_key, GG, ggkeys, lo_is_sbuf=False):
        st = self.ssq2
        self.act(self.junk[:, 0:512], ps_lo, AF.Square, [lo_key], ["junk", "ssq2"], accum_out=st[:, 0:1])
        self.act(self.junk[:, 512:1024], ps_hi, AF.Square, [hi_key], ["junk", "ssq2"], accum_out=st[:, 1:2], partial=True)
        self.V(lambda e: e.tensor_tensor(out=st[:, 2:3], in0=st[:, 0:1], in1=st[:, 1:2], op=ALU.add), ["ssq2"], ["ssq2s"])
        self.act(st[:, 3:4], st[:, 2:3], AF.Ln, ["ssq2s"], ["rstd2"], bias=EPS, scale=1.0 / D)
        self.act(st[:, 3:4], st[:, 3:4], AF.Exp, ["rstd2"], ["rstd2"], scale=-0.5)
        tmp = self.tmpf[0]
        for h, (src, key) in enumerate(((ps_lo, lo_key), (ps_hi, hi_key))):
            hs = slice(h * 512, (h + 1) * 512)
            self.V(lambda e, src=src, hs=hs: e.scalar_tensor_tensor(out=tmp[:, hs], in0=src, scalar=st[:, 3:4], in1=GG[:, hs],
                                                                   op0=ALU.mult, op1=ALU.mult),
                   [key, "rstd2"] + list(ggkeys), ["tmpf0"], partial=(h > 0))
        self.V(lambda e: e.tensor_tensor(out=self.xs[:, tt, :], in0=self.xs[:, tt, :], in1=tmp[:], op=ALU.add),
               ["tmpf0", ("xs", tt)], [("xs", tt)])

    def out_proj(self, l):
        for tt in range(NT):
            self.transpose_tile_to_actT(self.cat[:, tt, :], [("cat", tt, "a0"), ("cat", tt, "a1"), ("cat", tt, "b"), ("cat", tt, "c")], tt)
        r0, w0 = self.load_w_piece(self.w_out[l], 0, 512)
        r1, w1 = self.load_w_piece(self.w_out[l], 512, 1024)
        ggk = [("GG", 0, 0), ("GG", 0, 1)]
        for tt in range(NT):
            b = tt % 2
            pa, pb = self.ps[0 + b], self.ps[2 + b]
            pak, pbk = "ps%d" % b, "ps%d" % (2 + b)
            ts = slice(tt * 128, (tt + 1) * 128)
            for kc in range(8):
                self.mm(pa[:], self.actT[:, kc, ts], w0[:, kc, :], kc == 0, kc == 7, [("actT", tt), "ring%d" % r0], pak)
            for kc in range(8):
                self.mm(pb[:], self.actT[:, kc, ts], w1[:, kc, :], kc == 0, kc == 7, [("actT", tt), "ring%d" % r1], pbk)
            self.resid_update(tt, pa[:], pak, pb[:], pbk, self.GGm, ggk)

    def ffn(self, l):
        self.norm_to_actT(1)
        self.ar_reset()
        aT = self.ar([NFC, T], BF16)
        sg = [self.ar([512]) for _ in range(2)]
        fbuf = self.ar([NT, 512])
        it = 0
        for c0 in range(0, DFF, 512):
            c1 = min(c0 + 512, DFF)
            rg, wg = self.load_w_piece(self.w_gate[l], c0, c1)
            ru, wu = self.load_w_piece(self.w_up[l], c0, c1)
            for sub in range((c1 - c0) // 128):
                fc = c0 // 128 + sub
                for half in range(2):
                    hs = slice(half * 512, (half + 1) * 512)
                    b = it % 2
                    it += 1
                    pg, pu = self.ps[0 + b], self.ps[2 + b]
                    pgk, puk = "ps%d" % b, "ps%d" % (2 + b)
                    rd = [("actT", t) for t in range(half * 4, half * 4 + 4)]
                    for kc in range(8):
                        self.mm(pg[:], wg[:, kc, sub * 128:(sub + 1) * 128], self.actT[:, kc, hs], kc == 0, kc == 7,
                                rd + ["ring%d" % rg], pgk)
                    for kc in range(8):
                        self.mm(pu[:], wu[:, kc, sub * 128:(sub + 1) * 128], self.actT[:, kc, hs], kc == 0, kc == 7,
                                rd + ["ring%d" % ru], puk)
                    sgt, sgk = sg[b], "F_sg%d" % b
                    self.act(sgt, pg[:], AF.Silu, [pgk], [sgk])
                    self.V(lambda e, sgt=sgt, pu=pu, fc=fc, hs=hs: e.tensor_tensor(out=aT[:, fc, hs], in0=sgt, in1=pu[:],
                                                                                    op=ALU.mult),
                           [sgk, puk], [("F_aT", fc, half)])
        ggk = [("GG", 1, 0), ("GG", 1, 1)]
        groups = [(0, 8), (8, 8), (16, 6)]
        allaT = [("F_aT", fc, h) for fc in range(NFC) for h in range(2)]
        for half in range(2):
            pieces = []
            for (f0, nk) in groups:
                ri, wt = self.load_w_piece(self.w_down[l], half * 512, (half + 1) * 512, r0=f0 * 128, nk=nk)
                pieces.append((ri, wt, f0, nk))
            for tt in range(NT):
                b = tt % 2
                pf = self.ps[4 + b]
                pfk = "ps%d" % (4 + b)
                ts = slice(tt * 128, (tt + 1) * 128)
                n = 0
                for (ri, wt, f0, nk) in pieces:
                    for k in range(nk):
                        self.mm(pf[:], aT[:, f0 + k, ts], wt[:, k, :], n == 0, n == NFC - 1, allaT + ["ring%d" % ri], pfk)
                        n += 1
                if half == 0:
                    self.V(lambda e, tt=tt, pf=pf: e.tensor_copy(out=fbuf[:, tt, :], in_=pf[:]), [pfk], [("F_fbuf", tt)])
                else:
                    self.resid_update(tt, fbuf[:, tt, :], ("F_fbuf", tt), pf[:], pfk, self.GGf, ggk)

    def build(self):
        self.setup()
        for l in range(self.depth):
            self.layer(l)
        self.finish()
        st = self.S.emit()
        self.stats = st
        return self.nc

    def finish(self):
        for tt in range(NT):
            self.store(self.y[tt * 128:(tt + 1) * 128, :], self.xs[:, tt, :], ("xs", tt))

    def layer(self, l):
        self.mod_stage(l)
        self.norm_to_actT(0)
        self.attention(l)
        if "stop_after_att" in self.debug:
            return
        if "nogla" not in self.debug:
            self.gla(l)
        if "nodelta" not in self.debug:
            self.delta(l)
        self.out_proj(l)
        self.ffn(l)
        if "cat" in self.debug and l == 0:
            d = self.dbg("cat", [128, NT, D])
            self.S.dma("gpsimd", lambda e: e.dma_start(out=d, in_=self.cat[:]), "st_cat",
                       reads=[("cat", qb, k) for qb in range(NT) for k in ("a0", "a1", "b", "c")], store=True)
        if "actT0" in self.debug and l == 0:
            d = self.dbg("actT0", [128, 8, T])
            tmp = self.ar([8, T])
            self.V(lambda e: e.tensor_copy(out=tmp, in_=self.actT[:]), [("actT", t) for t in range(NT)], ["M_dbg_actT"])
            self.store(d, tmp, "M_dbg_actT")
            d2 = self.dbg("GG", [128, 2, D])
            self.store(d2[:, 0, :], self.GGm[:], ("GG", 0, 0))
            self.store(d2[:, 0, :], self.GGm[:], ("GG", 0, 1))
            self.store(d2[:, 1, :], self.GGf[:], ("GG", 1, 0))
            self.store(d2[:, 1, :], self.GGf[:], ("GG", 1, 1))


def rope_tables(sample):
    cos = np.ones((T, 64), np.float32)
    sin = np.zeros((T, 64), np.float32)
    if sample:
        t = np.arange(T)
        row = (t // 64).astype(np.float32)
        col = (t % 64).astype(np.float32)
        inv = (np.float32(10000.0) ** (-np.arange(16, dtype=np.float32) / np.float32(16))).astype(np.float32)
        ar = row[:, None] * inv
        ac = col[:, None] * inv
        ang = np.concatenate([ar, ar, ac, ac], axis=-1).astype(np.float32)
        cos = np.cos(ang).astype(np.float32)
        sin = np.sin(ang).astype(np.float32)
    sgn = np.concatenate([-np.ones(16), np.ones(16), -np.ones(16), np.ones(16)]).astype(np.float32)
    sins = sin * sgn
    tab = np.stack([np.tile(cos, (1, 10)), np.tile(sins, (1, 10))], 1)
    return np.ascontiguousarray(tab.astype(np.float32))


def core_tables(sample):
    ab = np.zeros((12, NT), np.float32)
    keep = np.ones((2, NT), np.float32)
    if not sample:
        ab[:] = NEG
        for kc in range(4, 12):
            for qb in range(NT):
                if (kc - 4) // 2 == qb // 2:
                    ab[kc, qb] = 0.0
        for tt in range(NT):
            if tt % 2 == 0:
                keep[0, tt] = 0.0
            if tt % 2 == 1:
                keep[1, tt] = 0.0
    abias = np.ascontiguousarray(np.broadcast_to(ab.reshape(1, -1), (128, 12 * NT))).astype(np.float32)
    keepr = np.ascontiguousarray(np.broadcast_to(keep.reshape(1, -1), (128, 2 * NT))).astype(np.float32)
    cflag = np.full((128, 1), 1.0 if sample else 0.0, np.float32)
    return abias, keepr, cflag


def make_masks():
    m = np.zeros((4, 128, 128), np.float32)
    i = np.arange(128)
    m[0] = (i[:, None] <= i[None, :])
    m[1] = (i[:, None] >= i[None, :])
    m[2] = (i[:, None] < i[None, :])
    m[3] = (i[:, None] > i[None, :])
    return np.ascontiguousarray(m.transpose(1, 0, 2))


def prep_shared(inp, L=DEPTH):
    sh = {}
    sh["ident"] = np.eye(128, dtype=np.float32)
    sh["w_mod"] = np.ascontiguousarray(inp["w_mod"], dtype=np.float32)
    bm = np.asarray(inp["b_mod"], np.float32)
    sh["bmod"] = np.ascontiguousarray(bm)
    sh["bmodT"] = np.ascontiguousarray(bm.reshape(L, 48, 128).transpose(0, 2, 1))
    ng = np.asarray(inp["norm_gains"], np.float32)
    sh["ng"] = np.ascontiguousarray(ng)
    sh["ngT"] = np.ascontiguousarray(ng.reshape(L, 4, 8, 128).transpose(0, 3, 1, 2))
    sh["w_in"] = np.ascontiguousarray(inp["w_in"], dtype=np.float32)
    qg = np.asarray(inp["qk_gain"], np.float32)
    sh["qkg"] = np.ascontiguousarray(np.concatenate([np.tile(qg[:, 0], (1, 8)), np.tile(qg[:, 1], (1, 2))], axis=1))
    wgg = np.zeros((L, 33, 256), np.float32)
    w = np.asarray(inp["w_gla_gate"], np.float32)
    b = np.asarray(inp["b_gla_gate"], np.float32)
    wgg[:, 0:16, 0:128] = w[:, 0]
    wgg[:, 16:32, 128:256] = w[:, 1]
    wgg[:, 32, :] = b.reshape(L, 256)
    sh["wgg"] = wgg
    sh["gla_norm"] = np.ascontiguousarray(inp["gla_norm"], dtype=np.float32)
    cwv = np.asarray(inp["conv_w"], np.float32)
    sh["cw"] = np.ascontiguousarray(cwv.reshape(L, 5, 6, 128).transpose(0, 3, 2, 1))
    sh["alog"] = np.ascontiguousarray(np.asarray(inp["a_log"], np.float32).reshape(L, 8))
    sh["dtb"] = np.ascontiguousarray(np.asarray(inp["dt_bias"], np.float32).reshape(L, 8))
    sh["delta_norm"] = np.ascontiguousarray(inp["delta_norm"], dtype=np.float32)
    for k in ("w_out", "w_gate", "w_up", "w_down"):
        sh[k] = np.ascontiguousarray(inp[k], dtype=np.float32)
    sh["masks"] = make_masks()
    sh["sel8"] = np.ascontiguousarray(np.broadcast_to(
        np.array([1, 1, 1, 1, 0, 0, 0, 0, 0, 0, 0, 0, 1, 1, 1, 1], np.float32)[None, :], (128, 16)))
    sh["hmask"] = (np.arange(128)[:, None] // 32 == np.arange(4)[None, :]).astype(np.float32)
    i = np.arange(128)
    cmk = np.zeros((7, 128, 128), np.float32)
    for k in range(7):
        bs, bb = 2 ** k, 2 ** (k + 1)
        cmk[k] = ((i[:, None] // bb == i[None, :] // bb) & (i[:, None] // bs != i[None, :] // bs)).astype(np.float32)
    sh["cmasks"] = np.ascontiguousarray(cmk.transpose(1, 0, 2))
    return sh


PER_LAYER = ("w_mod", "b_mod", "norm_gains", "w_in", "qk_gain", "w_gla_gate", "b_gla_gate", "gla_norm", "conv_w",
             "a_log", "dt_bias", "delta_norm", "w_out", "w_gate", "w_up", "w_down")
PER_LAYER1 = ("cache_k", "cache_v", "state_gla", "state_delta")


def make_in_maps(inp, L=DEPTH):
    if L != DEPTH:
        inp = dict(inp)
        for k in PER_LAYER:
            inp[k] = np.asarray(inp[k])[:L]
        for k in PER_LAYER1:
            inp[k] = np.asarray(inp[k])[:, :L]
    sh = prep_shared(inp, L)
    xs = np.asarray(inp["x_sample"], np.float32)
    xp = np.asarray(inp["x_prompt"], np.float32)
    maps = []
    tabs = {True: (rope_tables(True),) + core_tables(True), False: (rope_tables(False),) + core_tables(False)}
    for c in range(8):
        m = dict(sh)
        sample = c < 4
        if sample:
            b = c
            m["x"] = np.ascontiguousarray(xs[b])
            cond = np.asarray(inp["c"], np.float32)[b]
            m["ctx_k"] = np.ascontiguousarray(np.asarray(inp["cache_k"], np.float32)[b].reshape(L, 512, 128))
            m["ctx_v"] = np.ascontiguousarray(np.asarray(inp["cache_v"], np.float32)[b].reshape(L, 512, 128))
            m["s0_gla"] = np.ascontiguousarray(np.asarray(inp["state_gla"], np.float32)[b].reshape(L, 2, 128, 64))
            m["s0_delta"] = np.ascontiguousarray(np.asarray(inp["state_delta"], np.float32)[b].reshape(L, 2, 256, 64))
        else:
            j = c - 4
            m["x"] = np.ascontiguousarray(xp[4 * j:4 * j + 4].reshape(T, D))
            cond = np.asarray(inp["c_ctx"], np.float32)
            m["ctx_k"] = np.zeros((L, 512, 128), np.float32)
            m["ctx_v"] = np.zeros((L, 512, 128), np.float32)
            m["s0_gla"] = np.zeros((L, 2, 128, 64), np.float32)
            m["s0_delta"] = np.zeros((L, 2, 256, 64), np.float32)
        m["condT"] = np.ascontiguousarray(cond.reshape(8, 128).T)
        rope, abias, keep, cflag = tabs[sample]
        m["rope"] = rope
        m["abias"] = abias
        m["keep"] = keep
        m["cflag"] = cflag
        maps.append(m)
    return maps


_NC_CACHE = {}


def kernel(**inputs):
    maps = make_in_maps(inputs)
    if "nc" not in _NC_CACHE:
        _NC_CACHE["nc"] = Builder().build()
    nc = _NC_CACHE["nc"]
    res = run_bass_kernel_spmd(nc, maps, core_ids=list(range(8)))
    r = res.results
    L = DEPTH
    y_sample = np.stack([r[c]["y"] for c in range(4)], 0)
    y_prompt = np.concatenate([r[c]["y"].reshape(4, 256, D) for c in range(4, 8)], 0)
    nk = np.concatenate([r[c]["kout"].reshape(L, 4, 256, 2, 64).transpose(1, 0, 2, 3, 4) for c in range(4, 8)], 0)
    nv = np.concatenate([r[c]["vout"].reshape(L, 4, 256, 2, 64).transpose(1, 0, 2, 3, 4) for c in range(4, 8)], 0)
    sg = np.concatenate([r[c]["sg_out"].reshape(L, 2, 4, 4, 32, 64).transpose(2, 0, 1, 3, 4, 5) for c in range(4, 8)], 0)
    sd = np.concatenate([r[c]["sd_out"].reshape(L, 2, 4, 4, 64, 64).transpose(2, 0, 1, 3, 4, 5) for c in range(4, 8)], 0)
    return (y_prompt.astype(np.float32), y_sample.astype(np.float32), np.ascontiguousarray(nk, dtype=np.float32),
            np.ascontiguousarray(nv, dtype=np.float32), np.ascontiguousarray(sg, dtype=np.float32),
            np.ascontiguousarray(sd, dtype=np.float32))
```
